# Optimizing a Trainium2 kernel written in Bass

```python
import jax, jax.numpy as jnp
from jax import lax
import numpy as np

D_MODEL = 2048
BATCH = 8
SEQ = 2048
DEPTH = 2

GRID_W = 64
CTX_LEN = 256
HEAD_DIM = 128
N_HEAD_SLOTS = D_MODEL // HEAD_DIM
FOURIER_GROUPS = N_HEAD_SLOTS // 4
ATTN_Q_HEADS = N_HEAD_SLOTS - FOURIER_GROUPS
ATTN_KV_HEADS = ATTN_Q_HEADS // 3
GQA_GROUP = ATTN_Q_HEADS // ATTN_KV_HEADS
WINDOW = 128
BLOCK = 128
ROPE_BASE = 10000.0
Q_END = ATTN_Q_HEADS * HEAD_DIM
KV_W = ATTN_KV_HEADS * HEAD_DIM
K_END = Q_END + KV_W
V_END = K_END + KV_W
AB_IN = V_END + FOURIER_GROUPS * HEAD_DIM
AB_MIX = Q_END + FOURIER_GROUPS * HEAD_DIM
SG_GROUPS = N_HEAD_SLOTS // 2
SG_WIDTH = SG_GROUPS * HEAD_DIM
CHUNK = 128
CONV_CH = D_MODEL - SG_WIDTH
CONV_WIDTH = 31
CD_IN = 2 * SG_WIDTH + 2 * CONV_CH
CD_MIX = SG_WIDTH + CONV_CH
N_EXPERTS = 16
CAPACITY_FACTOR = 2
EXPERT_FF = 2048
ALPHA = (2 * DEPTH) ** 0.25
BETA = (8 * DEPTH) ** -0.25
LN_EPS = 1e-6
N_EVEN = (DEPTH + 1) // 2
N_ODD = DEPTH // 2

kernel_name = "hybrid_diffusion_swa_fnet_gmlp_conformer_ecmoe"


def layer_norm(x, g=None, b=None):
    xf = x.astype(jnp.float32)
    mu = jnp.mean(xf, -1, keepdims=True)
    var = jnp.mean(jnp.square(xf - mu), -1, keepdims=True)
    y = (xf - mu) * lax.rsqrt(var + LN_EPS)
    if g is not None:
        y = y * g.astype(jnp.float32) + b.astype(jnp.float32)
    return y.astype(x.dtype)


def modulate(x, shift, scale):
    return layer_norm(x) * (1 + scale) + shift


def axial_rope_angles(n):
    rows = n // GRID_W
    r, col = jnp.meshgrid(jnp.arange(rows), jnp.arange(GRID_W), indexing="ij")
    r = r.reshape(n).astype(jnp.float32)
    col = col.reshape(n).astype(jnp.float32)
    quarter = HEAD_DIM // 4
    inv = ROPE_BASE ** (-jnp.arange(quarter, dtype=jnp.float32) / quarter)
    return r[:, None] * inv, col[:, None] * inv


def _rotate(x, ang):
    cos = jnp.cos(ang)[None, :, None, :].astype(x.dtype)
    sin = jnp.sin(ang)[None, :, None, :].astype(x.dtype)
    x1, x2 = jnp.split(x, 2, axis=-1)
    return jnp.concatenate([x1 * cos - x2 * sin, x1 * sin + x2 * cos], -1)


def apply_axial_rope(x, ang_r, ang_c):
    half = HEAD_DIM // 2
    return jnp.concatenate([_rotate(x[..., :half], ang_r), _rotate(x[..., half:], ang_c)], -1)


def banded_gqa_with_context(q, k, v, kc, vc, sink):
    b, n, hq, d = q.shape
    lc = kc.shape[1]
    nb = n // BLOCK
    kw = BLOCK + 2 * WINDOW
    scale = HEAD_DIM ** -0.5
    qb = q.reshape(b, nb, BLOCK, ATTN_KV_HEADS, GQA_GROUP, d)
    span = jnp.arange(nb)[:, None] * BLOCK + jnp.arange(kw)[None, :]
    pad = ((0, 0), (WINDOW, WINDOW), (0, 0), (0, 0))
    kb = jnp.pad(k, pad)[:, span]
    vb = jnp.pad(v, pad)[:, span]
    key_pos = span - WINDOW
    rel = (jnp.arange(BLOCK)[:, None] + WINDOW) - jnp.arange(kw)[None, :]
    mask = (jnp.abs(rel) <= WINDOW)[None] & ((key_pos >= 0) & (key_pos < n))[:, None, :]
    s_win = jnp.einsum("bnqhgd,bnkhd->bhgnqk", qb, kb).astype(jnp.float32) * scale
    s_win = jnp.where(mask, s_win, -jnp.inf)
    s_ctx = jnp.einsum("bnqhgd,bchd->bhgnqc", qb, kc).astype(jnp.float32) * scale
    s_sink = jnp.broadcast_to(sink.astype(jnp.float32).reshape(1, ATTN_KV_HEADS, GQA_GROUP, 1, 1, 1),
                              s_win.shape[:-1] + (1,))
    p = jax.nn.softmax(jnp.concatenate([s_win, s_ctx, s_sink], -1), axis=-1).astype(v.dtype)
    o = (jnp.einsum("bhgnqk,bnkhd->bnqhgd", p[..., :kw], vb)
         + jnp.einsum("bhgnqc,bchd->bnqhgd", p[..., kw:kw + lc], vc))
    return o.reshape(b, n, hq * d)


def context_gqa(qc, kc, vc, sink):
    b, lc, hq, d = qc.shape
    qg = qc.reshape(b, lc, ATTN_KV_HEADS, GQA_GROUP, d)
    s = jnp.einsum("bqhgd,bkhd->bhgqk", qg, kc).astype(jnp.float32) * HEAD_DIM ** -0.5
    s_sink = jnp.broadcast_to(sink.astype(jnp.float32).reshape(1, ATTN_KV_HEADS, GQA_GROUP, 1, 1),
                              s.shape[:-1] + (1,))
    p = jax.nn.softmax(jnp.concatenate([s, s_sink], -1), axis=-1)[..., :lc].astype(vc.dtype)
    return jnp.einsum("bhgqk,bkhd->bqhgd", p, vc).reshape(b, lc, hq * d)


def fourier_mix(z):
    b, n, _ = z.shape
    zf = z.astype(jnp.float32).reshape(b, n, FOURIER_GROUPS, HEAD_DIM)
    f = jnp.fft.fft2(zf, axes=(1, 3), norm="ortho").real
    return f.reshape(b, n, FOURIER_GROUPS * HEAD_DIM).astype(z.dtype)


def mixer_ab(h_lat, h_ctx, w_in, w_out, sink, ang_r, ang_c, ctx_out):
    b, n, _ = h_lat.shape
    lc = h_ctx.shape[1]
    p = h_lat @ w_in
    q = apply_axial_rope(p[..., :Q_END].reshape(b, n, ATTN_Q_HEADS, HEAD_DIM), ang_r, ang_c)
    k = apply_axial_rope(p[..., Q_END:K_END].reshape(b, n, ATTN_KV_HEADS, HEAD_DIM), ang_r, ang_c)
    v = p[..., K_END:V_END].reshape(b, n, ATTN_KV_HEADS, HEAD_DIM)
    pkv = h_ctx @ w_in[:, Q_END:V_END]
    kc = pkv[..., :KV_W].reshape(b, lc, ATTN_KV_HEADS, HEAD_DIM)
    vc = pkv[..., KV_W:].reshape(b, lc, ATTN_KV_HEADS, HEAD_DIM)
    attn = banded_gqa_with_context(q, k, v, kc, vc, sink)
    y_lat = jnp.concatenate([attn, fourier_mix(p[..., V_END:])], -1) @ w_out
    if not ctx_out:
        return y_lat, None
    qc = (h_ctx @ w_in[:, :Q_END]).reshape(b, lc, ATTN_Q_HEADS, HEAD_DIM)
    fc = fourier_mix(h_ctx @ w_in[:, V_END:])
    y_ctx = jnp.concatenate([context_gqa(qc, kc, vc, sink), fc], -1) @ w_out
    return y_lat, y_ctx


def mixer_cd(h, w_in, w_out, sg_ln_g, sg_ln_b, sg_w, sg_b, conv_w, conv_b, conv_ln_g, conv_ln_b):
    b, n, _ = h.shape
    p = h @ w_in
    z = jax.nn.gelu(p[..., :2 * SG_WIDTH])
    u = z[..., :SG_WIDTH].reshape(b, n // CHUNK, CHUNK, SG_GROUPS, HEAD_DIM)
    vg = layer_norm(z[..., SG_WIDTH:].reshape(b, n, SG_GROUPS, HEAD_DIM), sg_ln_g, sg_ln_b)
    vg = vg.reshape(b, n // CHUNK, CHUNK, SG_GROUPS, HEAD_DIM)
    spatial = jnp.einsum("gpq,bnqgc->bnpgc", sg_w, vg) + sg_b.T[None, None, :, :, None]
    y_sg = (u * spatial).reshape(b, n, SG_WIDTH)
    a, gt = jnp.split(p[..., 2 * SG_WIDTH:], 2, axis=-1)
    xg = a * jax.nn.sigmoid(gt)
    pad = CONV_WIDTH // 2
    xc = lax.conv_general_dilated(xg, conv_w, window_strides=(1,), padding=[(pad, pad)],
                                  dimension_numbers=("NWC", "WIO", "NWC"),
                                  feature_group_count=CONV_CH) + conv_b
    y_cv = jax.nn.silu(layer_norm(xc, conv_ln_g, conv_ln_b))
    return jnp.concatenate([y_sg, y_cv], -1) @ w_out


def expert_choice_ffn(h, w_router, w_gate, w_up, w_down):
    b, n, _ = h.shape
    cap = CAPACITY_FACTOR * n // N_EXPERTS
    aff = jax.nn.softmax(jnp.einsum("bnd,de->bne", h, w_router).astype(jnp.float32), axis=-1)
    g, idx = lax.top_k(jnp.swapaxes(aff, 1, 2), cap)
    bidx = jnp.arange(b)[:, None, None]
    xs = h[bidx, idx]
    hid = (jax.nn.silu(jnp.einsum("becd,edf->becf", xs, w_gate))
           * jnp.einsum("becd,edf->becf", xs, w_up))
    ye = jnp.einsum("becf,efd->becd", hid, w_down) * g[..., None].astype(h.dtype)
    return jnp.zeros_like(h).at[bidx, idx].add(ye)


def context_read_after(l):
    return any(j % 2 == 0 for j in range(l + 1, DEPTH))


def setup_inputs(seed: int = 0) -> dict:
    key = jax.random.key(seed)
    ks = iter(jax.random.split(key, 32))

    def nrm(shape, scale):
        return jax.random.normal(next(ks), shape, jnp.float32) * scale

    D = D_MODEL
    return {
        "x": nrm((BATCH, SEQ, D), 1.0),
        "c": nrm((BATCH, D), 1.0),
        "ctx": nrm((BATCH, CTX_LEN, D), 1.0),
        "c_ctx": nrm((D,), 1.0),
        "w_mod": nrm((DEPTH, D, 6 * D), 0.5 * D ** -0.5),
        "b_mod": nrm((DEPTH, 6 * D), 0.02),
        "ln1_g": 1.0 + nrm((DEPTH, D), 0.02),
        "ln1_b": nrm((DEPTH, D), 0.02),
        "ln2_g": 1.0 + nrm((DEPTH, D), 0.02),
        "ln2_b": nrm((DEPTH, D), 0.02),
        "w_router": nrm((DEPTH, D, N_EXPERTS), D ** -0.5),
        "w_gate": nrm((DEPTH, N_EXPERTS, D, EXPERT_FF), D ** -0.5),
        "w_up": nrm((DEPTH, N_EXPERTS, D, EXPERT_FF), D ** -0.5),
        "w_down": nrm((DEPTH, N_EXPERTS, EXPERT_FF, D), BETA * EXPERT_FF ** -0.5),
        "ab_w_in": nrm((N_EVEN, D, AB_IN), D ** -0.5),
        "ab_w_out": nrm((N_EVEN, AB_MIX, D), BETA * AB_MIX ** -0.5),
        "sink": nrm((N_EVEN, ATTN_Q_HEADS), 1.0),
        "cd_w_in": nrm((N_ODD, D, CD_IN), D ** -0.5),
        "cd_w_out": nrm((N_ODD, CD_MIX, D), BETA * CD_MIX ** -0.5),
        "sg_ln_g": 1.0 + nrm((N_ODD, SG_GROUPS, HEAD_DIM), 0.02),
        "sg_ln_b": nrm((N_ODD, SG_GROUPS, HEAD_DIM), 0.02),
        "sg_w": nrm((N_ODD, SG_GROUPS, CHUNK, CHUNK), CHUNK ** -0.5),
        "sg_b": 1.0 + nrm((N_ODD, SG_GROUPS, CHUNK), 0.02),
        "conv_w": nrm((N_ODD, CONV_WIDTH, 1, CONV_CH), CONV_WIDTH ** -0.5),
        "conv_b": nrm((N_ODD, CONV_CH), 0.02),
        "conv_ln_g": 1.0 + nrm((N_ODD, CONV_CH), 0.02),
        "conv_ln_b": nrm((N_ODD, CONV_CH), 0.02),
    }


def reference(x, c, ctx, c_ctx, w_mod, b_mod, ln1_g, ln1_b, ln2_g, ln2_b, w_router, w_gate, w_up,
              w_down, ab_w_in, ab_w_out, sink, cd_w_in, cd_w_out, sg_ln_g, sg_ln_b, sg_w, sg_b,
              conv_w, conv_b, conv_ln_g, conv_ln_b):
    n = x.shape[1]
    ang_r, ang_c = axial_rope_angles(n)
    s_lat = jax.nn.silu(c)
    s_ctx = jax.nn.silu(c_ctx)
    x_lat, x_ctx = x, ctx
    for l in range(DEPTH):
        i = l // 2
        upd_ctx = context_read_after(l)
        sh1, sc1, g1, sh2, sc2, g2 = [m[:, None, :] for m in
                                      jnp.split(s_lat @ w_mod[l] + b_mod[l], 6, axis=-1)]
        use_ctx = (l % 2 == 0) or upd_ctx
        if use_ctx:
            csh1, csc1, cg1, csh2, csc2, cg2 = jnp.split(s_ctx @ w_mod[l] + b_mod[l], 6, axis=-1)
            h_ctx = modulate(x_ctx, csh1, csc1)
        h_lat = modulate(x_lat, sh1, sc1)
        if l % 2 == 0:
            y_lat, y_ctx = mixer_ab(h_lat, h_ctx, ab_w_in[i], ab_w_out[i], sink[i], ang_r, ang_c, upd_ctx)
        else:
            cd = (cd_w_in[i], cd_w_out[i], sg_ln_g[i], sg_ln_b[i], sg_w[i], sg_b[i],
                  conv_w[i], conv_b[i], conv_ln_g[i], conv_ln_b[i])
            y_lat = mixer_cd(h_lat, *cd)
            y_ctx = mixer_cd(h_ctx, *cd) if upd_ctx else None
        moe = (w_router[l], w_gate[l], w_up[l], w_down[l])
        x_lat = layer_norm(ALPHA * x_lat + g1 * y_lat, ln1_g[l], ln1_b[l])
        x_lat = layer_norm(ALPHA * x_lat + g2 * expert_choice_ffn(modulate(x_lat, sh2, sc2), *moe),
                           ln2_g[l], ln2_b[l])
        if upd_ctx:
            x_ctx = layer_norm(ALPHA * x_ctx + cg1 * y_ctx, ln1_g[l], ln1_b[l])
            x_ctx = layer_norm(ALPHA * x_ctx + cg2 * expert_choice_ffn(modulate(x_ctx, csh2, csc2), *moe),
                               ln2_g[l], ln2_b[l])
    return x_lat
```

```python
import numpy as np
import ml_dtypes
from contextlib import ExitStack
import concourse.bass as bass
import concourse.mybir as mybir
from concourse.bass_utils import run_bass_kernel_spmd

F32 = mybir.dt.float32
BF16 = mybir.dt.bfloat16
I32 = mybir.dt.int32
ALU = mybir.AluOpType
AF = mybir.ActivationFunctionType
AX = mybir.AxisListType

D = 2048
SEQ = 2048
CTX = 256
NT = SEQ // 128
NC_ = D // 128
ALPHA = 4 ** 0.25
EPS = 1e-6
NE = 16
CAP = 256


class Buf:
    __slots__ = ("name", "last_w", "reads")

    def __init__(self, name=""):
        self.name = name
        self.last_w = None
        self.reads = []


class KB:
    ND = 10

    def __init__(self, nc, es):
        self.nc = nc
        self.engs = {"pe": nc.tensor, "act": nc.scalar, "dve": nc.vector, "pool": nc.gpsimd, "sp": nc.sync}
        self.sem, self.cnt, self.seen = {}, {}, {}
        for e in self.engs:
            self.sem[e] = es.enter_context(nc.semaphore("pg_" + e))
            self.cnt[e] = 0
            self.seen[e] = {}
        self.dsem, self.dcnt, self.drr = {}, {}, {}
        for q in ("sp", "pool"):
            self.dsem[q] = [es.enter_context(nc.semaphore(f"dq_{q}{i}")) for i in range(self.ND)]
            self.dcnt[q] = [0] * self.ND
            self.drr[q] = 0
        self.nbuf = 0

    def buf(self, name=""):
        self.nbuf += 1
        return Buf(name or f"b{self.nbuf}")

    def bufs(self, n, name=""):
        return [self.buf(f"{name}{i}") for i in range(n)]

    def _wait(self, eng, tok):
        sem, val = tok
        key = id(sem)
        if self.seen[eng].get(key, 0) >= val:
            return
        self.engs[eng].wait_ge(sem, val)
        self.seen[eng][key] = val

    def _dep1(self, eng, tok):
        if tok[0] is self.sem[eng] and eng == "pe":
            return
        self._wait(eng, tok)

    def _deps(self, eng, reads, writes):
        for b in reads:
            if b.last_w is not None:
                self._dep1(eng, b.last_w)
        for b in writes:
            if b.last_w is not None:
                self._dep1(eng, b.last_w)
            for t in b.reads:
                self._dep1(eng, t)

    def _upd(self, tok, reads, writes):
        for b in reads:
            b.reads.append(tok)
        for b in writes:
            b.last_w = tok
            b.reads = []

    def op(self, eng, fn, reads=(), writes=()):
        self._deps(eng, reads, writes)
        ins = fn()
        self.cnt[eng] += 1
        ins.then_inc(self.sem[eng], 1)
        tok = (self.sem[eng], self.cnt[eng])
        self._upd(tok, reads, writes)
        return tok

    def dma(self, q, out, in_, reads=(), writes=(), indirect=None, **kw):
        i = self.drr[q]
        self.drr[q] = (i + 1) % self.ND
        sem = self.dsem[q][i]
        if self.dcnt[q][i] > 0:
            self._wait(q, (sem, self.dcnt[q][i]))
        self._deps(q, reads, writes)
        if indirect is not None:
            ins = self.engs[q].indirect_dma_start(out=out, in_=in_, **indirect)
        else:
            ins = self.engs[q].dma_start(out=out, in_=in_, **kw)
        ins.then_inc(sem, 16)
        self.dcnt[q][i] += 16
        tok = (sem, self.dcnt[q][i])
        self._upd(tok, reads, writes)
        return tok

    def barrier(self):
        toks = [(self.sem[e], self.cnt[e]) for e in self.engs if self.cnt[e] > 0]
        for q in self.dsem:
            for i, s in enumerate(self.dsem[q]):
                if self.dcnt[q][i] > 0:
                    toks.append((s, self.dcnt[q][i]))
        for e in self.engs:
            for t in toks:
                if t[0] is self.sem[e]:
                    continue
                self._wait(e, t)


def host_consts():
    c = {}
    c["ident"] = np.eye(128, dtype=np.float32)
    c["identb"] = np.eye(128).astype(ml_dtypes.bfloat16)
    t = np.arange(SEQ)
    r = (t // 64).astype(np.float32)
    col = (t % 64).astype(np.float32)
    inv = (np.float32(10000.0) ** (-np.arange(32, dtype=np.float32) / np.float32(32))).astype(np.float32)
    ang_r = (r[:, None] * inv).astype(np.float32)
    ang_c = (col[:, None] * inv).astype(np.float32)
    cosT = np.zeros((128, SEQ), np.float32)
    sinT = np.zeros((128, SEQ), np.float32)
    for base, ang in ((0, ang_r), (64, ang_c)):
        cs = np.cos(ang).astype(np.float32).T
        sn = np.sin(ang).astype(np.float32).T
        cosT[base:base + 32] = cs
        cosT[base + 32:base + 64] = cs
        sinT[base:base + 32] = -sn
        sinT[base + 32:base + 64] = sn
    c["cosT"] = cosT
    c["sinT"] = sinT
    k = np.arange(SEQ, dtype=np.int64)
    ph = (np.outer(k, k) % SEQ).astype(np.float64) * (2 * np.pi / SEQ)
    c["dft_c"] = np.cos(ph).astype(ml_dtypes.bfloat16)
    c["dft_s"] = np.sin(ph).astype(ml_dtypes.bfloat16)
    kc = np.arange(128, dtype=np.int64)
    phc = (np.outer(kc, kc) % 128).astype(np.float64) * (2 * np.pi / 128)
    c["dftc_c"] = (np.cos(phc) / 512.0).astype(ml_dtypes.bfloat16)
    c["dftc_sn"] = (-np.sin(phc) / 512.0).astype(ml_dtypes.bfloat16)
    kk = np.arange(128)[:, None]
    qq = np.arange(128)[None, :]
    mlo = (qq <= kk).astype(np.float32)
    mhi = (kk <= qq).astype(np.float32)
    c["mask_lo"] = np.tile(mlo, (1, 3)).astype(ml_dtypes.bfloat16)
    c["mask_hi"] = np.tile(mhi, (1, 3)).astype(ml_dtypes.bfloat16)
    c["iota256"] = np.tile(np.arange(256, dtype=np.float32)[None, :], (128, 1))
    c["iota_tok"] = np.tile(np.arange(SEQ, dtype=np.float32)[None, :], (128, 1))
    c["pcol"] = np.arange(128, dtype=np.float32).reshape(128, 1)
    c["tcol"] = np.tile(np.repeat(np.arange(16, dtype=np.float32), 16)[None, :], (128, 1))
    c["tri"] = (np.arange(128)[:, None] <= np.arange(128)[None, :]).astype(ml_dtypes.bfloat16)
    c["onesb"] = np.ones((128, 128), ml_dtypes.bfloat16)
    c["onesf"] = np.ones((128, 128), np.float32)
    return c


CONST_DT = {"identb": BF16, "dft_c": BF16, "dft_s": BF16, "dftc_c": BF16, "dftc_sn": BF16, "mask_lo": BF16,
            "mask_hi": BF16, "tri": BF16, "onesb": BF16}

INPUT_SHAPES = {
    "x": [SEQ, D], "c": [1, D], "ctx": [CTX, D], "c_ctx": [1, D],
    "w_mod": [2, D, 6 * D], "b_mod": [2, 6 * D], "ln1_g": [2, D], "ln1_b": [2, D], "ln2_g": [2, D], "ln2_b": [2, D],
    "w_router": [2, D, NE], "w_gate": [2, NE, D, D], "w_up": [2, NE, D, D], "w_down": [2, NE, D, D],
    "ab_w_in": [1, D, 3072], "ab_w_out": [1, D, D], "sink": [1, 12],
    "cd_w_in": [1, D, 4096], "cd_w_out": [1, D, D], "sg_ln_g": [1, 1024], "sg_ln_b": [1, 1024],
    "sg_w": [8, 128, 128], "sg_b": [1, 1024], "conv_w": [31, 1024], "conv_b": [1, 1024],
    "conv_ln_g": [1, 1024], "conv_ln_b": [1, 1024],
}


def build_program(upto=99, dbg=()):
    nc = bass.Bass("TRN2", target_bir_lowering=False)
    es = ExitStack()
    with es:
        kb = KB(nc, es)
        I = {}
        for name, shp in INPUT_SHAPES.items():
            I[name] = nc.dram_tensor(name, list(shp), F32, kind="ExternalInput").ap()
        hc = host_consts()
        C = {}
        for name, arr in hc.items():
            C[name] = nc.dram_tensor("k_" + name, list(arr.shape), CONST_DT.get(name, F32), kind="ExternalInput").ap()
        out_d = nc.dram_tensor("out", [SEQ, D], F32, kind="ExternalOutput").ap()

        def scratch(name, shape, dt=F32):
            kind = "ExternalOutput" if name in dbg else "Internal"
            return nc.dram_tensor(name, list(shape), dt, kind=kind).ap()

        mod_d = scratch("mod_d", [2, 2, 6 * D])
        qT_d = scratch("qT_d", [12, 128, SEQ], BF16)
        kT_d = scratch("kT_d", [4, 128, SEQ + CTX], BF16)
        v_d = scratch("v_d", [SEQ + CTX, 512], BF16)
        z_d = scratch("z_d", [SEQ, 512], BF16)
        mixT_d = scratch("mixT_d", [16, 128, SEQ], BF16)
        y_d = scratch("y_d", [SEQ, D])
        x1_d = scratch("x1_d", [SEQ, D])
        h2_d = scratch("h2_d", [SEQ, D], BF16)
        ye_d = scratch("ye_d", [2 * NE * 128, D], BF16)
        selT_d = scratch("selT_d", [2 * NE, 128, SEQ], BF16)
        moe_d = scratch("moe_d", [SEQ, D])
        xl1_d = scratch("xl1_d", [SEQ, D])
        uT_d = scratch("uT_d", [8, 128, SEQ], BF16)
        vg_d = scratch("vg_d", [SEQ, 1024], BF16)
        xgT_d = scratch("xgT_d", [8, 128, SEQ], BF16)
        aff_d = scratch("aff_d", [128, 256])
        B_ = {n: kb.buf(n) for n in ("mod", "qT", "kT", "v", "z", "mixT", "y", "x1", "h2", "ye", "selT", "moe", "xl1",
                                     "uT", "vg", "xgT", "aff", "out")}

        sbn = [0]

        def sb(st, name, shape, dt):
            sbn[0] += 1
            return st.enter_context(nc.sbuf_tensor(f"{name}_{sbn[0]}", list(shape), dt))

        PS = [es.enter_context(nc.psum_tensor(f"ps{i}", [128, 512], F32)) for i in range(8)]
        PSB = kb.bufs(8, "ps")
        ident = sb(es, "ident", [128, 128], F32)
        identb = sb(es, "identb", [128, 128], BF16)
        onesb = sb(es, "onesb", [128, 128], BF16)
        cB = kb.buf("consts")
        kb.dma("sp", ident[:], C["ident"], writes=[cB])
        kb.dma("sp", identb[:], C["identb"], writes=[cB])
        kb.dma("sp", onesb[:], C["onesb"], writes=[cB])

        evac_rr = [0]

        def evac(out, in_, reads, writes):
            evac_rr[0] ^= 1
            if evac_rr[0]:
                return kb.op("act", lambda: nc.scalar.copy(out=out, in_=in_), reads=reads, writes=writes)
            return kb.op("dve", lambda: nc.vector.tensor_copy(out=out, in_=in_), reads=reads, writes=writes)

        def ln_stats(st_tile, mv, rstd, nmr, xin, rB, wB):
            for k in range(4):
                kb.op("dve", lambda k=k: nc.vector.bn_stats(out=st_tile[:, k, :], in_=xin[:, k * 512:(k + 1) * 512]),
                      reads=rB, writes=[wB])
            kb.op("dve", lambda: nc.vector.bn_aggr(out=mv[:], in_=st_tile[:].rearrange("p a b -> p (a b)")), reads=[wB], writes=[wB])
            kb.op("act", lambda: nc.scalar.activation(out=rstd[:], in_=mv[:, 1:2], func=AF.Sqrt, bias=EPS, scale=1.0),
                  reads=[wB], writes=[wB])
            kb.op("dve", lambda: nc.vector.reciprocal(out=rstd[:], in_=rstd[:]), reads=[wB], writes=[wB])
            kb.op("dve", lambda: nc.vector.scalar_tensor_tensor(out=nmr[:], in0=mv[:, 0:1], scalar=-1.0, in1=rstd[:],
                                                                 op0=ALU.mult, op1=ALU.mult), reads=[wB], writes=[wB])

        def load_bc(tile, row_ap, wB, plus_one=False):
            kb.dma("sp", tile[:], row_ap.partition_broadcast(128), reads=[B_["mod"]], writes=[wB])
            if plus_one:
                kb.op("dve", lambda: nc.vector.tensor_scalar(out=tile[:], in0=tile[:], scalar1=1.0, scalar2=None, op0=ALU.add),
                      reads=[wB], writes=[wB])

        def stage_mod():
            with ExitStack() as st:
                sT = sb(st, "sT", [128, NC_, 2], F32)
                sTb = sb(st, "sTb", [128, NC_, 2], BF16)
                bm = sb(st, "bm", [2, 6 * D], F32)
                mo = sb(st, "mo", [2, 6 * D], F32)
                wb = [sb(st, f"wmod{i}", [128, NC_, 512], BF16) for i in range(2)]
                wB = kb.bufs(2, "wmod")
                sB, bB, moB = kb.buf(), kb.buf(), kb.buf()
                kb.dma("sp", sT[:, :, 0], I["c"][0].rearrange("(c p) -> p c", p=128), writes=[sB], allow_slow_non_contiguous=True)
                kb.dma("sp", sT[:, :, 1], I["c_ctx"][0].rearrange("(c p) -> p c", p=128), writes=[sB], allow_slow_non_contiguous=True)
                kb.op("act", lambda: nc.scalar.activation(out=sT[:], in_=sT[:], func=AF.Silu), reads=[sB], writes=[sB])
                kb.op("dve", lambda: nc.vector.tensor_copy(out=sTb[:], in_=sT[:]), reads=[sB], writes=[sB])
                n = 0
                for l in range(2):
                    kb.dma("sp", bm[0:1, :], I["b_mod"][l:l + 1, :], reads=[], writes=[bB])
                    kb.dma("sp", bm[1:2, :], I["b_mod"][l:l + 1, :], reads=[], writes=[bB])
                    for j in range(24):
                        w = wb[n % 2]
                        kb.dma("pool", w[:], I["w_mod"][l, :, j * 512:(j + 1) * 512].rearrange("(c p) n -> p c n", p=128),
                               writes=[wB[n % 2]])
                        ps = PS[n % 2]

                        def mm(w=w, ps=ps):
                            for cc in range(NC_):
                                ins = nc.tensor.matmul(ps[0:2, :], lhsT=sTb[:, cc, :], rhs=w[:, cc, :], start=(cc == 0), stop=(cc == NC_ - 1))
                            return ins
                        kb.op("pe", mm, reads=[sB, wB[n % 2]], writes=[PSB[n % 2]])
                        kb.op("dve", lambda ps=ps, j=j: nc.vector.tensor_tensor(out=mo[:, j * 512:(j + 1) * 512], in0=ps[0:2, :],
                                                                                 in1=bm[:, j * 512:(j + 1) * 512], op=ALU.add),
                              reads=[PSB[n % 2], bB], writes=[moB])
                        n += 1
                    kb.dma("sp", mod_d[l], mo[:], reads=[moB], writes=[B_["mod"]])
                kb.barrier()

        def stage_pre(st, src_d, srcB, ntiles, sh_row, sc_row, hT, hB, col0, tag):
            bsc = sb(st, tag + "bsc", [128, D], F32)
            bsh = sb(st, tag + "bsh", [128, D], F32)
            bcB = kb.buf()
            load_bc(bsc, sc_row, bcB, plus_one=True)
            load_bc(bsh, sh_row, bcB)
            xt = [sb(st, f"{tag}xt{i}", [128, D], F32) for i in range(2)]
            xB = kb.bufs(2)
            stt = sb(st, tag + "stt", [128, 4, 6], F32)
            mv = sb(st, tag + "mv", [128, 2], F32)
            rstd = sb(st, tag + "rstd", [128, 1], F32)
            nmr = sb(st, tag + "nmr", [128, 1], F32)
            smB = kb.buf()
            for t in range(ntiles):
                x_ = xt[t % 2]
                xb_ = xB[t % 2]
                kb.dma("sp", x_[:], src_d[t * 128:(t + 1) * 128, :], reads=[srcB], writes=[xb_])
                ln_stats(stt, mv, rstd, nmr, x_, [xb_], smB)
                kb.op("act", lambda x_=x_: nc.scalar.activation(out=x_[:], in_=x_[:], func=AF.Identity, bias=nmr[:], scale=rstd[:]),
                      reads=[smB, xb_], writes=[xb_])
                kb.op("dve", lambda x_=x_: nc.vector.tensor_tensor(out=x_[:], in0=x_[:], in1=bsc[:], op=ALU.mult), reads=[xb_, bcB], writes=[xb_])
                kb.op("dve", lambda x_=x_: nc.vector.tensor_tensor(out=x_[:], in0=x_[:], in1=bsh[:], op=ALU.add), reads=[xb_, bcB], writes=[xb_])
                for q4 in range(4):
                    pb = (t * 4 + q4) % 8

                    def tr(x_=x_, q4=q4, pb=pb):
                        for k in range(4):
                            cc = q4 * 4 + k
                            ins = nc.tensor.transpose(PS[pb][:, k * 128:(k + 1) * 128], x_[:, cc * 128:(cc + 1) * 128], ident[:])
                        return ins
                    kb.op("pe", tr, reads=[xb_, cB], writes=[PSB[pb]])
                    evac(hT[:, q4 * 4:(q4 + 1) * 4, col0 + t * 128: col0 + (t + 1) * 128],
                         PS[pb][:].rearrange("p (k n) -> p k n", k=4), [PSB[pb]], [hB])

        def stage_gemm_out(st, mixT, mB, w_dram, tag):
            wb = [sb(st, f"{tag}w{i}", [128, NC_, 512], BF16) for i in range(2)]
            wB = kb.bufs(2)
            stg = [sb(st, f"{tag}stg{i}", [128, 512], F32) for i in range(3)]
            sgB = kb.bufs(3)
            n = 0
            for jb in range(4):
                kb.dma("pool", wb[jb % 2][:], w_dram[:, jb * 512:(jb + 1) * 512].rearrange("(c p) n -> p c n", p=128), writes=[wB[jb % 2]])
                for t in range(NT):
                    pb = n % 4

                    def mm(jb=jb, t=t, pb=pb):
                        for cc in range(NC_):
                            ins = nc.tensor.matmul(PS[pb][:], lhsT=mixT[:, cc, t * 128:(t + 1) * 128], rhs=wb[jb % 2][:, cc, :],
                                                   start=(cc == 0), stop=(cc == NC_ - 1))
                        return ins
                    kb.op("pe", mm, reads=[mB, wB[jb % 2]], writes=[PSB[pb]])
                    s_ = n % 3
                    evac(stg[s_][:], PS[pb][:], [PSB[pb]], [sgB[s_]])
                    kb.dma("sp", y_d[t * 128:(t + 1) * 128, jb * 512:(jb + 1) * 512], stg[s_][:], reads=[sgB[s_]], writes=[B_["y"]])
                    n += 1

        def stage_inproj_ab(st, hT, hB):
            TT = SEQ + CTX
            w_in = I["ab_w_in"][0]
            cosT = sb(st, "cosT", [128, SEQ], F32)
            sinT = sb(st, "sinT", [128, SEQ], F32)
            rB = kb.buf()
            kb.dma("sp", cosT[:], C["cosT"], writes=[rB])
            kb.dma("sp", sinT[:], C["sinT"], writes=[rB])
            wb = [sb(st, f"abw{i}", [128, NC_, 512], BF16) for i in range(2)]
            ws = sb(st, "abws", [128, NC_, 512], BF16)
            wB = kb.bufs(2)
            wsB = kb.buf()
            t1 = [sb(st, f"rt1_{i}", [128, 512], F32) for i in range(2)]
            t2 = [sb(st, f"rt2_{i}", [128, 512], F32) for i in range(2)]
            tB = kb.bufs(2)
            stg = [sb(st, f"abstg{i}", [128, 512], BF16) for i in range(3)]
            sgB = kb.bufs(3)
            n = 0
            ns = 0
            for jb in range(6):
                w = wb[jb % 2]
                kb.dma("pool", w[:], w_in[:, jb * 512:(jb + 1) * 512].rearrange("(c p) n -> p c n", p=128), writes=[wB[jb % 2]])
                if jb < 4:
                    wv = w[:].rearrange("p c (g t j) -> p (c g) t j", t=2, j=32)
                    sv = ws[:].rearrange("p c (g t j) -> p (c g) t j", t=2, j=32)
                    kb.op("act", lambda wv=wv, sv=sv: nc.scalar.copy(out=sv[:, :, 0, :], in_=wv[:, :, 1, :]), reads=[wB[jb % 2]], writes=[wsB])
                    kb.op("dve", lambda wv=wv, sv=sv: nc.vector.tensor_copy(out=sv[:, :, 1, :], in_=wv[:, :, 0, :]), reads=[wB[jb % 2]], writes=[wsB])
                    for hh in range(4):
                        head = jb * 4 + hh
                        isk = head >= 12
                        for tb in range(4):
                            pa, pb = (n * 2) % 8, (n * 2 + 1) % 8
                            n += 1

                            def mm(wt, p_, hh=hh, tb=tb):
                                for cc in range(NC_):
                                    ins = nc.tensor.matmul(PS[p_][:], lhsT=wt[:, cc, hh * 128:(hh + 1) * 128],
                                                           rhs=hT[:, cc, tb * 512:(tb + 1) * 512], start=(cc == 0), stop=(cc == NC_ - 1))
                                return ins
                            kb.op("pe", lambda: mm(w, pa), reads=[hB, wB[jb % 2]], writes=[PSB[pa]])
                            kb.op("pe", lambda: mm(ws, pb), reads=[hB, wsB], writes=[PSB[pb]])
                            k2 = ns % 2
                            s_ = ns % 3
                            ns += 1
                            kb.op("dve", lambda: nc.vector.tensor_tensor(out=t1[k2][:], in0=PS[pa][:], in1=cosT[:, tb * 512:(tb + 1) * 512], op=ALU.mult),
                                  reads=[PSB[pa], rB], writes=[tB[k2]])
                            kb.op("dve", lambda: nc.vector.tensor_tensor(out=t2[k2][:], in0=PS[pb][:], in1=sinT[:, tb * 512:(tb + 1) * 512], op=ALU.mult),
                                  reads=[PSB[pb], rB], writes=[tB[k2]])
                            kb.op("dve", lambda: nc.vector.tensor_tensor(out=stg[s_][:], in0=t1[k2][:], in1=t2[k2][:], op=ALU.add),
                                  reads=[tB[k2]], writes=[sgB[s_]])
                            if isk:
                                kb.dma("sp", kT_d[head - 12, :, tb * 512:(tb + 1) * 512], stg[s_][:], reads=[sgB[s_]], writes=[B_["kT"]])
                            else:
                                kb.dma("sp", qT_d[head, :, tb * 512:(tb + 1) * 512], stg[s_][:], reads=[sgB[s_]], writes=[B_["qT"]])
                        if isk:
                            pa = (n * 2) % 8
                            n += 1

                            def mmc(hh=hh, pa=pa):
                                for cc in range(NC_):
                                    ins = nc.tensor.matmul(PS[pa][:, 0:CTX], lhsT=w[:, cc, hh * 128:(hh + 1) * 128],
                                                           rhs=hT[:, cc, SEQ:TT], start=(cc == 0), stop=(cc == NC_ - 1))
                                return ins
                            kb.op("pe", mmc, reads=[hB, wB[jb % 2]], writes=[PSB[pa]])
                            s_ = ns % 3
                            ns += 1
                            evac(stg[s_][:, 0:CTX], PS[pa][:, 0:CTX], [PSB[pa]], [sgB[s_]])
                            kb.dma("sp", kT_d[head - 12, :, SEQ:TT], stg[s_][:, 0:CTX], reads=[sgB[s_]], writes=[B_["kT"]])
                else:
                    ntile = TT // 128 if jb == 4 else NT
                    for t in range(ntile):
                        pa = (n * 2) % 8
                        n += 1

                        def mmv(t=t, pa=pa):
                            for cc in range(NC_):
                                ins = nc.tensor.matmul(PS[pa][:], lhsT=hT[:, cc, t * 128:(t + 1) * 128], rhs=w[:, cc, :],
                                                       start=(cc == 0), stop=(cc == NC_ - 1))
                            return ins
                        kb.op("pe", mmv, reads=[hB, wB[jb % 2]], writes=[PSB[pa]])
                        s_ = ns % 3
                        ns += 1
                        evac(stg[s_][:], PS[pa][:], [PSB[pa]], [sgB[s_]])
                        if jb == 4:
                            kb.dma("sp", v_d[t * 128:(t + 1) * 128, :], stg[s_][:], reads=[sgB[s_]], writes=[B_["v"]])
                        else:
                            kb.dma("sp", z_d[t * 128:(t + 1) * 128, :], stg[s_][:], reads=[sgB[s_]], writes=[B_["z"]])

        def stage_attn(st, mixT, mB):
            TT = SEQ + CTX
            mlo = sb(st, "mlo", [128, 384], BF16)
            mhi = sb(st, "mhi", [128, 384], BF16)
            snk = sb(st, "snk", [128, 12], F32)
            esk = sb(st, "esk", [128, 12, 128], F32)
            kB_ = kb.buf()
            kb.dma("sp", mlo[:], C["mask_lo"], writes=[kB_])
            kb.dma("sp", mhi[:], C["mask_hi"], writes=[kB_])
            kb.dma("sp", snk[:], I["sink"][0:1, :].partition_broadcast(128), writes=[kB_])
            kb.op("act", lambda: nc.scalar.activation(out=snk[:], in_=snk[:], func=AF.Exp), reads=[kB_], writes=[kB_])
            kb.op("dve", lambda: nc.vector.tensor_copy(out=esk[:], in_=snk[:].unsqueeze(2).to_broadcast([128, 12, 128])), reads=[kB_], writes=[kB_])
            kT = [sb(st, f"kT{i}", [128, TT], BF16) for i in range(2)]
            vt = [sb(st, f"vt{i}", [128, TT // 128, 128], BF16) for i in range(2)]
            qT = [sb(st, f"qT{i}", [128, 3, SEQ], BF16) for i in range(2)]
            hdB = kb.bufs(2)
            pT = [sb(st, f"pT{i}", [128, 384], BF16) for i in range(4)]
            pB = kb.bufs(4)
            den = [sb(st, f"den{i}", [128, 384], F32) for i in range(2)]
            dB = kb.bufs(2)
            scale = 128 ** -0.5
            npt = 0
            it = 0
            for h in range(4):
                k2 = h % 2
                kb.dma("sp", kT[k2][:], kT_d[h], reads=[B_["kT"]], writes=[hdB[k2]])
                kb.dma("sp", vt[k2][:], v_d[:, h * 128:(h + 1) * 128].rearrange("(t p) d -> p t d", p=128), reads=[B_["v"]], writes=[hdB[k2]])
                kb.dma("sp", qT[k2][:], qT_d[3 * h:3 * h + 3].rearrange("g p n -> p g n"), reads=[B_["qT"]], writes=[hdB[k2]])
                for i in range(NT):
                    blocks = []
                    if i > 0:
                        blocks.append((i - 1, mlo))
                    blocks.append((i, None))
                    if i < NT - 1:
                        blocks.append((i + 1, mhi))
                    blocks.append((16, None))
                    blocks.append((17, None))
                    po = 4 + (it % 2) * 2
                    d2 = it % 2
                    it += 1
                    for bi, (j, msk) in enumerate(blocks):
                        sbk = npt % 4
                        pp = npt % 4
                        npt += 1
                        kb.op("pe", lambda j=j, sbk=sbk: nc.tensor.matmul(PS[sbk][:, 0:384], lhsT=kT[k2][:, j * 128:(j + 1) * 128],
                                                                         rhs=qT[k2][:, :, i * 128:(i + 1) * 128], start=True, stop=True),
                              reads=[hdB[k2]], writes=[PSB[sbk]])
                        kb.op("act", lambda sbk=sbk, pp=pp: nc.scalar.activation(out=pT[pp][:], in_=PS[sbk][:, 0:384], func=AF.Exp, scale=scale),
                              reads=[PSB[sbk]], writes=[pB[pp]])
                        if msk is not None:
                            kb.op("dve", lambda pp=pp, msk=msk: nc.vector.tensor_tensor(out=pT[pp][:], in0=pT[pp][:], in1=msk[:], op=ALU.mult),
                                  reads=[pB[pp], kB_], writes=[pB[pp]])
                        first, last = bi == 0, bi == len(blocks) - 1

                        def mm2(j=j, pp=pp, first=first, last=last):
                            nc.tensor.matmul(PS[po][:, 0:384], lhsT=vt[k2][:, j, :], rhs=pT[pp][:], start=first, stop=last)
                            return nc.tensor.matmul(PS[po + 1][:, 0:384], lhsT=onesb[:], rhs=pT[pp][:], start=first, stop=last)
                        kb.op("pe", mm2, reads=[hdB[k2], pB[pp], cB], writes=[PSB[po], PSB[po + 1]])
                    dn = den[d2]
                    kb.op("dve", lambda dn=dn: nc.vector.tensor_tensor(out=dn[:], in0=PS[po + 1][:, 0:384],
                                                                     in1=esk[:, 3 * h:3 * h + 3, :].rearrange("p g n -> p (g n)"), op=ALU.add),
                          reads=[PSB[po + 1], kB_], writes=[dB[d2]])
                    kb.op("dve", lambda dn=dn: nc.vector.reciprocal(out=dn[:], in_=dn[:]), reads=[dB[d2]], writes=[dB[d2]])
                    kb.op("dve", lambda dn=dn: nc.vector.tensor_tensor(out=mixT[:, 3 * h:3 * h + 3, i * 128:(i + 1) * 128],
                                                                     in0=PS[po][:, 0:384].rearrange("p (g n) -> p g n", g=3),
                                                                     in1=dn[:].rearrange("p (g n) -> p g n", g=3), op=ALU.mult),
                          reads=[PSB[po], dB[d2]], writes=[mB])

        def stage_fourier(st, mixT, mB):
            Z = sb(st, "fz", [128, NT, 512], BF16)
            zB = kb.buf()
            kb.dma("sp", Z[:], z_d.rearrange("(t p) c -> p t c", p=128), reads=[B_["z"]], writes=[zB])
            cc_ = sb(st, "fcc", [128, 128], BF16)
            csn = sb(st, "fcs", [128, 128], BF16)
            kb.dma("sp", cc_[:], C["dftc_c"], writes=[zB])
            kb.dma("sp", csn[:], C["dftc_sn"], writes=[zB])
            Cn = [sb(st, f"fCn{i}", [128, NT, 512], BF16) for i in range(2)]
            Sn = [sb(st, f"fSn{i}", [128, NT, 512], BF16) for i in range(2)]
            tbB = kb.bufs(2)
            ab = [sb(st, f"fab{i}", [128, 512], BF16) for i in range(2)]
            bb = [sb(st, f"fbb{i}", [128, 512], BF16) for i in range(2)]
            abB = kb.bufs(2)
            n = 0
            for nb in range(4):
                k2 = nb % 2
                kb.dma("sp", Cn[k2][:], C["dft_c"][:, nb * 512:(nb + 1) * 512].rearrange("(t p) n -> p t n", p=128), writes=[tbB[k2]])
                kb.dma("sp", Sn[k2][:], C["dft_s"][:, nb * 512:(nb + 1) * 512].rearrange("(t p) n -> p t n", p=128), writes=[tbB[k2]])
                for g in range(4):
                    pa, pb, pc = (n * 3) % 6, (n * 3 + 1) % 6, 6 + n % 2
                    a2 = n % 2
                    n += 1

                    def mmA(tab, p_, g=g):
                        for t in range(NT):
                            ins = nc.tensor.matmul(PS[p_][:], lhsT=Z[:, t, g * 128:(g + 1) * 128], rhs=tab[:, t, :], start=(t == 0), stop=(t == NT - 1))
                        return ins
                    kb.op("pe", lambda: mmA(Cn[k2], pa), reads=[zB, tbB[k2]], writes=[PSB[pa]])
                    kb.op("pe", lambda: mmA(Sn[k2], pb), reads=[zB, tbB[k2]], writes=[PSB[pb]])
                    kb.op("act", lambda: nc.scalar.copy(out=ab[a2][:], in_=PS[pa][:]), reads=[PSB[pa]], writes=[abB[a2]])
                    kb.op("dve", lambda: nc.vector.tensor_copy(out=bb[a2][:], in_=PS[pb][:]), reads=[PSB[pb]], writes=[abB[a2]])

                    def mmB():
                        nc.tensor.matmul(PS[pc][:], lhsT=cc_[:], rhs=ab[a2][:], start=True, stop=False)
                        return nc.tensor.matmul(PS[pc][:], lhsT=csn[:], rhs=bb[a2][:], start=False, stop=True)
                    kb.op("pe", mmB, reads=[zB, abB[a2]], writes=[PSB[pc]])
                    evac(mixT[:, 12 + g, nb * 512:(nb + 1) * 512], PS[pc][:], [PSB[pc]], [mB])

        def stage_inproj_cd(st, hT, hB):
            w_in = I["cd_w_in"][0]
            wb = [sb(st, f"cdw{i}", [128, NC_, 512], BF16) for i in range(4)]
            wB = kb.bufs(4)
            lng = sb(st, "sglng", [128, 1024], F32)
            lnb = sb(st, "sglnb", [128, 1024], F32)
            lB = kb.buf()
            kb.dma("sp", lng[:], I["sg_ln_g"][0:1, :].partition_broadcast(128), writes=[lB])
            kb.dma("sp", lnb[:], I["sg_ln_b"][0:1, :].partition_broadcast(128), writes=[lB])
            stg = [sb(st, f"cdstg{i}", [128, 512], BF16) for i in range(3)]
            sgB = kb.bufs(3)
            zt = [sb(st, f"cdz{i}", [128, 512], F32) for i in range(2)]
            zB = kb.bufs(2)
            stt = sb(st, "cdstt", [128, 4, 6], F32)
            mv4 = sb(st, "cdmv4", [128, 4, 2], F32)
            rs4 = sb(st, "cdrs4", [128, 4], F32)
            smB = kb.buf()
            nw = 0
            n = 0
            ns = 0

            def loadw(jb):
                nonlocal nw
                k = nw % 4
                nw += 1
                kb.dma("pool", wb[k][:], w_in[:, jb * 512:(jb + 1) * 512].rearrange("(c p) n -> p c n", p=128), writes=[wB[k]])
                return k
            for jb in range(2):
                k = loadw(jb)
                for fc in range(4):
                    for tb in range(4):
                        pa = n % 8
                        n += 1

                        def mm(k=k, fc=fc, tb=tb, pa=pa):
                            for cc in range(NC_):
                                ins = nc.tensor.matmul(PS[pa][:], lhsT=wb[k][:, cc, fc * 128:(fc + 1) * 128], rhs=hT[:, cc, tb * 512:(tb + 1) * 512],
                                                       start=(cc == 0), stop=(cc == NC_ - 1))
                            return ins
                        kb.op("pe", mm, reads=[hB, wB[k]], writes=[PSB[pa]])
                        s_ = ns % 3
                        ns += 1
                        kb.op("act", lambda: nc.scalar.activation(out=stg[s_][:], in_=PS[pa][:], func=AF.Gelu_apprx_tanh), reads=[PSB[pa]], writes=[sgB[s_]])
                        kb.dma("sp", uT_d[jb * 4 + fc, :, tb * 512:(tb + 1) * 512], stg[s_][:], reads=[sgB[s_]], writes=[B_["uT"]])
            for jb in range(2, 4):
                k = loadw(jb)
                for t in range(NT):
                    pa = n % 8
                    n += 1

                    def mmv(k=k, t=t, pa=pa):
                        for cc in range(NC_):
                            ins = nc.tensor.matmul(PS[pa][:], lhsT=hT[:, cc, t * 128:(t + 1) * 128], rhs=wb[k][:, cc, :], start=(cc == 0), stop=(cc == NC_ - 1))
                        return ins
                    kb.op("pe", mmv, reads=[hB, wB[k]], writes=[PSB[pa]])
                    z_ = zt[t % 2]
                    zb_ = zB[t % 2]
                    kb.op("act", lambda: nc.scalar.activation(out=z_[:], in_=PS[pa][:], func=AF.Gelu_apprx_tanh), reads=[PSB[pa]], writes=[zb_])
                    for g in range(4):
                        kb.op("dve", lambda g=g: nc.vector.bn_stats(out=stt[:, g, :], in_=z_[:, g * 128:(g + 1) * 128]), reads=[zb_], writes=[smB])
                    for g in range(4):
                        kb.op("dve", lambda g=g: nc.vector.bn_aggr(out=mv4[:, g, :], in_=stt[:, g, :]), reads=[smB], writes=[smB])
                    kb.op("act", lambda: nc.scalar.activation(out=rs4[:], in_=mv4[:, :, 1], func=AF.Sqrt, bias=EPS, scale=1.0), reads=[smB], writes=[smB])
                    kb.op("dve", lambda: nc.vector.reciprocal(out=rs4[:], in_=rs4[:]), reads=[smB], writes=[smB])
                    zv = z_[:].rearrange("p (g c) -> p g c", g=4)
                    kb.op("dve", lambda: nc.vector.tensor_tensor(out=zv, in0=zv, in1=mv4[:, :, 0].unsqueeze(2).to_broadcast([128, 4, 128]), op=ALU.subtract),
                          reads=[zb_, smB], writes=[zb_])
                    kb.op("dve", lambda: nc.vector.tensor_tensor(out=zv, in0=zv, in1=rs4[:].unsqueeze(2).to_broadcast([128, 4, 128]), op=ALU.mult),
                          reads=[zb_, smB], writes=[zb_])
                    c0 = (jb - 2) * 512
                    kb.op("dve", lambda: nc.vector.tensor_tensor(out=z_[:], in0=z_[:], in1=lng[:, c0:c0 + 512], op=ALU.mult), reads=[zb_, lB], writes=[zb_])
                    s_ = ns % 3
                    ns += 1
                    kb.op("dve", lambda: nc.vector.tensor_tensor(out=stg[s_][:], in0=z_[:], in1=lnb[:, c0:c0 + 512], op=ALU.add), reads=[zb_, lB], writes=[sgB[s_]])
                    kb.dma("sp", vg_d[t * 128:(t + 1) * 128, c0:c0 + 512], stg[s_][:], reads=[sgB[s_]], writes=[B_["vg"]])
            for jb in range(2):
                ka = loadw(4 + jb)
                kg = loadw(6 + jb)
                for fc in range(4):
                    for tb in range(4):
                        pa, pg = (n * 2) % 8, (n * 2 + 1) % 8
                        n += 1

                        def mm(k, p_, fc=fc, tb=tb):
                            for cc in range(NC_):
                                ins = nc.tensor.matmul(PS[p_][:], lhsT=wb[k][:, cc, fc * 128:(fc + 1) * 128], rhs=hT[:, cc, tb * 512:(tb + 1) * 512],
                                                       start=(cc == 0), stop=(cc == NC_ - 1))
                            return ins
                        kb.op("pe", lambda: mm(ka, pa), reads=[hB, wB[ka]], writes=[PSB[pa]])
                        kb.op("pe", lambda: mm(kg, pg), reads=[hB, wB[kg]], writes=[PSB[pg]])
                        z_ = zt[n % 2]
                        zb_ = zB[n % 2]
                        kb.op("act", lambda: nc.scalar.activation(out=z_[:], in_=PS[pg][:], func=AF.Sigmoid), reads=[PSB[pg]], writes=[zb_])
                        s_ = ns % 3
                        ns += 1
                        kb.op("dve", lambda: nc.vector.tensor_tensor(out=stg[s_][:], in0=PS[pa][:], in1=z_[:], op=ALU.mult), reads=[PSB[pa], zb_], writes=[sgB[s_]])
                        kb.dma("sp", xgT_d[jb * 4 + fc, :, tb * 512:(tb + 1) * 512], stg[s_][:], reads=[sgB[s_]], writes=[B_["xgT"]])

        def stage_sg(st, mixT, mB):
            swn = sb(st, "swn", [128, 8, 128], F32)
            swT = sb(st, "swT", [128, 8, 128], BF16)
            sgb = sb(st, "sgb", [128, 1024], F32)
            wB_ = kb.buf()
            kb.dma("sp", swn[:], I["sg_w"].rearrange("g p q -> p g q"), writes=[wB_])
            kb.dma("sp", sgb[:], I["sg_b"][0:1, :].partition_broadcast(128), writes=[wB_])
            for g4 in range(2):
                def tr(g4=g4):
                    for k in range(4):
                        g = g4 * 4 + k
                        ins = nc.tensor.transpose(PS[g4][:, k * 128:(k + 1) * 128], swn[:, g, :], ident[:])
                    return ins
                kb.op("pe", tr, reads=[wB_, cB], writes=[PSB[g4]])
                kb.op("dve", lambda g4=g4: nc.vector.tensor_copy(out=swT[:, g4 * 4:(g4 + 1) * 4, :], in_=PS[g4][:].rearrange("p (k n) -> p k n", k=4)),
                      reads=[PSB[g4]], writes=[wB_])
            vg = sb(st, "sgvg", [128, NT, 1024], BF16)
            uT = sb(st, "sguT", [128, 8, SEQ], BF16)
            dB = kb.buf()
            kb.dma("sp", vg[:], vg_d.rearrange("(t p) c -> p t c", p=128), reads=[B_["vg"]], writes=[dB])
            kb.dma("sp", uT[:], uT_d.rearrange("g p n -> p g n"), reads=[B_["uT"]], writes=[dB])
            tmp = [sb(st, f"sgtmp{i}", [128, 512], F32) for i in range(2)]
            tB = kb.bufs(2)
            n = 0
            for g in range(8):
                for n4 in range(4):
                    pb = n % 8
                    t2 = n % 2
                    n += 1

                    def mm(g=g, n4=n4, pb=pb):
                        for k in range(4):
                            ins = nc.tensor.matmul(PS[pb][:, k * 128:(k + 1) * 128], lhsT=vg[:, n4 * 4 + k, g * 128:(g + 1) * 128], rhs=swT[:, g, :],
                                                   start=True, stop=True)
                        return ins
                    kb.op("pe", mm, reads=[dB, wB_], writes=[PSB[pb]])
                    kb.op("dve", lambda: nc.vector.tensor_tensor(out=tmp[t2][:].rearrange("p (k n) -> p k n", k=4), in0=PS[pb][:].rearrange("p (k n) -> p k n", k=4),
                                                                 in1=sgb[:, g * 128:(g + 1) * 128].unsqueeze(1).to_broadcast([128, 4, 128]), op=ALU.add),
                          reads=[PSB[pb], wB_], writes=[tB[t2]])
                    kb.op("dve", lambda: nc.vector.tensor_tensor(out=mixT[:, g, n4 * 512:(n4 + 1) * 512], in0=tmp[t2][:], in1=uT[:, g, n4 * 512:(n4 + 1) * 512], op=ALU.mult),
                          reads=[tB[t2], dB], writes=[mB])

        def stage_conv(st, mixT, mB):
            PADW = SEQ + 30
            xg = sb(st, "cvxg", [128, 8, PADW], BF16)
            xB = kb.buf()
            kb.op("dve", lambda: nc.vector.memset(xg[:, :, 0:15], 0.0), writes=[xB])
            kb.op("dve", lambda: nc.vector.memset(xg[:, :, 15 + SEQ:PADW], 0.0), writes=[xB])
            kb.dma("sp", xg[:, :, 15:15 + SEQ], xgT_d.rearrange("g p n -> p g n"), reads=[B_["xgT"]], writes=[xB])
            cwn = sb(st, "cvwn", [31, 1024], F32)
            cw = sb(st, "cvw", [128, 8, 32], F32)
            cb = sb(st, "cvb", [128, 8], F32)
            lg = sb(st, "cvlg", [128, 8], F32)
            lb = sb(st, "cvlb", [128, 8], F32)
            onesf = sb(st, "cvones", [128, 128], F32)
            pB = kb.buf()
            kb.dma("sp", cwn[:], I["conv_w"], writes=[pB])
            kb.dma("sp", onesf[:], C["onesf"], writes=[pB])
            kb.dma("sp", cb[:], I["conv_b"][0].rearrange("(g p) -> p g", p=128), writes=[pB], allow_slow_non_contiguous=True)
            kb.dma("sp", lg[:], I["conv_ln_g"][0].rearrange("(g p) -> p g", p=128), writes=[pB], allow_slow_non_contiguous=True)
            kb.dma("sp", lb[:], I["conv_ln_b"][0].rearrange("(g p) -> p g", p=128), writes=[pB], allow_slow_non_contiguous=True)

            def trw():
                for g in range(8):
                    ins = nc.tensor.transpose(PS[0][:, g * 32:g * 32 + 31], cwn[0:31, g * 128:(g + 1) * 128], ident[0:31, 0:31])
                return ins
            kb.op("pe", trw, reads=[pB, cB], writes=[PSB[0]])
            kb.op("dve", lambda: nc.vector.tensor_copy(out=cw[:, :, 0:31], in_=PS[0][:, 0:256].rearrange("p (g k) -> p g k", g=8)[:, :, 0:31]), reads=[PSB[0]], writes=[pB])
            dg = sb(st, "cvdg", [128, 8, 31, 128], BF16)
            dgB = kb.buf()
            for g in range(8):
                for k in range(31):
                    kb.op("dve", lambda g=g, k=k: nc.vector.tensor_scalar(out=dg[:, g, k, :], in0=ident[:], scalar1=cw[:, g, k:k + 1], scalar2=None, op0=ALU.mult),
                          reads=[pB, cB], writes=[dgB])
            xc = sb(st, "cvxc", [128, 8, 512], F32)
            xcB = kb.buf()
            sq = [sb(st, f"cvsq{i}", [128, 512], F32) for i in range(2)]
            sqB = kb.bufs(2)
            mean = sb(st, "cvmean", [128, 512], F32)
            rstd = sb(st, "cvrstd", [128, 512], F32)
            stB = kb.buf()
            tmp = [sb(st, f"cvtmp{i}", [128, 512], F32) for i in range(2)]
            tB = kb.bufs(2)
            n = 0
            for tb in range(4):
                for g in range(8):
                    pb = n % 4
                    n += 1

                    def mmc(g=g, pb=pb):
                        for k in range(31):
                            ins = nc.tensor.matmul(PS[pb][:], lhsT=dg[:, g, k, :], rhs=xg[:, g, tb * 512 + k: tb * 512 + k + 512], start=(k == 0), stop=(k == 30))
                        return ins
                    kb.op("pe", mmc, reads=[dgB, xB], writes=[PSB[pb]])
                    kb.op("act", lambda g=g, pb=pb: nc.scalar.activation(out=xc[:, g, :], in_=PS[pb][:], func=AF.Identity, bias=cb[:, g:g + 1], scale=1.0),
                          reads=[PSB[pb], pB], writes=[xcB])
                for g in range(8):
                    s2 = g % 2
                    kb.op("act", lambda g=g, s2=s2: nc.scalar.activation(out=sq[s2][:], in_=xc[:, g, :], func=AF.Square), reads=[xcB], writes=[sqB[s2]])

                    def mms(g=g, s2=s2):
                        nc.tensor.matmul(PS[4][:], lhsT=onesf[:], rhs=xc[:, g, :], start=(g == 0), stop=(g == 7))
                        return nc.tensor.matmul(PS[5][:], lhsT=onesf[:], rhs=sq[s2][:], start=(g == 0), stop=(g == 7))
                    kb.op("pe", mms, reads=[xcB, sqB[s2], pB], writes=[PSB[4], PSB[5]])
                kb.op("act", lambda: nc.scalar.activation(out=mean[:], in_=PS[4][:], func=AF.Copy, scale=1.0 / 1024), reads=[PSB[4]], writes=[stB])
                kb.op("dve", lambda: nc.vector.tensor_tensor(out=rstd[:], in0=mean[:], in1=mean[:], op=ALU.mult), reads=[stB], writes=[stB])
                kb.op("dve", lambda: nc.vector.scalar_tensor_tensor(out=rstd[:], in0=PS[5][:], scalar=1.0 / 1024, in1=rstd[:], op0=ALU.mult, op1=ALU.subtract),
                      reads=[PSB[5], stB], writes=[stB])
                kb.op("act", lambda: nc.scalar.activation(out=rstd[:], in_=rstd[:], func=AF.Sqrt, bias=EPS, scale=1.0), reads=[stB], writes=[stB])
                kb.op("dve", lambda: nc.vector.reciprocal(out=rstd[:], in_=rstd[:]), reads=[stB], writes=[stB])
                for g in range(8):
                    t2 = g % 2
                    kb.op("dve", lambda g=g, t2=t2: nc.vector.tensor_tensor(out=tmp[t2][:], in0=xc[:, g, :], in1=mean[:], op=ALU.subtract), reads=[xcB, stB], writes=[tB[t2]])
                    kb.op("dve", lambda g=g, t2=t2: nc.vector.tensor_tensor(out=tmp[t2][:], in0=tmp[t2][:], in1=rstd[:], op=ALU.mult), reads=[tB[t2], stB], writes=[tB[t2]])
                    kb.op("act", lambda g=g, t2=t2: nc.scalar.activation(out=mixT[:, 8 + g, tb * 512:(tb + 1) * 512], in_=tmp[t2][:], func=AF.Silu,
                                                                       bias=lb[:, g:g + 1], scale=lg[:, g:g + 1]), reads=[tB[t2], pB], writes=[mB])

        def stage_post(st, l, xsrc_d, xsrcB, aff, affB):
            bt = {}
            bcB = kb.buf()
            for nm, row, p1 in (("g1", mod_d[l, 0:1, 2 * D:3 * D], False), ("sc2", mod_d[l, 0:1, 4 * D:5 * D], True),
                                ("sh2", mod_d[l, 0:1, 3 * D:4 * D], False), ("lg", I["ln1_g"][l:l + 1, :], False),
                                ("lb", I["ln1_b"][l:l + 1, :], False)):
                bt[nm] = sb(st, "pb_" + nm, [128, D], F32)
                load_bc(bt[nm], row, bcB, plus_one=p1)
            wr = sb(st, "wr", [128, NC_, NE], F32)
            kb.dma("sp", wr[:], I["w_router"][l].rearrange("(c p) e -> p c e", p=128), writes=[bcB])
            xt = [sb(st, f"pxt{i}", [128, D], F32) for i in range(2)]
            yt = [sb(st, f"pyt{i}", [128, D], F32) for i in range(2)]
            xB = kb.bufs(2)
            yB = kb.bufs(2)
            hb = [sb(st, f"phb{i}", [128, D], BF16) for i in range(2)]
            hbB = kb.bufs(2)
            hT = sb(st, "phT", [128, NC_, 128], F32)
            hTB = kb.buf()
            stt = sb(st, "pstt", [128, 4, 6], F32)
            mv = sb(st, "pmv", [128, 2], F32)
            rstd = sb(st, "prstd", [128, 1], F32)
            nmr = sb(st, "pnmr", [128, 1], F32)
            smB = kb.buf()
            lg_ = sb(st, "plg", [128, NE], F32)
            mx = sb(st, "pmx", [128, 1], F32)
            sm = sb(st, "psm", [128, 1], F32)
            lB = kb.buf()
            for t in range(NT):
                x_, y_, xb_, yb_ = xt[t % 2], yt[t % 2], xB[t % 2], yB[t % 2]
                kb.dma("sp", x_[:], xsrc_d[t * 128:(t + 1) * 128, :], reads=[xsrcB], writes=[xb_])
                kb.dma("sp", y_[:], y_d[t * 128:(t + 1) * 128, :], reads=[B_["y"]], writes=[yb_])
                kb.op("dve", lambda y_=y_: nc.vector.tensor_tensor(out=y_[:], in0=y_[:], in1=bt["g1"][:], op=ALU.mult), reads=[yb_, bcB], writes=[yb_])
                kb.op("dve", lambda x_=x_, y_=y_: nc.vector.scalar_tensor_tensor(out=x_[:], in0=x_[:], scalar=float(ALPHA), in1=y_[:], op0=ALU.mult, op1=ALU.add),
                      reads=[xb_, yb_], writes=[xb_])
                ln_stats(stt, mv, rstd, nmr, x_, [xb_], smB)
                kb.op("act", lambda x_=x_: nc.scalar.activation(out=x_[:], in_=x_[:], func=AF.Identity, bias=nmr[:], scale=rstd[:]), reads=[smB, xb_], writes=[xb_])
                kb.op("dve", lambda x_=x_: nc.vector.tensor_tensor(out=x_[:], in0=x_[:], in1=bt["lg"][:], op=ALU.mult), reads=[xb_, bcB], writes=[xb_])
                kb.op("dve", lambda x_=x_: nc.vector.tensor_tensor(out=x_[:], in0=x_[:], in1=bt["lb"][:], op=ALU.add), reads=[xb_, bcB], writes=[xb_])
                kb.dma("sp", x1_d[t * 128:(t + 1) * 128, :], x_[:], reads=[xb_], writes=[B_["x1"]])
                ln_stats(stt, mv, rstd, nmr, x_, [xb_], smB)
                kb.op("act", lambda x_=x_, y_=y_: nc.scalar.activation(out=y_[:], in_=x_[:], func=AF.Identity, bias=nmr[:], scale=rstd[:]), reads=[smB, xb_], writes=[yb_])
                kb.op("dve", lambda y_=y_: nc.vector.tensor_tensor(out=y_[:], in0=y_[:], in1=bt["sc2"][:], op=ALU.mult), reads=[yb_, bcB], writes=[yb_])
                kb.op("dve", lambda y_=y_: nc.vector.tensor_tensor(out=y_[:], in0=y_[:], in1=bt["sh2"][:], op=ALU.add), reads=[yb_, bcB], writes=[yb_])
                h_ = hb[t % 2]
                kb.op("act", lambda y_=y_, h_=h_: nc.scalar.copy(out=h_[:], in_=y_[:]), reads=[yb_], writes=[hbB[t % 2]])
                kb.dma("sp", h2_d[t * 128:(t + 1) * 128, :], h_[:], reads=[hbB[t % 2]], writes=[B_["h2"]])
                for q4 in range(4):
                    pb = (t * 4 + q4) % 4

                    def tr(y_=y_, q4=q4, pb=pb):
                        for k in range(4):
                            cc = q4 * 4 + k
                            ins = nc.tensor.transpose(PS[pb][:, k * 128:(k + 1) * 128], y_[:, cc * 128:(cc + 1) * 128], ident[:])
                        return ins
                    kb.op("pe", tr, reads=[yb_, cB], writes=[PSB[pb]])
                    evac(hT[:, q4 * 4:(q4 + 1) * 4, :], PS[pb][:].rearrange("p (k n) -> p k n", k=4), [PSB[pb]], [hTB])
                pr = 4 + t % 2

                def mmr(pr=pr):
                    for cc in range(NC_):
                        ins = nc.tensor.matmul(PS[pr][:, 0:NE], lhsT=hT[:, cc, :], rhs=wr[:, cc, :], start=(cc == 0), stop=(cc == NC_ - 1))
                    return ins
                kb.op("pe", mmr, reads=[hTB, bcB], writes=[PSB[pr]])
                kb.op("dve", lambda pr=pr: nc.vector.tensor_copy(out=lg_[:], in_=PS[pr][:, 0:NE]), reads=[PSB[pr]], writes=[lB])
                kb.op("dve", lambda: nc.vector.reduce_max(out=mx[:], in_=lg_[:], axis=AX.X), reads=[lB], writes=[lB])
                kb.op("dve", lambda: nc.vector.tensor_scalar(out=mx[:], in0=mx[:], scalar1=-1.0, scalar2=None, op0=ALU.mult), reads=[lB], writes=[lB])
                kb.op("act", lambda: nc.scalar.activation(out=lg_[:], in_=lg_[:], func=AF.Exp, bias=mx[:], scale=1.0), reads=[lB], writes=[lB])
                kb.op("dve", lambda: nc.vector.reduce_sum(out=sm[:], in_=lg_[:], axis=AX.X), reads=[lB], writes=[lB])
                kb.op("dve", lambda: nc.vector.reciprocal(out=sm[:], in_=sm[:]), reads=[lB], writes=[lB])
                kb.op("dve", lambda t=t: nc.vector.tensor_scalar(out=aff[:, t, :], in0=lg_[:], scalar1=sm[:], scalar2=None, op0=ALU.mult), reads=[lB], writes=[affB])

        def stage_moe(st, l, aff, affB, nexp=NE):
            mask = sb(st, "mmask", [128, NT, NE], F32)
            maskb = sb(st, "mmaskb", [128, NT, NE], BF16)
            key = sb(st, "mkey", [128, NT, NE], F32)
            tri = sb(st, "mtri", [128, 128], BF16)
            R = sb(st, "mR", [128, NT, NE, 8], BF16)
            r1 = sb(st, "mr1", [128, NT, NE], F32)
            pcol = sb(st, "mpcol", [128, 1], F32)
            tcol = sb(st, "mtcol", [128, NT, NE], F32)
            rt = ExitStack()
            affT = sb(rt, "affT", [NE, SEQ], F32)
            work = sb(rt, "mwork", [NE, SEQ], F32)
            m8 = sb(rt, "m8", [NE, 8], F32)
            aTB = kb.buf()
            for t4 in range(4):
                def tr(t4=t4):
                    for k in range(4):
                        t = t4 * 4 + k
                        ins = nc.tensor.transpose(PS[t4][0:NE, k * 128:(k + 1) * 128], aff[:, t, :], ident[:])
                    return ins
                kb.op("pe", tr, reads=[affB, cB], writes=[PSB[t4]])
                kb.op("dve", lambda t4=t4: nc.vector.tensor_copy(out=affT[:, t4 * 512:(t4 + 1) * 512], in_=PS[t4][0:NE, :]), reads=[PSB[t4]], writes=[aTB])
            kb.op("dve", lambda: nc.vector.tensor_copy(out=work[:], in_=affT[:]), reads=[aTB], writes=[aTB])
            for r in range(CAP // 8):
                kb.op("dve", lambda: nc.vector.max(out=m8[:], in_=work[:]), reads=[aTB], writes=[aTB])
                if r < CAP // 8 - 1:
                    kb.op("dve", lambda: nc.vector.match_replace(out=work[:], in_to_replace=m8[:], in_values=work[:], imm_value=-1.0), reads=[aTB], writes=[aTB])
            kb.op("dve", lambda: nc.vector.tensor_scalar(out=work[:], in0=affT[:], scalar1=m8[:, 7:8], scalar2=None, op0=ALU.is_ge), reads=[aTB], writes=[aTB])
            mkB = kb.buf()
            for t4 in range(4):
                def tr2(t4=t4):
                    for k in range(4):
                        t = t4 * 4 + k
                        ins = nc.tensor.transpose(PS[4 + t4][:, k * NE:(k + 1) * NE], work[:, t * 128:(t + 1) * 128], ident[0:NE, 0:NE])
                    return ins
                kb.op("pe", tr2, reads=[aTB, cB], writes=[PSB[4 + t4]])
                kb.op("dve", lambda t4=t4: nc.vector.tensor_copy(out=mask[:, t4 * 4:(t4 + 1) * 4, :],
                                                               in_=PS[4 + t4][:, 0:4 * NE].rearrange("p (k e) -> p k e", k=4)), reads=[PSB[4 + t4]], writes=[mkB])
            kb.op("dve", lambda: nc.vector.tensor_copy(out=maskb[:], in_=mask[:]), reads=[mkB], writes=[mkB])
            kb.dma("sp", tri[:], C["tri"], writes=[mkB])

            def cums():
                for t in range(NT):
                    for i2 in range(t):
                        nc.tensor.matmul(PS[0][:, t * NE:(t + 1) * NE], lhsT=onesb[:], rhs=maskb[:, i2, :], start=(i2 == 0), stop=False)
                    ins = nc.tensor.matmul(PS[0][:, t * NE:(t + 1) * NE], lhsT=tri[:], rhs=maskb[:, t, :], start=(t == 0), stop=True)
                return ins
            kb.op("pe", cums, reads=[mkB, cB], writes=[PSB[0]])
            kb.op("dve", lambda: nc.vector.tensor_tensor(out=key[:].rearrange("p t e -> p (t e)"), in0=PS[0][:, 0:NT * NE],
                                                         in1=mask[:].rearrange("p t e -> p (t e)"), op=ALU.mult), reads=[PSB[0], mkB], writes=[mkB])
            kb.op("dve", lambda: nc.vector.tensor_scalar(out=key[:], in0=key[:], scalar1=-1.0, scalar2=None, op0=ALU.add), reads=[mkB], writes=[mkB])
            kb.dma("sp", pcol[:], C["pcol"], writes=[mkB])
            kb.dma("sp", tcol[:].rearrange("p t e -> p (t e)"), C["tcol"], writes=[mkB])
            kb.op("dve", lambda: nc.vector.memset(R[:], 0.0), writes=[mkB])
            kb.op("dve", lambda: nc.vector.tensor_copy(out=R[:, :, :, 0], in_=pcol[:].unsqueeze(2).to_broadcast([128, NT, NE])), reads=[mkB], writes=[mkB])
            kb.op("dve", lambda: nc.vector.tensor_copy(out=R[:, :, :, 1], in_=tcol[:]), reads=[mkB], writes=[mkB])
            kb.op("dve", lambda: nc.vector.tensor_copy(out=R[:, :, :, 2], in_=aff[:]), reads=[mkB, affB], writes=[mkB])
            kb.op("dve", lambda: nc.vector.tensor_tensor(out=r1[:], in0=aff[:], in1=R[:, :, :, 2], op=ALU.subtract), reads=[mkB, affB], writes=[mkB])
            kb.op("dve", lambda: nc.vector.tensor_copy(out=R[:, :, :, 3], in_=r1[:]), reads=[mkB], writes=[mkB])
            kb.op("dve", lambda: nc.vector.tensor_tensor(out=r1[:], in0=r1[:], in1=R[:, :, :, 3], op=ALU.subtract), reads=[mkB], writes=[mkB])
            kb.op("dve", lambda: nc.vector.tensor_copy(out=R[:, :, :, 4], in_=r1[:]), reads=[mkB], writes=[mkB])
            kb.barrier()
            rt.close()

            io256 = sb(st, "io256", [128, 256], F32)
            iotok = sb(st, "iotok", [128, SEQ], F32)
            kb.dma("sp", io256[:], C["iota256"], writes=[mkB])
            kb.dma("sp", iotok[:], C["iota_tok"], writes=[mkB])
            ex = ExitStack()
            Sel = sb(ex, "mSel", [128, NT, 256], BF16)
            selB = kb.buf()
            ig = sb(ex, "mig", [128, 2, 8], F32)
            idxf = sb(ex, "midxf", [128, 2], F32)
            idxi = sb(ex, "midxi", [128, 2], I32)
            gg = sb(ex, "mgg", [128, 2], F32)
            igB = kb.buf()
            xs = [sb(ex, f"mxs{i}", [128, 2, D], BF16) for i in range(2)]
            xsB = kb.bufs(2)
            xsT = [sb(ex, f"mxsT{i}", [128, NC_, 256], BF16) for i in range(2)]
            xsTB = kb.bufs(2)
            wg = [sb(ex, f"mwg{i}", [128, NC_, 512], BF16) for i in range(2)]
            wu = [sb(ex, f"mwu{i}", [128, NC_, 512], BF16) for i in range(2)]
            wd = [sb(ex, f"mwd{i}", [128, NC_, 512], BF16) for i in range(2)]
            wgB, wuB, wdB = kb.bufs(2), kb.bufs(2), kb.bufs(2)
            sgt = [sb(ex, f"msg{i}", [128, 256], F32) for i in range(2)]
            sgB = kb.bufs(2)
            hidT = sb(ex, "mhidT", [128, NC_, 256], BF16)
            hidB = kb.buf()
            yeb = [sb(ex, f"myeb{i}", [128, 2, D], BF16) for i in range(2)]
            yeB = kb.bufs(2)
            slT = [sb(ex, f"mslT{i}", [128, SEQ], BF16) for i in range(2)]
            slB = kb.bufs(2)
            nw = 0
            nwd = 0
            nsg = 0
            nsl = 0
            for e in range(nexp):
                e2 = e % 2
                for t in range(NT):
                    kb.op("dve", lambda t=t: nc.vector.tensor_scalar(out=Sel[:, t, :], in0=io256[:], scalar1=key[:, t, e:e + 1], scalar2=None, op0=ALU.is_equal),
                          reads=[mkB], writes=[selB])

                def mmi():
                    for half in range(2):
                        for t in range(NT):
                            ins = nc.tensor.matmul(PS[7][:, half * 8:half * 8 + 8], lhsT=Sel[:, t, half * 128:(half + 1) * 128], rhs=R[:, t, e, :],
                                                   start=(t == 0), stop=(t == NT - 1))
                    return ins
                kb.op("pe", mmi, reads=[selB, mkB], writes=[PSB[7]])
                kb.op("dve", lambda: nc.vector.tensor_copy(out=ig[:], in_=PS[7][:, 0:16].rearrange("p (h k) -> p h k", h=2)), reads=[PSB[7]], writes=[igB])
                kb.op("dve", lambda: nc.vector.scalar_tensor_tensor(out=idxf[:], in0=ig[:, :, 1], scalar=128.0, in1=ig[:, :, 0], op0=ALU.mult, op1=ALU.add),
                      reads=[igB], writes=[igB])
                kb.op("dve", lambda: nc.vector.tensor_copy(out=idxi[:], in_=idxf[:]), reads=[igB], writes=[igB])
                kb.op("dve", lambda: nc.vector.tensor_tensor(out=gg[:], in0=ig[:, :, 2], in1=ig[:, :, 3], op=ALU.add), reads=[igB], writes=[igB])
                kb.op("dve", lambda: nc.vector.tensor_tensor(out=gg[:], in0=gg[:], in1=ig[:, :, 4], op=ALU.add), reads=[igB], writes=[igB])
                for half in range(2):
                    kb.dma("pool", xs[e2][:, half, :], h2_d, reads=[igB, B_["h2"]], writes=[xsB[e2]],
                           indirect=dict(out_offset=None, in_offset=bass.IndirectOffsetOnAxis(ap=idxi[:, half:half + 1], axis=0)))
                for half in range(2):
                    s2 = nsl % 2
                    nsl += 1
                    kb.op("dve", lambda half=half, s2=s2: nc.vector.tensor_scalar(out=slT[s2][:], in0=iotok[:], scalar1=idxf[:, half:half + 1], scalar2=None, op0=ALU.is_equal),
                          reads=[igB, mkB], writes=[slB[s2]])
                    kb.dma("sp", selT_d[e * 2 + half], slT[s2][:], reads=[slB[s2]], writes=[B_["selT"]])
                for half in range(2):
                    for cg in range(2):
                        pb = half * 2 + cg
                        psv = PS[pb][:].bitcast(BF16)

                        def trx(half=half, cg=cg, psv=psv):
                            for k in range(8):
                                cc = cg * 8 + k
                                ins = nc.tensor.transpose(psv[:, k * 128:(k + 1) * 128], xs[e2][:, half, cc * 128:(cc + 1) * 128], identb[:])
                            return ins
                        kb.op("pe", trx, reads=[xsB[e2], cB], writes=[PSB[pb]])
                        evac(xsT[e2][:, cg * 8:(cg + 1) * 8, half * 128:(half + 1) * 128], psv.rearrange("p (k n) -> p k n", k=8), [PSB[pb]], [xsTB[e2]])
                for fb in range(4):
                    w2 = nw % 2
                    nw += 1
                    kb.dma("pool", wg[w2][:], I["w_gate"][l, e, :, fb * 512:(fb + 1) * 512].rearrange("(c p) n -> p c n", p=128), writes=[wgB[w2]])
                    kb.dma("pool", wu[w2][:], I["w_up"][l, e, :, fb * 512:(fb + 1) * 512].rearrange("(c p) n -> p c n", p=128), writes=[wuB[w2]])
                    for fc in range(4):
                        pg, pu = 4 + (fc % 2) * 2, 5 + (fc % 2) * 2

                        def mmg(wt, p_, fc=fc):
                            for cc in range(NC_):
                                ins = nc.tensor.matmul(PS[p_][:, 0:256], lhsT=wt[:, cc, fc * 128:(fc + 1) * 128], rhs=xsT[e2][:, cc, :],
                                                       start=(cc == 0), stop=(cc == NC_ - 1))
                            return ins
                        kb.op("pe", lambda: mmg(wg[w2], pg), reads=[wgB[w2], xsTB[e2]], writes=[PSB[pg]])
                        kb.op("pe", lambda: mmg(wu[w2], pu), reads=[wuB[w2], xsTB[e2]], writes=[PSB[pu]])
                        s2 = nsg % 2
                        nsg += 1
                        kb.op("act", lambda: nc.scalar.activation(out=sgt[s2][:], in_=PS[pg][:, 0:256], func=AF.Silu), reads=[PSB[pg]], writes=[sgB[s2]])
                        kb.op("dve", lambda: nc.vector.tensor_tensor(out=hidT[:, fb * 4 + fc, :], in0=PS[pu][:, 0:256], in1=sgt[s2][:], op=ALU.mult),
                              reads=[PSB[pu], sgB[s2]], writes=[hidB])
                for db in range(4):
                    w2 = nwd % 2
                    nwd += 1
                    kb.dma("pool", wd[w2][:], I["w_down"][l, e, :, db * 512:(db + 1) * 512].rearrange("(c p) n -> p c n", p=128), writes=[wdB[w2]])
                    for half in range(2):
                        pb = (db * 2 + half) % 4

                        def mmd(half=half, pb=pb):
                            for fc in range(NC_):
                                ins = nc.tensor.matmul(PS[pb][:], lhsT=hidT[:, fc, half * 128:(half + 1) * 128], rhs=wd[w2][:, fc, :],
                                                       start=(fc == 0), stop=(fc == NC_ - 1))
                            return ins
                        kb.op("pe", mmd, reads=[hidB, wdB[w2]], writes=[PSB[pb]])
                        kb.op("act", lambda half=half, pb=pb: nc.scalar.activation(out=yeb[e2][:, half, db * 512:(db + 1) * 512], in_=PS[pb][:], func=AF.Copy,
                                                                                  scale=gg[:, half:half + 1]), reads=[PSB[pb], igB], writes=[yeB[e2]])
                for half in range(2):
                    r0 = (e * 2 + half) * 128
                    kb.dma("sp", ye_d[r0:r0 + 128, :], yeb[e2][:, half, :], reads=[yeB[e2]], writes=[B_["ye"]])
            kb.barrier()
            ex.close()
            cx = ExitStack()
            nk = 2 * nexp
            YE = sb(cx, "cYE", [128, nk, 512], BF16)
            YB = kb.buf()
            sl = [sb(cx, f"csl{i}", [128, nk, 128], BF16) for i in range(2)]
            slB2 = kb.bufs(2)
            stg = [sb(cx, f"cstg{i}", [128, 512], F32) for i in range(2)]
            stB = kb.bufs(2)
            n = 0
            for db in range(4):
                kb.dma("sp", YE[:], ye_d[0:nk * 128, db * 512:(db + 1) * 512].rearrange("(k p) n -> p k n", p=128), reads=[B_["ye"]], writes=[YB])
                for t in range(NT):
                    s2 = n % 2
                    pb = n % 4
                    n += 1
                    kb.dma("sp", sl[s2][:], selT_d[0:nk, :, t * 128:(t + 1) * 128].rearrange("k p n -> p k n"), reads=[B_["selT"]], writes=[slB2[s2]])

                    def mmc(s2=s2, pb=pb):
                        for k in range(nk):
                            ins = nc.tensor.matmul(PS[pb][:], lhsT=sl[s2][:, k, :], rhs=YE[:, k, :], start=(k == 0), stop=(k == nk - 1))
                        return ins
                    kb.op("pe", mmc, reads=[slB2[s2], YB], writes=[PSB[pb]])
                    evac(stg[s2][:], PS[pb][:], [PSB[pb]], [stB[s2]])
                    kb.dma("sp", moe_d[t * 128:(t + 1) * 128, db * 512:(db + 1) * 512], stg[s2][:], reads=[stB[s2]], writes=[B_["moe"]])
            kb.barrier()
            cx.close()

        def stage_postmoe(st, l, dst_d, dstB):
            bt = {}
            bcB = kb.buf()
            for nm, row in (("g2", mod_d[l, 0:1, 5 * D:6 * D]), ("lg", I["ln2_g"][l:l + 1, :]), ("lb", I["ln2_b"][l:l + 1, :])):
                bt[nm] = sb(st, "qb_" + nm, [128, D], F32)
                load_bc(bt[nm], row, bcB)
            xt = [sb(st, f"qxt{i}", [128, D], F32) for i in range(2)]
            yt = [sb(st, f"qyt{i}", [128, D], F32) for i in range(2)]
            xB, yB = kb.bufs(2), kb.bufs(2)
            stt = sb(st, "qstt", [128, 4, 6], F32)
            mv = sb(st, "qmv", [128, 2], F32)
            rstd = sb(st, "qrstd", [128, 1], F32)
            nmr = sb(st, "qnmr", [128, 1], F32)
            smB = kb.buf()
            toks = []
            for t in range(NT):
                x_, y_, xb_, yb_ = xt[t % 2], yt[t % 2], xB[t % 2], yB[t % 2]
                kb.dma("sp", x_[:], x1_d[t * 128:(t + 1) * 128, :], reads=[B_["x1"]], writes=[xb_])
                kb.dma("sp", y_[:], moe_d[t * 128:(t + 1) * 128, :], reads=[B_["moe"]], writes=[yb_])
                kb.op("dve", lambda y_=y_: nc.vector.tensor_tensor(out=y_[:], in0=y_[:], in1=bt["g2"][:], op=ALU.mult), reads=[yb_, bcB], writes=[yb_])
                kb.op("dve", lambda x_=x_, y_=y_: nc.vector.scalar_tensor_tensor(out=x_[:], in0=x_[:], scalar=float(ALPHA), in1=y_[:], op0=ALU.mult, op1=ALU.add),
                      reads=[xb_, yb_], writes=[xb_])
                ln_stats(stt, mv, rstd, nmr, x_, [xb_], smB)
                kb.op("act", lambda x_=x_: nc.scalar.activation(out=x_[:], in_=x_[:], func=AF.Identity, bias=nmr[:], scale=rstd[:]), reads=[smB, xb_], writes=[xb_])
                kb.op("dve", lambda x_=x_: nc.vector.tensor_tensor(out=x_[:], in0=x_[:], in1=bt["lg"][:], op=ALU.mult), reads=[xb_, bcB], writes=[xb_])
                kb.op("dve", lambda x_=x_: nc.vector.tensor_tensor(out=x_[:], in0=x_[:], in1=bt["lb"][:], op=ALU.add), reads=[xb_, bcB], writes=[xb_])
                toks.append(kb.dma("sp", dst_d[t * 128:(t + 1) * 128, :], x_[:], reads=[xb_], writes=[dstB]))
            return toks

        xB_in = kb.buf("xin")
        stage_mod()
        hT_d = scratch("hT_d", [NC_, 128, SEQ + CTX], BF16)
        if upto >= 1:
            with ExitStack() as st:
                hT = sb(st, "hT0", [128, NC_, SEQ + CTX], BF16)
                hB = kb.buf()
                with ExitStack() as s2:
                    stage_pre(s2, I["x"], xB_in, NT, mod_d[0, 0:1, 0:D], mod_d[0, 0:1, D:2 * D], hT, hB, 0, "a")
                    kb.barrier()
                if upto >= 2:
                    with ExitStack() as s2:
                        stage_pre(s2, I["ctx"], xB_in, CTX // 128, mod_d[0, 1:2, 0:D], mod_d[0, 1:2, D:2 * D], hT, hB, SEQ, "b")
                        kb.barrier()
                if "hT_d" in dbg:
                    kb.dma("sp", hT_d.rearrange("c p n -> p c n"), hT[:], reads=[hB], writes=[B_["mixT"]])
                if upto >= 3:
                    with ExitStack() as s2:
                        stage_inproj_ab(s2, hT, hB)
                        kb.barrier()
                kb.barrier()
        if upto >= 4:
            with ExitStack() as st:
                mixT = sb(st, "mixT", [128, NC_, SEQ], BF16)
                mB = kb.buf()
                with ExitStack() as s2:
                    stage_attn(s2, mixT, mB)
                    kb.barrier()
                if upto >= 5:
                    with ExitStack() as s2:
                        stage_fourier(s2, mixT, mB)
                        kb.barrier()
                if "mixT_d" in dbg:
                    kb.dma("sp", mixT_d.rearrange("c p n -> p c n"), mixT[:], reads=[mB], writes=[B_["mixT"]])
                if upto >= 6:
                    with ExitStack() as s2:
                        stage_gemm_out(s2, mixT, mB, I["ab_w_out"][0], "go")
                        kb.barrier()
                kb.barrier()

        def moe_block(l, xsrc_d, xsrcB, dst_d, dstB, nexp=NE):
            with ExitStack() as st:
                aff = sb(st, "aff", [128, NT, NE], F32)
                affB = kb.buf()
                with ExitStack() as s2:
                    stage_post(s2, l, xsrc_d, xsrcB, aff, affB)
                    kb.barrier()
                if "aff_d" in dbg:
                    kb.dma("sp", aff_d, aff[:].rearrange("p t e -> p (t e)"), reads=[affB], writes=[B_["aff"]])
                with ExitStack() as s2:
                    stage_moe(s2, l, aff, affB, nexp)
                    kb.barrier()
            with ExitStack() as s2:
                toks = stage_postmoe(s2, l, dst_d, dstB)
                kb.barrier()
            return toks

        out_toks = []
        if upto >= 7:
            moe_block(0, I["x"], xB_in, xl1_d, B_["xl1"], nexp=(NE if upto >= 8 else 1))
        if upto >= 9:
            with ExitStack() as st:
                hT = sb(st, "hT1", [128, NC_, SEQ], BF16)
                hB = kb.buf()
                with ExitStack() as s2:
                    stage_pre(s2, xl1_d, B_["xl1"], NT, mod_d[1, 0:1, 0:D], mod_d[1, 0:1, D:2 * D], hT, hB, 0, "c")
                    kb.barrier()
                with ExitStack() as s2:
                    stage_inproj_cd(s2, hT, hB)
                    kb.barrier()
                kb.barrier()
        if upto >= 10:
            with ExitStack() as st:
                mixT = sb(st, "mixT1", [128, NC_, SEQ], BF16)
                mB = kb.buf()
                with ExitStack() as s2:
                    stage_sg(s2, mixT, mB)
                    kb.barrier()
                with ExitStack() as s2:
                    stage_conv(s2, mixT, mB)
                    kb.barrier()
                if "mixT_d" in dbg:
                    kb.dma("sp", mixT_d.rearrange("c p n -> p c n"), mixT[:], reads=[mB], writes=[B_["mixT"]])
                with ExitStack() as s2:
                    stage_gemm_out(s2, mixT, mB, I["cd_w_out"][0], "go1")
                    kb.barrier()
                kb.barrier()
        if upto >= 11:
            moe_block(1, xl1_d, B_["xl1"], out_d, B_["out"])
        kb.barrier()
    return nc, hc


_CACHE = {}


def make_in_maps(inputs, hc, ncores=8):
    shared = {}
    for k in INPUT_SHAPES:
        if k in ("x", "c", "ctx"):
            continue
        a = np.ascontiguousarray(np.asarray(inputs[k], dtype=np.float32)).reshape(INPUT_SHAPES[k])
        shared[k] = a
    for k, v in hc.items():
        shared["k_" + k] = v
    maps = []
    for b in range(ncores):
        m = dict(shared)
        m["x"] = np.ascontiguousarray(inputs["x"][b], dtype=np.float32)
        m["c"] = np.ascontiguousarray(inputs["c"][b], dtype=np.float32).reshape(1, D)
        m["ctx"] = np.ascontiguousarray(inputs["ctx"][b], dtype=np.float32)
        maps.append(m)
    return maps


def kernel(**inputs):
    if "prog" not in _CACHE:
        _CACHE["prog"] = build_program()
    nc, hc = _CACHE["prog"]
    maps = make_in_maps(inputs, hc, 8)
    res = run_bass_kernel_spmd(nc, maps, core_ids=list(range(8)))
    return np.stack([np.asarray(r["out"], dtype=np.float32) for r in res.results], axis=0)
```

```python
import numpy as np
import ml_dtypes
from contextlib import ExitStack
import concourse.bass as bass
import concourse.mybir as mybir
from concourse.bass_utils import run_bass_kernel_spmd

F32 = mybir.dt.float32
BF16 = mybir.dt.bfloat16
I32 = mybir.dt.int32
ALU = mybir.AluOpType
AF = mybir.ActivationFunctionType
AX = mybir.AxisListType

D = 2048
SEQ = 2048
CTX = 256
NT = SEQ // 128
NC_ = D // 128
ALPHA = 4 ** 0.25
EPS = 1e-6
NE = 16
CAP = 256


class Buf:
    __slots__ = ("name", "last_w", "reads")

    def __init__(self, name=""):
        self.name = name
        self.last_w = None
        self.reads = []


class KB:
    ND = 10

    def __init__(self, nc, es):
        self.nc = nc
        self.engs = {"pe": nc.tensor, "act": nc.scalar, "dve": nc.vector, "pool": nc.gpsimd, "sp": nc.sync}
        self.sem, self.cnt, self.seen = {}, {}, {}
        for e in self.engs:
            self.sem[e] = es.enter_context(nc.semaphore("pg_" + e))
            self.cnt[e] = 0
            self.seen[e] = {}
        self.dsem, self.dcnt, self.drr = {}, {}, {}
        for q in ("sp", "pool"):
            self.dsem[q] = [es.enter_context(nc.semaphore(f"dq_{q}{i}")) for i in range(self.ND)]
            self.dcnt[q] = [0] * self.ND
            self.drr[q] = 0
        self.nbuf = 0

    def buf(self, name=""):
        self.nbuf += 1
        return Buf(name or f"b{self.nbuf}")

    def bufs(self, n, name=""):
        return [self.buf(f"{name}{i}") for i in range(n)]

    def _wait(self, eng, tok):
        sem, val = tok
        key = id(sem)
        if self.seen[eng].get(key, 0) >= val:
            return
        self.engs[eng].wait_ge(sem, val)
        self.seen[eng][key] = val

    def _dep1(self, eng, tok):
        if tok[0] is self.sem[eng] and eng == "pe":
            return
        self._wait(eng, tok)

    def _deps(self, eng, reads, writes):
        for b in reads:
            if b.last_w is not None:
                self._dep1(eng, b.last_w)
        for b in writes:
            if b.last_w is not None:
                self._dep1(eng, b.last_w)
            for t in b.reads:
                self._dep1(eng, t)

    def _upd(self, tok, reads, writes):
        for b in reads:
            b.reads.append(tok)
        for b in writes:
            b.last_w = tok
            b.reads = []

    def op(self, eng, fn, reads=(), writes=()):
        self._deps(eng, reads, writes)
        ins = fn()
        self.cnt[eng] += 1
        ins.then_inc(self.sem[eng], 1)
        tok = (self.sem[eng], self.cnt[eng])
        self._upd(tok, reads, writes)
        return tok

    def dma(self, q, out, in_, reads=(), writes=(), indirect=None, **kw):
        i = self.drr[q]
        self.drr[q] = (i + 1) % self.ND
        sem = self.dsem[q][i]
        if self.dcnt[q][i] > 0:
            self._wait(q, (sem, self.dcnt[q][i]))
        self._deps(q, reads, writes)
        if indirect is not None:
            ins = self.engs[q].indirect_dma_start(out=out, in_=in_, **indirect)
        else:
            ins = self.engs[q].dma_start(out=out, in_=in_, **kw)
        ins.then_inc(sem, 16)
        self.dcnt[q][i] += 16
        tok = (sem, self.dcnt[q][i])
        self._upd(tok, reads, writes)
        return tok

    def barrier(self):
        toks = [(self.sem[e], self.cnt[e]) for e in self.engs if self.cnt[e] > 0]
        for q in self.dsem:
            for i, s in enumerate(self.dsem[q]):
                if self.dcnt[q][i] > 0:
                    toks.append((s, self.dcnt[q][i]))
        for e in self.engs:
            for t in toks:
                if t[0] is self.sem[e]:
                    continue
                self._wait(e, t)


def host_consts():
    c = {}
    c["ident"] = np.eye(128, dtype=np.float32)
    c["identb"] = np.eye(128).astype(ml_dtypes.bfloat16)
    t = np.arange(SEQ)
    r = (t // 64).astype(np.float32)
    col = (t % 64).astype(np.float32)
    inv = (np.float32(10000.0) ** (-np.arange(32, dtype=np.float32) / np.float32(32))).astype(np.float32)
    ang_r = (r[:, None] * inv).astype(np.float32)
    ang_c = (col[:, None] * inv).astype(np.float32)
    cosT = np.zeros((128, SEQ), np.float32)
    sinT = np.zeros((128, SEQ), np.float32)
    for base, ang in ((0, ang_r), (64, ang_c)):
        cs = np.cos(ang).astype(np.float32).T
        sn = np.sin(ang).astype(np.float32).T
        cosT[base:base + 32] = cs
        cosT[base + 32:base + 64] = cs
        sinT[base:base + 32] = -sn
        sinT[base + 32:base + 64] = sn
    c["cosT"] = cosT
    c["sinT"] = sinT
    k = np.arange(SEQ, dtype=np.int64)
    ph = (np.outer(k, k) % SEQ).astype(np.float64) * (2 * np.pi / SEQ)
    c["dft_c"] = np.cos(ph).astype(ml_dtypes.bfloat16)
    c["dft_s"] = np.sin(ph).astype(ml_dtypes.bfloat16)
    kc = np.arange(128, dtype=np.int64)
    phc = (np.outer(kc, kc) % 128).astype(np.float64) * (2 * np.pi / 128)
    c["dftc_c"] = (np.cos(phc) / 512.0).astype(ml_dtypes.bfloat16)
    c["dftc_sn"] = (-np.sin(phc) / 512.0).astype(ml_dtypes.bfloat16)
    kk = np.arange(128)[:, None]
    qq = np.arange(128)[None, :]
    mlo = (qq <= kk).astype(np.float32)
    mhi = (kk <= qq).astype(np.float32)
    c["mask_lo"] = np.tile(mlo, (1, 3)).astype(ml_dtypes.bfloat16)
    c["mask_hi"] = np.tile(mhi, (1, 3)).astype(ml_dtypes.bfloat16)
    c["iota256"] = np.tile(np.arange(256, dtype=np.float32)[None, :], (128, 1))
    c["iota_tok"] = np.tile(np.arange(SEQ, dtype=np.float32)[None, :], (128, 1))
    c["pcol"] = np.arange(128, dtype=np.float32).reshape(128, 1)
    c["tcol"] = np.tile(np.repeat(np.arange(16, dtype=np.float32), 16)[None, :], (128, 1))
    c["tri"] = (np.arange(128)[:, None] <= np.arange(128)[None, :]).astype(ml_dtypes.bfloat16)
    c["onesb"] = np.ones((128, 128), ml_dtypes.bfloat16)
    c["onesf"] = np.ones((128, 128), np.float32)
    return c


CONST_DT = {"identb": BF16, "dft_c": BF16, "dft_s": BF16, "dftc_c": BF16, "dftc_sn": BF16, "mask_lo": BF16,
            "mask_hi": BF16, "tri": BF16, "onesb": BF16}

INPUT_SHAPES = {
    "x": [SEQ, D], "c": [1, D], "ctx": [CTX, D], "c_ctx": [1, D],
    "w_mod": [2, D, 6 * D], "b_mod": [2, 6 * D], "ln1_g": [2, D], "ln1_b": [2, D], "ln2_g": [2, D], "ln2_b": [2, D],
    "w_router": [2, D, NE], "w_gate": [2, NE, D, D], "w_up": [2, NE, D, D], "w_down": [2, NE, D, D],
    "ab_w_in": [1, D, 3072], "ab_w_out": [1, D, D], "sink": [1, 12],
    "cd_w_in": [1, D, 4096], "cd_w_out": [1, D, D], "sg_ln_g": [1, 1024], "sg_ln_b": [1, 1024],
    "sg_w": [8, 128, 128], "sg_b": [1, 1024], "conv_w": [31, 1024], "conv_b": [1, 1024],
    "conv_ln_g": [1, 1024], "conv_ln_b": [1, 1024],
}


def build_program(upto=99, dbg=()):
    nc = bass.Bass("TRN2", target_bir_lowering=False)
    es = ExitStack()
    with es:
        kb = KB(nc, es)
        I = {}
        for name, shp in INPUT_SHAPES.items():
            I[name] = nc.dram_tensor(name, list(shp), F32, kind="ExternalInput").ap()
        hc = host_consts()
        C = {}
        for name, arr in hc.items():
            C[name] = nc.dram_tensor("k_" + name, list(arr.shape), CONST_DT.get(name, F32), kind="ExternalInput").ap()
        out_d = nc.dram_tensor("out", [SEQ, D], F32, kind="ExternalOutput").ap()

        def scratch(name, shape, dt=F32):
            kind = "ExternalOutput" if name in dbg else "Internal"
            return nc.dram_tensor(name, list(shape), dt, kind=kind).ap()

        mod_d = scratch("mod_d", [2, 2, 6 * D])
        qT_d = scratch("qT_d", [12, 128, SEQ], BF16)
        kT_d = scratch("kT_d", [4, 128, SEQ + CTX], BF16)
        v_d = scratch("v_d", [SEQ + CTX, 512], BF16)
        z_d = scratch("z_d", [SEQ, 512], BF16)
        mixT_d = scratch("mixT_d", [16, 128, SEQ], BF16)
        y_d = scratch("y_d", [SEQ, D])
        x1_d = scratch("x1_d", [SEQ, D])
        h2_d = scratch("h2_d", [SEQ, D], BF16)
        ye_d = scratch("ye_d", [2 * NE * 128, D], BF16)
        selT_d = scratch("selT_d", [2 * NE, 128, SEQ], BF16)
        moe_d = scratch("moe_d", [SEQ, D])
        xl1_d = scratch("xl1_d", [SEQ, D])
        uT_d = scratch("uT_d", [8, 128, SEQ], BF16)
        vg_d = scratch("vg_d", [SEQ, 1024], BF16)
        xgT_d = scratch("xgT_d", [8, 128, SEQ], BF16)
        aff_d = scratch("aff_d", [128, 256])
        B_ = {n: kb.buf(n) for n in ("mod", "qT", "kT", "v", "z", "mixT", "y", "x1", "h2", "ye", "selT", "moe", "xl1",
                                     "uT", "vg", "xgT", "aff", "out")}

        sbn = [0]

        def sb(st, name, shape, dt):
            sbn[0] += 1
            return st.enter_context(nc.sbuf_tensor(f"{name}_{sbn[0]}", list(shape), dt))

        PS = [es.enter_context(nc.psum_tensor(f"ps{i}", [128, 512], F32)) for i in range(8)]
        PSB = kb.bufs(8, "ps")
        ident = sb(es, "ident", [128, 128], F32)
        identb = sb(es, "identb", [128, 128], BF16)
        onesb = sb(es, "onesb", [128, 128], BF16)
        cB = kb.buf("consts")
        kb.dma("sp", ident[:], C["ident"], writes=[cB])
        kb.dma("sp", identb[:], C["identb"], writes=[cB])
        kb.dma("sp", onesb[:], C["onesb"], writes=[cB])

        evac_rr = [0]

        def evac(out, in_, reads, writes):
            evac_rr[0] ^= 1
            if evac_rr[0]:
                return kb.op("act", lambda: nc.scalar.copy(out=out, in_=in_), reads=reads, writes=writes)
            return kb.op("dve", lambda: nc.vector.tensor_copy(out=out, in_=in_), reads=reads, writes=writes)

        def ln_stats(st_tile, mv, rstd, nmr, xin, rB, wB):
            for k in range(4):
                kb.op("dve", lambda k=k: nc.vector.bn_stats(out=st_tile[:, k, :], in_=xin[:, k * 512:(k + 1) * 512]),
                      reads=rB, writes=[wB])
            kb.op("dve", lambda: nc.vector.bn_aggr(out=mv[:], in_=st_tile[:].rearrange("p a b -> p (a b)")), reads=[wB], writes=[wB])
            kb.op("act", lambda: nc.scalar.activation(out=rstd[:], in_=mv[:, 1:2], func=AF.Sqrt, bias=EPS, scale=1.0),
                  reads=[wB], writes=[wB])
            kb.op("dve", lambda: nc.vector.reciprocal(out=rstd[:], in_=rstd[:]), reads=[wB], writes=[wB])
            kb.op("dve", lambda: nc.vector.scalar_tensor_tensor(out=nmr[:], in0=mv[:, 0:1], scalar=-1.0, in1=rstd[:],
                                                                 op0=ALU.mult, op1=ALU.mult), reads=[wB], writes=[wB])

        def load_bc(tile, row_ap, wB, plus_one=False):
            kb.dma("sp", tile[:], row_ap.partition_broadcast(128), reads=[B_["mod"]], writes=[wB])
            if plus_one:
                kb.op("dve", lambda: nc.vector.tensor_scalar(out=tile[:], in0=tile[:], scalar1=1.0, scalar2=None, op0=ALU.add),
                      reads=[wB], writes=[wB])

        def stage_mod():
            with ExitStack() as st:
                sT = sb(st, "sT", [128, NC_, 2], F32)
                sTb = sb(st, "sTb", [128, NC_, 2], BF16)
                bm = sb(st, "bm", [2, 6 * D], F32)
                mo = sb(st, "mo", [2, 6 * D], F32)
                wb = [sb(st, f"wmod{i}", [128, NC_, 512], BF16) for i in range(2)]
                wB = kb.bufs(2, "wmod")
                sB, bB, moB = kb.buf(), kb.buf(), kb.buf()
                kb.dma("sp", sT[:, :, 0], I["c"][0].rearrange("(c p) -> p c", p=128), writes=[sB], allow_slow_non_contiguous=True)
                kb.dma("sp", sT[:, :, 1], I["c_ctx"][0].rearrange("(c p) -> p c", p=128), writes=[sB], allow_slow_non_contiguous=True)
                kb.op("act", lambda: nc.scalar.activation(out=sT[:], in_=sT[:], func=AF.Silu), reads=[sB], writes=[sB])
                kb.op("dve", lambda: nc.vector.tensor_copy(out=sTb[:], in_=sT[:]), reads=[sB], writes=[sB])
                n = 0
                for l in range(2):
                    kb.dma("sp", bm[0:1, :], I["b_mod"][l:l + 1, :], reads=[], writes=[bB])
                    kb.dma("sp", bm[1:2, :], I["b_mod"][l:l + 1, :], reads=[], writes=[bB])
                    for j in range(24):
                        w = wb[n % 2]
                        kb.dma("pool", w[:], I["w_mod"][l, :, j * 512:(j + 1) * 512].rearrange("(c p) n -> p c n", p=128),
                               writes=[wB[n % 2]])
                        ps = PS[n % 2]

                        def mm(w=w, ps=ps):
                            for cc in range(NC_):
                                ins = nc.tensor.matmul(ps[0:2, :], lhsT=sTb[:, cc, :], rhs=w[:, cc, :], start=(cc == 0), stop=(cc == NC_ - 1))
                            return ins
                        kb.op("pe", mm, reads=[sB, wB[n % 2]], writes=[PSB[n % 2]])
                        kb.op("dve", lambda ps=ps, j=j: nc.vector.tensor_tensor(out=mo[:, j * 512:(j + 1) * 512], in0=ps[0:2, :],
                                                                                 in1=bm[:, j * 512:(j + 1) * 512], op=ALU.add),
                              reads=[PSB[n % 2], bB], writes=[moB])
                        n += 1
                    kb.dma("sp", mod_d[l], mo[:], reads=[moB], writes=[B_["mod"]])
                kb.barrier()

        def stage_pre(st, src_d, srcB, ntiles, sh_row, sc_row, hT, hB, col0, tag):
            bsc = sb(st, tag + "bsc", [128, D], F32)
            bsh = sb(st, tag + "bsh", [128, D], F32)
            bcB = kb.buf()
            load_bc(bsc, sc_row, bcB, plus_one=True)
            load_bc(bsh, sh_row, bcB)
            xt = [sb(st, f"{tag}xt{i}", [128, D], F32) for i in range(2)]
            xB = kb.bufs(2)
            stt = sb(st, tag + "stt", [128, 4, 6], F32)
            mv = sb(st, tag + "mv", [128, 2], F32)
            rstd = sb(st, tag + "rstd", [128, 1], F32)
            nmr = sb(st, tag + "nmr", [128, 1], F32)
            smB = kb.buf()
            for t in range(ntiles):
                x_ = xt[t % 2]
                xb_ = xB[t % 2]
                kb.dma("sp", x_[:], src_d[t * 128:(t + 1) * 128, :], reads=[srcB], writes=[xb_])
                ln_stats(stt, mv, rstd, nmr, x_, [xb_], smB)
                kb.op("act", lambda x_=x_: nc.scalar.activation(out=x_[:], in_=x_[:], func=AF.Identity, bias=nmr[:], scale=rstd[:]),
                      reads=[smB, xb_], writes=[xb_])
                kb.op("dve", lambda x_=x_: nc.vector.tensor_tensor(out=x_[:], in0=x_[:], in1=bsc[:], op=ALU.mult), reads=[xb_, bcB], writes=[xb_])
                kb.op("dve", lambda x_=x_: nc.vector.tensor_tensor(out=x_[:], in0=x_[:], in1=bsh[:], op=ALU.add), reads=[xb_, bcB], writes=[xb_])
                for q4 in range(4):
                    pb = (t * 4 + q4) % 8

                    def tr(x_=x_, q4=q4, pb=pb):
                        for k in range(4):
                            cc = q4 * 4 + k
                            ins = nc.tensor.transpose(PS[pb][:, k * 128:(k + 1) * 128], x_[:, cc * 128:(cc + 1) * 128], ident[:])
                        return ins
                    kb.op("pe", tr, reads=[xb_, cB], writes=[PSB[pb]])
                    evac(hT[:, q4 * 4:(q4 + 1) * 4, col0 + t * 128: col0 + (t + 1) * 128],
                         PS[pb][:].rearrange("p (k n) -> p k n", k=4), [PSB[pb]], [hB])

        def stage_gemm_out(st, mixT, mB, w_dram, tag):
            wb = [sb(st, f"{tag}w{i}", [128, NC_, 512], BF16) for i in range(2)]
            wB = kb.bufs(2)
            stg = [sb(st, f"{tag}stg{i}", [128, 512], F32) for i in range(3)]
            sgB = kb.bufs(3)
            n = 0
            for jb in range(4):
                kb.dma("pool", wb[jb % 2][:], w_dram[:, jb * 512:(jb + 1) * 512].rearrange("(c p) n -> p c n", p=128), writes=[wB[jb % 2]])
                for t in range(NT):
                    pb = n % 4

                    def mm(jb=jb, t=t, pb=pb):
                        for cc in range(NC_):
                            ins = nc.tensor.matmul(PS[pb][:], lhsT=mixT[:, cc, t * 128:(t + 1) * 128], rhs=wb[jb % 2][:, cc, :],
                                                   start=(cc == 0), stop=(cc == NC_ - 1))
                        return ins
                    kb.op("pe", mm, reads=[mB, wB[jb % 2]], writes=[PSB[pb]])
                    s_ = n % 3
                    evac(stg[s_][:], PS[pb][:], [PSB[pb]], [sgB[s_]])
                    kb.dma("sp", y_d[t * 128:(t + 1) * 128, jb * 512:(jb + 1) * 512], stg[s_][:], reads=[sgB[s_]], writes=[B_["y"]])
                    n += 1

        def stage_inproj_ab(st, hT, hB):
            TT = SEQ + CTX
            w_in = I["ab_w_in"][0]
            cosT = sb(st, "cosT", [128, SEQ], F32)
            sinT = sb(st, "sinT", [128, SEQ], F32)
            rB = kb.buf()
            kb.dma("sp", cosT[:], C["cosT"], writes=[rB])
            kb.dma("sp", sinT[:], C["sinT"], writes=[rB])
            wb = [sb(st, f"abw{i}", [128, NC_, 512], BF16) for i in range(2)]
            ws = sb(st, "abws", [128, NC_, 512], BF16)
            wB = kb.bufs(2)
            wsB = kb.buf()
            t1 = [sb(st, f"rt1_{i}", [128, 512], F32) for i in range(2)]
            t2 = [sb(st, f"rt2_{i}", [128, 512], F32) for i in range(2)]
            tB = kb.bufs(2)
            stg = [sb(st, f"abstg{i}", [128, 512], BF16) for i in range(3)]
            sgB = kb.bufs(3)
            n = 0
            ns = 0
            for jb in range(6):
                w = wb[jb % 2]
                kb.dma("pool", w[:], w_in[:, jb * 512:(jb + 1) * 512].rearrange("(c p) n -> p c n", p=128), writes=[wB[jb % 2]])
                if jb < 4:
                    wv = w[:].rearrange("p c (g t j) -> p (c g) t j", t=2, j=32)
                    sv = ws[:].rearrange("p c (g t j) -> p (c g) t j", t=2, j=32)
                    kb.op("act", lambda wv=wv, sv=sv: nc.scalar.copy(out=sv[:, :, 0, :], in_=wv[:, :, 1, :]), reads=[wB[jb % 2]], writes=[wsB])
                    kb.op("dve", lambda wv=wv, sv=sv: nc.vector.tensor_copy(out=sv[:, :, 1, :], in_=wv[:, :, 0, :]), reads=[wB[jb % 2]], writes=[wsB])
                    for hh in range(4):
                        head = jb * 4 + hh
                        isk = head >= 12
                        for tb in range(4):
                            pa, pb = (n * 2) % 8, (n * 2 + 1) % 8
                            n += 1

                            def mm(wt, p_, hh=hh, tb=tb):
                                for cc in range(NC_):
                                    ins = nc.tensor.matmul(PS[p_][:], lhsT=wt[:, cc, hh * 128:(hh + 1) * 128],
                                                           rhs=hT[:, cc, tb * 512:(tb + 1) * 512], start=(cc == 0), stop=(cc == NC_ - 1))
                                return ins
                            kb.op("pe", lambda: mm(w, pa), reads=[hB, wB[jb % 2]], writes=[PSB[pa]])
                            kb.op("pe", lambda: mm(ws, pb), reads=[hB, wsB], writes=[PSB[pb]])
                            k2 = ns % 2
                            s_ = ns % 3
                            ns += 1
                            kb.op("dve", lambda: nc.vector.tensor_tensor(out=t1[k2][:], in0=PS[pa][:], in1=cosT[:, tb * 512:(tb + 1) * 512], op=ALU.mult),
                                  reads=[PSB[pa], rB], writes=[tB[k2]])
                            kb.op("dve", lambda: nc.vector.tensor_tensor(out=t2[k2][:], in0=PS[pb][:], in1=sinT[:, tb * 512:(tb + 1) * 512], op=ALU.mult),
                                  reads=[PSB[pb], rB], writes=[tB[k2]])
                            kb.op("dve", lambda: nc.vector.tensor_tensor(out=stg[s_][:], in0=t1[k2][:], in1=t2[k2][:], op=ALU.add),
                                  reads=[tB[k2]], writes=[sgB[s_]])
                            if isk:
                                kb.dma("sp", kT_d[head - 12, :, tb * 512:(tb + 1) * 512], stg[s_][:], reads=[sgB[s_]], writes=[B_["kT"]])
                            else:
                                kb.dma("sp", qT_d[head, :, tb * 512:(tb + 1) * 512], stg[s_][:], reads=[sgB[s_]], writes=[B_["qT"]])
                        if isk:
                            pa = (n * 2) % 8
                            n += 1

                            def mmc(hh=hh, pa=pa):
                                for cc in range(NC_):
                                    ins = nc.tensor.matmul(PS[pa][:, 0:CTX], lhsT=w[:, cc, hh * 128:(hh + 1) * 128],
                                                           rhs=hT[:, cc, SEQ:TT], start=(cc == 0), stop=(cc == NC_ - 1))
                                return ins
                            kb.op("pe", mmc, reads=[hB, wB[jb % 2]], writes=[PSB[pa]])
                            s_ = ns % 3
                            ns += 1
                            evac(stg[s_][:, 0:CTX], PS[pa][:, 0:CTX], [PSB[pa]], [sgB[s_]])
                            kb.dma("sp", kT_d[head - 12, :, SEQ:TT], stg[s_][:, 0:CTX], reads=[sgB[s_]], writes=[B_["kT"]])
                else:
                    ntile = TT // 128 if jb == 4 else NT
                    for t in range(ntile):
                        pa = (n * 2) % 8
                        n += 1

                        def mmv(t=t, pa=pa):
                            for cc in range(NC_):
                                ins = nc.tensor.matmul(PS[pa][:], lhsT=hT[:, cc, t * 128:(t + 1) * 128], rhs=w[:, cc, :],
                                                       start=(cc == 0), stop=(cc == NC_ - 1))
                            return ins
                        kb.op("pe", mmv, reads=[hB, wB[jb % 2]], writes=[PSB[pa]])
                        s_ = ns % 3
                        ns += 1
                        evac(stg[s_][:], PS[pa][:], [PSB[pa]], [sgB[s_]])
                        if jb == 4:
                            kb.dma("sp", v_d[t * 128:(t + 1) * 128, :], stg[s_][:], reads=[sgB[s_]], writes=[B_["v"]])
                        else:
                            kb.dma("sp", z_d[t * 128:(t + 1) * 128, :], stg[s_][:], reads=[sgB[s_]], writes=[B_["z"]])

        def stage_attn(st, mixT, mB):
            TT = SEQ + CTX
            mlo = sb(st, "mlo", [128, 384], BF16)
            mhi = sb(st, "mhi", [128, 384], BF16)
            snk = sb(st, "snk", [128, 12], F32)
            esk = sb(st, "esk", [128, 12, 128], F32)
            kB_ = kb.buf()
            kb.dma("sp", mlo[:], C["mask_lo"], writes=[kB_])
            kb.dma("sp", mhi[:], C["mask_hi"], writes=[kB_])
            kb.dma("sp", snk[:], I["sink"][0:1, :].partition_broadcast(128), writes=[kB_])
            kb.op("act", lambda: nc.scalar.activation(out=snk[:], in_=snk[:], func=AF.Exp), reads=[kB_], writes=[kB_])
            kb.op("dve", lambda: nc.vector.tensor_copy(out=esk[:], in_=snk[:].unsqueeze(2).to_broadcast([128, 12, 128])), reads=[kB_], writes=[kB_])
            kT = [sb(st, f"kT{i}", [128, TT], BF16) for i in range(2)]
            vt = [sb(st, f"vt{i}", [128, TT // 128, 128], BF16) for i in range(2)]
            qT = [sb(st, f"qT{i}", [128, 3, SEQ], BF16) for i in range(2)]
            hdB = kb.bufs(2)
            pT = [sb(st, f"pT{i}", [128, 384], BF16) for i in range(4)]
            pB = kb.bufs(4)
            den = [sb(st, f"den{i}", [128, 384], F32) for i in range(2)]
            dB = kb.bufs(2)
            scale = 128 ** -0.5
            items = []
            it = 0
            for h in range(4):
                for i in range(NT):
                    blocks = []
                    if i > 0:
                        blocks.append((i - 1, mlo))
                    blocks.append((i, None))
                    if i < NT - 1:
                        blocks.append((i + 1, mhi))
                    blocks.append((16, None))
                    blocks.append((17, None))
                    for bi, (j, msk) in enumerate(blocks):
                        items.append(dict(h=h, i=i, j=j, msk=msk, first=(bi == 0), last=(bi == len(blocks) - 1), po=4 + (it % 2) * 2, d2=it % 2,
                                          n=len(items), newh=(i == 0 and bi == 0)))
                    it += 1

            def emitS(a):
                h, i, j, msk, n = a["h"], a["i"], a["j"], a["msk"], a["n"]
                k2 = h % 2
                if a["newh"]:
                    kb.dma("sp", kT[k2][:], kT_d[h], reads=[B_["kT"]], writes=[hdB[k2]])
                    kb.dma("sp", vt[k2][:], v_d[:, h * 128:(h + 1) * 128].rearrange("(t p) d -> p t d", p=128), reads=[B_["v"]], writes=[hdB[k2]])
                    kb.dma("sp", qT[k2][:], qT_d[3 * h:3 * h + 3].rearrange("g p n -> p g n"), reads=[B_["qT"]], writes=[hdB[k2]])
                sbk = n % 4
                kb.op("pe", lambda: nc.tensor.matmul(PS[sbk][:, 0:384], lhsT=kT[k2][:, j * 128:(j + 1) * 128],
                                                     rhs=qT[k2][:, :, i * 128:(i + 1) * 128], start=True, stop=True),
                      reads=[hdB[k2]], writes=[PSB[sbk]])
                kb.op("act", lambda: nc.scalar.activation(out=pT[sbk][:], in_=PS[sbk][:, 0:384], func=AF.Exp, scale=scale),
                      reads=[PSB[sbk]], writes=[pB[sbk]])
                if msk is not None:
                    kb.op("dve", lambda: nc.vector.tensor_tensor(out=pT[sbk][:], in0=pT[sbk][:], in1=msk[:], op=ALU.mult),
                          reads=[pB[sbk], kB_], writes=[pB[sbk]])

            def emitPV(a):
                h, i, j, n, po, d2 = a["h"], a["i"], a["j"], a["n"], a["po"], a["d2"]
                k2 = h % 2
                pp = n % 4

                def mm2():
                    nc.tensor.matmul(PS[po][:, 0:384], lhsT=vt[k2][:, j, :], rhs=pT[pp][:], start=a["first"], stop=a["last"])
                    return nc.tensor.matmul(PS[po + 1][:, 0:384], lhsT=onesb[:], rhs=pT[pp][:], start=a["first"], stop=a["last"])
                kb.op("pe", mm2, reads=[hdB[k2], pB[pp], cB], writes=[PSB[po], PSB[po + 1]])
                if a["last"]:
                    dn = den[d2]
                    kb.op("dve", lambda: nc.vector.tensor_tensor(out=dn[:], in0=PS[po + 1][:, 0:384],
                                                                 in1=esk[:, 3 * h:3 * h + 3, :].rearrange("p g n -> p (g n)"), op=ALU.add),
                          reads=[PSB[po + 1], kB_], writes=[dB[d2]])
                    kb.op("dve", lambda: nc.vector.reciprocal(out=dn[:], in_=dn[:]), reads=[dB[d2]], writes=[dB[d2]])
                    kb.op("dve", lambda: nc.vector.tensor_tensor(out=mixT[:, 3 * h:3 * h + 3, i * 128:(i + 1) * 128],
                                                                 in0=PS[po][:, 0:384].rearrange("p (g n) -> p g n", g=3),
                                                                 in1=dn[:].rearrange("p (g n) -> p g n", g=3), op=ALU.mult),
                          reads=[PSB[po], dB[d2]], writes=[mB])
            LA = 2
            for n in range(len(items) + LA):
                if n < len(items):
                    emitS(items[n])
                if n - LA >= 0:
                    emitPV(items[n - LA])

        def stage_fourier(st, mixT, mB):
            Z = sb(st, "fz", [128, NT, 512], BF16)
            zB = kb.buf()
            kb.dma("sp", Z[:], z_d.rearrange("(t p) c -> p t c", p=128), reads=[B_["z"]], writes=[zB])
            cc_ = sb(st, "fcc", [128, 128], BF16)
            csn = sb(st, "fcs", [128, 128], BF16)
            kb.dma("sp", cc_[:], C["dftc_c"], writes=[zB])
            kb.dma("sp", csn[:], C["dftc_sn"], writes=[zB])
            Cn = [sb(st, f"fCn{i}", [128, NT, 512], BF16) for i in range(2)]
            Sn = [sb(st, f"fSn{i}", [128, NT, 512], BF16) for i in range(2)]
            tbB = kb.bufs(2)
            ab = [sb(st, f"fab{i}", [128, 512], BF16) for i in range(2)]
            bb = [sb(st, f"fbb{i}", [128, 512], BF16) for i in range(2)]
            abB = kb.bufs(2)
            n = 0
            for nb in range(4):
                k2 = nb % 2
                kb.dma("sp", Cn[k2][:], C["dft_c"][:, nb * 512:(nb + 1) * 512].rearrange("(t p) n -> p t n", p=128), writes=[tbB[k2]])
                kb.dma("sp", Sn[k2][:], C["dft_s"][:, nb * 512:(nb + 1) * 512].rearrange("(t p) n -> p t n", p=128), writes=[tbB[k2]])
                for g in range(4):
                    pa, pb, pc = (n * 3) % 6, (n * 3 + 1) % 6, 6 + n % 2
                    a2 = n % 2
                    n += 1

                    def mmA(tab, p_, g=g):
                        for t in range(NT):
                            ins = nc.tensor.matmul(PS[p_][:], lhsT=Z[:, t, g * 128:(g + 1) * 128], rhs=tab[:, t, :], start=(t == 0), stop=(t == NT - 1))
                        return ins
                    kb.op("pe", lambda: mmA(Cn[k2], pa), reads=[zB, tbB[k2]], writes=[PSB[pa]])
                    kb.op("pe", lambda: mmA(Sn[k2], pb), reads=[zB, tbB[k2]], writes=[PSB[pb]])
                    kb.op("act", lambda: nc.scalar.copy(out=ab[a2][:], in_=PS[pa][:]), reads=[PSB[pa]], writes=[abB[a2]])
                    kb.op("dve", lambda: nc.vector.tensor_copy(out=bb[a2][:], in_=PS[pb][:]), reads=[PSB[pb]], writes=[abB[a2]])

                    def mmB():
                        nc.tensor.matmul(PS[pc][:], lhsT=cc_[:], rhs=ab[a2][:], start=True, stop=False)
                        return nc.tensor.matmul(PS[pc][:], lhsT=csn[:], rhs=bb[a2][:], start=False, stop=True)
                    kb.op("pe", mmB, reads=[zB, abB[a2]], writes=[PSB[pc]])
                    evac(mixT[:, 12 + g, nb * 512:(nb + 1) * 512], PS[pc][:], [PSB[pc]], [mB])

        def stage_inproj_cd(st, hT, hB):
            w_in = I["cd_w_in"][0]
            wb = [sb(st, f"cdw{i}", [128, NC_, 512], BF16) for i in range(4)]
            wB = kb.bufs(4)
            lng = sb(st, "sglng", [128, 1024], F32)
            lnb = sb(st, "sglnb", [128, 1024], F32)
            lB = kb.buf()
            kb.dma("sp", lng[:], I["sg_ln_g"][0:1, :].partition_broadcast(128), writes=[lB])
            kb.dma("sp", lnb[:], I["sg_ln_b"][0:1, :].partition_broadcast(128), writes=[lB])
            stg = [sb(st, f"cdstg{i}", [128, 512], BF16) for i in range(3)]
            sgB = kb.bufs(3)
            zt = [sb(st, f"cdz{i}", [128, 512], F32) for i in range(2)]
            zB = kb.bufs(2)
            stt = sb(st, "cdstt", [128, 4, 6], F32)
            mv4 = sb(st, "cdmv4", [128, 4, 2], F32)
            rs4 = sb(st, "cdrs4", [128, 4], F32)
            smB = kb.buf()
            nw = 0
            n = 0
            ns = 0

            def loadw(jb):
                nonlocal nw
                k = nw % 4
                nw += 1
                kb.dma("pool", wb[k][:], w_in[:, jb * 512:(jb + 1) * 512].rearrange("(c p) n -> p c n", p=128), writes=[wB[k]])
                return k
            for jb in range(2):
                k = loadw(jb)
                for fc in range(4):
                    for tb in range(4):
                        pa = n % 8
                        n += 1

                        def mm(k=k, fc=fc, tb=tb, pa=pa):
                            for cc in range(NC_):
                                ins = nc.tensor.matmul(PS[pa][:], lhsT=wb[k][:, cc, fc * 128:(fc + 1) * 128], rhs=hT[:, cc, tb * 512:(tb + 1) * 512],
                                                       start=(cc == 0), stop=(cc == NC_ - 1))
                            return ins
                        kb.op("pe", mm, reads=[hB, wB[k]], writes=[PSB[pa]])
                        s_ = ns % 3
                        ns += 1
                        kb.op("act", lambda: nc.scalar.activation(out=stg[s_][:], in_=PS[pa][:], func=AF.Gelu_apprx_tanh), reads=[PSB[pa]], writes=[sgB[s_]])
                        kb.dma("sp", uT_d[jb * 4 + fc, :, tb * 512:(tb + 1) * 512], stg[s_][:], reads=[sgB[s_]], writes=[B_["uT"]])
            for jb in range(2, 4):
                k = loadw(jb)
                for t in range(NT):
                    pa = n % 8
                    n += 1

                    def mmv(k=k, t=t, pa=pa):
                        for cc in range(NC_):
                            ins = nc.tensor.matmul(PS[pa][:], lhsT=hT[:, cc, t * 128:(t + 1) * 128], rhs=wb[k][:, cc, :], start=(cc == 0), stop=(cc == NC_ - 1))
                        return ins
                    kb.op("pe", mmv, reads=[hB, wB[k]], writes=[PSB[pa]])
                    z_ = zt[t % 2]
                    zb_ = zB[t % 2]
                    kb.op("act", lambda: nc.scalar.activation(out=z_[:], in_=PS[pa][:], func=AF.Gelu_apprx_tanh), reads=[PSB[pa]], writes=[zb_])
                    for g in range(4):
                        kb.op("dve", lambda g=g: nc.vector.bn_stats(out=stt[:, g, :], in_=z_[:, g * 128:(g + 1) * 128]), reads=[zb_], writes=[smB])
                    for g in range(4):
                        kb.op("dve", lambda g=g: nc.vector.bn_aggr(out=mv4[:, g, :], in_=stt[:, g, :]), reads=[smB], writes=[smB])
                    kb.op("act", lambda: nc.scalar.activation(out=rs4[:], in_=mv4[:, :, 1], func=AF.Sqrt, bias=EPS, scale=1.0), reads=[smB], writes=[smB])
                    kb.op("dve", lambda: nc.vector.reciprocal(out=rs4[:], in_=rs4[:]), reads=[smB], writes=[smB])
                    zv = z_[:].rearrange("p (g c) -> p g c", g=4)
                    kb.op("dve", lambda: nc.vector.tensor_tensor(out=zv, in0=zv, in1=mv4[:, :, 0].unsqueeze(2).to_broadcast([128, 4, 128]), op=ALU.subtract),
                          reads=[zb_, smB], writes=[zb_])
                    kb.op("dve", lambda: nc.vector.tensor_tensor(out=zv, in0=zv, in1=rs4[:].unsqueeze(2).to_broadcast([128, 4, 128]), op=ALU.mult),
                          reads=[zb_, smB], writes=[zb_])
                    c0 = (jb - 2) * 512
                    kb.op("dve", lambda: nc.vector.tensor_tensor(out=z_[:], in0=z_[:], in1=lng[:, c0:c0 + 512], op=ALU.mult), reads=[zb_, lB], writes=[zb_])
                    s_ = ns % 3
                    ns += 1
                    kb.op("dve", lambda: nc.vector.tensor_tensor(out=stg[s_][:], in0=z_[:], in1=lnb[:, c0:c0 + 512], op=ALU.add), reads=[zb_, lB], writes=[sgB[s_]])
                    kb.dma("sp", vg_d[t * 128:(t + 1) * 128, c0:c0 + 512], stg[s_][:], reads=[sgB[s_]], writes=[B_["vg"]])
            for jb in range(2):
                ka = loadw(4 + jb)
                kg = loadw(6 + jb)
                for fc in range(4):
                    for tb in range(4):
                        pa, pg = (n * 2) % 8, (n * 2 + 1) % 8
                        n += 1

                        def mm(k, p_, fc=fc, tb=tb):
                            for cc in range(NC_):
                                ins = nc.tensor.matmul(PS[p_][:], lhsT=wb[k][:, cc, fc * 128:(fc + 1) * 128], rhs=hT[:, cc, tb * 512:(tb + 1) * 512],
                                                       start=(cc == 0), stop=(cc == NC_ - 1))
                            return ins
                        kb.op("pe", lambda: mm(ka, pa), reads=[hB, wB[ka]], writes=[PSB[pa]])
                        kb.op("pe", lambda: mm(kg, pg), reads=[hB, wB[kg]], writes=[PSB[pg]])
                        z_ = zt[n % 2]
                        zb_ = zB[n % 2]
                        kb.op("act", lambda: nc.scalar.activation(out=z_[:], in_=PS[pg][:], func=AF.Sigmoid), reads=[PSB[pg]], writes=[zb_])
                        s_ = ns % 3
                        ns += 1
                        kb.op("dve", lambda: nc.vector.tensor_tensor(out=stg[s_][:], in0=PS[pa][:], in1=z_[:], op=ALU.mult), reads=[PSB[pa], zb_], writes=[sgB[s_]])
                        kb.dma("sp", xgT_d[jb * 4 + fc, :, tb * 512:(tb + 1) * 512], stg[s_][:], reads=[sgB[s_]], writes=[B_["xgT"]])

        def stage_sg(st, mixT, mB):
            swn = sb(st, "swn", [128, 8, 128], F32)
            swT = sb(st, "swT", [128, 8, 128], BF16)
            sgb = sb(st, "sgb", [128, 1024], F32)
            wB_ = kb.buf()
            kb.dma("sp", swn[:], I["sg_w"].rearrange("g p q -> p g q"), writes=[wB_])
            kb.dma("sp", sgb[:], I["sg_b"][0:1, :].partition_broadcast(128), writes=[wB_])
            for g4 in range(2):
                def tr(g4=g4):
                    for k in range(4):
                        g = g4 * 4 + k
                        ins = nc.tensor.transpose(PS[g4][:, k * 128:(k + 1) * 128], swn[:, g, :], ident[:])
                    return ins
                kb.op("pe", tr, reads=[wB_, cB], writes=[PSB[g4]])
                kb.op("dve", lambda g4=g4: nc.vector.tensor_copy(out=swT[:, g4 * 4:(g4 + 1) * 4, :], in_=PS[g4][:].rearrange("p (k n) -> p k n", k=4)),
                      reads=[PSB[g4]], writes=[wB_])
            vg = sb(st, "sgvg", [128, NT, 1024], BF16)
            uT = sb(st, "sguT", [128, 8, SEQ], BF16)
            dB = kb.buf()
            kb.dma("sp", vg[:], vg_d.rearrange("(t p) c -> p t c", p=128), reads=[B_["vg"]], writes=[dB])
            kb.dma("sp", uT[:], uT_d.rearrange("g p n -> p g n"), reads=[B_["uT"]], writes=[dB])
            tmp = [sb(st, f"sgtmp{i}", [128, 512], F32) for i in range(2)]
            tB = kb.bufs(2)
            n = 0
            for g in range(8):
                for n4 in range(4):
                    pb = n % 8
                    t2 = n % 2
                    n += 1

                    def mm(g=g, n4=n4, pb=pb):
                        for k in range(4):
                            ins = nc.tensor.matmul(PS[pb][:, k * 128:(k + 1) * 128], lhsT=vg[:, n4 * 4 + k, g * 128:(g + 1) * 128], rhs=swT[:, g, :],
                                                   start=True, stop=True)
                        return ins
                    kb.op("pe", mm, reads=[dB, wB_], writes=[PSB[pb]])
                    kb.op("dve", lambda: nc.vector.tensor_tensor(out=tmp[t2][:].rearrange("p (k n) -> p k n", k=4), in0=PS[pb][:].rearrange("p (k n) -> p k n", k=4),
                                                                 in1=sgb[:, g * 128:(g + 1) * 128].unsqueeze(1).to_broadcast([128, 4, 128]), op=ALU.add),
                          reads=[PSB[pb], wB_], writes=[tB[t2]])
                    kb.op("dve", lambda: nc.vector.tensor_tensor(out=mixT[:, g, n4 * 512:(n4 + 1) * 512], in0=tmp[t2][:], in1=uT[:, g, n4 * 512:(n4 + 1) * 512], op=ALU.mult),
                          reads=[tB[t2], dB], writes=[mB])

        def stage_conv(st, mixT, mB):
            PADW = SEQ + 30
            xg = sb(st, "cvxg", [128, 8, PADW], BF16)
            xB = kb.buf()
            kb.op("dve", lambda: nc.vector.memset(xg[:, :, 0:15], 0.0), writes=[xB])
            kb.op("dve", lambda: nc.vector.memset(xg[:, :, 15 + SEQ:PADW], 0.0), writes=[xB])
            kb.dma("sp", xg[:, :, 15:15 + SEQ], xgT_d.rearrange("g p n -> p g n"), reads=[B_["xgT"]], writes=[xB])
            cwn = sb(st, "cvwn", [31, 1024], F32)
            cw = sb(st, "cvw", [128, 8, 32], F32)
            cb = sb(st, "cvb", [128, 8], F32)
            lg = sb(st, "cvlg", [128, 8], F32)
            lb = sb(st, "cvlb", [128, 8], F32)
            onesf = sb(st, "cvones", [128, 128], F32)
            pB = kb.buf()
            kb.dma("sp", cwn[:], I["conv_w"], writes=[pB])
            kb.dma("sp", onesf[:], C["onesf"], writes=[pB])
            kb.dma("sp", cb[:], I["conv_b"][0].rearrange("(g p) -> p g", p=128), writes=[pB], allow_slow_non_contiguous=True)
            kb.dma("sp", lg[:], I["conv_ln_g"][0].rearrange("(g p) -> p g", p=128), writes=[pB], allow_slow_non_contiguous=True)
            kb.dma("sp", lb[:], I["conv_ln_b"][0].rearrange("(g p) -> p g", p=128), writes=[pB], allow_slow_non_contiguous=True)

            def trw():
                for g in range(8):
                    ins = nc.tensor.transpose(PS[0][:, g * 32:g * 32 + 31], cwn[0:31, g * 128:(g + 1) * 128], ident[0:31, 0:31])
                return ins
            kb.op("pe", trw, reads=[pB, cB], writes=[PSB[0]])
            kb.op("dve", lambda: nc.vector.tensor_copy(out=cw[:, :, 0:31], in_=PS[0][:, 0:256].rearrange("p (g k) -> p g k", g=8)[:, :, 0:31]), reads=[PSB[0]], writes=[pB])
            dg = sb(st, "cvdg", [128, 8, 31, 128], BF16)
            dgB = kb.buf()
            for g in range(8):
                for k in range(31):
                    kb.op("dve", lambda g=g, k=k: nc.vector.tensor_scalar(out=dg[:, g, k, :], in0=ident[:], scalar1=cw[:, g, k:k + 1], scalar2=None, op0=ALU.mult),
                          reads=[pB, cB], writes=[dgB])
            xc = sb(st, "cvxc", [128, 8, 512], F32)
            xcB = kb.buf()
            sq = [sb(st, f"cvsq{i}", [128, 512], F32) for i in range(2)]
            sqB = kb.bufs(2)
            mean = sb(st, "cvmean", [128, 512], F32)
            rstd = sb(st, "cvrstd", [128, 512], F32)
            stB = kb.buf()
            tmp = [sb(st, f"cvtmp{i}", [128, 512], F32) for i in range(2)]
            tB = kb.bufs(2)
            n = 0
            for tb in range(4):
                for g in range(8):
                    pb = n % 4
                    n += 1

                    def mmc(g=g, pb=pb):
                        for k in range(31):
                            ins = nc.tensor.matmul(PS[pb][:], lhsT=dg[:, g, k, :], rhs=xg[:, g, tb * 512 + k: tb * 512 + k + 512], start=(k == 0), stop=(k == 30))
                        return ins
                    kb.op("pe", mmc, reads=[dgB, xB], writes=[PSB[pb]])
                    kb.op("act", lambda g=g, pb=pb: nc.scalar.activation(out=xc[:, g, :], in_=PS[pb][:], func=AF.Identity, bias=cb[:, g:g + 1], scale=1.0),
                          reads=[PSB[pb], pB], writes=[xcB])
                for g in range(8):
                    s2 = g % 2
                    kb.op("act", lambda g=g, s2=s2: nc.scalar.activation(out=sq[s2][:], in_=xc[:, g, :], func=AF.Square), reads=[xcB], writes=[sqB[s2]])

                    def mms(g=g, s2=s2):
                        nc.tensor.matmul(PS[4][:], lhsT=onesf[:], rhs=xc[:, g, :], start=(g == 0), stop=(g == 7))
                        return nc.tensor.matmul(PS[5][:], lhsT=onesf[:], rhs=sq[s2][:], start=(g == 0), stop=(g == 7))
                    kb.op("pe", mms, reads=[xcB, sqB[s2], pB], writes=[PSB[4], PSB[5]])
                kb.op("act", lambda: nc.scalar.activation(out=mean[:], in_=PS[4][:], func=AF.Copy, scale=1.0 / 1024), reads=[PSB[4]], writes=[stB])
                kb.op("dve", lambda: nc.vector.tensor_tensor(out=rstd[:], in0=mean[:], in1=mean[:], op=ALU.mult), reads=[stB], writes=[stB])
                kb.op("dve", lambda: nc.vector.scalar_tensor_tensor(out=rstd[:], in0=PS[5][:], scalar=1.0 / 1024, in1=rstd[:], op0=ALU.mult, op1=ALU.subtract),
                      reads=[PSB[5], stB], writes=[stB])
                kb.op("act", lambda: nc.scalar.activation(out=rstd[:], in_=rstd[:], func=AF.Sqrt, bias=EPS, scale=1.0), reads=[stB], writes=[stB])
                kb.op("dve", lambda: nc.vector.reciprocal(out=rstd[:], in_=rstd[:]), reads=[stB], writes=[stB])
                for g in range(8):
                    t2 = g % 2
                    kb.op("dve", lambda g=g, t2=t2: nc.vector.tensor_tensor(out=tmp[t2][:], in0=xc[:, g, :], in1=mean[:], op=ALU.subtract), reads=[xcB, stB], writes=[tB[t2]])
                    kb.op("dve", lambda g=g, t2=t2: nc.vector.tensor_tensor(out=tmp[t2][:], in0=tmp[t2][:], in1=rstd[:], op=ALU.mult), reads=[tB[t2], stB], writes=[tB[t2]])
                    kb.op("act", lambda g=g, t2=t2: nc.scalar.activation(out=mixT[:, 8 + g, tb * 512:(tb + 1) * 512], in_=tmp[t2][:], func=AF.Silu,
                                                                       bias=lb[:, g:g + 1], scale=lg[:, g:g + 1]), reads=[tB[t2], pB], writes=[mB])

        def stage_post(st, l, xsrc_d, xsrcB, aff, affB):
            bt = {}
            bcB = kb.buf()
            for nm, row, p1 in (("g1", mod_d[l, 0:1, 2 * D:3 * D], False), ("sc2", mod_d[l, 0:1, 4 * D:5 * D], True),
                                ("sh2", mod_d[l, 0:1, 3 * D:4 * D], False), ("lg", I["ln1_g"][l:l + 1, :], False),
                                ("lb", I["ln1_b"][l:l + 1, :], False)):
                bt[nm] = sb(st, "pb_" + nm, [128, D], F32)
                load_bc(bt[nm], row, bcB, plus_one=p1)
            wr = sb(st, "wr", [128, NC_, NE], F32)
            kb.dma("sp", wr[:], I["w_router"][l].rearrange("(c p) e -> p c e", p=128), writes=[bcB])
            xt = [sb(st, f"pxt{i}", [128, D], F32) for i in range(2)]
            yt = [sb(st, f"pyt{i}", [128, D], F32) for i in range(2)]
            xB = kb.bufs(2)
            yB = kb.bufs(2)
            hb = [sb(st, f"phb{i}", [128, D], BF16) for i in range(2)]
            hbB = kb.bufs(2)
            hT = sb(st, "phT", [128, NC_, 128], F32)
            hTB = kb.buf()
            stt = sb(st, "pstt", [128, 4, 6], F32)
            mv = sb(st, "pmv", [128, 2], F32)
            rstd = sb(st, "prstd", [128, 1], F32)
            nmr = sb(st, "pnmr", [128, 1], F32)
            smB = kb.buf()
            lg_ = sb(st, "plg", [128, NE], F32)
            mx = sb(st, "pmx", [128, 1], F32)
            sm = sb(st, "psm", [128, 1], F32)
            lB = kb.buf()
            for t in range(NT):
                x_, y_, xb_, yb_ = xt[t % 2], yt[t % 2], xB[t % 2], yB[t % 2]
                kb.dma("sp", x_[:], xsrc_d[t * 128:(t + 1) * 128, :], reads=[xsrcB], writes=[xb_])
                kb.dma("sp", y_[:], y_d[t * 128:(t + 1) * 128, :], reads=[B_["y"]], writes=[yb_])
                kb.op("dve", lambda y_=y_: nc.vector.tensor_tensor(out=y_[:], in0=y_[:], in1=bt["g1"][:], op=ALU.mult), reads=[yb_, bcB], writes=[yb_])
                kb.op("dve", lambda x_=x_, y_=y_: nc.vector.scalar_tensor_tensor(out=x_[:], in0=x_[:], scalar=float(ALPHA), in1=y_[:], op0=ALU.mult, op1=ALU.add),
                      reads=[xb_, yb_], writes=[xb_])
                ln_stats(stt, mv, rstd, nmr, x_, [xb_], smB)
                kb.op("act", lambda x_=x_: nc.scalar.activation(out=x_[:], in_=x_[:], func=AF.Identity, bias=nmr[:], scale=rstd[:]), reads=[smB, xb_], writes=[xb_])
                kb.op("dve", lambda x_=x_: nc.vector.tensor_tensor(out=x_[:], in0=x_[:], in1=bt["lg"][:], op=ALU.mult), reads=[xb_, bcB], writes=[xb_])
                kb.op("dve", lambda x_=x_: nc.vector.tensor_tensor(out=x_[:], in0=x_[:], in1=bt["lb"][:], op=ALU.add), reads=[xb_, bcB], writes=[xb_])
                kb.dma("sp", x1_d[t * 128:(t + 1) * 128, :], x_[:], reads=[xb_], writes=[B_["x1"]])
                ln_stats(stt, mv, rstd, nmr, x_, [xb_], smB)
                kb.op("act", lambda x_=x_, y_=y_: nc.scalar.activation(out=y_[:], in_=x_[:], func=AF.Identity, bias=nmr[:], scale=rstd[:]), reads=[smB, xb_], writes=[yb_])
                kb.op("dve", lambda y_=y_: nc.vector.tensor_tensor(out=y_[:], in0=y_[:], in1=bt["sc2"][:], op=ALU.mult), reads=[yb_, bcB], writes=[yb_])
                kb.op("dve", lambda y_=y_: nc.vector.tensor_tensor(out=y_[:], in0=y_[:], in1=bt["sh2"][:], op=ALU.add), reads=[yb_, bcB], writes=[yb_])
                h_ = hb[t % 2]
                kb.op("act", lambda y_=y_, h_=h_: nc.scalar.copy(out=h_[:], in_=y_[:]), reads=[yb_], writes=[hbB[t % 2]])
                kb.dma("sp", h2_d[t * 128:(t + 1) * 128, :], h_[:], reads=[hbB[t % 2]], writes=[B_["h2"]])
                for q4 in range(4):
                    pb = (t * 4 + q4) % 4

                    def tr(y_=y_, q4=q4, pb=pb):
                        for k in range(4):
                            cc = q4 * 4 + k
                            ins = nc.tensor.transpose(PS[pb][:, k * 128:(k + 1) * 128], y_[:, cc * 128:(cc + 1) * 128], ident[:])
                        return ins
                    kb.op("pe", tr, reads=[yb_, cB], writes=[PSB[pb]])
                    evac(hT[:, q4 * 4:(q4 + 1) * 4, :], PS[pb][:].rearrange("p (k n) -> p k n", k=4), [PSB[pb]], [hTB])
                pr = 4 + t % 2

                def mmr(pr=pr):
                    for cc in range(NC_):
                        ins = nc.tensor.matmul(PS[pr][:, 0:NE], lhsT=hT[:, cc, :], rhs=wr[:, cc, :], start=(cc == 0), stop=(cc == NC_ - 1))
                    return ins
                kb.op("pe", mmr, reads=[hTB, bcB], writes=[PSB[pr]])
                kb.op("dve", lambda pr=pr: nc.vector.tensor_copy(out=lg_[:], in_=PS[pr][:, 0:NE]), reads=[PSB[pr]], writes=[lB])
                kb.op("dve", lambda: nc.vector.reduce_max(out=mx[:], in_=lg_[:], axis=AX.X), reads=[lB], writes=[lB])
                kb.op("dve", lambda: nc.vector.tensor_scalar(out=mx[:], in0=mx[:], scalar1=-1.0, scalar2=None, op0=ALU.mult), reads=[lB], writes=[lB])
                kb.op("act", lambda: nc.scalar.activation(out=lg_[:], in_=lg_[:], func=AF.Exp, bias=mx[:], scale=1.0), reads=[lB], writes=[lB])
                kb.op("dve", lambda: nc.vector.reduce_sum(out=sm[:], in_=lg_[:], axis=AX.X), reads=[lB], writes=[lB])
                kb.op("dve", lambda: nc.vector.reciprocal(out=sm[:], in_=sm[:]), reads=[lB], writes=[lB])
                kb.op("dve", lambda t=t: nc.vector.tensor_scalar(out=aff[:, t, :], in0=lg_[:], scalar1=sm[:], scalar2=None, op0=ALU.mult), reads=[lB], writes=[affB])

        def stage_moe(st, l, aff, affB, idxf_all, idxB, nexp=NE):
            mask = sb(st, "mmask", [128, NT, NE], F32)
            maskb = sb(st, "mmaskb", [128, NT, NE], BF16)
            key = sb(st, "mkey", [128, NT, NE], F32)
            tri = sb(st, "mtri", [128, 128], BF16)
            R = sb(st, "mR", [128, NT, NE, 8], BF16)
            r1 = sb(st, "mr1", [128, NT, NE], F32)
            pcol = sb(st, "mpcol", [128, 1], F32)
            tcol = sb(st, "mtcol", [128, NT, NE], F32)
            rt = ExitStack()
            affT = sb(rt, "affT", [NE, SEQ], F32)
            work = sb(rt, "mwork", [NE, SEQ], F32)
            m8 = sb(rt, "m8", [NE, 8], F32)
            aTB = kb.buf()
            for t4 in range(4):
                def tr(t4=t4):
                    for k in range(4):
                        t = t4 * 4 + k
                        ins = nc.tensor.transpose(PS[t4][0:NE, k * 128:(k + 1) * 128], aff[:, t, :], ident[:])
                    return ins
                kb.op("pe", tr, reads=[affB, cB], writes=[PSB[t4]])
                kb.op("dve", lambda t4=t4: nc.vector.tensor_copy(out=affT[:, t4 * 512:(t4 + 1) * 512], in_=PS[t4][0:NE, :]), reads=[PSB[t4]], writes=[aTB])
            kb.op("dve", lambda: nc.vector.tensor_copy(out=work[:], in_=affT[:]), reads=[aTB], writes=[aTB])
            for r in range(CAP // 8):
                kb.op("dve", lambda: nc.vector.max(out=m8[:], in_=work[:]), reads=[aTB], writes=[aTB])
                if r < CAP // 8 - 1:
                    kb.op("dve", lambda: nc.vector.match_replace(out=work[:], in_to_replace=m8[:], in_values=work[:], imm_value=-1.0), reads=[aTB], writes=[aTB])
            kb.op("dve", lambda: nc.vector.tensor_scalar(out=work[:], in0=affT[:], scalar1=m8[:, 7:8], scalar2=None, op0=ALU.is_ge), reads=[aTB], writes=[aTB])
            mkB = kb.buf()
            for t4 in range(4):
                def tr2(t4=t4):
                    for k in range(4):
                        t = t4 * 4 + k
                        ins = nc.tensor.transpose(PS[4 + t4][:, k * NE:(k + 1) * NE], work[:, t * 128:(t + 1) * 128], ident[0:NE, 0:NE])
                    return ins
                kb.op("pe", tr2, reads=[aTB, cB], writes=[PSB[4 + t4]])
                kb.op("dve", lambda t4=t4: nc.vector.tensor_copy(out=mask[:, t4 * 4:(t4 + 1) * 4, :],
                                                               in_=PS[4 + t4][:, 0:4 * NE].rearrange("p (k e) -> p k e", k=4)), reads=[PSB[4 + t4]], writes=[mkB])
            kb.op("dve", lambda: nc.vector.tensor_copy(out=maskb[:], in_=mask[:]), reads=[mkB], writes=[mkB])
            kb.dma("sp", tri[:], C["tri"], writes=[mkB])

            def cums():
                for t in range(NT):
                    for i2 in range(t):
                        nc.tensor.matmul(PS[0][:, t * NE:(t + 1) * NE], lhsT=onesb[:], rhs=maskb[:, i2, :], start=(i2 == 0), stop=False)
                    ins = nc.tensor.matmul(PS[0][:, t * NE:(t + 1) * NE], lhsT=tri[:], rhs=maskb[:, t, :], start=(t == 0), stop=True)
                return ins
            kb.op("pe", cums, reads=[mkB, cB], writes=[PSB[0]])
            kb.op("dve", lambda: nc.vector.tensor_tensor(out=key[:].rearrange("p t e -> p (t e)"), in0=PS[0][:, 0:NT * NE],
                                                         in1=mask[:].rearrange("p t e -> p (t e)"), op=ALU.mult), reads=[PSB[0], mkB], writes=[mkB])
            kb.op("dve", lambda: nc.vector.tensor_scalar(out=key[:], in0=key[:], scalar1=-1.0, scalar2=None, op0=ALU.add), reads=[mkB], writes=[mkB])
            kb.dma("sp", pcol[:], C["pcol"], writes=[mkB])
            kb.dma("sp", tcol[:].rearrange("p t e -> p (t e)"), C["tcol"], writes=[mkB])
            kb.op("dve", lambda: nc.vector.memset(R[:], 0.0), writes=[mkB])
            kb.op("dve", lambda: nc.vector.tensor_copy(out=R[:, :, :, 0], in_=pcol[:].unsqueeze(2).to_broadcast([128, NT, NE])), reads=[mkB], writes=[mkB])
            kb.op("dve", lambda: nc.vector.tensor_copy(out=R[:, :, :, 1], in_=tcol[:]), reads=[mkB], writes=[mkB])
            kb.op("dve", lambda: nc.vector.tensor_copy(out=R[:, :, :, 2], in_=aff[:]), reads=[mkB, affB], writes=[mkB])
            kb.op("dve", lambda: nc.vector.tensor_tensor(out=r1[:], in0=aff[:], in1=R[:, :, :, 2], op=ALU.subtract), reads=[mkB, affB], writes=[mkB])
            kb.op("dve", lambda: nc.vector.tensor_copy(out=R[:, :, :, 3], in_=r1[:]), reads=[mkB], writes=[mkB])
            kb.op("dve", lambda: nc.vector.tensor_tensor(out=r1[:], in0=r1[:], in1=R[:, :, :, 3], op=ALU.subtract), reads=[mkB], writes=[mkB])
            kb.op("dve", lambda: nc.vector.tensor_copy(out=R[:, :, :, 4], in_=r1[:]), reads=[mkB], writes=[mkB])
            kb.barrier()
            rt.close()

            io256 = sb(st, "io256", [128, 256], F32)
            kb.dma("sp", io256[:], C["iota256"], writes=[mkB])
            ig_all = sb(st, "mig", [128, 2 * NE, 8], F32)
            idxi = sb(st, "midxi", [128, 2 * NE], I32)
            gg = sb(st, "mgg", [128, 2 * NE], F32)
            igB = kb.buf()
            sx = ExitStack()
            Sel = [sb(sx, f"mSel{i}", [128, NT, 256], BF16) for i in range(2)]
            selB = kb.bufs(2)
            for e in range(nexp):
                e2 = e % 2
                for t in range(NT):
                    kb.op("dve", lambda t=t: nc.vector.tensor_scalar(out=Sel[e2][:, t, :], in0=io256[:], scalar1=key[:, t, e:e + 1], scalar2=None, op0=ALU.is_equal),
                          reads=[mkB], writes=[selB[e2]])
                pi = 6 + e2

                def mmi():
                    for half in range(2):
                        for t in range(NT):
                            ins = nc.tensor.matmul(PS[pi][:, half * 8:half * 8 + 8], lhsT=Sel[e2][:, t, half * 128:(half + 1) * 128], rhs=R[:, t, e, :],
                                                   start=(t == 0), stop=(t == NT - 1))
                    return ins
                kb.op("pe", mmi, reads=[selB[e2], mkB], writes=[PSB[pi]])
                kb.op("act", lambda: nc.scalar.copy(out=ig_all[:, 2 * e:2 * e + 2, :], in_=PS[pi][:, 0:16].rearrange("p (h k) -> p h k", h=2)), reads=[PSB[pi]], writes=[igB])
            ne2 = 2 * nexp
            kb.op("dve", lambda: nc.vector.scalar_tensor_tensor(out=idxf_all[:, 0:ne2], in0=ig_all[:, 0:ne2, 1], scalar=128.0, in1=ig_all[:, 0:ne2, 0], op0=ALU.mult, op1=ALU.add),
                  reads=[igB], writes=[igB, idxB])
            kb.op("dve", lambda: nc.vector.tensor_copy(out=idxi[:, 0:ne2], in_=idxf_all[:, 0:ne2]), reads=[igB], writes=[igB])
            kb.op("dve", lambda: nc.vector.tensor_tensor(out=gg[:, 0:ne2], in0=ig_all[:, 0:ne2, 2], in1=ig_all[:, 0:ne2, 3], op=ALU.add), reads=[igB], writes=[igB])
            kb.op("dve", lambda: nc.vector.tensor_tensor(out=gg[:, 0:ne2], in0=gg[:, 0:ne2], in1=ig_all[:, 0:ne2, 4], op=ALU.add), reads=[igB], writes=[igB])
            kb.barrier()
            sx.close()

            ex = ExitStack()
            xs = [sb(ex, f"mxs{i}", [128, 2, D], BF16) for i in range(2)]
            xsB = kb.bufs(2)
            xsT = [sb(ex, f"mxsT{i}", [128, NC_, 256], BF16) for i in range(2)]
            xsTB = kb.bufs(2)
            NWB = 3
            wg = [sb(ex, f"mwg{i}", [128, NC_, 512], BF16) for i in range(NWB)]
            wu = [sb(ex, f"mwu{i}", [128, NC_, 512], BF16) for i in range(NWB)]
            wd = [sb(ex, f"mwd{i}", [128, NC_, 512], BF16) for i in range(2)]
            wgB, wuB, wdB = kb.bufs(NWB), kb.bufs(NWB), kb.bufs(2)
            sgt = [sb(ex, f"msg{i}", [128, 256], F32) for i in range(2)]
            sgB = kb.bufs(2)
            hidT = sb(ex, "mhidT", [128, NC_, 256], BF16)
            hidB = kb.buf()
            yeb = [sb(ex, f"myeb{i}", [128, 2, D], BF16) for i in range(2)]
            yeB = kb.bufs(2)
            nw = 0
            nwd = 0
            nsg = 0

            def gather(e):
                for half in range(2):
                    kb.dma("pool", xs[e % 2][:, half, :], h2_d, reads=[igB, B_["h2"]], writes=[xsB[e % 2]],
                           indirect=dict(out_offset=None, in_offset=bass.IndirectOffsetOnAxis(ap=idxi[:, 2 * e + half:2 * e + half + 1], axis=0)))
            gather(0)
            for e in range(nexp):
                e2 = e % 2
                if e + 1 < nexp:
                    gather(e + 1)
                for half in range(2):
                    for cg in range(2):
                        pb = half * 2 + cg
                        psv = PS[pb][:].bitcast(BF16)

                        def trx(half=half, cg=cg, psv=psv):
                            for k in range(8):
                                cc = cg * 8 + k
                                ins = nc.tensor.transpose(psv[:, k * 128:(k + 1) * 128], xs[e2][:, half, cc * 128:(cc + 1) * 128], identb[:])
                            return ins
                        kb.op("pe", trx, reads=[xsB[e2], cB], writes=[PSB[pb]])
                        evac(xsT[e2][:, cg * 8:(cg + 1) * 8, half * 128:(half + 1) * 128], psv.rearrange("p (k n) -> p k n", k=8), [PSB[pb]], [xsTB[e2]])
                for fb in range(4):
                    w2 = nw % NWB
                    nw += 1
                    kb.dma("pool", wg[w2][:], I["w_gate"][l, e, :, fb * 512:(fb + 1) * 512].rearrange("(c p) n -> p c n", p=128), writes=[wgB[w2]])
                    kb.dma("pool", wu[w2][:], I["w_up"][l, e, :, fb * 512:(fb + 1) * 512].rearrange("(c p) n -> p c n", p=128), writes=[wuB[w2]])
                    for fc in range(4):
                        pg, pu = 4 + (fc % 2) * 2, 5 + (fc % 2) * 2

                        def mmg(wt, p_, fc=fc):
                            for cc in range(NC_):
                                ins = nc.tensor.matmul(PS[p_][:, 0:256], lhsT=wt[:, cc, fc * 128:(fc + 1) * 128], rhs=xsT[e2][:, cc, :],
                                                       start=(cc == 0), stop=(cc == NC_ - 1))
                            return ins
                        kb.op("pe", lambda: mmg(wg[w2], pg), reads=[wgB[w2], xsTB[e2]], writes=[PSB[pg]])
                        kb.op("pe", lambda: mmg(wu[w2], pu), reads=[wuB[w2], xsTB[e2]], writes=[PSB[pu]])
                        s2 = nsg % 2
                        nsg += 1
                        kb.op("act", lambda: nc.scalar.activation(out=sgt[s2][:], in_=PS[pg][:, 0:256], func=AF.Silu), reads=[PSB[pg]], writes=[sgB[s2]])
                        kb.op("dve", lambda: nc.vector.tensor_tensor(out=hidT[:, fb * 4 + fc, :], in0=PS[pu][:, 0:256], in1=sgt[s2][:], op=ALU.mult),
                              reads=[PSB[pu], sgB[s2]], writes=[hidB])
                for db in range(4):
                    w2 = nwd % 2
                    nwd += 1
                    kb.dma("pool", wd[w2][:], I["w_down"][l, e, :, db * 512:(db + 1) * 512].rearrange("(c p) n -> p c n", p=128), writes=[wdB[w2]])
                    for half in range(2):
                        pb = (db * 2 + half) % 4

                        def mmd(half=half, pb=pb):
                            for fc in range(NC_):
                                ins = nc.tensor.matmul(PS[pb][:], lhsT=hidT[:, fc, half * 128:(half + 1) * 128], rhs=wd[w2][:, fc, :],
                                                       start=(fc == 0), stop=(fc == NC_ - 1))
                            return ins
                        kb.op("pe", mmd, reads=[hidB, wdB[w2]], writes=[PSB[pb]])
                        kb.op("act", lambda half=half, pb=pb: nc.scalar.activation(out=yeb[e2][:, half, db * 512:(db + 1) * 512], in_=PS[pb][:], func=AF.Copy,
                                                                                  scale=gg[:, 2 * e + half:2 * e + half + 1]), reads=[PSB[pb], igB], writes=[yeB[e2]])
                for half in range(2):
                    r0 = (e * 2 + half) * 128
                    kb.dma("sp", ye_d[r0:r0 + 128, :], yeb[e2][:, half, :], reads=[yeB[e2]], writes=[B_["ye"]])
            kb.barrier()
            ex.close()

        def stage_combine(st, l, idxf_all, idxB, dst_d, dstB, nexp=NE):
            nk = 2 * nexp
            YE = sb(st, "cYE", [128, nk, D], BF16)
            YB = kb.buf()
            for db in range(4):
                kb.dma("sp", YE[:, :, db * 512:(db + 1) * 512], ye_d[0:nk * 128, db * 512:(db + 1) * 512].rearrange("(k p) n -> p k n", p=128),
                       reads=[B_["ye"]], writes=[YB])
            io128 = sb(st, "cio", [128, 128], F32)
            bt = {}
            bcB = kb.buf()
            kb.dma("sp", io128[:], C["iota256"][:, 0:128], writes=[bcB])
            for nm, row in (("g2", mod_d[l, 0:1, 5 * D:6 * D]), ("lg", I["ln2_g"][l:l + 1, :]), ("lb", I["ln2_b"][l:l + 1, :])):
                bt[nm] = sb(st, "qb_" + nm, [128, D], F32)
                load_bc(bt[nm], row, bcB)
            sl = [sb(st, f"csl{i}", [128, nk, 128], BF16) for i in range(2)]
            slB2 = kb.bufs(2)
            idt = sb(st, "cidt", [128, 2 * NE], F32)
            idB = kb.buf()
            xt = [sb(st, f"cxt{i}", [128, D], F32) for i in range(2)]
            xB = kb.bufs(2)
            rt_ = sb(st, "crt", [128, D], F32)
            rB = kb.buf()
            stt = sb(st, "qstt", [128, 4, 6], F32)
            mv = sb(st, "qmv", [128, 2], F32)
            rstd = sb(st, "qrstd", [128, 1], F32)
            nmr = sb(st, "qnmr", [128, 1], F32)
            smB = kb.buf()
            toks = []
            for t in range(NT):
                s2 = t % 2
                x_, xb_ = xt[s2], xB[s2]
                kb.dma("sp", x_[:], x1_d[t * 128:(t + 1) * 128, :], reads=[B_["x1"]], writes=[xb_])
                kb.op("dve", lambda: nc.vector.tensor_scalar(out=idt[:, 0:nk], in0=idxf_all[:, 0:nk], scalar1=float(-128 * t), scalar2=None, op0=ALU.add),
                      reads=[idxB], writes=[idB])
                kb.op("dve", lambda: nc.vector.tensor_tensor(out=sl[s2][:], in0=io128[:].unsqueeze(1).to_broadcast([128, nk, 128]),
                                                             in1=idt[:, 0:nk].unsqueeze(2).to_broadcast([128, nk, 128]), op=ALU.is_equal),
                      reads=[idB, bcB], writes=[slB2[s2]])
                for db in range(4):
                    pb = db + 4 * s2

                    def mmc(db=db, pb=pb):
                        for k in range(nk):
                            ins = nc.tensor.matmul(PS[pb][:], lhsT=sl[s2][:, k, :], rhs=YE[:, k, db * 512:(db + 1) * 512], start=(k == 0), stop=(k == nk - 1))
                        return ins
                    kb.op("pe", mmc, reads=[slB2[s2], YB], writes=[PSB[pb]])
                    kb.op("dve", lambda db=db, pb=pb: nc.vector.tensor_tensor(out=rt_[:, db * 512:(db + 1) * 512], in0=PS[pb][:], in1=bt["g2"][:, db * 512:(db + 1) * 512], op=ALU.mult),
                          reads=[PSB[pb], bcB], writes=[rB])
                kb.op("dve", lambda: nc.vector.scalar_tensor_tensor(out=x_[:], in0=x_[:], scalar=float(ALPHA), in1=rt_[:], op0=ALU.mult, op1=ALU.add),
                      reads=[xb_, rB], writes=[xb_])
                ln_stats(stt, mv, rstd, nmr, x_, [xb_], smB)
                kb.op("act", lambda: nc.scalar.activation(out=x_[:], in_=x_[:], func=AF.Identity, bias=nmr[:], scale=rstd[:]), reads=[smB, xb_], writes=[xb_])
                kb.op("dve", lambda: nc.vector.tensor_tensor(out=x_[:], in0=x_[:], in1=bt["lg"][:], op=ALU.mult), reads=[xb_, bcB], writes=[xb_])
                kb.op("dve", lambda: nc.vector.tensor_tensor(out=x_[:], in0=x_[:], in1=bt["lb"][:], op=ALU.add), reads=[xb_, bcB], writes=[xb_])
                toks.append(kb.dma("sp", dst_d[t * 128:(t + 1) * 128, :], x_[:], reads=[xb_], writes=[dstB]))
            return toks

        def stage_postmoe(st, l, dst_d, dstB):
            bt = {}
            bcB = kb.buf()
            for nm, row in (("g2", mod_d[l, 0:1, 5 * D:6 * D]), ("lg", I["ln2_g"][l:l + 1, :]), ("lb", I["ln2_b"][l:l + 1, :])):
                bt[nm] = sb(st, "qb_" + nm, [128, D], F32)
                load_bc(bt[nm], row, bcB)
            xt = [sb(st, f"qxt{i}", [128, D], F32) for i in range(2)]
            yt = [sb(st, f"qyt{i}", [128, D], F32) for i in range(2)]
            xB, yB = kb.bufs(2), kb.bufs(2)
            stt = sb(st, "qstt", [128, 4, 6], F32)
            mv = sb(st, "qmv", [128, 2], F32)
            rstd = sb(st, "qrstd", [128, 1], F32)
            nmr = sb(st, "qnmr", [128, 1], F32)
            smB = kb.buf()
            toks = []
            for t in range(NT):
                x_, y_, xb_, yb_ = xt[t % 2], yt[t % 2], xB[t % 2], yB[t % 2]
                kb.dma("sp", x_[:], x1_d[t * 128:(t + 1) * 128, :], reads=[B_["x1"]], writes=[xb_])
                kb.dma("sp", y_[:], moe_d[t * 128:(t + 1) * 128, :], reads=[B_["moe"]], writes=[yb_])
                kb.op("dve", lambda y_=y_: nc.vector.tensor_tensor(out=y_[:], in0=y_[:], in1=bt["g2"][:], op=ALU.mult), reads=[yb_, bcB], writes=[yb_])
                kb.op("dve", lambda x_=x_, y_=y_: nc.vector.scalar_tensor_tensor(out=x_[:], in0=x_[:], scalar=float(ALPHA), in1=y_[:], op0=ALU.mult, op1=ALU.add),
                      reads=[xb_, yb_], writes=[xb_])
                ln_stats(stt, mv, rstd, nmr, x_, [xb_], smB)
                kb.op("act", lambda x_=x_: nc.scalar.activation(out=x_[:], in_=x_[:], func=AF.Identity, bias=nmr[:], scale=rstd[:]), reads=[smB, xb_], writes=[xb_])
                kb.op("dve", lambda x_=x_: nc.vector.tensor_tensor(out=x_[:], in0=x_[:], in1=bt["lg"][:], op=ALU.mult), reads=[xb_, bcB], writes=[xb_])
                kb.op("dve", lambda x_=x_: nc.vector.tensor_tensor(out=x_[:], in0=x_[:], in1=bt["lb"][:], op=ALU.add), reads=[xb_, bcB], writes=[xb_])
                toks.append(kb.dma("sp", dst_d[t * 128:(t + 1) * 128, :], x_[:], reads=[xb_], writes=[dstB]))
            return toks

        xB_in = kb.buf("xin")
        stage_mod()
        hT_d = scratch("hT_d", [NC_, 128, SEQ + CTX], BF16)
        if upto >= 1:
            with ExitStack() as st:
                hT = sb(st, "hT0", [128, NC_, SEQ + CTX], BF16)
                hB = kb.buf()
                with ExitStack() as s2:
                    stage_pre(s2, I["x"], xB_in, NT, mod_d[0, 0:1, 0:D], mod_d[0, 0:1, D:2 * D], hT, hB, 0, "a")
                    kb.barrier()
                if upto >= 2:
                    with ExitStack() as s2:
                        stage_pre(s2, I["ctx"], xB_in, CTX // 128, mod_d[0, 1:2, 0:D], mod_d[0, 1:2, D:2 * D], hT, hB, SEQ, "b")
                        kb.barrier()
                if "hT_d" in dbg:
                    kb.dma("sp", hT_d.rearrange("c p n -> p c n"), hT[:], reads=[hB], writes=[B_["mixT"]])
                if upto >= 3:
                    with ExitStack() as s2:
                        stage_inproj_ab(s2, hT, hB)
                        kb.barrier()
                kb.barrier()
        if upto >= 4:
            with ExitStack() as st:
                mixT = sb(st, "mixT", [128, NC_, SEQ], BF16)
                mB = kb.buf()
                with ExitStack() as s2:
                    stage_attn(s2, mixT, mB)
                    kb.barrier()
                if upto >= 5:
                    with ExitStack() as s2:
                        stage_fourier(s2, mixT, mB)
                        kb.barrier()
                if "mixT_d" in dbg:
                    kb.dma("sp", mixT_d.rearrange("c p n -> p c n"), mixT[:], reads=[mB], writes=[B_["mixT"]])
                if upto >= 6:
                    with ExitStack() as s2:
                        stage_gemm_out(s2, mixT, mB, I["ab_w_out"][0], "go")
                        kb.barrier()
                kb.barrier()

        def moe_block(l, xsrc_d, xsrcB, dst_d, dstB, nexp=NE):
            with ExitStack() as so:
                idxf_all = sb(so, "idxf_all", [128, 2 * NE], F32)
                idxB = kb.buf()
                with ExitStack() as st:
                    aff = sb(st, "aff", [128, NT, NE], F32)
                    affB = kb.buf()
                    with ExitStack() as s2:
                        stage_post(s2, l, xsrc_d, xsrcB, aff, affB)
                        kb.barrier()
                    if "aff_d" in dbg:
                        kb.dma("sp", aff_d, aff[:].rearrange("p t e -> p (t e)"), reads=[affB], writes=[B_["aff"]])
                    with ExitStack() as s2:
                        stage_moe(s2, l, aff, affB, idxf_all, idxB, nexp)
                        kb.barrier()
                    kb.barrier()
                with ExitStack() as s2:
                    toks = stage_combine(s2, l, idxf_all, idxB, dst_d, dstB, nexp)
                    kb.barrier()
                kb.barrier()
            return toks

        out_toks = []
        if upto >= 7:
            moe_block(0, I["x"], xB_in, xl1_d, B_["xl1"], nexp=(NE if upto >= 8 else 1))
        if upto >= 9:
            with ExitStack() as st:
                hT = sb(st, "hT1", [128, NC_, SEQ], BF16)
                hB = kb.buf()
                with ExitStack() as s2:
                    stage_pre(s2, xl1_d, B_["xl1"], NT, mod_d[1, 0:1, 0:D], mod_d[1, 0:1, D:2 * D], hT, hB, 0, "c")
                    kb.barrier()
                with ExitStack() as s2:
                    stage_inproj_cd(s2, hT, hB)
                    kb.barrier()
                kb.barrier()
        if upto >= 10:
            with ExitStack() as st:
                mixT = sb(st, "mixT1", [128, NC_, SEQ], BF16)
                mB = kb.buf()
                with ExitStack() as s2:
                    stage_sg(s2, mixT, mB)
                    kb.barrier()
                with ExitStack() as s2:
                    stage_conv(s2, mixT, mB)
                    kb.barrier()
                if "mixT_d" in dbg:
                    kb.dma("sp", mixT_d.rearrange("c p n -> p c n"), mixT[:], reads=[mB], writes=[B_["mixT"]])
                with ExitStack() as s2:
                    stage_gemm_out(s2, mixT, mB, I["cd_w_out"][0], "go1")
                    kb.barrier()
                kb.barrier()
        if upto >= 11:
            moe_block(1, xl1_d, B_["xl1"], out_d, B_["out"])
        kb.barrier()
    return nc, hc


_CACHE = {}


def make_in_maps(inputs, hc, ncores=8):
    shared = {}
    for k in INPUT_SHAPES:
        if k in ("x", "c", "ctx"):
            continue
        a = np.ascontiguousarray(np.asarray(inputs[k], dtype=np.float32)).reshape(INPUT_SHAPES[k])
        shared[k] = a
    for k, v in hc.items():
        shared["k_" + k] = v
    maps = []
    for b in range(ncores):
        m = dict(shared)
        m["x"] = np.ascontiguousarray(inputs["x"][b], dtype=np.float32)
        m["c"] = np.ascontiguousarray(inputs["c"][b], dtype=np.float32).reshape(1, D)
        m["ctx"] = np.ascontiguousarray(inputs["ctx"][b], dtype=np.float32)
        maps.append(m)
    return maps


def kernel(**inputs):
    if "prog" not in _CACHE:
        _CACHE["prog"] = build_program()
    nc, hc = _CACHE["prog"]
    maps = make_in_maps(inputs, hc, 8)
    res = run_bass_kernel_spmd(nc, maps, core_ids=list(range(8)))
    return np.stack([np.asarray(r["out"], dtype=np.float32) for r in res.results], axis=0)
```

```python
import numpy as np
import ml_dtypes
from contextlib import ExitStack
import concourse.bass as bass
import concourse.mybir as mybir
from concourse.bass_utils import run_bass_kernel_spmd

F32 = mybir.dt.float32
BF16 = mybir.dt.bfloat16
I32 = mybir.dt.int32
ALU = mybir.AluOpType
AF = mybir.ActivationFunctionType
AX = mybir.AxisListType

D = 2048
SEQ = 2048
CTX = 256
NT = SEQ // 128
NC_ = D // 128
ALPHA = 4 ** 0.25
EPS = 1e-6
NE = 16
CAP = 256


class Buf:
    __slots__ = ("name", "last_w", "reads")

    def __init__(self, name=""):
        self.name = name
        self.last_w = None
        self.reads = []


class KB:
    ND = 10

    def __init__(self, nc, es):
        self.nc = nc
        self.engs = {"pe": nc.tensor, "act": nc.scalar, "dve": nc.vector, "pool": nc.gpsimd, "sp": nc.sync}
        self.sem, self.cnt, self.seen = {}, {}, {}
        for e in self.engs:
            self.sem[e] = es.enter_context(nc.semaphore("pg_" + e))
            self.cnt[e] = 0
            self.seen[e] = {}
        self.dsem, self.dcnt, self.drr = {}, {}, {}
        for q in ("sp", "pool"):
            self.dsem[q] = [es.enter_context(nc.semaphore(f"dq_{q}{i}")) for i in range(self.ND)]
            self.dcnt[q] = [0] * self.ND
            self.drr[q] = 0
        self.nbuf = 0

    def buf(self, name=""):
        self.nbuf += 1
        return Buf(name or f"b{self.nbuf}")

    def bufs(self, n, name=""):
        return [self.buf(f"{name}{i}") for i in range(n)]

    def _wait(self, eng, tok):
        sem, val = tok
        key = id(sem)
        if self.seen[eng].get(key, 0) >= val:
            return
        self.engs[eng].wait_ge(sem, val)
        self.seen[eng][key] = val

    def _dep1(self, eng, tok):
        if tok[0] is self.sem[eng] and eng == "pe":
            return
        self._wait(eng, tok)

    def _deps(self, eng, reads, writes):
        for b in reads:
            if b.last_w is not None:
                self._dep1(eng, b.last_w)
        for b in writes:
            if b.last_w is not None:
                self._dep1(eng, b.last_w)
            for t in b.reads:
                self._dep1(eng, t)

    def _upd(self, tok, reads, writes):
        for b in reads:
            b.reads.append(tok)
        for b in writes:
            b.last_w = tok
            b.reads = []

    def op(self, eng, fn, reads=(), writes=()):
        self._deps(eng, reads, writes)
        ins = fn()
        self.cnt[eng] += 1
        ins.then_inc(self.sem[eng], 1)
        tok = (self.sem[eng], self.cnt[eng])
        self._upd(tok, reads, writes)
        return tok

    def dma(self, q, out, in_, reads=(), writes=(), indirect=None, **kw):
        i = self.drr[q]
        self.drr[q] = (i + 1) % self.ND
        sem = self.dsem[q][i]
        if self.dcnt[q][i] > 0:
            self._wait(q, (sem, self.dcnt[q][i]))
        self._deps(q, reads, writes)
        if indirect is not None:
            ins = self.engs[q].indirect_dma_start(out=out, in_=in_, **indirect)
        else:
            ins = self.engs[q].dma_start(out=out, in_=in_, **kw)
        ins.then_inc(sem, 16)
        self.dcnt[q][i] += 16
        tok = (sem, self.dcnt[q][i])
        self._upd(tok, reads, writes)
        return tok

    def barrier(self):
        toks = [(self.sem[e], self.cnt[e]) for e in self.engs if self.cnt[e] > 0]
        for q in self.dsem:
            for i, s in enumerate(self.dsem[q]):
                if self.dcnt[q][i] > 0:
                    toks.append((s, self.dcnt[q][i]))
        for e in self.engs:
            for t in toks:
                if t[0] is self.sem[e]:
                    continue
                self._wait(e, t)


def host_consts():
    c = {}
    c["ident"] = np.eye(128, dtype=np.float32)
    c["identb"] = np.eye(128).astype(ml_dtypes.bfloat16)
    t = np.arange(SEQ)
    r = (t // 64).astype(np.float32)
    col = (t % 64).astype(np.float32)
    inv = (np.float32(10000.0) ** (-np.arange(32, dtype=np.float32) / np.float32(32))).astype(np.float32)
    ang_r = (r[:, None] * inv).astype(np.float32)
    ang_c = (col[:, None] * inv).astype(np.float32)
    cosT = np.zeros((128, SEQ), np.float32)
    sinT = np.zeros((128, SEQ), np.float32)
    for base, ang in ((0, ang_r), (64, ang_c)):
        cs = np.cos(ang).astype(np.float32).T
        sn = np.sin(ang).astype(np.float32).T
        cosT[base:base + 32] = cs
        cosT[base + 32:base + 64] = cs
        sinT[base:base + 32] = -sn
        sinT[base + 32:base + 64] = sn
    c["cosT"] = cosT
    c["sinT"] = sinT
    k = np.arange(SEQ, dtype=np.int64)
    ph = (np.outer(k, k) % SEQ).astype(np.float64) * (2 * np.pi / SEQ)
    c["dft_c"] = np.cos(ph).astype(ml_dtypes.bfloat16)
    c["dft_s"] = np.sin(ph).astype(ml_dtypes.bfloat16)
    kc = np.arange(128, dtype=np.int64)
    phc = (np.outer(kc, kc) % 128).astype(np.float64) * (2 * np.pi / 128)
    c["dftc_c"] = (np.cos(phc) / 512.0).astype(ml_dtypes.bfloat16)
    c["dftc_sn"] = (-np.sin(phc) / 512.0).astype(ml_dtypes.bfloat16)
    kk = np.arange(128)[:, None]
    qq = np.arange(128)[None, :]
    mlo = (qq <= kk).astype(np.float32)
    mhi = (kk <= qq).astype(np.float32)
    c["mask_lo"] = np.tile(mlo, (1, 3)).astype(ml_dtypes.bfloat16)
    c["mask_hi"] = np.tile(mhi, (1, 3)).astype(ml_dtypes.bfloat16)
    c["iota256"] = np.tile(np.arange(256, dtype=np.float32)[None, :], (128, 1))
    c["iota_tok"] = np.tile(np.arange(SEQ, dtype=np.float32)[None, :], (128, 1))
    c["pcol"] = np.arange(128, dtype=np.float32).reshape(128, 1)
    c["tcol"] = np.tile(np.repeat(np.arange(16, dtype=np.float32), 16)[None, :], (128, 1))
    c["tri"] = (np.arange(128)[:, None] <= np.arange(128)[None, :]).astype(ml_dtypes.bfloat16)
    c["onesb"] = np.ones((128, 128), ml_dtypes.bfloat16)
    c["onesf"] = np.ones((128, 128), np.float32)
    return c


CONST_DT = {"identb": BF16, "dft_c": BF16, "dft_s": BF16, "dftc_c": BF16, "dftc_sn": BF16, "mask_lo": BF16,
            "mask_hi": BF16, "tri": BF16, "onesb": BF16}

INPUT_SHAPES = {
    "x": [SEQ, D], "c": [1, D], "ctx": [CTX, D], "c_ctx": [1, D],
    "w_mod": [2, D, 6 * D], "b_mod": [2, 6 * D], "ln1_g": [2, D], "ln1_b": [2, D], "ln2_g": [2, D], "ln2_b": [2, D],
    "w_router": [2, D, NE], "w_gate": [2, NE, D, D], "w_up": [2, NE, D, D], "w_down": [2, NE, D, D],
    "ab_w_in": [1, D, 3072], "ab_w_out": [1, D, D], "sink": [1, 12],
    "cd_w_in": [1, D, 4096], "cd_w_out": [1, D, D], "sg_ln_g": [1, 1024], "sg_ln_b": [1, 1024],
    "sg_w": [8, 128, 128], "sg_b": [1, 1024], "conv_w": [31, 1024], "conv_b": [1, 1024],
    "conv_ln_g": [1, 1024], "conv_ln_b": [1, 1024],
}


def build_program(upto=99, dbg=()):
    nc = bass.Bass("TRN2", target_bir_lowering=False)
    es = ExitStack()
    with es:
        kb = KB(nc, es)
        I = {}
        for name, shp in INPUT_SHAPES.items():
            I[name] = nc.dram_tensor(name, list(shp), F32, kind="ExternalInput").ap()
        hc = host_consts()
        C = {}
        for name, arr in hc.items():
            C[name] = nc.dram_tensor("k_" + name, list(arr.shape), CONST_DT.get(name, F32), kind="ExternalInput").ap()
        out_d = nc.dram_tensor("out", [SEQ, D], F32, kind="ExternalOutput").ap()

        def scratch(name, shape, dt=F32):
            kind = "ExternalOutput" if name in dbg else "Internal"
            return nc.dram_tensor(name, list(shape), dt, kind=kind).ap()

        mod_d = scratch("mod_d", [2, 2, 6 * D])
        qT_d = scratch("qT_d", [12, 128, SEQ], BF16)
        kT_d = scratch("kT_d", [4, 128, SEQ + CTX], BF16)
        v_d = scratch("v_d", [SEQ + CTX, 512], BF16)
        z_d = scratch("z_d", [SEQ, 512], BF16)
        mixT_d = scratch("mixT_d", [16, 128, SEQ], BF16)
        y_d = scratch("y_d", [SEQ, D])
        x1_d = scratch("x1_d", [SEQ, D])
        h2_d = scratch("h2_d", [SEQ, D], BF16)
        ye_d = scratch("ye_d", [2 * NE * 128, D], BF16)
        selT_d = scratch("selT_d", [2 * NE, 128, SEQ], BF16)
        moe_d = scratch("moe_d", [SEQ, D])
        xl1_d = scratch("xl1_d", [SEQ, D])
        uT_d = scratch("uT_d", [8, 128, SEQ], BF16)
        vg_d = scratch("vg_d", [SEQ, 1024], BF16)
        xgT_d = scratch("xgT_d", [8, 128, SEQ], BF16)
        aff_d = scratch("aff_d", [128, 256])
        B_ = {n: kb.buf(n) for n in ("mod", "qT", "kT", "v", "z", "mixT", "y", "x1", "h2", "ye", "selT", "moe", "xl1",
                                     "uT", "vg", "xgT", "aff", "out")}

        sbn = [0]

        def sb(st, name, shape, dt):
            sbn[0] += 1
            return st.enter_context(nc.sbuf_tensor(f"{name}_{sbn[0]}", list(shape), dt))

        PS = [es.enter_context(nc.psum_tensor(f"ps{i}", [128, 512], F32)) for i in range(8)]
        PSB = kb.bufs(8, "ps")
        ident = sb(es, "ident", [128, 128], F32)
        identb = sb(es, "identb", [128, 128], BF16)
        onesb = sb(es, "onesb", [128, 128], BF16)
        cB = kb.buf("consts")
        kb.dma("sp", ident[:], C["ident"], writes=[cB])
        kb.dma("sp", identb[:], C["identb"], writes=[cB])
        kb.dma("sp", onesb[:], C["onesb"], writes=[cB])

        evac_rr = [0]

        def evac(out, in_, reads, writes):
            evac_rr[0] ^= 1
            if evac_rr[0]:
                return kb.op("act", lambda: nc.scalar.copy(out=out, in_=in_), reads=reads, writes=writes)
            return kb.op("dve", lambda: nc.vector.tensor_copy(out=out, in_=in_), reads=reads, writes=writes)

        def ln_stats(st_tile, mv, rstd, nmr, xin, rB, wB):
            for k in range(4):
                kb.op("dve", lambda k=k: nc.vector.bn_stats(out=st_tile[:, k, :], in_=xin[:, k * 512:(k + 1) * 512]),
                      reads=rB, writes=[wB])
            kb.op("dve", lambda: nc.vector.bn_aggr(out=mv[:], in_=st_tile[:].rearrange("p a b -> p (a b)")), reads=[wB], writes=[wB])
            kb.op("act", lambda: nc.scalar.activation(out=rstd[:], in_=mv[:, 1:2], func=AF.Sqrt, bias=EPS, scale=1.0),
                  reads=[wB], writes=[wB])
            kb.op("dve", lambda: nc.vector.reciprocal(out=rstd[:], in_=rstd[:]), reads=[wB], writes=[wB])
            kb.op("dve", lambda: nc.vector.scalar_tensor_tensor(out=nmr[:], in0=mv[:, 0:1], scalar=-1.0, in1=rstd[:],
                                                                 op0=ALU.mult, op1=ALU.mult), reads=[wB], writes=[wB])

        def ln_stats_g(st_tile, mv, rstd, nmr, xin, rB, wB):
            for k in range(4):
                kb.op("dve", lambda k=k: nc.vector.bn_stats(out=st_tile[:, k, :], in_=xin[:, k * 512:(k + 1) * 512]), reads=rB, writes=[wB])
            kb.op("dve", lambda: nc.vector.bn_aggr(out=mv[:], in_=st_tile[:].rearrange("p a b -> p (a b)")), reads=[wB], writes=[wB])
            kb.op("act", lambda: nc.scalar.activation(out=rstd[:], in_=mv[:, 1:2], func=AF.Sqrt, bias=EPS, scale=1.0), reads=[wB], writes=[wB])
            yield
            kb.op("dve", lambda: nc.vector.reciprocal(out=rstd[:], in_=rstd[:]), reads=[wB], writes=[wB])
            kb.op("dve", lambda: nc.vector.scalar_tensor_tensor(out=nmr[:], in0=mv[:, 0:1], scalar=-1.0, in1=rstd[:], op0=ALU.mult, op1=ALU.mult),
                  reads=[wB], writes=[wB])

        def interleave(gens, depth=2):
            pending = list(gens)
            active = []
            while pending or active:
                while len(active) < depth and pending:
                    active.append(pending.pop(0))
                for g in list(active):
                    try:
                        next(g)
                    except StopIteration:
                        active.remove(g)

        def load_bc(tile, row_ap, wB, plus_one=False):
            kb.dma("sp", tile[:], row_ap.partition_broadcast(128), reads=[B_["mod"]], writes=[wB])
            if plus_one:
                kb.op("dve", lambda: nc.vector.tensor_scalar(out=tile[:], in0=tile[:], scalar1=1.0, scalar2=None, op0=ALU.add),
                      reads=[wB], writes=[wB])

        def mod_gen(st, l, psb=(0, 1)):
            sT = sb(st, "sT", [128, NC_, 2], F32)
            sTb = sb(st, "sTb", [128, NC_, 2], BF16)
            bm = [sb(st, f"bm{i}", [2, 512], F32) for i in range(2)]
            mo = [sb(st, f"mo{i}", [2, 512], F32) for i in range(2)]
            wb = [sb(st, f"wmod{i}", [128, NC_, 512], BF16) for i in range(2)]
            wB = kb.bufs(2, "wmod")
            sB, bB, moB = kb.buf(), kb.bufs(2), kb.bufs(2)
            kb.dma("sp", sT[:, :, 0], I["c"][0].rearrange("(c p) -> p c", p=128), writes=[sB], allow_slow_non_contiguous=True)
            kb.dma("sp", sT[:, :, 1], I["c_ctx"][0].rearrange("(c p) -> p c", p=128), writes=[sB], allow_slow_non_contiguous=True)
            kb.op("act", lambda: nc.scalar.activation(out=sT[:], in_=sT[:], func=AF.Silu), reads=[sB], writes=[sB])
            kb.op("dve", lambda: nc.vector.tensor_copy(out=sTb[:], in_=sT[:]), reads=[sB], writes=[sB])
            for j in range(24):
                k = j % 2
                w = wb[k]
                kb.dma("sp", bm[k][0:1, :], I["b_mod"][l:l + 1, j * 512:(j + 1) * 512], writes=[bB[k]])
                kb.dma("sp", bm[k][1:2, :], I["b_mod"][l:l + 1, j * 512:(j + 1) * 512], writes=[bB[k]])
                kb.dma("pool", w[:], I["w_mod"][l, :, j * 512:(j + 1) * 512].rearrange("(c p) n -> p c n", p=128), writes=[wB[k]])
                yield
                ps = PS[psb[k]]

                def mm():
                    for cc in range(NC_):
                        ins = nc.tensor.matmul(ps[0:2, :], lhsT=sTb[:, cc, :], rhs=w[:, cc, :], start=(cc == 0), stop=(cc == NC_ - 1))
                    return ins
                kb.op("pe", mm, reads=[sB, wB[k]], writes=[PSB[psb[k]]])
                kb.op("dve", lambda: nc.vector.tensor_tensor(out=mo[k][:], in0=ps[0:2, :], in1=bm[k][:], op=ALU.add),
                      reads=[PSB[psb[k]], bB[k]], writes=[moB[k]])
                kb.dma("sp", mod_d[l, :, j * 512:(j + 1) * 512], mo[k][:], reads=[moB[k]], writes=[B_["mod"]])

        def stage_mod():
            with ExitStack() as st:
                for _ in mod_gen(st, 0):
                    pass
                kb.barrier()

        def stage_pre(st, src_d, srcB, ntiles, sh_row, sc_row, hT, hB, col0, tag):
            bsc = sb(st, tag + "bsc", [128, D], F32)
            bsh = sb(st, tag + "bsh", [128, D], F32)
            bcB = kb.buf()
            load_bc(bsc, sc_row, bcB, plus_one=True)
            load_bc(bsh, sh_row, bcB)
            xt = [sb(st, f"{tag}xt{i}", [128, D], F32) for i in range(2)]
            xB = kb.bufs(2)
            stt = [sb(st, f"{tag}stt{i}", [128, 4, 6], F32) for i in range(2)]
            mv = [sb(st, f"{tag}mv{i}", [128, 2], F32) for i in range(2)]
            rstd = [sb(st, f"{tag}rstd{i}", [128, 1], F32) for i in range(2)]
            nmr = [sb(st, f"{tag}nmr{i}", [128, 1], F32) for i in range(2)]
            smB = kb.bufs(2)

            def tile(t):
                p = t % 2
                x_, xb_ = xt[p], xB[p]
                kb.dma("sp", x_[:], src_d[t * 128:(t + 1) * 128, :], reads=[srcB], writes=[xb_])
                yield
                yield from ln_stats_g(stt[p], mv[p], rstd[p], nmr[p], x_, [xb_], smB[p])
                kb.op("act", lambda: nc.scalar.activation(out=x_[:], in_=x_[:], func=AF.Identity, bias=nmr[p][:], scale=rstd[p][:]),
                      reads=[smB[p], xb_], writes=[xb_])
                yield
                kb.op("dve", lambda: nc.vector.tensor_tensor(out=x_[:], in0=x_[:], in1=bsc[:], op=ALU.mult), reads=[xb_, bcB], writes=[xb_])
                kb.op("pool", lambda: nc.gpsimd.tensor_tensor(out=x_[:], in0=x_[:], in1=bsh[:], op=ALU.add), reads=[xb_, bcB], writes=[xb_])
                yield
                for q4 in range(4):
                    pb = p * 4 + q4

                    def tr():
                        for k in range(4):
                            cc = q4 * 4 + k
                            ins = nc.tensor.transpose(PS[pb][:, k * 128:(k + 1) * 128], x_[:, cc * 128:(cc + 1) * 128], ident[:])
                        return ins
                    kb.op("pe", tr, reads=[xb_, cB], writes=[PSB[pb]])
                yield
                hb_t = kb.buf()
                for q4 in range(4):
                    pb = p * 4 + q4
                    evac(hT[:, q4 * 4:(q4 + 1) * 4, col0 + t * 128: col0 + (t + 1) * 128],
                         PS[pb][:].rearrange("p (k n) -> p k n", k=4), [PSB[pb]], [hb_t])
            interleave([tile(t) for t in range(ntiles)])

        def stage_gemm_out(st, mixT, mB, w_dram, tag):
            wb = [sb(st, f"{tag}w{i}", [128, NC_, 512], BF16) for i in range(2)]
            wB = kb.bufs(2)
            stg = [sb(st, f"{tag}stg{i}", [128, 512], F32) for i in range(3)]
            sgB = kb.bufs(3)
            n = 0
            for jb in range(4):
                kb.dma("pool", wb[jb % 2][:], w_dram[:, jb * 512:(jb + 1) * 512].rearrange("(c p) n -> p c n", p=128), writes=[wB[jb % 2]])
                for t in range(NT):
                    pb = n % 4

                    def mm(jb=jb, t=t, pb=pb):
                        for cc in range(NC_):
                            ins = nc.tensor.matmul(PS[pb][:], lhsT=mixT[:, cc, t * 128:(t + 1) * 128], rhs=wb[jb % 2][:, cc, :],
                                                   start=(cc == 0), stop=(cc == NC_ - 1))
                        return ins
                    kb.op("pe", mm, reads=[mB, wB[jb % 2]], writes=[PSB[pb]])
                    s_ = n % 3
                    evac(stg[s_][:], PS[pb][:], [PSB[pb]], [sgB[s_]])
                    kb.dma("sp", y_d[t * 128:(t + 1) * 128, jb * 512:(jb + 1) * 512], stg[s_][:], reads=[sgB[s_]], writes=[B_["y"]])
                    n += 1

        def stage_inproj_ab(st, hT, hB, side=None):
            def step():
                if side is not None:
                    next(side, None)
            TT = SEQ + CTX
            w_in = I["ab_w_in"][0]
            cosT = sb(st, "cosT", [128, SEQ], F32)
            sinT = sb(st, "sinT", [128, SEQ], F32)
            rB = kb.buf()
            kb.dma("sp", cosT[:], C["cosT"], writes=[rB])
            kb.dma("sp", sinT[:], C["sinT"], writes=[rB])
            wb = [sb(st, f"abw{i}", [128, NC_, 512], BF16) for i in range(2)]
            ws = sb(st, "abws", [128, NC_, 512], BF16)
            wB = kb.bufs(2)
            wsB = kb.buf()
            t1 = [sb(st, f"rt1_{i}", [128, 512], F32) for i in range(2)]
            t2 = [sb(st, f"rt2_{i}", [128, 512], F32) for i in range(2)]
            tB = kb.bufs(2)
            stg = [sb(st, f"abstg{i}", [128, 512], BF16) for i in range(3)]
            sgB = kb.bufs(3)
            n = 0
            ns = 0
            for jb in range(6):
                w = wb[jb % 2]
                kb.dma("pool", w[:], w_in[:, jb * 512:(jb + 1) * 512].rearrange("(c p) n -> p c n", p=128), writes=[wB[jb % 2]])
                if jb < 4:
                    wv = w[:].rearrange("p c (g t j) -> p (c g) t j", t=2, j=32)
                    sv = ws[:].rearrange("p c (g t j) -> p (c g) t j", t=2, j=32)
                    kb.op("act", lambda wv=wv, sv=sv: nc.scalar.copy(out=sv[:, :, 0, :], in_=wv[:, :, 1, :]), reads=[wB[jb % 2]], writes=[wsB])
                    kb.op("dve", lambda wv=wv, sv=sv: nc.vector.tensor_copy(out=sv[:, :, 1, :], in_=wv[:, :, 0, :]), reads=[wB[jb % 2]], writes=[wsB])
                    for hh in range(4):
                        head = jb * 4 + hh
                        isk = head >= 12
                        for tb in range(4):
                            pa, pb = (n * 2) % 8, (n * 2 + 1) % 8
                            n += 1

                            def mm(wt, p_, hh=hh, tb=tb):
                                for cc in range(NC_):
                                    ins = nc.tensor.matmul(PS[p_][:], lhsT=wt[:, cc, hh * 128:(hh + 1) * 128],
                                                           rhs=hT[:, cc, tb * 512:(tb + 1) * 512], start=(cc == 0), stop=(cc == NC_ - 1))
                                return ins
                            kb.op("pe", lambda: mm(w, pa), reads=[hB, wB[jb % 2]], writes=[PSB[pa]])
                            kb.op("pe", lambda: mm(ws, pb), reads=[hB, wsB], writes=[PSB[pb]])
                            k2 = ns % 2
                            s_ = ns % 3
                            ns += 1
                            kb.op("dve", lambda: nc.vector.tensor_tensor(out=t1[k2][:], in0=PS[pa][:], in1=cosT[:, tb * 512:(tb + 1) * 512], op=ALU.mult),
                                  reads=[PSB[pa], rB], writes=[tB[k2]])
                            kb.op("dve", lambda: nc.vector.tensor_tensor(out=t2[k2][:], in0=PS[pb][:], in1=sinT[:, tb * 512:(tb + 1) * 512], op=ALU.mult),
                                  reads=[PSB[pb], rB], writes=[tB[k2]])
                            kb.op("dve", lambda: nc.vector.tensor_tensor(out=stg[s_][:], in0=t1[k2][:], in1=t2[k2][:], op=ALU.add),
                                  reads=[tB[k2]], writes=[sgB[s_]])
                            if isk:
                                kb.dma("sp", kT_d[head - 12, :, tb * 512:(tb + 1) * 512], stg[s_][:], reads=[sgB[s_]], writes=[B_["kT"]])
                            else:
                                kb.dma("sp", qT_d[head, :, tb * 512:(tb + 1) * 512], stg[s_][:], reads=[sgB[s_]], writes=[B_["qT"]])
                            step()
                        if isk:
                            pa = (n * 2) % 8
                            n += 1

                            def mmc(hh=hh, pa=pa):
                                for cc in range(NC_):
                                    ins = nc.tensor.matmul(PS[pa][:, 0:CTX], lhsT=w[:, cc, hh * 128:(hh + 1) * 128],
                                                           rhs=hT[:, cc, SEQ:TT], start=(cc == 0), stop=(cc == NC_ - 1))
                                return ins
                            kb.op("pe", mmc, reads=[hB, wB[jb % 2]], writes=[PSB[pa]])
                            s_ = ns % 3
                            ns += 1
                            evac(stg[s_][:, 0:CTX], PS[pa][:, 0:CTX], [PSB[pa]], [sgB[s_]])
                            kb.dma("sp", kT_d[head - 12, :, SEQ:TT], stg[s_][:, 0:CTX], reads=[sgB[s_]], writes=[B_["kT"]])
                else:
                    ntile = TT // 128 if jb == 4 else NT
                    for t in range(ntile):
                        pa = (n * 2) % 8
                        n += 1

                        def mmv(t=t, pa=pa):
                            for cc in range(NC_):
                                ins = nc.tensor.matmul(PS[pa][:], lhsT=hT[:, cc, t * 128:(t + 1) * 128], rhs=w[:, cc, :],
                                                       start=(cc == 0), stop=(cc == NC_ - 1))
                            return ins
                        kb.op("pe", mmv, reads=[hB, wB[jb % 2]], writes=[PSB[pa]])
                        s_ = ns % 3
                        ns += 1
                        evac(stg[s_][:], PS[pa][:], [PSB[pa]], [sgB[s_]])
                        if jb == 4:
                            kb.dma("sp", v_d[t * 128:(t + 1) * 128, :], stg[s_][:], reads=[sgB[s_]], writes=[B_["v"]])
                        else:
                            kb.dma("sp", z_d[t * 128:(t + 1) * 128, :], stg[s_][:], reads=[sgB[s_]], writes=[B_["z"]])

            if side is not None:
                for _ in side:
                    pass

        def stage_attn(st, mixT, mB):
            TT = SEQ + CTX
            mlo = sb(st, "mlo", [128, 384], BF16)
            mhi = sb(st, "mhi", [128, 384], BF16)
            snk = sb(st, "snk", [128, 12], F32)
            esk = sb(st, "esk", [128, 12, 128], F32)
            kB_ = kb.buf()
            kb.dma("sp", mlo[:], C["mask_lo"], writes=[kB_])
            kb.dma("sp", mhi[:], C["mask_hi"], writes=[kB_])
            kb.dma("sp", snk[:], I["sink"][0:1, :].partition_broadcast(128), writes=[kB_])
            kb.op("act", lambda: nc.scalar.activation(out=snk[:], in_=snk[:], func=AF.Exp), reads=[kB_], writes=[kB_])
            kb.op("dve", lambda: nc.vector.tensor_copy(out=esk[:], in_=snk[:].unsqueeze(2).to_broadcast([128, 12, 128])), reads=[kB_], writes=[kB_])
            kT = [sb(st, f"kT{i}", [128, TT], BF16) for i in range(2)]
            vt = [sb(st, f"vt{i}", [128, TT // 128, 128], BF16) for i in range(2)]
            qT = [sb(st, f"qT{i}", [128, 3, SEQ], BF16) for i in range(2)]
            hdB = kb.bufs(2)
            pT = [sb(st, f"pT{i}", [128, 384], BF16) for i in range(4)]
            pB = kb.bufs(4)
            den = [sb(st, f"den{i}", [128, 384], F32) for i in range(2)]
            dB = kb.bufs(2)
            scale = 128 ** -0.5
            items = []
            it = 0
            for h in range(4):
                for i in range(NT):
                    blocks = []
                    if i > 0:
                        blocks.append((i - 1, mlo))
                    blocks.append((i, None))
                    if i < NT - 1:
                        blocks.append((i + 1, mhi))
                    blocks.append((16, None))
                    blocks.append((17, None))
                    for bi, (j, msk) in enumerate(blocks):
                        items.append(dict(h=h, i=i, j=j, msk=msk, first=(bi == 0), last=(bi == len(blocks) - 1), po=4 + (it % 2) * 2, d2=it % 2,
                                          n=len(items), newh=(i == 0 and bi == 0)))
                    it += 1

            def emitS(a):
                h, i, j, msk, n = a["h"], a["i"], a["j"], a["msk"], a["n"]
                k2 = h % 2
                if a["newh"]:
                    kb.dma("sp", kT[k2][:], kT_d[h], reads=[B_["kT"]], writes=[hdB[k2]])
                    kb.dma("sp", vt[k2][:], v_d[:, h * 128:(h + 1) * 128].rearrange("(t p) d -> p t d", p=128), reads=[B_["v"]], writes=[hdB[k2]])
                    kb.dma("sp", qT[k2][:], qT_d[3 * h:3 * h + 3].rearrange("g p n -> p g n"), reads=[B_["qT"]], writes=[hdB[k2]])
                sbk = n % 4
                kb.op("pe", lambda: nc.tensor.matmul(PS[sbk][:, 0:384], lhsT=kT[k2][:, j * 128:(j + 1) * 128],
                                                     rhs=qT[k2][:, :, i * 128:(i + 1) * 128], start=True, stop=True),
                      reads=[hdB[k2]], writes=[PSB[sbk]])
                kb.op("act", lambda: nc.scalar.activation(out=pT[sbk][:], in_=PS[sbk][:, 0:384], func=AF.Exp, scale=scale),
                      reads=[PSB[sbk]], writes=[pB[sbk]])
                if msk is not None:
                    kb.op("dve", lambda: nc.vector.tensor_tensor(out=pT[sbk][:], in0=pT[sbk][:], in1=msk[:], op=ALU.mult),
                          reads=[pB[sbk], kB_], writes=[pB[sbk]])

            def emitPV(a):
                h, i, j, n, po, d2 = a["h"], a["i"], a["j"], a["n"], a["po"], a["d2"]
                k2 = h % 2
                pp = n % 4

                def mm2():
                    nc.tensor.matmul(PS[po][:, 0:384], lhsT=vt[k2][:, j, :], rhs=pT[pp][:], start=a["first"], stop=a["last"])
                    return nc.tensor.matmul(PS[po + 1][:, 0:384], lhsT=onesb[:], rhs=pT[pp][:], start=a["first"], stop=a["last"])
                kb.op("pe", mm2, reads=[hdB[k2], pB[pp], cB], writes=[PSB[po], PSB[po + 1]])
                if a["last"]:
                    dn = den[d2]
                    kb.op("dve", lambda: nc.vector.tensor_tensor(out=dn[:], in0=PS[po + 1][:, 0:384],
                                                                 in1=esk[:, 3 * h:3 * h + 3, :].rearrange("p g n -> p (g n)"), op=ALU.add),
                          reads=[PSB[po + 1], kB_], writes=[dB[d2]])
                    kb.op("dve", lambda: nc.vector.reciprocal(out=dn[:], in_=dn[:]), reads=[dB[d2]], writes=[dB[d2]])
                    kb.op("dve", lambda: nc.vector.tensor_tensor(out=mixT[:, 3 * h:3 * h + 3, i * 128:(i + 1) * 128],
                                                                 in0=PS[po][:, 0:384].rearrange("p (g n) -> p g n", g=3),
                                                                 in1=dn[:].rearrange("p (g n) -> p g n", g=3), op=ALU.mult),
                          reads=[PSB[po], dB[d2]], writes=[mB])
            LA = 2
            for n in range(len(items) + LA):
                if n < len(items):
                    emitS(items[n])
                if n - LA >= 0:
                    emitPV(items[n - LA])

        def stage_fourier(st, mixT, mB):
            Z = sb(st, "fz", [128, NT, 512], BF16)
            zB = kb.buf()
            kb.dma("sp", Z[:], z_d.rearrange("(t p) c -> p t c", p=128), reads=[B_["z"]], writes=[zB])
            cc_ = sb(st, "fcc", [128, 128], BF16)
            csn = sb(st, "fcs", [128, 128], BF16)
            kb.dma("sp", cc_[:], C["dftc_c"], writes=[zB])
            kb.dma("sp", csn[:], C["dftc_sn"], writes=[zB])
            Cn = [sb(st, f"fCn{i}", [128, NT, 512], BF16) for i in range(2)]
            Sn = [sb(st, f"fSn{i}", [128, NT, 512], BF16) for i in range(2)]
            tbB = kb.bufs(2)
            ab = [sb(st, f"fab{i}", [128, 512], BF16) for i in range(2)]
            bb = [sb(st, f"fbb{i}", [128, 512], BF16) for i in range(2)]
            abB = kb.bufs(2)
            n = 0
            for nb in range(4):
                k2 = nb % 2
                kb.dma("sp", Cn[k2][:], C["dft_c"][:, nb * 512:(nb + 1) * 512].rearrange("(t p) n -> p t n", p=128), writes=[tbB[k2]])
                kb.dma("sp", Sn[k2][:], C["dft_s"][:, nb * 512:(nb + 1) * 512].rearrange("(t p) n -> p t n", p=128), writes=[tbB[k2]])
                for g in range(4):
                    pa, pb, pc = (n * 3) % 6, (n * 3 + 1) % 6, 6 + n % 2
                    a2 = n % 2
                    n += 1

                    def mmA(tab, p_, g=g):
                        for t in range(NT):
                            ins = nc.tensor.matmul(PS[p_][:], lhsT=Z[:, t, g * 128:(g + 1) * 128], rhs=tab[:, t, :], start=(t == 0), stop=(t == NT - 1))
                        return ins
                    kb.op("pe", lambda: mmA(Cn[k2], pa), reads=[zB, tbB[k2]], writes=[PSB[pa]])
                    kb.op("pe", lambda: mmA(Sn[k2], pb), reads=[zB, tbB[k2]], writes=[PSB[pb]])
                    kb.op("act", lambda: nc.scalar.copy(out=ab[a2][:], in_=PS[pa][:]), reads=[PSB[pa]], writes=[abB[a2]])
                    kb.op("dve", lambda: nc.vector.tensor_copy(out=bb[a2][:], in_=PS[pb][:]), reads=[PSB[pb]], writes=[abB[a2]])

                    def mmB():
                        nc.tensor.matmul(PS[pc][:], lhsT=cc_[:], rhs=ab[a2][:], start=True, stop=False)
                        return nc.tensor.matmul(PS[pc][:], lhsT=csn[:], rhs=bb[a2][:], start=False, stop=True)
                    kb.op("pe", mmB, reads=[zB, abB[a2]], writes=[PSB[pc]])
                    evac(mixT[:, 12 + g, nb * 512:(nb + 1) * 512], PS[pc][:], [PSB[pc]], [mB])

        def stage_inproj_cd(st, hT, hB):
            w_in = I["cd_w_in"][0]
            wb = [sb(st, f"cdw{i}", [128, NC_, 512], BF16) for i in range(4)]
            wB = kb.bufs(4)
            lng = sb(st, "sglng", [128, 1024], F32)
            lnb = sb(st, "sglnb", [128, 1024], F32)
            lB = kb.buf()
            kb.dma("sp", lng[:], I["sg_ln_g"][0:1, :].partition_broadcast(128), writes=[lB])
            kb.dma("sp", lnb[:], I["sg_ln_b"][0:1, :].partition_broadcast(128), writes=[lB])
            stg = [sb(st, f"cdstg{i}", [128, 512], BF16) for i in range(3)]
            sgB = kb.bufs(3)
            zt = [sb(st, f"cdz{i}", [128, 512], F32) for i in range(2)]
            zB = kb.bufs(2)
            stt = sb(st, "cdstt", [128, 4, 6], F32)
            mv4 = sb(st, "cdmv4", [128, 4, 2], F32)
            rs4 = sb(st, "cdrs4", [128, 4], F32)
            smB = kb.buf()
            nw = 0
            n = 0
            ns = 0

            def loadw(jb):
                nonlocal nw
                k = nw % 4
                nw += 1
                kb.dma("pool", wb[k][:], w_in[:, jb * 512:(jb + 1) * 512].rearrange("(c p) n -> p c n", p=128), writes=[wB[k]])
                return k
            for jb in range(2):
                k = loadw(jb)
                for fc in range(4):
                    for tb in range(4):
                        pa = n % 8
                        n += 1

                        def mm(k=k, fc=fc, tb=tb, pa=pa):
                            for cc in range(NC_):
                                ins = nc.tensor.matmul(PS[pa][:], lhsT=wb[k][:, cc, fc * 128:(fc + 1) * 128], rhs=hT[:, cc, tb * 512:(tb + 1) * 512],
                                                       start=(cc == 0), stop=(cc == NC_ - 1))
                            return ins
                        kb.op("pe", mm, reads=[hB, wB[k]], writes=[PSB[pa]])
                        s_ = ns % 3
                        ns += 1
                        kb.op("act", lambda: nc.scalar.activation(out=stg[s_][:], in_=PS[pa][:], func=AF.Gelu_apprx_tanh), reads=[PSB[pa]], writes=[sgB[s_]])
                        kb.dma("sp", uT_d[jb * 4 + fc, :, tb * 512:(tb + 1) * 512], stg[s_][:], reads=[sgB[s_]], writes=[B_["uT"]])
            for jb in range(2, 4):
                k = loadw(jb)
                for t in range(NT):
                    pa = n % 8
                    n += 1

                    def mmv(k=k, t=t, pa=pa):
                        for cc in range(NC_):
                            ins = nc.tensor.matmul(PS[pa][:], lhsT=hT[:, cc, t * 128:(t + 1) * 128], rhs=wb[k][:, cc, :], start=(cc == 0), stop=(cc == NC_ - 1))
                        return ins
                    kb.op("pe", mmv, reads=[hB, wB[k]], writes=[PSB[pa]])
                    z_ = zt[t % 2]
                    zb_ = zB[t % 2]
                    kb.op("act", lambda: nc.scalar.activation(out=z_[:], in_=PS[pa][:], func=AF.Gelu_apprx_tanh), reads=[PSB[pa]], writes=[zb_])
                    for g in range(4):
                        kb.op("dve", lambda g=g: nc.vector.bn_stats(out=stt[:, g, :], in_=z_[:, g * 128:(g + 1) * 128]), reads=[zb_], writes=[smB])
                    for g in range(4):
                        kb.op("dve", lambda g=g: nc.vector.bn_aggr(out=mv4[:, g, :], in_=stt[:, g, :]), reads=[smB], writes=[smB])
                    kb.op("act", lambda: nc.scalar.activation(out=rs4[:], in_=mv4[:, :, 1], func=AF.Sqrt, bias=EPS, scale=1.0), reads=[smB], writes=[smB])
                    kb.op("dve", lambda: nc.vector.reciprocal(out=rs4[:], in_=rs4[:]), reads=[smB], writes=[smB])
                    zv = z_[:].rearrange("p (g c) -> p g c", g=4)
                    kb.op("dve", lambda: nc.vector.tensor_tensor(out=zv, in0=zv, in1=mv4[:, :, 0].unsqueeze(2).to_broadcast([128, 4, 128]), op=ALU.subtract),
                          reads=[zb_, smB], writes=[zb_])
                    kb.op("dve", lambda: nc.vector.tensor_tensor(out=zv, in0=zv, in1=rs4[:].unsqueeze(2).to_broadcast([128, 4, 128]), op=ALU.mult),
                          reads=[zb_, smB], writes=[zb_])
                    c0 = (jb - 2) * 512
                    kb.op("dve", lambda: nc.vector.tensor_tensor(out=z_[:], in0=z_[:], in1=lng[:, c0:c0 + 512], op=ALU.mult), reads=[zb_, lB], writes=[zb_])
                    s_ = ns % 3
                    ns += 1
                    kb.op("dve", lambda: nc.vector.tensor_tensor(out=stg[s_][:], in0=z_[:], in1=lnb[:, c0:c0 + 512], op=ALU.add), reads=[zb_, lB], writes=[sgB[s_]])
                    kb.dma("sp", vg_d[t * 128:(t + 1) * 128, c0:c0 + 512], stg[s_][:], reads=[sgB[s_]], writes=[B_["vg"]])
            for jb in range(2):
                ka = loadw(4 + jb)
                kg = loadw(6 + jb)
                for fc in range(4):
                    for tb in range(4):
                        pa, pg = (n * 2) % 8, (n * 2 + 1) % 8
                        n += 1

                        def mm(k, p_, fc=fc, tb=tb):
                            for cc in range(NC_):
                                ins = nc.tensor.matmul(PS[p_][:], lhsT=wb[k][:, cc, fc * 128:(fc + 1) * 128], rhs=hT[:, cc, tb * 512:(tb + 1) * 512],
                                                       start=(cc == 0), stop=(cc == NC_ - 1))
                            return ins
                        kb.op("pe", lambda: mm(ka, pa), reads=[hB, wB[ka]], writes=[PSB[pa]])
                        kb.op("pe", lambda: mm(kg, pg), reads=[hB, wB[kg]], writes=[PSB[pg]])
                        z_ = zt[n % 2]
                        zb_ = zB[n % 2]
                        kb.op("act", lambda: nc.scalar.activation(out=z_[:], in_=PS[pg][:], func=AF.Sigmoid), reads=[PSB[pg]], writes=[zb_])
                        s_ = ns % 3
                        ns += 1
                        kb.op("dve", lambda: nc.vector.tensor_tensor(out=stg[s_][:], in0=PS[pa][:], in1=z_[:], op=ALU.mult), reads=[PSB[pa], zb_], writes=[sgB[s_]])
                        kb.dma("sp", xgT_d[jb * 4 + fc, :, tb * 512:(tb + 1) * 512], stg[s_][:], reads=[sgB[s_]], writes=[B_["xgT"]])

        def stage_sg(st, mixT, mB):
            swn = sb(st, "swn", [128, 8, 128], F32)
            swT = sb(st, "swT", [128, 8, 128], BF16)
            sgb = sb(st, "sgb", [128, 1024], F32)
            wB_ = kb.buf()
            kb.dma("sp", swn[:], I["sg_w"].rearrange("g p q -> p g q"), writes=[wB_])
            kb.dma("sp", sgb[:], I["sg_b"][0:1, :].partition_broadcast(128), writes=[wB_])
            for g4 in range(2):
                def tr(g4=g4):
                    for k in range(4):
                        g = g4 * 4 + k
                        ins = nc.tensor.transpose(PS[g4][:, k * 128:(k + 1) * 128], swn[:, g, :], ident[:])
                    return ins
                kb.op("pe", tr, reads=[wB_, cB], writes=[PSB[g4]])
                kb.op("dve", lambda g4=g4: nc.vector.tensor_copy(out=swT[:, g4 * 4:(g4 + 1) * 4, :], in_=PS[g4][:].rearrange("p (k n) -> p k n", k=4)),
                      reads=[PSB[g4]], writes=[wB_])
            vg = sb(st, "sgvg", [128, NT, 1024], BF16)
            uT = sb(st, "sguT", [128, 8, SEQ], BF16)
            dB = kb.buf()
            kb.dma("sp", vg[:], vg_d.rearrange("(t p) c -> p t c", p=128), reads=[B_["vg"]], writes=[dB])
            kb.dma("sp", uT[:], uT_d.rearrange("g p n -> p g n"), reads=[B_["uT"]], writes=[dB])
            tmp = [sb(st, f"sgtmp{i}", [128, 512], F32) for i in range(2)]
            tB = kb.bufs(2)
            n = 0
            for g in range(8):
                for n4 in range(4):
                    pb = n % 8
                    t2 = n % 2
                    n += 1

                    def mm(g=g, n4=n4, pb=pb):
                        for k in range(4):
                            ins = nc.tensor.matmul(PS[pb][:, k * 128:(k + 1) * 128], lhsT=vg[:, n4 * 4 + k, g * 128:(g + 1) * 128], rhs=swT[:, g, :],
                                                   start=True, stop=True)
                        return ins
                    kb.op("pe", mm, reads=[dB, wB_], writes=[PSB[pb]])
                    kb.op("dve", lambda: nc.vector.tensor_tensor(out=tmp[t2][:].rearrange("p (k n) -> p k n", k=4), in0=PS[pb][:].rearrange("p (k n) -> p k n", k=4),
                                                                 in1=sgb[:, g * 128:(g + 1) * 128].unsqueeze(1).to_broadcast([128, 4, 128]), op=ALU.add),
                          reads=[PSB[pb], wB_], writes=[tB[t2]])
                    kb.op("dve", lambda: nc.vector.tensor_tensor(out=mixT[:, g, n4 * 512:(n4 + 1) * 512], in0=tmp[t2][:], in1=uT[:, g, n4 * 512:(n4 + 1) * 512], op=ALU.mult),
                          reads=[tB[t2], dB], writes=[mB])

        def stage_conv(st, mixT, mB):
            PADW = SEQ + 30
            xg = sb(st, "cvxg", [128, 8, PADW], BF16)
            xB = kb.buf()
            kb.op("dve", lambda: nc.vector.memset(xg[:, :, 0:15], 0.0), writes=[xB])
            kb.op("dve", lambda: nc.vector.memset(xg[:, :, 15 + SEQ:PADW], 0.0), writes=[xB])
            kb.dma("sp", xg[:, :, 15:15 + SEQ], xgT_d.rearrange("g p n -> p g n"), reads=[B_["xgT"]], writes=[xB])
            cwn = sb(st, "cvwn", [31, 1024], F32)
            cw = sb(st, "cvw", [128, 8, 32], F32)
            cb = sb(st, "cvb", [128, 8], F32)
            lg = sb(st, "cvlg", [128, 8], F32)
            lb = sb(st, "cvlb", [128, 8], F32)
            onesf = sb(st, "cvones", [128, 128], F32)
            pB = kb.buf()
            kb.dma("sp", cwn[:], I["conv_w"], writes=[pB])
            kb.dma("sp", onesf[:], C["onesf"], writes=[pB])
            kb.dma("sp", cb[:], I["conv_b"][0].rearrange("(g p) -> p g", p=128), writes=[pB], allow_slow_non_contiguous=True)
            kb.dma("sp", lg[:], I["conv_ln_g"][0].rearrange("(g p) -> p g", p=128), writes=[pB], allow_slow_non_contiguous=True)
            kb.dma("sp", lb[:], I["conv_ln_b"][0].rearrange("(g p) -> p g", p=128), writes=[pB], allow_slow_non_contiguous=True)

            def trw():
                for g in range(8):
                    ins = nc.tensor.transpose(PS[0][:, g * 32:g * 32 + 31], cwn[0:31, g * 128:(g + 1) * 128], ident[0:31, 0:31])
                return ins
            kb.op("pe", trw, reads=[pB, cB], writes=[PSB[0]])
            kb.op("dve", lambda: nc.vector.tensor_copy(out=cw[:, :, 0:31], in_=PS[0][:, 0:256].rearrange("p (g k) -> p g k", g=8)[:, :, 0:31]), reads=[PSB[0]], writes=[pB])
            dg = sb(st, "cvdg", [128, 8, 31, 128], BF16)
            dgB = kb.buf()
            for g in range(8):
                for k in range(31):
                    kb.op("dve", lambda g=g, k=k: nc.vector.tensor_scalar(out=dg[:, g, k, :], in0=ident[:], scalar1=cw[:, g, k:k + 1], scalar2=None, op0=ALU.mult),
                          reads=[pB, cB], writes=[dgB])
            xc = sb(st, "cvxc", [128, 8, 512], F32)
            xcB = kb.buf()
            sq = [sb(st, f"cvsq{i}", [128, 512], F32) for i in range(2)]
            sqB = kb.bufs(2)
            mean = sb(st, "cvmean", [128, 512], F32)
            rstd = sb(st, "cvrstd", [128, 512], F32)
            stB = kb.buf()
            tmp = [sb(st, f"cvtmp{i}", [128, 512], F32) for i in range(2)]
            tB = kb.bufs(2)
            n = 0
            for tb in range(4):
                for g in range(8):
                    pb = n % 4
                    n += 1

                    def mmc(g=g, pb=pb):
                        for k in range(31):
                            ins = nc.tensor.matmul(PS[pb][:], lhsT=dg[:, g, k, :], rhs=xg[:, g, tb * 512 + k: tb * 512 + k + 512], start=(k == 0), stop=(k == 30))
                        return ins
                    kb.op("pe", mmc, reads=[dgB, xB], writes=[PSB[pb]])
                    kb.op("act", lambda g=g, pb=pb: nc.scalar.activation(out=xc[:, g, :], in_=PS[pb][:], func=AF.Identity, bias=cb[:, g:g + 1], scale=1.0),
                          reads=[PSB[pb], pB], writes=[xcB])
                for g in range(8):
                    s2 = g % 2
                    kb.op("act", lambda g=g, s2=s2: nc.scalar.activation(out=sq[s2][:], in_=xc[:, g, :], func=AF.Square), reads=[xcB], writes=[sqB[s2]])

                    def mms(g=g, s2=s2):
                        nc.tensor.matmul(PS[4][:], lhsT=onesf[:], rhs=xc[:, g, :], start=(g == 0), stop=(g == 7))
                        return nc.tensor.matmul(PS[5][:], lhsT=onesf[:], rhs=sq[s2][:], start=(g == 0), stop=(g == 7))
                    kb.op("pe", mms, reads=[xcB, sqB[s2], pB], writes=[PSB[4], PSB[5]])
                kb.op("act", lambda: nc.scalar.activation(out=mean[:], in_=PS[4][:], func=AF.Copy, scale=1.0 / 1024), reads=[PSB[4]], writes=[stB])
                kb.op("dve", lambda: nc.vector.tensor_tensor(out=rstd[:], in0=mean[:], in1=mean[:], op=ALU.mult), reads=[stB], writes=[stB])
                kb.op("dve", lambda: nc.vector.scalar_tensor_tensor(out=rstd[:], in0=PS[5][:], scalar=1.0 / 1024, in1=rstd[:], op0=ALU.mult, op1=ALU.subtract),
                      reads=[PSB[5], stB], writes=[stB])
                kb.op("act", lambda: nc.scalar.activation(out=rstd[:], in_=rstd[:], func=AF.Sqrt, bias=EPS, scale=1.0), reads=[stB], writes=[stB])
                kb.op("dve", lambda: nc.vector.reciprocal(out=rstd[:], in_=rstd[:]), reads=[stB], writes=[stB])
                for g in range(8):
                    t2 = g % 2
                    kb.op("dve", lambda g=g, t2=t2: nc.vector.tensor_tensor(out=tmp[t2][:], in0=xc[:, g, :], in1=mean[:], op=ALU.subtract), reads=[xcB, stB], writes=[tB[t2]])
                    kb.op("dve", lambda g=g, t2=t2: nc.vector.tensor_tensor(out=tmp[t2][:], in0=tmp[t2][:], in1=rstd[:], op=ALU.mult), reads=[tB[t2], stB], writes=[tB[t2]])
                    kb.op("act", lambda g=g, t2=t2: nc.scalar.activation(out=mixT[:, 8 + g, tb * 512:(tb + 1) * 512], in_=tmp[t2][:], func=AF.Silu,
                                                                       bias=lb[:, g:g + 1], scale=lg[:, g:g + 1]), reads=[tB[t2], pB], writes=[mB])

        def stage_post(st, l, xsrc_d, xsrcB, aff, affB):
            bt = {}
            bcB = kb.buf()
            for nm, row, p1 in (("g1", mod_d[l, 0:1, 2 * D:3 * D], False), ("sc2", mod_d[l, 0:1, 4 * D:5 * D], True),
                                ("sh2", mod_d[l, 0:1, 3 * D:4 * D], False), ("lg", I["ln1_g"][l:l + 1, :], False),
                                ("lb", I["ln1_b"][l:l + 1, :], False)):
                bt[nm] = sb(st, "pb_" + nm, [128, D], F32)
                load_bc(bt[nm], row, bcB, plus_one=p1)
            wr = sb(st, "wr", [128, NC_, NE], F32)
            kb.dma("sp", wr[:], I["w_router"][l].rearrange("(c p) e -> p c e", p=128), writes=[bcB])
            xt = [sb(st, f"pxt{i}", [128, D], F32) for i in range(2)]
            yt = [sb(st, f"pyt{i}", [128, D], F32) for i in range(2)]
            xB = kb.bufs(2)
            yB = kb.bufs(2)
            hb = [sb(st, f"phb{i}", [128, D], BF16) for i in range(2)]
            hbB = kb.bufs(2)
            hT = [sb(st, f"phT{i}", [128, NC_, 128], F32) for i in range(2)]
            hTB = kb.bufs(2)
            stt = [sb(st, f"pstt{i}", [128, 4, 6], F32) for i in range(2)]
            mv = [sb(st, f"pmv{i}", [128, 2], F32) for i in range(2)]
            rstd = [sb(st, f"prstd{i}", [128, 1], F32) for i in range(2)]
            nmr = [sb(st, f"pnmr{i}", [128, 1], F32) for i in range(2)]
            smB = kb.bufs(2)
            lg_ = [sb(st, f"plg{i}", [128, NE], F32) for i in range(2)]
            mx = [sb(st, f"pmx{i}", [128, 1], F32) for i in range(2)]
            sm = [sb(st, f"psm{i}", [128, 1], F32) for i in range(2)]
            lB = kb.bufs(2)
            affBs = []

            def tile(t):
                p = t % 2
                x_, y_, xb_, yb_ = xt[p], yt[p], xB[p], yB[p]
                kb.dma("sp", x_[:], xsrc_d[t * 128:(t + 1) * 128, :], reads=[xsrcB], writes=[xb_])
                kb.dma("sp", y_[:], y_d[t * 128:(t + 1) * 128, :], reads=[B_["y"]], writes=[yb_])
                yield
                kb.op("pool", lambda: nc.gpsimd.tensor_tensor(out=y_[:], in0=y_[:], in1=bt["g1"][:], op=ALU.mult), reads=[yb_, bcB], writes=[yb_])
                yield
                kb.op("dve", lambda: nc.vector.scalar_tensor_tensor(out=x_[:], in0=x_[:], scalar=float(ALPHA), in1=y_[:], op0=ALU.mult, op1=ALU.add),
                      reads=[xb_, yb_], writes=[xb_])
                yield from ln_stats_g(stt[p], mv[p], rstd[p], nmr[p], x_, [xb_], smB[p])
                kb.op("act", lambda: nc.scalar.activation(out=x_[:], in_=x_[:], func=AF.Identity, bias=nmr[p][:], scale=rstd[p][:]), reads=[smB[p], xb_], writes=[xb_])
                yield
                kb.op("dve", lambda: nc.vector.tensor_tensor(out=x_[:], in0=x_[:], in1=bt["lg"][:], op=ALU.mult), reads=[xb_, bcB], writes=[xb_])
                kb.op("pool", lambda: nc.gpsimd.tensor_tensor(out=x_[:], in0=x_[:], in1=bt["lb"][:], op=ALU.add), reads=[xb_, bcB], writes=[xb_])
                yield
                kb.dma("sp", x1_d[t * 128:(t + 1) * 128, :], x_[:], reads=[xb_], writes=[B_["x1"]])
                yield from ln_stats_g(stt[p], mv[p], rstd[p], nmr[p], x_, [xb_], smB[p])
                kb.op("act", lambda: nc.scalar.activation(out=y_[:], in_=x_[:], func=AF.Identity, bias=nmr[p][:], scale=rstd[p][:]), reads=[smB[p], xb_], writes=[yb_])
                yield
                kb.op("dve", lambda: nc.vector.tensor_tensor(out=y_[:], in0=y_[:], in1=bt["sc2"][:], op=ALU.mult), reads=[yb_, bcB], writes=[yb_])
                kb.op("pool", lambda: nc.gpsimd.tensor_tensor(out=y_[:], in0=y_[:], in1=bt["sh2"][:], op=ALU.add), reads=[yb_, bcB], writes=[yb_])
                yield
                h_ = hb[p]
                kb.op("act", lambda: nc.scalar.copy(out=h_[:], in_=y_[:]), reads=[yb_], writes=[hbB[p]])
                kb.dma("sp", h2_d[t * 128:(t + 1) * 128, :], h_[:], reads=[hbB[p]], writes=[B_["h2"]])
                for q4 in range(4):
                    pb = p * 4 + q4

                    def tr():
                        for k in range(4):
                            cc = q4 * 4 + k
                            ins = nc.tensor.transpose(PS[pb][:, k * 128:(k + 1) * 128], y_[:, cc * 128:(cc + 1) * 128], ident[:])
                        return ins
                    kb.op("pe", tr, reads=[yb_, cB], writes=[PSB[pb]])
                yield
                for q4 in range(4):
                    pb = p * 4 + q4
                    evac(hT[p][:, q4 * 4:(q4 + 1) * 4, :], PS[pb][:].rearrange("p (k n) -> p k n", k=4), [PSB[pb]], [hTB[p]])
                pr = p * 4

                def mmr():
                    for cc in range(NC_):
                        ins = nc.tensor.matmul(PS[pr][:, 0:NE], lhsT=hT[p][:, cc, :], rhs=wr[:, cc, :], start=(cc == 0), stop=(cc == NC_ - 1))
                    return ins
                kb.op("pe", mmr, reads=[hTB[p], bcB], writes=[PSB[pr]])
                yield
                kb.op("dve", lambda: nc.vector.tensor_copy(out=lg_[p][:], in_=PS[pr][:, 0:NE]), reads=[PSB[pr]], writes=[lB[p]])
                kb.op("dve", lambda: nc.vector.reduce_max(out=mx[p][:], in_=lg_[p][:], axis=AX.X), reads=[lB[p]], writes=[lB[p]])
                kb.op("dve", lambda: nc.vector.tensor_scalar(out=mx[p][:], in0=mx[p][:], scalar1=-1.0, scalar2=None, op0=ALU.mult), reads=[lB[p]], writes=[lB[p]])
                kb.op("act", lambda: nc.scalar.activation(out=lg_[p][:], in_=lg_[p][:], func=AF.Exp, bias=mx[p][:], scale=1.0), reads=[lB[p]], writes=[lB[p]])
                yield
                kb.op("dve", lambda: nc.vector.reduce_sum(out=sm[p][:], in_=lg_[p][:], axis=AX.X), reads=[lB[p]], writes=[lB[p]])
                kb.op("dve", lambda: nc.vector.reciprocal(out=sm[p][:], in_=sm[p][:]), reads=[lB[p]], writes=[lB[p]])
                ab_ = kb.buf()
                affBs.append(ab_)
                kb.op("dve", lambda: nc.vector.tensor_scalar(out=aff[:, t, :], in0=lg_[p][:], scalar1=sm[p][:], scalar2=None, op0=ALU.mult), reads=[lB[p]], writes=[ab_])
            interleave([tile(t) for t in range(NT)])

        def stage_moe(st, l, aff, affB, idxf_all, idxB, nexp=NE):
            mask = sb(st, "mmask", [128, NT, NE], F32)
            maskb = sb(st, "mmaskb", [128, NT, NE], BF16)
            key = sb(st, "mkey", [128, NT, NE], F32)
            tri = sb(st, "mtri", [128, 128], BF16)
            R = sb(st, "mR", [128, NT, NE, 8], BF16)
            r1 = sb(st, "mr1", [128, NT, NE], F32)
            pcol = sb(st, "mpcol", [128, 1], F32)
            tcol = sb(st, "mtcol", [128, NT, NE], F32)
            rt = ExitStack()
            affT = sb(rt, "affT", [NE, SEQ], F32)
            work = sb(rt, "mwork", [NE, SEQ], F32)
            m8 = sb(rt, "m8", [NE, 8], F32)
            aTB = kb.buf()
            for t4 in range(4):
                def tr(t4=t4):
                    for k in range(4):
                        t = t4 * 4 + k
                        ins = nc.tensor.transpose(PS[t4][0:NE, k * 128:(k + 1) * 128], aff[:, t, :], ident[:])
                    return ins
                kb.op("pe", tr, reads=[affB, cB], writes=[PSB[t4]])
                kb.op("dve", lambda t4=t4: nc.vector.tensor_copy(out=affT[:, t4 * 512:(t4 + 1) * 512], in_=PS[t4][0:NE, :]), reads=[PSB[t4]], writes=[aTB])
            kb.op("dve", lambda: nc.vector.tensor_copy(out=work[:], in_=affT[:]), reads=[aTB], writes=[aTB])
            for r in range(CAP // 8):
                kb.op("dve", lambda: nc.vector.max(out=m8[:], in_=work[:]), reads=[aTB], writes=[aTB])
                if r < CAP // 8 - 1:
                    kb.op("dve", lambda: nc.vector.match_replace(out=work[:], in_to_replace=m8[:], in_values=work[:], imm_value=-1.0), reads=[aTB], writes=[aTB])
            kb.op("dve", lambda: nc.vector.tensor_scalar(out=work[:], in0=affT[:], scalar1=m8[:, 7:8], scalar2=None, op0=ALU.is_ge), reads=[aTB], writes=[aTB])
            mkB = kb.buf()
            for t4 in range(4):
                def tr2(t4=t4):
                    for k in range(4):
                        t = t4 * 4 + k
                        ins = nc.tensor.transpose(PS[4 + t4][:, k * NE:(k + 1) * NE], work[:, t * 128:(t + 1) * 128], ident[0:NE, 0:NE])
                    return ins
                kb.op("pe", tr2, reads=[aTB, cB], writes=[PSB[4 + t4]])
                kb.op("dve", lambda t4=t4: nc.vector.tensor_copy(out=mask[:, t4 * 4:(t4 + 1) * 4, :],
                                                               in_=PS[4 + t4][:, 0:4 * NE].rearrange("p (k e) -> p k e", k=4)), reads=[PSB[4 + t4]], writes=[mkB])
            kb.op("dve", lambda: nc.vector.tensor_copy(out=maskb[:], in_=mask[:]), reads=[mkB], writes=[mkB])
            kb.dma("sp", tri[:], C["tri"], writes=[mkB])

            def cums():
                for t in range(NT):
                    for i2 in range(t):
                        nc.tensor.matmul(PS[0][:, t * NE:(t + 1) * NE], lhsT=onesb[:], rhs=maskb[:, i2, :], start=(i2 == 0), stop=False)
                    ins = nc.tensor.matmul(PS[0][:, t * NE:(t + 1) * NE], lhsT=tri[:], rhs=maskb[:, t, :], start=(t == 0), stop=True)
                return ins
            kb.op("pe", cums, reads=[mkB, cB], writes=[PSB[0]])
            kb.op("dve", lambda: nc.vector.tensor_tensor(out=key[:].rearrange("p t e -> p (t e)"), in0=PS[0][:, 0:NT * NE],
                                                         in1=mask[:].rearrange("p t e -> p (t e)"), op=ALU.mult), reads=[PSB[0], mkB], writes=[mkB])
            kb.op("dve", lambda: nc.vector.tensor_scalar(out=key[:], in0=key[:], scalar1=-1.0, scalar2=None, op0=ALU.add), reads=[mkB], writes=[mkB])
            kb.dma("sp", pcol[:], C["pcol"], writes=[mkB])
            kb.dma("sp", tcol[:].rearrange("p t e -> p (t e)"), C["tcol"], writes=[mkB])
            kb.op("dve", lambda: nc.vector.memset(R[:], 0.0), writes=[mkB])
            kb.op("dve", lambda: nc.vector.tensor_copy(out=R[:, :, :, 0], in_=pcol[:].unsqueeze(2).to_broadcast([128, NT, NE])), reads=[mkB], writes=[mkB])
            kb.op("dve", lambda: nc.vector.tensor_copy(out=R[:, :, :, 1], in_=tcol[:]), reads=[mkB], writes=[mkB])
            kb.op("dve", lambda: nc.vector.tensor_copy(out=R[:, :, :, 2], in_=aff[:]), reads=[mkB, affB], writes=[mkB])
            kb.op("dve", lambda: nc.vector.tensor_tensor(out=r1[:], in0=aff[:], in1=R[:, :, :, 2], op=ALU.subtract), reads=[mkB, affB], writes=[mkB])
            kb.op("dve", lambda: nc.vector.tensor_copy(out=R[:, :, :, 3], in_=r1[:]), reads=[mkB], writes=[mkB])
            kb.op("dve", lambda: nc.vector.tensor_tensor(out=r1[:], in0=r1[:], in1=R[:, :, :, 3], op=ALU.subtract), reads=[mkB], writes=[mkB])
            kb.op("dve", lambda: nc.vector.tensor_copy(out=R[:, :, :, 4], in_=r1[:]), reads=[mkB], writes=[mkB])
            kb.barrier()
            rt.close()

            io256 = sb(st, "io256", [128, 256], F32)
            kb.dma("sp", io256[:], C["iota256"], writes=[mkB])
            ig_all = sb(st, "mig", [128, 2 * NE, 8], F32)
            idxi = sb(st, "midxi", [128, 2 * NE], I32)
            gg = sb(st, "mgg", [128, 2 * NE], F32)
            igB = kb.buf()
            sx = ExitStack()
            Sel = [sb(sx, f"mSel{i}", [128, NT, 256], BF16) for i in range(2)]
            selB = kb.bufs(2)
            for e in range(nexp):
                e2 = e % 2
                for t in range(NT):
                    kb.op("dve", lambda t=t: nc.vector.tensor_scalar(out=Sel[e2][:, t, :], in0=io256[:], scalar1=key[:, t, e:e + 1], scalar2=None, op0=ALU.is_equal),
                          reads=[mkB], writes=[selB[e2]])
                pi = 6 + e2

                def mmi():
                    for half in range(2):
                        for t in range(NT):
                            ins = nc.tensor.matmul(PS[pi][:, half * 8:half * 8 + 8], lhsT=Sel[e2][:, t, half * 128:(half + 1) * 128], rhs=R[:, t, e, :],
                                                   start=(t == 0), stop=(t == NT - 1))
                    return ins
                kb.op("pe", mmi, reads=[selB[e2], mkB], writes=[PSB[pi]])
                kb.op("act", lambda: nc.scalar.copy(out=ig_all[:, 2 * e:2 * e + 2, :], in_=PS[pi][:, 0:16].rearrange("p (h k) -> p h k", h=2)), reads=[PSB[pi]], writes=[igB])
            ne2 = 2 * nexp
            kb.op("dve", lambda: nc.vector.scalar_tensor_tensor(out=idxf_all[:, 0:ne2], in0=ig_all[:, 0:ne2, 1], scalar=128.0, in1=ig_all[:, 0:ne2, 0], op0=ALU.mult, op1=ALU.add),
                  reads=[igB], writes=[igB, idxB])
            kb.op("dve", lambda: nc.vector.tensor_copy(out=idxi[:, 0:ne2], in_=idxf_all[:, 0:ne2]), reads=[igB], writes=[igB])
            kb.op("dve", lambda: nc.vector.tensor_tensor(out=gg[:, 0:ne2], in0=ig_all[:, 0:ne2, 2], in1=ig_all[:, 0:ne2, 3], op=ALU.add), reads=[igB], writes=[igB])
            kb.op("dve", lambda: nc.vector.tensor_tensor(out=gg[:, 0:ne2], in0=gg[:, 0:ne2], in1=ig_all[:, 0:ne2, 4], op=ALU.add), reads=[igB], writes=[igB])
            kb.barrier()
            sx.close()

            ex = ExitStack()
            xs = [sb(ex, f"mxs{i}", [128, 2, D], BF16) for i in range(2)]
            xsB = kb.bufs(2)
            xsT = [sb(ex, f"mxsT{i}", [128, NC_, 256], BF16) for i in range(2)]
            xsTB = kb.bufs(2)
            NWB = 3
            wg = [sb(ex, f"mwg{i}", [128, NC_, 512], BF16) for i in range(NWB)]
            wu = [sb(ex, f"mwu{i}", [128, NC_, 512], BF16) for i in range(NWB)]
            wd = [sb(ex, f"mwd{i}", [128, NC_, 512], BF16) for i in range(2)]
            wgB, wuB, wdB = kb.bufs(NWB), kb.bufs(NWB), kb.bufs(2)
            sgt = [sb(ex, f"msg{i}", [128, 256], F32) for i in range(2)]
            sgB = kb.bufs(2)
            hidT = sb(ex, "mhidT", [128, NC_, 256], BF16)
            hidB = kb.buf()
            yeb = [sb(ex, f"myeb{i}", [128, 2, D], BF16) for i in range(2)]
            yeB = kb.bufs(2)
            nw = 0
            nwd = 0
            nsg = 0

            def gather(e):
                for half in range(2):
                    kb.dma("pool", xs[e % 2][:, half, :], h2_d, reads=[igB, B_["h2"]], writes=[xsB[e % 2]],
                           indirect=dict(out_offset=None, in_offset=bass.IndirectOffsetOnAxis(ap=idxi[:, 2 * e + half:2 * e + half + 1], axis=0)))
            gather(0)
            for e in range(nexp):
                e2 = e % 2
                if e + 1 < nexp:
                    gather(e + 1)
                for half in range(2):
                    for cg in range(2):
                        pb = half * 2 + cg
                        psv = PS[pb][:].bitcast(BF16)

                        def trx(half=half, cg=cg, psv=psv):
                            for k in range(8):
                                cc = cg * 8 + k
                                ins = nc.tensor.transpose(psv[:, k * 128:(k + 1) * 128], xs[e2][:, half, cc * 128:(cc + 1) * 128], identb[:])
                            return ins
                        kb.op("pe", trx, reads=[xsB[e2], cB], writes=[PSB[pb]])
                        evac(xsT[e2][:, cg * 8:(cg + 1) * 8, half * 128:(half + 1) * 128], psv.rearrange("p (k n) -> p k n", k=8), [PSB[pb]], [xsTB[e2]])
                for fb in range(4):
                    w2 = nw % NWB
                    nw += 1
                    kb.dma("pool", wg[w2][:], I["w_gate"][l, e, :, fb * 512:(fb + 1) * 512].rearrange("(c p) n -> p c n", p=128), writes=[wgB[w2]])
                    kb.dma("pool", wu[w2][:], I["w_up"][l, e, :, fb * 512:(fb + 1) * 512].rearrange("(c p) n -> p c n", p=128), writes=[wuB[w2]])
                    for fc in range(4):
                        pg, pu = 4 + (fc % 2) * 2, 5 + (fc % 2) * 2

                        def mmg(wt, p_, fc=fc):
                            for cc in range(NC_):
                                ins = nc.tensor.matmul(PS[p_][:, 0:256], lhsT=wt[:, cc, fc * 128:(fc + 1) * 128], rhs=xsT[e2][:, cc, :],
                                                       start=(cc == 0), stop=(cc == NC_ - 1))
                            return ins
                        kb.op("pe", lambda: mmg(wg[w2], pg), reads=[wgB[w2], xsTB[e2]], writes=[PSB[pg]])
                        kb.op("pe", lambda: mmg(wu[w2], pu), reads=[wuB[w2], xsTB[e2]], writes=[PSB[pu]])
                        s2 = nsg % 2
                        nsg += 1
                        kb.op("act", lambda: nc.scalar.activation(out=sgt[s2][:], in_=PS[pg][:, 0:256], func=AF.Silu), reads=[PSB[pg]], writes=[sgB[s2]])
                        kb.op("dve", lambda: nc.vector.tensor_tensor(out=hidT[:, fb * 4 + fc, :], in0=PS[pu][:, 0:256], in1=sgt[s2][:], op=ALU.mult),
                              reads=[PSB[pu], sgB[s2]], writes=[hidB])
                for db in range(4):
                    w2 = nwd % 2
                    nwd += 1
                    kb.dma("pool", wd[w2][:], I["w_down"][l, e, :, db * 512:(db + 1) * 512].rearrange("(c p) n -> p c n", p=128), writes=[wdB[w2]])
                    for half in range(2):
                        pb = (db * 2 + half) % 4

                        def mmd(half=half, pb=pb):
                            for fc in range(NC_):
                                ins = nc.tensor.matmul(PS[pb][:], lhsT=hidT[:, fc, half * 128:(half + 1) * 128], rhs=wd[w2][:, fc, :],
                                                       start=(fc == 0), stop=(fc == NC_ - 1))
                            return ins
                        kb.op("pe", mmd, reads=[hidB, wdB[w2]], writes=[PSB[pb]])
                        kb.op("act", lambda half=half, pb=pb: nc.scalar.activation(out=yeb[e2][:, half, db * 512:(db + 1) * 512], in_=PS[pb][:], func=AF.Copy,
                                                                                  scale=gg[:, 2 * e + half:2 * e + half + 1]), reads=[PSB[pb], igB], writes=[yeB[e2]])
                for half in range(2):
                    r0 = (e * 2 + half) * 128
                    kb.dma("sp", ye_d[r0:r0 + 128, :], yeb[e2][:, half, :], reads=[yeB[e2]], writes=[B_["ye"]])
            kb.barrier()
            ex.close()

        def stage_combine(st, l, idxf_all, idxB, dst_d, dstB, nexp=NE):
            nk = 2 * nexp
            YE = sb(st, "cYE", [128, nk, D], BF16)
            YB = kb.buf()
            for db in range(4):
                kb.dma("sp", YE[:, :, db * 512:(db + 1) * 512], ye_d[0:nk * 128, db * 512:(db + 1) * 512].rearrange("(k p) n -> p k n", p=128),
                       reads=[B_["ye"]], writes=[YB])
            io128 = sb(st, "cio", [128, 128], F32)
            bt = {}
            bcB = kb.buf()
            kb.dma("sp", io128[:], C["iota256"][:, 0:128], writes=[bcB])
            for nm, row in (("g2", mod_d[l, 0:1, 5 * D:6 * D]), ("lg", I["ln2_g"][l:l + 1, :]), ("lb", I["ln2_b"][l:l + 1, :])):
                bt[nm] = sb(st, "qb_" + nm, [128, D], F32)
                load_bc(bt[nm], row, bcB)
            sl = [sb(st, f"csl{i}", [128, nk, 128], BF16) for i in range(2)]
            slB2 = kb.bufs(2)
            idt = sb(st, "cidt", [128, 2 * NE], F32)
            idB = kb.buf()
            xt = [sb(st, f"cxt{i}", [128, D], F32) for i in range(2)]
            xB = kb.bufs(2)
            rt_ = sb(st, "crt", [128, D], F32)
            rB = kb.buf()
            stt = sb(st, "qstt", [128, 4, 6], F32)
            mv = sb(st, "qmv", [128, 2], F32)
            rstd = sb(st, "qrstd", [128, 1], F32)
            nmr = sb(st, "qnmr", [128, 1], F32)
            smB = kb.buf()
            toks = []

            def prep(t):
                s2 = t % 2
                kb.dma("sp", xt[s2][:], x1_d[t * 128:(t + 1) * 128, :], reads=[B_["x1"]], writes=[xB[s2]])
                kb.op("dve", lambda: nc.vector.tensor_scalar(out=idt[:, 0:nk], in0=idxf_all[:, 0:nk], scalar1=float(-128 * t), scalar2=None, op0=ALU.add),
                      reads=[idxB], writes=[idB])
                kb.op("dve", lambda: nc.vector.tensor_tensor(out=sl[s2][:], in0=io128[:].unsqueeze(1).to_broadcast([128, nk, 128]),
                                                             in1=idt[:, 0:nk].unsqueeze(2).to_broadcast([128, nk, 128]), op=ALU.is_equal),
                      reads=[idB, bcB], writes=[slB2[s2]])

            def mm(t):
                s2 = t % 2
                for db in range(4):
                    pb = db + 4 * s2

                    def mmc(db=db, pb=pb):
                        for k in range(nk):
                            ins = nc.tensor.matmul(PS[pb][:], lhsT=sl[s2][:, k, :], rhs=YE[:, k, db * 512:(db + 1) * 512], start=(k == 0), stop=(k == nk - 1))
                        return ins
                    kb.op("pe", mmc, reads=[slB2[s2], YB], writes=[PSB[pb]])

            def post(t):
                s2 = t % 2
                x_, xb_ = xt[s2], xB[s2]
                for db in range(4):
                    pb = db + 4 * s2
                    kb.op("dve", lambda db=db, pb=pb: nc.vector.tensor_tensor(out=rt_[:, db * 512:(db + 1) * 512], in0=PS[pb][:], in1=bt["g2"][:, db * 512:(db + 1) * 512], op=ALU.mult),
                          reads=[PSB[pb], bcB], writes=[rB])
                kb.op("dve", lambda: nc.vector.scalar_tensor_tensor(out=x_[:], in0=x_[:], scalar=float(ALPHA), in1=rt_[:], op0=ALU.mult, op1=ALU.add),
                      reads=[xb_, rB], writes=[xb_])
                ln_stats(stt, mv, rstd, nmr, x_, [xb_], smB)
                kb.op("act", lambda: nc.scalar.activation(out=x_[:], in_=x_[:], func=AF.Identity, bias=nmr[:], scale=rstd[:]), reads=[smB, xb_], writes=[xb_])
                kb.op("dve", lambda: nc.vector.tensor_tensor(out=x_[:], in0=x_[:], in1=bt["lg"][:], op=ALU.mult), reads=[xb_, bcB], writes=[xb_])
                kb.op("pool", lambda: nc.gpsimd.tensor_tensor(out=x_[:], in0=x_[:], in1=bt["lb"][:], op=ALU.add), reads=[xb_, bcB], writes=[xb_])
                toks.append(kb.dma("sp", dst_d[t * 128:(t + 1) * 128, :], x_[:], reads=[xb_], writes=[dstB]))
            prep(0)
            for t in range(NT):
                mm(t)
                if t + 1 < NT:
                    prep(t + 1)
                post(t)
            return toks

        def stage_postmoe(st, l, dst_d, dstB):
            bt = {}
            bcB = kb.buf()
            for nm, row in (("g2", mod_d[l, 0:1, 5 * D:6 * D]), ("lg", I["ln2_g"][l:l + 1, :]), ("lb", I["ln2_b"][l:l + 1, :])):
                bt[nm] = sb(st, "qb_" + nm, [128, D], F32)
                load_bc(bt[nm], row, bcB)
            xt = [sb(st, f"qxt{i}", [128, D], F32) for i in range(2)]
            yt = [sb(st, f"qyt{i}", [128, D], F32) for i in range(2)]
            xB, yB = kb.bufs(2), kb.bufs(2)
            stt = sb(st, "qstt", [128, 4, 6], F32)
            mv = sb(st, "qmv", [128, 2], F32)
            rstd = sb(st, "qrstd", [128, 1], F32)
            nmr = sb(st, "qnmr", [128, 1], F32)
            smB = kb.buf()
            toks = []
            for t in range(NT):
                x_, y_, xb_, yb_ = xt[t % 2], yt[t % 2], xB[t % 2], yB[t % 2]
                kb.dma("sp", x_[:], x1_d[t * 128:(t + 1) * 128, :], reads=[B_["x1"]], writes=[xb_])
                kb.dma("sp", y_[:], moe_d[t * 128:(t + 1) * 128, :], reads=[B_["moe"]], writes=[yb_])
                kb.op("dve", lambda y_=y_: nc.vector.tensor_tensor(out=y_[:], in0=y_[:], in1=bt["g2"][:], op=ALU.mult), reads=[yb_, bcB], writes=[yb_])
                kb.op("dve", lambda x_=x_, y_=y_: nc.vector.scalar_tensor_tensor(out=x_[:], in0=x_[:], scalar=float(ALPHA), in1=y_[:], op0=ALU.mult, op1=ALU.add),
                      reads=[xb_, yb_], writes=[xb_])
                ln_stats(stt, mv, rstd, nmr, x_, [xb_], smB)
                kb.op("act", lambda x_=x_: nc.scalar.activation(out=x_[:], in_=x_[:], func=AF.Identity, bias=nmr[:], scale=rstd[:]), reads=[smB, xb_], writes=[xb_])
                kb.op("dve", lambda x_=x_: nc.vector.tensor_tensor(out=x_[:], in0=x_[:], in1=bt["lg"][:], op=ALU.mult), reads=[xb_, bcB], writes=[xb_])
                kb.op("dve", lambda x_=x_: nc.vector.tensor_tensor(out=x_[:], in0=x_[:], in1=bt["lb"][:], op=ALU.add), reads=[xb_, bcB], writes=[xb_])
                toks.append(kb.dma("sp", dst_d[t * 128:(t + 1) * 128, :], x_[:], reads=[xb_], writes=[dstB]))
            return toks

        xB_in = kb.buf("xin")
        stage_mod()
        hT_d = scratch("hT_d", [NC_, 128, SEQ + CTX], BF16)
        if upto >= 1:
            with ExitStack() as st:
                hT = sb(st, "hT0", [128, NC_, SEQ + CTX], BF16)
                hB = kb.buf()
                with ExitStack() as s2:
                    stage_pre(s2, I["x"], xB_in, NT, mod_d[0, 0:1, 0:D], mod_d[0, 0:1, D:2 * D], hT, hB, 0, "a")
                    kb.barrier()
                if upto >= 2:
                    with ExitStack() as s2:
                        stage_pre(s2, I["ctx"], xB_in, CTX // 128, mod_d[0, 1:2, 0:D], mod_d[0, 1:2, D:2 * D], hT, hB, SEQ, "b")
                        kb.barrier()
                if "hT_d" in dbg:
                    kb.dma("sp", hT_d.rearrange("c p n -> p c n"), hT[:], reads=[hB], writes=[B_["mixT"]])
                if upto >= 3:
                    with ExitStack() as s2:
                        stage_inproj_ab(s2, hT, hB, mod_gen(s2, 1))
                        kb.barrier()
                kb.barrier()
        if upto >= 4:
            with ExitStack() as st:
                mixT = sb(st, "mixT", [128, NC_, SEQ], BF16)
                mB = kb.buf()
                with ExitStack() as s2:
                    stage_attn(s2, mixT, mB)
                    kb.barrier()
                if upto >= 5:
                    with ExitStack() as s2:
                        stage_fourier(s2, mixT, mB)
                        kb.barrier()
                if "mixT_d" in dbg:
                    kb.dma("sp", mixT_d.rearrange("c p n -> p c n"), mixT[:], reads=[mB], writes=[B_["mixT"]])
                if upto >= 6:
                    with ExitStack() as s2:
                        stage_gemm_out(s2, mixT, mB, I["ab_w_out"][0], "go")
                        kb.barrier()
                kb.barrier()

        def moe_block(l, xsrc_d, xsrcB, dst_d, dstB, nexp=NE):
            with ExitStack() as so:
                idxf_all = sb(so, "idxf_all", [128, 2 * NE], F32)
                idxB = kb.buf()
                with ExitStack() as st:
                    aff = sb(st, "aff", [128, NT, NE], F32)
                    affB = kb.buf()
                    with ExitStack() as s2:
                        stage_post(s2, l, xsrc_d, xsrcB, aff, affB)
                        kb.barrier()
                    if "aff_d" in dbg:
                        kb.dma("sp", aff_d, aff[:].rearrange("p t e -> p (t e)"), reads=[affB], writes=[B_["aff"]])
                    with ExitStack() as s2:
                        stage_moe(s2, l, aff, affB, idxf_all, idxB, nexp)
                        kb.barrier()
                    kb.barrier()
                with ExitStack() as s2:
                    toks = stage_combine(s2, l, idxf_all, idxB, dst_d, dstB, nexp)
                    kb.barrier()
                kb.barrier()
            return toks

        out_toks = []
        if upto >= 7:
            moe_block(0, I["x"], xB_in, xl1_d, B_["xl1"], nexp=(NE if upto >= 8 else 1))
        if upto >= 9:
            with ExitStack() as st:
                hT = sb(st, "hT1", [128, NC_, SEQ], BF16)
                hB = kb.buf()
                with ExitStack() as s2:
                    stage_pre(s2, xl1_d, B_["xl1"], NT, mod_d[1, 0:1, 0:D], mod_d[1, 0:1, D:2 * D], hT, hB, 0, "c")
                    kb.barrier()
                with ExitStack() as s2:
                    stage_inproj_cd(s2, hT, hB)
                    kb.barrier()
                kb.barrier()
        if upto >= 10:
            with ExitStack() as st:
                mixT = sb(st, "mixT1", [128, NC_, SEQ], BF16)
                mB = kb.buf()
                with ExitStack() as s2:
                    stage_sg(s2, mixT, mB)
                    kb.barrier()
                with ExitStack() as s2:
                    stage_conv(s2, mixT, mB)
                    kb.barrier()
                if "mixT_d" in dbg:
                    kb.dma("sp", mixT_d.rearrange("c p n -> p c n"), mixT[:], reads=[mB], writes=[B_["mixT"]])
                with ExitStack() as s2:
                    stage_gemm_out(s2, mixT, mB, I["cd_w_out"][0], "go1")
                    kb.barrier()
                kb.barrier()
        if upto >= 11:
            moe_block(1, xl1_d, B_["xl1"], out_d, B_["out"])
        kb.barrier()
    return nc, hc


_CACHE = {}


def make_in_maps(inputs, hc, ncores=8):
    shared = {}
    for k in INPUT_SHAPES:
        if k in ("x", "c", "ctx"):
            continue
        a = np.ascontiguousarray(np.asarray(inputs[k], dtype=np.float32)).reshape(INPUT_SHAPES[k])
        shared[k] = a
    for k, v in hc.items():
        shared["k_" + k] = v
    maps = []
    for b in range(ncores):
        m = dict(shared)
        m["x"] = np.ascontiguousarray(inputs["x"][b], dtype=np.float32)
        m["c"] = np.ascontiguousarray(inputs["c"][b], dtype=np.float32).reshape(1, D)
        m["ctx"] = np.ascontiguousarray(inputs["ctx"][b], dtype=np.float32)
        maps.append(m)
    return maps


def kernel(**inputs):
    if "prog" not in _CACHE:
        _CACHE["prog"] = build_program()
    nc, hc = _CACHE["prog"]
    maps = make_in_maps(inputs, hc, 8)
    res = run_bass_kernel_spmd(nc, maps, core_ids=list(range(8)))
    return np.stack([np.asarray(r["out"], dtype=np.float32) for r in res.results], axis=0)
```

```python
import numpy as np
import ml_dtypes
from contextlib import ExitStack
import concourse.bass as bass
import concourse.mybir as mybir
from concourse.bass_utils import run_bass_kernel_spmd

F32 = mybir.dt.float32
BF16 = mybir.dt.bfloat16
I32 = mybir.dt.int32
ALU = mybir.AluOpType
AF = mybir.ActivationFunctionType
AX = mybir.AxisListType

D = 2048
SEQ = 2048
CTX = 256
NT = SEQ // 128
NC_ = D // 128
ALPHA = 4 ** 0.25
EPS = 1e-6
NE = 16
CAP = 256


class Buf:
    __slots__ = ("name", "last_w", "reads")

    def __init__(self, name=""):
        self.name = name
        self.last_w = None
        self.reads = []


class KB:
    ND = 10

    def __init__(self, nc, es):
        self.nc = nc
        self.engs = {"pe": nc.tensor, "act": nc.scalar, "dve": nc.vector, "pool": nc.gpsimd, "sp": nc.sync}
        self.sem, self.cnt, self.seen = {}, {}, {}
        for e in self.engs:
            self.sem[e] = es.enter_context(nc.semaphore("pg_" + e))
            self.cnt[e] = 0
            self.seen[e] = {}
        self.dsem, self.dcnt, self.drr = {}, {}, {}
        for q in ("sp", "pool"):
            self.dsem[q] = [es.enter_context(nc.semaphore(f"dq_{q}{i}")) for i in range(self.ND)]
            self.dcnt[q] = [0] * self.ND
            self.drr[q] = 0
        self.nbuf = 0

    def buf(self, name=""):
        self.nbuf += 1
        return Buf(name or f"b{self.nbuf}")

    def bufs(self, n, name=""):
        return [self.buf(f"{name}{i}") for i in range(n)]

    def _wait(self, eng, tok):
        sem, val = tok
        key = id(sem)
        if self.seen[eng].get(key, 0) >= val:
            return
        self.engs[eng].wait_ge(sem, val)
        self.seen[eng][key] = val

    def _dep1(self, eng, tok):
        if tok[0] is self.sem[eng] and eng == "pe":
            return
        self._wait(eng, tok)

    def _deps(self, eng, reads, writes):
        for b in reads:
            if b.last_w is not None:
                self._dep1(eng, b.last_w)
        for b in writes:
            if b.last_w is not None:
                self._dep1(eng, b.last_w)
            for t in b.reads:
                self._dep1(eng, t)

    def _upd(self, tok, reads, writes):
        for b in reads:
            b.reads.append(tok)
        for b in writes:
            b.last_w = tok
            b.reads = []

    def op(self, eng, fn, reads=(), writes=()):
        self._deps(eng, reads, writes)
        ins = fn()
        self.cnt[eng] += 1
        ins.then_inc(self.sem[eng], 1)
        tok = (self.sem[eng], self.cnt[eng])
        self._upd(tok, reads, writes)
        return tok

    def dma(self, q, out, in_, reads=(), writes=(), indirect=None, **kw):
        i = self.drr[q]
        self.drr[q] = (i + 1) % self.ND
        sem = self.dsem[q][i]
        if self.dcnt[q][i] > 0:
            self._wait(q, (sem, self.dcnt[q][i]))
        self._deps(q, reads, writes)
        if indirect is not None:
            ins = self.engs[q].indirect_dma_start(out=out, in_=in_, **indirect)
        else:
            ins = self.engs[q].dma_start(out=out, in_=in_, **kw)
        ins.then_inc(sem, 16)
        self.dcnt[q][i] += 16
        tok = (sem, self.dcnt[q][i])
        self._upd(tok, reads, writes)
        return tok

    def barrier(self):
        toks = [(self.sem[e], self.cnt[e]) for e in self.engs if self.cnt[e] > 0]
        for q in self.dsem:
            for i, s in enumerate(self.dsem[q]):
                if self.dcnt[q][i] > 0:
                    toks.append((s, self.dcnt[q][i]))
        for e in self.engs:
            for t in toks:
                if t[0] is self.sem[e]:
                    continue
                self._wait(e, t)


def host_consts():
    c = {}
    c["ident"] = np.eye(128, dtype=np.float32)
    c["identb"] = np.eye(128).astype(ml_dtypes.bfloat16)
    t = np.arange(SEQ)
    r = (t // 64).astype(np.float32)
    col = (t % 64).astype(np.float32)
    inv = (np.float32(10000.0) ** (-np.arange(32, dtype=np.float32) / np.float32(32))).astype(np.float32)
    ang_r = (r[:, None] * inv).astype(np.float32)
    ang_c = (col[:, None] * inv).astype(np.float32)
    cosT = np.zeros((128, SEQ), np.float32)
    sinT = np.zeros((128, SEQ), np.float32)
    for base, ang in ((0, ang_r), (64, ang_c)):
        cs = np.cos(ang).astype(np.float32).T
        sn = np.sin(ang).astype(np.float32).T
        cosT[base:base + 32] = cs
        cosT[base + 32:base + 64] = cs
        sinT[base:base + 32] = -sn
        sinT[base + 32:base + 64] = sn
    c["cosT"] = cosT
    c["sinT"] = sinT
    k = np.arange(SEQ, dtype=np.int64)
    ph = (np.outer(k, k) % SEQ).astype(np.float64) * (2 * np.pi / SEQ)
    c["dft_c"] = np.cos(ph).astype(ml_dtypes.bfloat16)
    c["dft_s"] = np.sin(ph).astype(ml_dtypes.bfloat16)
    kc = np.arange(128, dtype=np.int64)
    phc = (np.outer(kc, kc) % 128).astype(np.float64) * (2 * np.pi / 128)
    c["dftc_c"] = (np.cos(phc) / 512.0).astype(ml_dtypes.bfloat16)
    c["dftc_sn"] = (-np.sin(phc) / 512.0).astype(ml_dtypes.bfloat16)
    kk = np.arange(128)[:, None]
    qq = np.arange(128)[None, :]
    mlo = (qq <= kk).astype(np.float32)
    mhi = (kk <= qq).astype(np.float32)
    c["mask_lo"] = np.tile(mlo, (1, 3)).astype(ml_dtypes.bfloat16)
    c["mask_hi"] = np.tile(mhi, (1, 3)).astype(ml_dtypes.bfloat16)
    c["iota256"] = np.tile(np.arange(256, dtype=np.float32)[None, :], (128, 1))
    c["iota_tok"] = np.tile(np.arange(SEQ, dtype=np.float32)[None, :], (128, 1))
    c["pcol"] = np.arange(128, dtype=np.float32).reshape(128, 1)
    c["tcol"] = np.tile(np.repeat(np.arange(16, dtype=np.float32), 16)[None, :], (128, 1))
    c["tri"] = (np.arange(128)[:, None] <= np.arange(128)[None, :]).astype(ml_dtypes.bfloat16)
    c["onesb"] = np.ones((128, 128), ml_dtypes.bfloat16)
    c["onesf"] = np.ones((128, 128), np.float32)
    return c


CONST_DT = {"identb": BF16, "dft_c": BF16, "dft_s": BF16, "dftc_c": BF16, "dftc_sn": BF16, "mask_lo": BF16,
            "mask_hi": BF16, "tri": BF16, "onesb": BF16}

INPUT_SHAPES = {
    "x": [SEQ, D], "c": [1, D], "ctx": [CTX, D], "c_ctx": [1, D],
    "w_mod": [2, D, 6 * D], "b_mod": [2, 6 * D], "ln1_g": [2, D], "ln1_b": [2, D], "ln2_g": [2, D], "ln2_b": [2, D],
    "w_router": [2, D, NE], "w_gate": [2, NE, D, D], "w_up": [2, NE, D, D], "w_down": [2, NE, D, D],
    "ab_w_in": [1, D, 3072], "ab_w_out": [1, D, D], "sink": [1, 12],
    "cd_w_in": [1, D, 4096], "cd_w_out": [1, D, D], "sg_ln_g": [1, 1024], "sg_ln_b": [1, 1024],
    "sg_w": [8, 128, 128], "sg_b": [1, 1024], "conv_w": [31, 1024], "conv_b": [1, 1024],
    "conv_ln_g": [1, 1024], "conv_ln_b": [1, 1024],
}


def build_program(upto=99, dbg=()):
    nc = bass.Bass("TRN2", target_bir_lowering=False)
    es = ExitStack()
    with es:
        kb = KB(nc, es)
        I = {}
        for name, shp in INPUT_SHAPES.items():
            I[name] = nc.dram_tensor(name, list(shp), F32, kind="ExternalInput").ap()
        hc = host_consts()
        C = {}
        for name, arr in hc.items():
            C[name] = nc.dram_tensor("k_" + name, list(arr.shape), CONST_DT.get(name, F32), kind="ExternalInput").ap()
        out_d = nc.dram_tensor("out", [SEQ, D], F32, kind="ExternalOutput").ap()

        def scratch(name, shape, dt=F32):
            kind = "ExternalOutput" if name in dbg else "Internal"
            return nc.dram_tensor(name, list(shape), dt, kind=kind).ap()

        mod_d = scratch("mod_d", [2, 2, 6 * D])
        qT_d = scratch("qT_d", [12, 128, SEQ], BF16)
        kT_d = scratch("kT_d", [4, 128, SEQ + CTX], BF16)
        v_d = scratch("v_d", [SEQ + CTX, 512], BF16)
        z_d = scratch("z_d", [SEQ, 512], BF16)
        mixT_d = scratch("mixT_d", [16, 128, SEQ], BF16)
        y_d = scratch("y_d", [SEQ, D])
        x1_d = scratch("x1_d", [SEQ, D])
        h2_d = scratch("h2_d", [SEQ, D], BF16)
        ye_d = scratch("ye_d", [2 * NE * 128, D], BF16)
        selT_d = scratch("selT_d", [2 * NE, 128, SEQ], BF16)
        moe_d = scratch("moe_d", [SEQ, D])
        xl1_d = scratch("xl1_d", [SEQ, D])
        uT_d = scratch("uT_d", [8, 128, SEQ], BF16)
        vg_d = scratch("vg_d", [SEQ, 1024], BF16)
        xgT_d = scratch("xgT_d", [8, 128, SEQ], BF16)
        aff_d = scratch("aff_d", [128, 256])
        B_ = {n: kb.buf(n) for n in ("mod", "qT", "kT", "v", "z", "mixT", "y", "x1", "h2", "ye", "selT", "moe", "xl1",
                                     "uT", "vg", "xgT", "aff", "out")}

        sbn = [0]

        def sb(st, name, shape, dt):
            sbn[0] += 1
            return st.enter_context(nc.sbuf_tensor(f"{name}_{sbn[0]}", list(shape), dt))

        PS = [es.enter_context(nc.psum_tensor(f"ps{i}", [128, 512], F32)) for i in range(8)]
        PSB = kb.bufs(8, "ps")
        ident = sb(es, "ident", [128, 128], F32)
        identb = sb(es, "identb", [128, 128], BF16)
        onesb = sb(es, "onesb", [128, 128], BF16)
        cB = kb.buf("consts")
        kb.dma("sp", ident[:], C["ident"], writes=[cB])
        kb.dma("sp", identb[:], C["identb"], writes=[cB])
        kb.dma("sp", onesb[:], C["onesb"], writes=[cB])

        evac_rr = [0]

        def evac(out, in_, reads, writes):
            evac_rr[0] ^= 1
            if evac_rr[0]:
                return kb.op("act", lambda: nc.scalar.copy(out=out, in_=in_), reads=reads, writes=writes)
            return kb.op("dve", lambda: nc.vector.tensor_copy(out=out, in_=in_), reads=reads, writes=writes)

        def ln_stats(st_tile, mv, rstd, nmr, xin, rB, wB):
            for k in range(4):
                kb.op("dve", lambda k=k: nc.vector.bn_stats(out=st_tile[:, k, :], in_=xin[:, k * 512:(k + 1) * 512]),
                      reads=rB, writes=[wB])
            kb.op("dve", lambda: nc.vector.bn_aggr(out=mv[:], in_=st_tile[:].rearrange("p a b -> p (a b)")), reads=[wB], writes=[wB])
            kb.op("act", lambda: nc.scalar.activation(out=rstd[:], in_=mv[:, 1:2], func=AF.Sqrt, bias=EPS, scale=1.0),
                  reads=[wB], writes=[wB])
            kb.op("dve", lambda: nc.vector.reciprocal(out=rstd[:], in_=rstd[:]), reads=[wB], writes=[wB])
            kb.op("dve", lambda: nc.vector.scalar_tensor_tensor(out=nmr[:], in0=mv[:, 0:1], scalar=-1.0, in1=rstd[:],
                                                                 op0=ALU.mult, op1=ALU.mult), reads=[wB], writes=[wB])

        def ln_stats_g(st_tile, mv, rstd, nmr, xin, rB, wB):
            for k in range(4):
                kb.op("dve", lambda k=k: nc.vector.bn_stats(out=st_tile[:, k, :], in_=xin[:, k * 512:(k + 1) * 512]), reads=rB, writes=[wB])
            kb.op("dve", lambda: nc.vector.bn_aggr(out=mv[:], in_=st_tile[:].rearrange("p a b -> p (a b)")), reads=[wB], writes=[wB])
            kb.op("act", lambda: nc.scalar.activation(out=rstd[:], in_=mv[:, 1:2], func=AF.Sqrt, bias=EPS, scale=1.0), reads=[wB], writes=[wB])
            yield
            kb.op("dve", lambda: nc.vector.reciprocal(out=rstd[:], in_=rstd[:]), reads=[wB], writes=[wB])
            kb.op("dve", lambda: nc.vector.scalar_tensor_tensor(out=nmr[:], in0=mv[:, 0:1], scalar=-1.0, in1=rstd[:], op0=ALU.mult, op1=ALU.mult),
                  reads=[wB], writes=[wB])

        def interleave(gens, depth=2, side=None, side_every=3):
            pending = list(gens)
            active = []
            rnd = 0
            while pending or active:
                while len(active) < depth and pending:
                    active.append(pending.pop(0))
                for g in list(active):
                    try:
                        next(g)
                    except StopIteration:
                        active.remove(g)
                rnd += 1
                if side is not None and rnd % side_every == 0:
                    next(side, None)

        def load_bc(tile, row_ap, wB, plus_one=False):
            kb.dma("sp", tile[:], row_ap.partition_broadcast(128), reads=[B_["mod"]], writes=[wB])
            if plus_one:
                kb.op("dve", lambda: nc.vector.tensor_scalar(out=tile[:], in0=tile[:], scalar1=1.0, scalar2=None, op0=ALU.add),
                      reads=[wB], writes=[wB])

        def mod_gen(st, blocks, psb=(0, 1)):
            sT = sb(st, "sT", [128, NC_, 2], F32)
            sTb = sb(st, "sTb", [128, NC_, 2], BF16)
            bm = [sb(st, f"bm{i}", [2, 512], F32) for i in range(2)]
            mo = [sb(st, f"mo{i}", [2, 512], F32) for i in range(2)]
            wb = [sb(st, f"wmod{i}", [128, NC_, 512], BF16) for i in range(2)]
            wB = kb.bufs(2, "wmod")
            sB, bB, moB = kb.buf(), kb.bufs(2), kb.bufs(2)
            kb.dma("sp", sT[:, :, 0], I["c"][0].rearrange("(c p) -> p c", p=128), writes=[sB], allow_slow_non_contiguous=True)
            kb.dma("sp", sT[:, :, 1], I["c_ctx"][0].rearrange("(c p) -> p c", p=128), writes=[sB], allow_slow_non_contiguous=True)
            kb.op("act", lambda: nc.scalar.activation(out=sT[:], in_=sT[:], func=AF.Silu), reads=[sB], writes=[sB])
            kb.op("dve", lambda: nc.vector.tensor_copy(out=sTb[:], in_=sT[:]), reads=[sB], writes=[sB])
            for n, (l, j) in enumerate(blocks):
                k = n % 2
                w = wb[k]
                kb.dma("sp", bm[k][0:1, :], I["b_mod"][l:l + 1, j * 512:(j + 1) * 512], writes=[bB[k]])
                kb.dma("sp", bm[k][1:2, :], I["b_mod"][l:l + 1, j * 512:(j + 1) * 512], writes=[bB[k]])
                kb.dma("pool", w[:], I["w_mod"][l, :, j * 512:(j + 1) * 512].rearrange("(c p) n -> p c n", p=128), writes=[wB[k]])
                yield
                ps = PS[psb[k]]

                def mm():
                    for cc in range(NC_):
                        ins = nc.tensor.matmul(ps[0:2, :], lhsT=sTb[:, cc, :], rhs=w[:, cc, :], start=(cc == 0), stop=(cc == NC_ - 1))
                    return ins
                kb.op("pe", mm, reads=[sB, wB[k]], writes=[PSB[psb[k]]])
                kb.op("dve", lambda: nc.vector.tensor_tensor(out=mo[k][:], in0=ps[0:2, :], in1=bm[k][:], op=ALU.add),
                      reads=[PSB[psb[k]], bB[k]], writes=[moB[k]])
                kb.dma("sp", mod_d[l, :, j * 512:(j + 1) * 512], mo[k][:], reads=[moB[k]], writes=[B_["mod"]])

        def stage_mod():
            with ExitStack() as st:
                for _ in mod_gen(st, [(0, j) for j in range(8)]):
                    pass
                kb.barrier()

        def stage_pre(st, src_d, srcB, ntiles, sh_row, sc_row, hT, hB, col0, tag, side=None):
            bsc = sb(st, tag + "bsc", [128, D], F32)
            bsh = sb(st, tag + "bsh", [128, D], F32)
            bcB = kb.buf()
            load_bc(bsc, sc_row, bcB, plus_one=True)
            load_bc(bsh, sh_row, bcB)
            xt = [sb(st, f"{tag}xt{i}", [128, D], F32) for i in range(2)]
            xB = kb.bufs(2)
            stt = [sb(st, f"{tag}stt{i}", [128, 4, 6], F32) for i in range(2)]
            mv = [sb(st, f"{tag}mv{i}", [128, 2], F32) for i in range(2)]
            rstd = [sb(st, f"{tag}rstd{i}", [128, 1], F32) for i in range(2)]
            nmr = [sb(st, f"{tag}nmr{i}", [128, 1], F32) for i in range(2)]
            smB = kb.bufs(2)

            def tile(t):
                p = t % 2
                x_, xb_ = xt[p], xB[p]
                kb.dma("sp", x_[:], src_d[t * 128:(t + 1) * 128, :], reads=[srcB], writes=[xb_])
                yield
                yield from ln_stats_g(stt[p], mv[p], rstd[p], nmr[p], x_, [xb_], smB[p])
                kb.op("act", lambda: nc.scalar.activation(out=x_[:], in_=x_[:], func=AF.Identity, bias=nmr[p][:], scale=rstd[p][:]),
                      reads=[smB[p], xb_], writes=[xb_])
                yield
                kb.op("dve", lambda: nc.vector.tensor_tensor(out=x_[:], in0=x_[:], in1=bsc[:], op=ALU.mult), reads=[xb_, bcB], writes=[xb_])
                kb.op("pool", lambda: nc.gpsimd.tensor_tensor(out=x_[:], in0=x_[:], in1=bsh[:], op=ALU.add), reads=[xb_, bcB], writes=[xb_])
                yield
                for q4 in range(4):
                    pb = p * 4 + q4

                    def tr():
                        for k in range(4):
                            cc = q4 * 4 + k
                            ins = nc.tensor.transpose(PS[pb][:, k * 128:(k + 1) * 128], x_[:, cc * 128:(cc + 1) * 128], ident[:])
                        return ins
                    kb.op("pe", tr, reads=[xb_, cB], writes=[PSB[pb]])
                hb_t = kb.buf()
                for q4 in range(4):
                    pb = p * 4 + q4
                    evac(hT[:, q4 * 4:(q4 + 1) * 4, col0 + t * 128: col0 + (t + 1) * 128],
                         PS[pb][:].rearrange("p (k n) -> p k n", k=4), [PSB[pb]], [hb_t])
            interleave([tile(t) for t in range(ntiles)], side=side, side_every=3)

        def stage_gemm_out(st, mixT, mB, w_dram, tag):
            wb = [sb(st, f"{tag}w{i}", [128, NC_, 512], BF16) for i in range(2)]
            wB = kb.bufs(2)
            stg = [sb(st, f"{tag}stg{i}", [128, 512], F32) for i in range(3)]
            sgB = kb.bufs(3)
            n = 0
            for jb in range(4):
                kb.dma("pool", wb[jb % 2][:], w_dram[:, jb * 512:(jb + 1) * 512].rearrange("(c p) n -> p c n", p=128), writes=[wB[jb % 2]])
                for t in range(NT):
                    pb = n % 4

                    def mm(jb=jb, t=t, pb=pb):
                        for cc in range(NC_):
                            ins = nc.tensor.matmul(PS[pb][:], lhsT=mixT[:, cc, t * 128:(t + 1) * 128], rhs=wb[jb % 2][:, cc, :],
                                                   start=(cc == 0), stop=(cc == NC_ - 1))
                        return ins
                    kb.op("pe", mm, reads=[mB, wB[jb % 2]], writes=[PSB[pb]])
                    s_ = n % 3
                    evac(stg[s_][:], PS[pb][:], [PSB[pb]], [sgB[s_]])
                    kb.dma("sp", y_d[t * 128:(t + 1) * 128, jb * 512:(jb + 1) * 512], stg[s_][:], reads=[sgB[s_]], writes=[B_["y"]])
                    n += 1

        def stage_inproj_ab(st, hT, hB, side=None):
            stepn = [0]

            def step():
                stepn[0] += 1
                if side is not None and stepn[0] % 3 == 0:
                    next(side, None)
            TT = SEQ + CTX
            w_in = I["ab_w_in"][0]
            cosT = sb(st, "cosT", [128, SEQ], F32)
            sinT = sb(st, "sinT", [128, SEQ], F32)
            rB = kb.buf()
            kb.dma("sp", cosT[:], C["cosT"], writes=[rB])
            kb.dma("sp", sinT[:], C["sinT"], writes=[rB])
            wb = [sb(st, f"abw{i}", [128, NC_, 512], BF16) for i in range(2)]
            ws = sb(st, "abws", [128, NC_, 512], BF16)
            wB = kb.bufs(2)
            wsB = kb.buf()
            t1 = [sb(st, f"rt1_{i}", [128, 512], F32) for i in range(2)]
            t2 = [sb(st, f"rt2_{i}", [128, 512], F32) for i in range(2)]
            tB = kb.bufs(2)
            stg = [sb(st, f"abstg{i}", [128, 512], BF16) for i in range(3)]
            sgB = kb.bufs(3)
            n = 0
            ns = 0
            for jb in range(6):
                w = wb[jb % 2]
                kb.dma("pool", w[:], w_in[:, jb * 512:(jb + 1) * 512].rearrange("(c p) n -> p c n", p=128), writes=[wB[jb % 2]])
                if jb < 4:
                    wv = w[:].rearrange("p c (g t j) -> p (c g) t j", t=2, j=32)
                    sv = ws[:].rearrange("p c (g t j) -> p (c g) t j", t=2, j=32)
                    kb.op("act", lambda wv=wv, sv=sv: nc.scalar.copy(out=sv[:, :, 0, :], in_=wv[:, :, 1, :]), reads=[wB[jb % 2]], writes=[wsB])
                    kb.op("dve", lambda wv=wv, sv=sv: nc.vector.tensor_copy(out=sv[:, :, 1, :], in_=wv[:, :, 0, :]), reads=[wB[jb % 2]], writes=[wsB])
                    for hh in range(4):
                        head = jb * 4 + hh
                        isk = head >= 12
                        for tb in range(4):
                            pa, pb = (n * 2) % 8, (n * 2 + 1) % 8
                            n += 1

                            def mm(wt, p_, hh=hh, tb=tb):
                                for cc in range(NC_):
                                    ins = nc.tensor.matmul(PS[p_][:], lhsT=wt[:, cc, hh * 128:(hh + 1) * 128],
                                                           rhs=hT[:, cc, tb * 512:(tb + 1) * 512], start=(cc == 0), stop=(cc == NC_ - 1))
                                return ins
                            kb.op("pe", lambda: mm(w, pa), reads=[hB, wB[jb % 2]], writes=[PSB[pa]])
                            kb.op("pe", lambda: mm(ws, pb), reads=[hB, wsB], writes=[PSB[pb]])
                            k2 = ns % 2
                            s_ = ns % 3
                            ns += 1
                            kb.op("dve", lambda: nc.vector.tensor_tensor(out=t1[k2][:], in0=PS[pa][:], in1=cosT[:, tb * 512:(tb + 1) * 512], op=ALU.mult),
                                  reads=[PSB[pa], rB], writes=[tB[k2]])
                            kb.op("dve", lambda: nc.vector.tensor_tensor(out=t2[k2][:], in0=PS[pb][:], in1=sinT[:, tb * 512:(tb + 1) * 512], op=ALU.mult),
                                  reads=[PSB[pb], rB], writes=[tB[k2]])
                            kb.op("dve", lambda: nc.vector.tensor_tensor(out=stg[s_][:], in0=t1[k2][:], in1=t2[k2][:], op=ALU.add),
                                  reads=[tB[k2]], writes=[sgB[s_]])
                            if isk:
                                kb.dma("sp", kT_d[head - 12, :, tb * 512:(tb + 1) * 512], stg[s_][:], reads=[sgB[s_]], writes=[B_["kT"]])
                            else:
                                kb.dma("sp", qT_d[head, :, tb * 512:(tb + 1) * 512], stg[s_][:], reads=[sgB[s_]], writes=[B_["qT"]])
                            step()
                        if isk:
                            pa = (n * 2) % 8
                            n += 1

                            def mmc(hh=hh, pa=pa):
                                for cc in range(NC_):
                                    ins = nc.tensor.matmul(PS[pa][:, 0:CTX], lhsT=w[:, cc, hh * 128:(hh + 1) * 128],
                                                           rhs=hT[:, cc, SEQ:TT], start=(cc == 0), stop=(cc == NC_ - 1))
                                return ins
                            kb.op("pe", mmc, reads=[hB, wB[jb % 2]], writes=[PSB[pa]])
                            s_ = ns % 3
                            ns += 1
                            evac(stg[s_][:, 0:CTX], PS[pa][:, 0:CTX], [PSB[pa]], [sgB[s_]])
                            kb.dma("sp", kT_d[head - 12, :, SEQ:TT], stg[s_][:, 0:CTX], reads=[sgB[s_]], writes=[B_["kT"]])
                else:
                    ntile = TT // 128 if jb == 4 else NT
                    for t in range(ntile):
                        pa = (n * 2) % 8
                        n += 1

                        def mmv(t=t, pa=pa):
                            for cc in range(NC_):
                                ins = nc.tensor.matmul(PS[pa][:], lhsT=hT[:, cc, t * 128:(t + 1) * 128], rhs=w[:, cc, :],
                                                       start=(cc == 0), stop=(cc == NC_ - 1))
                            return ins
                        kb.op("pe", mmv, reads=[hB, wB[jb % 2]], writes=[PSB[pa]])
                        s_ = ns % 3
                        ns += 1
                        evac(stg[s_][:], PS[pa][:], [PSB[pa]], [sgB[s_]])
                        if jb == 4:
                            kb.dma("sp", v_d[t * 128:(t + 1) * 128, :], stg[s_][:], reads=[sgB[s_]], writes=[B_["v"]])
                        else:
                            kb.dma("sp", z_d[t * 128:(t + 1) * 128, :], stg[s_][:], reads=[sgB[s_]], writes=[B_["z"]])

            if side is not None:
                for _ in side:
                    pass

        def stage_attn(st, mixT, mB):
            TT = SEQ + CTX
            mlo = sb(st, "mlo", [128, 384], BF16)
            mhi = sb(st, "mhi", [128, 384], BF16)
            snk = sb(st, "snk", [128, 12], F32)
            esk = sb(st, "esk", [128, 12, 128], F32)
            kB_ = kb.buf()
            kb.dma("sp", mlo[:], C["mask_lo"], writes=[kB_])
            kb.dma("sp", mhi[:], C["mask_hi"], writes=[kB_])
            kb.dma("sp", snk[:], I["sink"][0:1, :].partition_broadcast(128), writes=[kB_])
            kb.op("act", lambda: nc.scalar.activation(out=snk[:], in_=snk[:], func=AF.Exp), reads=[kB_], writes=[kB_])
            kb.op("dve", lambda: nc.vector.tensor_copy(out=esk[:], in_=snk[:].unsqueeze(2).to_broadcast([128, 12, 128])), reads=[kB_], writes=[kB_])
            kT = [sb(st, f"kT{i}", [128, TT], BF16) for i in range(2)]
            vt = [sb(st, f"vt{i}", [128, TT // 128, 128], BF16) for i in range(2)]
            qT = [sb(st, f"qT{i}", [128, 3, SEQ], BF16) for i in range(2)]
            hdB = kb.bufs(2)
            pT = [sb(st, f"pT{i}", [128, 384], BF16) for i in range(4)]
            pB = kb.bufs(4)
            den = [sb(st, f"den{i}", [128, 384], F32) for i in range(2)]
            dB = kb.bufs(2)
            scale = 128 ** -0.5
            items = []
            it = 0
            for h in range(4):
                for i in range(NT):
                    blocks = []
                    if i > 0:
                        blocks.append((i - 1, mlo))
                    blocks.append((i, None))
                    if i < NT - 1:
                        blocks.append((i + 1, mhi))
                    blocks.append((16, None))
                    blocks.append((17, None))
                    for bi, (j, msk) in enumerate(blocks):
                        items.append(dict(h=h, i=i, j=j, msk=msk, first=(bi == 0), last=(bi == len(blocks) - 1), po=4 + (it % 2) * 2, d2=it % 2,
                                          n=len(items), newh=(i == 0 and bi == 0)))
                    it += 1

            def emitS(a):
                h, i, j, msk, n = a["h"], a["i"], a["j"], a["msk"], a["n"]
                k2 = h % 2
                if a["newh"]:
                    kb.dma("sp", kT[k2][:], kT_d[h], reads=[B_["kT"]], writes=[hdB[k2]])
                    kb.dma("sp", vt[k2][:], v_d[:, h * 128:(h + 1) * 128].rearrange("(t p) d -> p t d", p=128), reads=[B_["v"]], writes=[hdB[k2]])
                    kb.dma("sp", qT[k2][:], qT_d[3 * h:3 * h + 3].rearrange("g p n -> p g n"), reads=[B_["qT"]], writes=[hdB[k2]])
                sbk = n % 4
                kb.op("pe", lambda: nc.tensor.matmul(PS[sbk][:, 0:384], lhsT=kT[k2][:, j * 128:(j + 1) * 128],
                                                     rhs=qT[k2][:, :, i * 128:(i + 1) * 128], start=True, stop=True),
                      reads=[hdB[k2]], writes=[PSB[sbk]])
                kb.op("act", lambda: nc.scalar.activation(out=pT[sbk][:], in_=PS[sbk][:, 0:384], func=AF.Exp, scale=scale),
                      reads=[PSB[sbk]], writes=[pB[sbk]])
                if msk is not None:
                    kb.op("dve", lambda: nc.vector.tensor_tensor(out=pT[sbk][:], in0=pT[sbk][:], in1=msk[:], op=ALU.mult),
                          reads=[pB[sbk], kB_], writes=[pB[sbk]])

            def emitPV(a):
                h, i, j, n, po, d2 = a["h"], a["i"], a["j"], a["n"], a["po"], a["d2"]
                k2 = h % 2
                pp = n % 4

                def mm2():
                    nc.tensor.matmul(PS[po][:, 0:384], lhsT=vt[k2][:, j, :], rhs=pT[pp][:], start=a["first"], stop=a["last"])
                    return nc.tensor.matmul(PS[po + 1][:, 0:384], lhsT=onesb[:], rhs=pT[pp][:], start=a["first"], stop=a["last"])
                kb.op("pe", mm2, reads=[hdB[k2], pB[pp], cB], writes=[PSB[po], PSB[po + 1]])
                if a["last"]:
                    dn = den[d2]
                    kb.op("dve", lambda: nc.vector.tensor_tensor(out=dn[:], in0=PS[po + 1][:, 0:384],
                                                                 in1=esk[:, 3 * h:3 * h + 3, :].rearrange("p g n -> p (g n)"), op=ALU.add),
                          reads=[PSB[po + 1], kB_], writes=[dB[d2]])
                    kb.op("dve", lambda: nc.vector.reciprocal(out=dn[:], in_=dn[:]), reads=[dB[d2]], writes=[dB[d2]])
                    kb.op("dve", lambda: nc.vector.tensor_tensor(out=mixT[:, 3 * h:3 * h + 3, i * 128:(i + 1) * 128],
                                                                 in0=PS[po][:, 0:384].rearrange("p (g n) -> p g n", g=3),
                                                                 in1=dn[:].rearrange("p (g n) -> p g n", g=3), op=ALU.mult),
                          reads=[PSB[po], dB[d2]], writes=[mB])
            LA = 3
            for n in range(len(items) + LA):
                if n < len(items):
                    emitS(items[n])
                if n - LA >= 0:
                    emitPV(items[n - LA])

        def stage_fourier(st, mixT, mB):
            Z = sb(st, "fz", [128, NT, 512], BF16)
            zB = kb.buf()
            kb.dma("sp", Z[:], z_d.rearrange("(t p) c -> p t c", p=128), reads=[B_["z"]], writes=[zB])
            cc_ = sb(st, "fcc", [128, 128], BF16)
            csn = sb(st, "fcs", [128, 128], BF16)
            kb.dma("sp", cc_[:], C["dftc_c"], writes=[zB])
            kb.dma("sp", csn[:], C["dftc_sn"], writes=[zB])
            Cn = [sb(st, f"fCn{i}", [128, NT, 512], BF16) for i in range(2)]
            Sn = [sb(st, f"fSn{i}", [128, NT, 512], BF16) for i in range(2)]
            tbB = kb.bufs(2)
            ab = [sb(st, f"fab{i}", [128, 512], BF16) for i in range(2)]
            bb = [sb(st, f"fbb{i}", [128, 512], BF16) for i in range(2)]
            abB = kb.bufs(2)
            n = 0
            for nb in range(4):
                k2 = nb % 2
                kb.dma("sp", Cn[k2][:], C["dft_c"][:, nb * 512:(nb + 1) * 512].rearrange("(t p) n -> p t n", p=128), writes=[tbB[k2]])
                kb.dma("sp", Sn[k2][:], C["dft_s"][:, nb * 512:(nb + 1) * 512].rearrange("(t p) n -> p t n", p=128), writes=[tbB[k2]])
                for g in range(4):
                    pa, pb, pc = (n * 3) % 6, (n * 3 + 1) % 6, 6 + n % 2
                    a2 = n % 2
                    n += 1

                    def mmA(tab, p_, g=g):
                        for t in range(NT):
                            ins = nc.tensor.matmul(PS[p_][:], lhsT=Z[:, t, g * 128:(g + 1) * 128], rhs=tab[:, t, :], start=(t == 0), stop=(t == NT - 1))
                        return ins
                    kb.op("pe", lambda: mmA(Cn[k2], pa), reads=[zB, tbB[k2]], writes=[PSB[pa]])
                    kb.op("pe", lambda: mmA(Sn[k2], pb), reads=[zB, tbB[k2]], writes=[PSB[pb]])
                    kb.op("act", lambda: nc.scalar.copy(out=ab[a2][:], in_=PS[pa][:]), reads=[PSB[pa]], writes=[abB[a2]])
                    kb.op("dve", lambda: nc.vector.tensor_copy(out=bb[a2][:], in_=PS[pb][:]), reads=[PSB[pb]], writes=[abB[a2]])

                    def mmB():
                        nc.tensor.matmul(PS[pc][:], lhsT=cc_[:], rhs=ab[a2][:], start=True, stop=False)
                        return nc.tensor.matmul(PS[pc][:], lhsT=csn[:], rhs=bb[a2][:], start=False, stop=True)
                    kb.op("pe", mmB, reads=[zB, abB[a2]], writes=[PSB[pc]])
                    evac(mixT[:, 12 + g, nb * 512:(nb + 1) * 512], PS[pc][:], [PSB[pc]], [mB])

        def stage_inproj_cd(st, hT, hB):
            w_in = I["cd_w_in"][0]
            wb = [sb(st, f"cdw{i}", [128, NC_, 512], BF16) for i in range(4)]
            wB = kb.bufs(4)
            lng = sb(st, "sglng", [128, 1024], F32)
            lnb = sb(st, "sglnb", [128, 1024], F32)
            lB = kb.buf()
            kb.dma("sp", lng[:], I["sg_ln_g"][0:1, :].partition_broadcast(128), writes=[lB])
            kb.dma("sp", lnb[:], I["sg_ln_b"][0:1, :].partition_broadcast(128), writes=[lB])
            stg = [sb(st, f"cdstg{i}", [128, 512], BF16) for i in range(3)]
            sgB = kb.bufs(3)
            zt = [sb(st, f"cdz{i}", [128, 512], F32) for i in range(2)]
            zB = kb.bufs(2)
            stt = sb(st, "cdstt", [128, 4, 6], F32)
            mv4 = sb(st, "cdmv4", [128, 4, 2], F32)
            rs4 = sb(st, "cdrs4", [128, 4], F32)
            smB = kb.buf()
            nw = 0
            n = 0
            ns = 0

            def loadw(jb):
                nonlocal nw
                k = nw % 4
                nw += 1
                kb.dma("pool", wb[k][:], w_in[:, jb * 512:(jb + 1) * 512].rearrange("(c p) n -> p c n", p=128), writes=[wB[k]])
                return k
            for jb in range(2):
                k = loadw(jb)
                for fc in range(4):
                    for tb in range(4):
                        pa = n % 8
                        n += 1

                        def mm(k=k, fc=fc, tb=tb, pa=pa):
                            for cc in range(NC_):
                                ins = nc.tensor.matmul(PS[pa][:], lhsT=wb[k][:, cc, fc * 128:(fc + 1) * 128], rhs=hT[:, cc, tb * 512:(tb + 1) * 512],
                                                       start=(cc == 0), stop=(cc == NC_ - 1))
                            return ins
                        kb.op("pe", mm, reads=[hB, wB[k]], writes=[PSB[pa]])
                        s_ = ns % 3
                        ns += 1
                        kb.op("act", lambda: nc.scalar.activation(out=stg[s_][:], in_=PS[pa][:], func=AF.Gelu_apprx_tanh), reads=[PSB[pa]], writes=[sgB[s_]])
                        kb.dma("sp", uT_d[jb * 4 + fc, :, tb * 512:(tb + 1) * 512], stg[s_][:], reads=[sgB[s_]], writes=[B_["uT"]])
            for jb in range(2, 4):
                k = loadw(jb)
                for t in range(NT):
                    pa = n % 8
                    n += 1

                    def mmv(k=k, t=t, pa=pa):
                        for cc in range(NC_):
                            ins = nc.tensor.matmul(PS[pa][:], lhsT=hT[:, cc, t * 128:(t + 1) * 128], rhs=wb[k][:, cc, :], start=(cc == 0), stop=(cc == NC_ - 1))
                        return ins
                    kb.op("pe", mmv, reads=[hB, wB[k]], writes=[PSB[pa]])
                    z_ = zt[t % 2]
                    zb_ = zB[t % 2]
                    kb.op("act", lambda: nc.scalar.activation(out=z_[:], in_=PS[pa][:], func=AF.Gelu_apprx_tanh), reads=[PSB[pa]], writes=[zb_])
                    for g in range(4):
                        kb.op("dve", lambda g=g: nc.vector.bn_stats(out=stt[:, g, :], in_=z_[:, g * 128:(g + 1) * 128]), reads=[zb_], writes=[smB])
                    for g in range(4):
                        kb.op("dve", lambda g=g: nc.vector.bn_aggr(out=mv4[:, g, :], in_=stt[:, g, :]), reads=[smB], writes=[smB])
                    kb.op("act", lambda: nc.scalar.activation(out=rs4[:], in_=mv4[:, :, 1], func=AF.Sqrt, bias=EPS, scale=1.0), reads=[smB], writes=[smB])
                    kb.op("dve", lambda: nc.vector.reciprocal(out=rs4[:], in_=rs4[:]), reads=[smB], writes=[smB])
                    zv = z_[:].rearrange("p (g c) -> p g c", g=4)
                    kb.op("dve", lambda: nc.vector.tensor_tensor(out=zv, in0=zv, in1=mv4[:, :, 0].unsqueeze(2).to_broadcast([128, 4, 128]), op=ALU.subtract),
                          reads=[zb_, smB], writes=[zb_])
                    kb.op("dve", lambda: nc.vector.tensor_tensor(out=zv, in0=zv, in1=rs4[:].unsqueeze(2).to_broadcast([128, 4, 128]), op=ALU.mult),
                          reads=[zb_, smB], writes=[zb_])
                    c0 = (jb - 2) * 512
                    kb.op("dve", lambda: nc.vector.tensor_tensor(out=z_[:], in0=z_[:], in1=lng[:, c0:c0 + 512], op=ALU.mult), reads=[zb_, lB], writes=[zb_])
                    s_ = ns % 3
                    ns += 1
                    kb.op("dve", lambda: nc.vector.tensor_tensor(out=stg[s_][:], in0=z_[:], in1=lnb[:, c0:c0 + 512], op=ALU.add), reads=[zb_, lB], writes=[sgB[s_]])
                    kb.dma("sp", vg_d[t * 128:(t + 1) * 128, c0:c0 + 512], stg[s_][:], reads=[sgB[s_]], writes=[B_["vg"]])
            for jb in range(2):
                ka = loadw(4 + jb)
                kg = loadw(6 + jb)
                for fc in range(4):
                    for tb in range(4):
                        pa, pg = (n * 2) % 8, (n * 2 + 1) % 8
                        n += 1

                        def mm(k, p_, fc=fc, tb=tb):
                            for cc in range(NC_):
                                ins = nc.tensor.matmul(PS[p_][:], lhsT=wb[k][:, cc, fc * 128:(fc + 1) * 128], rhs=hT[:, cc, tb * 512:(tb + 1) * 512],
                                                       start=(cc == 0), stop=(cc == NC_ - 1))
                            return ins
                        kb.op("pe", lambda: mm(ka, pa), reads=[hB, wB[ka]], writes=[PSB[pa]])
                        kb.op("pe", lambda: mm(kg, pg), reads=[hB, wB[kg]], writes=[PSB[pg]])
                        z_ = zt[n % 2]
                        zb_ = zB[n % 2]
                        kb.op("act", lambda: nc.scalar.activation(out=z_[:], in_=PS[pg][:], func=AF.Sigmoid), reads=[PSB[pg]], writes=[zb_])
                        s_ = ns % 3
                        ns += 1
                        kb.op("dve", lambda: nc.vector.tensor_tensor(out=stg[s_][:], in0=PS[pa][:], in1=z_[:], op=ALU.mult), reads=[PSB[pa], zb_], writes=[sgB[s_]])
                        kb.dma("sp", xgT_d[jb * 4 + fc, :, tb * 512:(tb + 1) * 512], stg[s_][:], reads=[sgB[s_]], writes=[B_["xgT"]])

        def stage_sg(st, mixT, mB):
            swn = sb(st, "swn", [128, 8, 128], F32)
            swT = sb(st, "swT", [128, 8, 128], BF16)
            sgb = sb(st, "sgb", [128, 1024], F32)
            wB_ = kb.buf()
            kb.dma("sp", swn[:], I["sg_w"].rearrange("g p q -> p g q"), writes=[wB_])
            kb.dma("sp", sgb[:], I["sg_b"][0:1, :].partition_broadcast(128), writes=[wB_])
            for g4 in range(2):
                def tr(g4=g4):
                    for k in range(4):
                        g = g4 * 4 + k
                        ins = nc.tensor.transpose(PS[g4][:, k * 128:(k + 1) * 128], swn[:, g, :], ident[:])
                    return ins
                kb.op("pe", tr, reads=[wB_, cB], writes=[PSB[g4]])
                kb.op("dve", lambda g4=g4: nc.vector.tensor_copy(out=swT[:, g4 * 4:(g4 + 1) * 4, :], in_=PS[g4][:].rearrange("p (k n) -> p k n", k=4)),
                      reads=[PSB[g4]], writes=[wB_])
            vg = sb(st, "sgvg", [128, NT, 1024], BF16)
            uT = sb(st, "sguT", [128, 8, SEQ], BF16)
            dB = kb.buf()
            kb.dma("sp", vg[:], vg_d.rearrange("(t p) c -> p t c", p=128), reads=[B_["vg"]], writes=[dB])
            kb.dma("sp", uT[:], uT_d.rearrange("g p n -> p g n"), reads=[B_["uT"]], writes=[dB])
            tmp = [sb(st, f"sgtmp{i}", [128, 512], F32) for i in range(2)]
            tB = kb.bufs(2)
            n = 0
            for g in range(8):
                for n4 in range(4):
                    pb = n % 8
                    t2 = n % 2
                    n += 1

                    def mm(g=g, n4=n4, pb=pb):
                        for k in range(4):
                            ins = nc.tensor.matmul(PS[pb][:, k * 128:(k + 1) * 128], lhsT=vg[:, n4 * 4 + k, g * 128:(g + 1) * 128], rhs=swT[:, g, :],
                                                   start=True, stop=True)
                        return ins
                    kb.op("pe", mm, reads=[dB, wB_], writes=[PSB[pb]])
                    kb.op("dve", lambda: nc.vector.tensor_tensor(out=tmp[t2][:].rearrange("p (k n) -> p k n", k=4), in0=PS[pb][:].rearrange("p (k n) -> p k n", k=4),
                                                                 in1=sgb[:, g * 128:(g + 1) * 128].unsqueeze(1).to_broadcast([128, 4, 128]), op=ALU.add),
                          reads=[PSB[pb], wB_], writes=[tB[t2]])
                    kb.op("dve", lambda: nc.vector.tensor_tensor(out=mixT[:, g, n4 * 512:(n4 + 1) * 512], in0=tmp[t2][:], in1=uT[:, g, n4 * 512:(n4 + 1) * 512], op=ALU.mult),
                          reads=[tB[t2], dB], writes=[mB])

        def stage_conv(st, mixT, mB):
            PADW = SEQ + 30
            xg = sb(st, "cvxg", [128, 8, PADW], BF16)
            xB = kb.buf()
            kb.op("dve", lambda: nc.vector.memset(xg[:, :, 0:15], 0.0), writes=[xB])
            kb.op("dve", lambda: nc.vector.memset(xg[:, :, 15 + SEQ:PADW], 0.0), writes=[xB])
            kb.dma("sp", xg[:, :, 15:15 + SEQ], xgT_d.rearrange("g p n -> p g n"), reads=[B_["xgT"]], writes=[xB])
            cwn = sb(st, "cvwn", [31, 1024], F32)
            cw = sb(st, "cvw", [128, 8, 32], F32)
            cb = sb(st, "cvb", [128, 8], F32)
            lg = sb(st, "cvlg", [128, 8], F32)
            lb = sb(st, "cvlb", [128, 8], F32)
            onesf = sb(st, "cvones", [128, 128], F32)
            pB = kb.buf()
            kb.dma("sp", cwn[:], I["conv_w"], writes=[pB])
            kb.dma("sp", onesf[:], C["onesf"], writes=[pB])
            kb.dma("sp", cb[:], I["conv_b"][0].rearrange("(g p) -> p g", p=128), writes=[pB], allow_slow_non_contiguous=True)
            kb.dma("sp", lg[:], I["conv_ln_g"][0].rearrange("(g p) -> p g", p=128), writes=[pB], allow_slow_non_contiguous=True)
            kb.dma("sp", lb[:], I["conv_ln_b"][0].rearrange("(g p) -> p g", p=128), writes=[pB], allow_slow_non_contiguous=True)

            def trw():
                for g in range(8):
                    ins = nc.tensor.transpose(PS[0][:, g * 32:g * 32 + 31], cwn[0:31, g * 128:(g + 1) * 128], ident[0:31, 0:31])
                return ins
            kb.op("pe", trw, reads=[pB, cB], writes=[PSB[0]])
            kb.op("dve", lambda: nc.vector.tensor_copy(out=cw[:, :, 0:31], in_=PS[0][:, 0:256].rearrange("p (g k) -> p g k", g=8)[:, :, 0:31]), reads=[PSB[0]], writes=[pB])
            dg = sb(st, "cvdg", [128, 8, 31, 128], BF16)
            dgB = kb.buf()
            for g in range(8):
                for k in range(31):
                    kb.op("dve", lambda g=g, k=k: nc.vector.tensor_scalar(out=dg[:, g, k, :], in0=ident[:], scalar1=cw[:, g, k:k + 1], scalar2=None, op0=ALU.mult),
                          reads=[pB, cB], writes=[dgB])
            xc = sb(st, "cvxc", [128, 8, 512], F32)
            xcB = kb.buf()
            sq = [sb(st, f"cvsq{i}", [128, 512], F32) for i in range(2)]
            sqB = kb.bufs(2)
            mean = sb(st, "cvmean", [128, 512], F32)
            rstd = sb(st, "cvrstd", [128, 512], F32)
            stB = kb.buf()
            tmp = [sb(st, f"cvtmp{i}", [128, 512], F32) for i in range(2)]
            tB = kb.bufs(2)
            n = 0
            for tb in range(4):
                for g in range(8):
                    pb = n % 4
                    n += 1

                    def mmc(g=g, pb=pb):
                        for k in range(31):
                            ins = nc.tensor.matmul(PS[pb][:], lhsT=dg[:, g, k, :], rhs=xg[:, g, tb * 512 + k: tb * 512 + k + 512], start=(k == 0), stop=(k == 30))
                        return ins
                    kb.op("pe", mmc, reads=[dgB, xB], writes=[PSB[pb]])
                    kb.op("act", lambda g=g, pb=pb: nc.scalar.activation(out=xc[:, g, :], in_=PS[pb][:], func=AF.Identity, bias=cb[:, g:g + 1], scale=1.0),
                          reads=[PSB[pb], pB], writes=[xcB])
                for g in range(8):
                    s2 = g % 2
                    kb.op("act", lambda g=g, s2=s2: nc.scalar.activation(out=sq[s2][:], in_=xc[:, g, :], func=AF.Square), reads=[xcB], writes=[sqB[s2]])

                    def mms(g=g, s2=s2):
                        nc.tensor.matmul(PS[4][:], lhsT=onesf[:], rhs=xc[:, g, :], start=(g == 0), stop=(g == 7))
                        return nc.tensor.matmul(PS[5][:], lhsT=onesf[:], rhs=sq[s2][:], start=(g == 0), stop=(g == 7))
                    kb.op("pe", mms, reads=[xcB, sqB[s2], pB], writes=[PSB[4], PSB[5]])
                kb.op("act", lambda: nc.scalar.activation(out=mean[:], in_=PS[4][:], func=AF.Copy, scale=1.0 / 1024), reads=[PSB[4]], writes=[stB])
                kb.op("dve", lambda: nc.vector.tensor_tensor(out=rstd[:], in0=mean[:], in1=mean[:], op=ALU.mult), reads=[stB], writes=[stB])
                kb.op("dve", lambda: nc.vector.scalar_tensor_tensor(out=rstd[:], in0=PS[5][:], scalar=1.0 / 1024, in1=rstd[:], op0=ALU.mult, op1=ALU.subtract),
                      reads=[PSB[5], stB], writes=[stB])
                kb.op("act", lambda: nc.scalar.activation(out=rstd[:], in_=rstd[:], func=AF.Sqrt, bias=EPS, scale=1.0), reads=[stB], writes=[stB])
                kb.op("dve", lambda: nc.vector.reciprocal(out=rstd[:], in_=rstd[:]), reads=[stB], writes=[stB])
                for g in range(8):
                    t2 = g % 2
                    kb.op("dve", lambda g=g, t2=t2: nc.vector.tensor_tensor(out=tmp[t2][:], in0=xc[:, g, :], in1=mean[:], op=ALU.subtract), reads=[xcB, stB], writes=[tB[t2]])
                    kb.op("dve", lambda g=g, t2=t2: nc.vector.tensor_tensor(out=tmp[t2][:], in0=tmp[t2][:], in1=rstd[:], op=ALU.mult), reads=[tB[t2], stB], writes=[tB[t2]])
                    kb.op("act", lambda g=g, t2=t2: nc.scalar.activation(out=mixT[:, 8 + g, tb * 512:(tb + 1) * 512], in_=tmp[t2][:], func=AF.Silu,
                                                                       bias=lb[:, g:g + 1], scale=lg[:, g:g + 1]), reads=[tB[t2], pB], writes=[mB])

        def stage_post(st, l, xsrc_d, xsrcB, aff, affB):
            bt = {}
            bcB = kb.buf()
            for nm, row, p1 in (("g1", mod_d[l, 0:1, 2 * D:3 * D], False), ("sc2", mod_d[l, 0:1, 4 * D:5 * D], True),
                                ("sh2", mod_d[l, 0:1, 3 * D:4 * D], False), ("lg", I["ln1_g"][l:l + 1, :], False),
                                ("lb", I["ln1_b"][l:l + 1, :], False)):
                bt[nm] = sb(st, "pb_" + nm, [128, D], F32)
                load_bc(bt[nm], row, bcB, plus_one=p1)
            wr = sb(st, "wr", [128, NC_, NE], F32)
            kb.dma("sp", wr[:], I["w_router"][l].rearrange("(c p) e -> p c e", p=128), writes=[bcB])
            ND_ = 2
            xt = [sb(st, f"pxt{i}", [128, D], F32) for i in range(ND_)]
            yt = [sb(st, f"pyt{i}", [128, D], F32) for i in range(ND_)]
            xB = kb.bufs(ND_)
            yB = kb.bufs(ND_)
            hb = [sb(st, f"phb{i}", [128, D], BF16) for i in range(ND_)]
            hbB = kb.bufs(ND_)
            hT = [sb(st, f"phT{i}", [128, NC_, 128], F32) for i in range(ND_)]
            hTB = kb.bufs(ND_)
            stt = [sb(st, f"pstt{i}", [128, 4, 6], F32) for i in range(ND_)]
            mv = [sb(st, f"pmv{i}", [128, 2], F32) for i in range(ND_)]
            rstd = [sb(st, f"prstd{i}", [128, 1], F32) for i in range(ND_)]
            nmr = [sb(st, f"pnmr{i}", [128, 1], F32) for i in range(ND_)]
            smB = kb.bufs(ND_)
            lg_ = [sb(st, f"plg{i}", [128, NE], F32) for i in range(ND_)]
            mx = [sb(st, f"pmx{i}", [128, 1], F32) for i in range(ND_)]
            sm = [sb(st, f"psm{i}", [128, 1], F32) for i in range(ND_)]
            lB = kb.bufs(ND_)
            affBs = []

            def tile(t):
                p = t % ND_
                x_, y_, xb_, yb_ = xt[p], yt[p], xB[p], yB[p]
                kb.dma("sp", x_[:], xsrc_d[t * 128:(t + 1) * 128, :], reads=[xsrcB], writes=[xb_])
                kb.dma("sp", y_[:], y_d[t * 128:(t + 1) * 128, :], reads=[B_["y"]], writes=[yb_])
                yield
                kb.op("pool", lambda: nc.gpsimd.tensor_tensor(out=y_[:], in0=y_[:], in1=bt["g1"][:], op=ALU.mult), reads=[yb_, bcB], writes=[yb_])
                yield
                kb.op("dve", lambda: nc.vector.scalar_tensor_tensor(out=x_[:], in0=x_[:], scalar=float(ALPHA), in1=y_[:], op0=ALU.mult, op1=ALU.add),
                      reads=[xb_, yb_], writes=[xb_])
                yield from ln_stats_g(stt[p], mv[p], rstd[p], nmr[p], x_, [xb_], smB[p])
                kb.op("act", lambda: nc.scalar.activation(out=x_[:], in_=x_[:], func=AF.Identity, bias=nmr[p][:], scale=rstd[p][:]), reads=[smB[p], xb_], writes=[xb_])
                yield
                kb.op("dve", lambda: nc.vector.tensor_tensor(out=x_[:], in0=x_[:], in1=bt["lg"][:], op=ALU.mult), reads=[xb_, bcB], writes=[xb_])
                kb.op("pool", lambda: nc.gpsimd.tensor_tensor(out=x_[:], in0=x_[:], in1=bt["lb"][:], op=ALU.add), reads=[xb_, bcB], writes=[xb_])
                yield
                kb.dma("sp", x1_d[t * 128:(t + 1) * 128, :], x_[:], reads=[xb_], writes=[B_["x1"]])
                yield from ln_stats_g(stt[p], mv[p], rstd[p], nmr[p], x_, [xb_], smB[p])
                kb.op("act", lambda: nc.scalar.activation(out=y_[:], in_=x_[:], func=AF.Identity, bias=nmr[p][:], scale=rstd[p][:]), reads=[smB[p], xb_], writes=[yb_])
                yield
                kb.op("dve", lambda: nc.vector.tensor_tensor(out=y_[:], in0=y_[:], in1=bt["sc2"][:], op=ALU.mult), reads=[yb_, bcB], writes=[yb_])
                kb.op("pool", lambda: nc.gpsimd.tensor_tensor(out=y_[:], in0=y_[:], in1=bt["sh2"][:], op=ALU.add), reads=[yb_, bcB], writes=[yb_])
                yield
                h_ = hb[p]
                kb.op("act", lambda: nc.scalar.copy(out=h_[:], in_=y_[:]), reads=[yb_], writes=[hbB[p]])
                kb.dma("sp", h2_d[t * 128:(t + 1) * 128, :], h_[:], reads=[hbB[p]], writes=[B_["h2"]])
                pp = t % 2
                for q4 in range(4):
                    pb = pp * 4 + q4

                    def tr():
                        for k in range(4):
                            cc = q4 * 4 + k
                            ins = nc.tensor.transpose(PS[pb][:, k * 128:(k + 1) * 128], y_[:, cc * 128:(cc + 1) * 128], ident[:])
                        return ins
                    kb.op("pe", tr, reads=[yb_, cB], writes=[PSB[pb]])
                yield
                for q4 in range(4):
                    pb = pp * 4 + q4
                    evac(hT[p][:, q4 * 4:(q4 + 1) * 4, :], PS[pb][:].rearrange("p (k n) -> p k n", k=4), [PSB[pb]], [hTB[p]])
                pr = pp * 4

                def mmr():
                    for cc in range(NC_):
                        ins = nc.tensor.matmul(PS[pr][:, 0:NE], lhsT=hT[p][:, cc, :], rhs=wr[:, cc, :], start=(cc == 0), stop=(cc == NC_ - 1))
                    return ins
                kb.op("pe", mmr, reads=[hTB[p], bcB], writes=[PSB[pr]])
                yield
                kb.op("dve", lambda: nc.vector.tensor_copy(out=lg_[p][:], in_=PS[pr][:, 0:NE]), reads=[PSB[pr]], writes=[lB[p]])
                kb.op("dve", lambda: nc.vector.reduce_max(out=mx[p][:], in_=lg_[p][:], axis=AX.X), reads=[lB[p]], writes=[lB[p]])
                kb.op("dve", lambda: nc.vector.tensor_scalar(out=mx[p][:], in0=mx[p][:], scalar1=-1.0, scalar2=None, op0=ALU.mult), reads=[lB[p]], writes=[lB[p]])
                kb.op("act", lambda: nc.scalar.activation(out=lg_[p][:], in_=lg_[p][:], func=AF.Exp, bias=mx[p][:], scale=1.0), reads=[lB[p]], writes=[lB[p]])
                yield
                kb.op("dve", lambda: nc.vector.reduce_sum(out=sm[p][:], in_=lg_[p][:], axis=AX.X), reads=[lB[p]], writes=[lB[p]])
                kb.op("dve", lambda: nc.vector.reciprocal(out=sm[p][:], in_=sm[p][:]), reads=[lB[p]], writes=[lB[p]])
                ab_ = kb.buf()
                affBs.append(ab_)
                kb.op("dve", lambda: nc.vector.tensor_scalar(out=aff[:, t, :], in0=lg_[p][:], scalar1=sm[p][:], scalar2=None, op0=ALU.mult), reads=[lB[p]], writes=[ab_])
            interleave([tile(t) for t in range(NT)], depth=2)

        def stage_moe(st, l, aff, affB, idxf_all, idxB, nexp=NE):
            mask = sb(st, "mmask", [128, NT, NE], F32)
            maskb = sb(st, "mmaskb", [128, NT, NE], BF16)
            key = sb(st, "mkey", [128, NT, NE], F32)
            tri = sb(st, "mtri", [128, 128], BF16)
            R = sb(st, "mR", [128, NT, NE, 8], BF16)
            r1 = sb(st, "mr1", [128, NT, NE], F32)
            pcol = sb(st, "mpcol", [128, 1], F32)
            tcol = sb(st, "mtcol", [128, NT, NE], F32)
            rt = ExitStack()
            affT = sb(rt, "affT", [NE, SEQ], F32)
            work = sb(rt, "mwork", [NE, SEQ], F32)
            m8 = sb(rt, "m8", [NE, 8], F32)
            aTB = kb.buf()
            for t4 in range(4):
                def tr(t4=t4):
                    for k in range(4):
                        t = t4 * 4 + k
                        ins = nc.tensor.transpose(PS[t4][0:NE, k * 128:(k + 1) * 128], aff[:, t, :], ident[:])
                    return ins
                kb.op("pe", tr, reads=[affB, cB], writes=[PSB[t4]])
                kb.op("dve", lambda t4=t4: nc.vector.tensor_copy(out=affT[:, t4 * 512:(t4 + 1) * 512], in_=PS[t4][0:NE, :]), reads=[PSB[t4]], writes=[aTB])
            kb.op("dve", lambda: nc.vector.tensor_copy(out=work[:], in_=affT[:]), reads=[aTB], writes=[aTB])
            for r in range(CAP // 8):
                kb.op("dve", lambda: nc.vector.max(out=m8[:], in_=work[:]), reads=[aTB], writes=[aTB])
                if r < CAP // 8 - 1:
                    kb.op("dve", lambda: nc.vector.match_replace(out=work[:], in_to_replace=m8[:], in_values=work[:], imm_value=-1.0), reads=[aTB], writes=[aTB])
            kb.op("dve", lambda: nc.vector.tensor_scalar(out=work[:], in0=affT[:], scalar1=m8[:, 7:8], scalar2=None, op0=ALU.is_ge), reads=[aTB], writes=[aTB])
            mkB = kb.buf()
            for t4 in range(4):
                def tr2(t4=t4):
                    for k in range(4):
                        t = t4 * 4 + k
                        ins = nc.tensor.transpose(PS[4 + t4][:, k * NE:(k + 1) * NE], work[:, t * 128:(t + 1) * 128], ident[0:NE, 0:NE])
                    return ins
                kb.op("pe", tr2, reads=[aTB, cB], writes=[PSB[4 + t4]])
                kb.op("dve", lambda t4=t4: nc.vector.tensor_copy(out=mask[:, t4 * 4:(t4 + 1) * 4, :],
                                                               in_=PS[4 + t4][:, 0:4 * NE].rearrange("p (k e) -> p k e", k=4)), reads=[PSB[4 + t4]], writes=[mkB])
            kb.op("dve", lambda: nc.vector.tensor_copy(out=maskb[:], in_=mask[:]), reads=[mkB], writes=[mkB])
            kb.dma("sp", tri[:], C["tri"], writes=[mkB])

            def cums():
                for t in range(NT):
                    for i2 in range(t):
                        nc.tensor.matmul(PS[0][:, t * NE:(t + 1) * NE], lhsT=onesb[:], rhs=maskb[:, i2, :], start=(i2 == 0), stop=False)
                    ins = nc.tensor.matmul(PS[0][:, t * NE:(t + 1) * NE], lhsT=tri[:], rhs=maskb[:, t, :], start=(t == 0), stop=True)
                return ins
            kb.op("pe", cums, reads=[mkB, cB], writes=[PSB[0]])
            kb.op("dve", lambda: nc.vector.tensor_tensor(out=key[:].rearrange("p t e -> p (t e)"), in0=PS[0][:, 0:NT * NE],
                                                         in1=mask[:].rearrange("p t e -> p (t e)"), op=ALU.mult), reads=[PSB[0], mkB], writes=[mkB])
            kb.op("dve", lambda: nc.vector.tensor_scalar(out=key[:], in0=key[:], scalar1=-1.0, scalar2=None, op0=ALU.add), reads=[mkB], writes=[mkB])
            kb.dma("sp", pcol[:], C["pcol"], writes=[mkB])
            kb.dma("sp", tcol[:].rearrange("p t e -> p (t e)"), C["tcol"], writes=[mkB])
            kb.op("dve", lambda: nc.vector.memset(R[:], 0.0), writes=[mkB])
            kb.op("dve", lambda: nc.vector.tensor_copy(out=R[:, :, :, 0], in_=pcol[:].unsqueeze(2).to_broadcast([128, NT, NE])), reads=[mkB], writes=[mkB])
            kb.op("dve", lambda: nc.vector.tensor_copy(out=R[:, :, :, 1], in_=tcol[:]), reads=[mkB], writes=[mkB])
            kb.op("dve", lambda: nc.vector.tensor_copy(out=R[:, :, :, 2], in_=aff[:]), reads=[mkB, affB], writes=[mkB])
            kb.op("dve", lambda: nc.vector.tensor_tensor(out=r1[:], in0=aff[:], in1=R[:, :, :, 2], op=ALU.subtract), reads=[mkB, affB], writes=[mkB])
            kb.op("dve", lambda: nc.vector.tensor_copy(out=R[:, :, :, 3], in_=r1[:]), reads=[mkB], writes=[mkB])
            kb.op("dve", lambda: nc.vector.tensor_tensor(out=r1[:], in0=r1[:], in1=R[:, :, :, 3], op=ALU.subtract), reads=[mkB], writes=[mkB])
            kb.op("dve", lambda: nc.vector.tensor_copy(out=R[:, :, :, 4], in_=r1[:]), reads=[mkB], writes=[mkB])
            kb.barrier()
            rt.close()

            io256 = sb(st, "io256", [128, 256], F32)
            kb.dma("sp", io256[:], C["iota256"], writes=[mkB])
            ig_all = sb(st, "mig", [128, 2 * NE, 8], F32)
            idxi = sb(st, "midxi", [128, 2 * NE], I32)
            gg = sb(st, "mgg", [128, 2 * NE], F32)
            igB = kb.buf()
            sx = ExitStack()
            Sel = [sb(sx, f"mSel{i}", [128, NT, 256], BF16) for i in range(2)]
            selB = kb.bufs(2)
            for e in range(nexp):
                e2 = e % 2
                kb.op("dve", lambda: nc.vector.tensor_tensor(out=Sel[e2][:], in0=io256[:].unsqueeze(1).to_broadcast([128, NT, 256]),
                                                             in1=key[:, :, e].unsqueeze(2).to_broadcast([128, NT, 256]), op=ALU.is_equal),
                      reads=[mkB], writes=[selB[e2]])
                pi = 6 + e2

                def mmi():
                    for half in range(2):
                        for t in range(NT):
                            ins = nc.tensor.matmul(PS[pi][:, half * 8:half * 8 + 8], lhsT=Sel[e2][:, t, half * 128:(half + 1) * 128], rhs=R[:, t, e, :],
                                                   start=(t == 0), stop=(t == NT - 1))
                    return ins
                kb.op("pe", mmi, reads=[selB[e2], mkB], writes=[PSB[pi]])
                kb.op("act", lambda: nc.scalar.copy(out=ig_all[:, 2 * e:2 * e + 2, :], in_=PS[pi][:, 0:16].rearrange("p (h k) -> p h k", h=2)), reads=[PSB[pi]], writes=[igB])
            ne2 = 2 * nexp
            kb.op("dve", lambda: nc.vector.scalar_tensor_tensor(out=idxf_all[:, 0:ne2], in0=ig_all[:, 0:ne2, 1], scalar=128.0, in1=ig_all[:, 0:ne2, 0], op0=ALU.mult, op1=ALU.add),
                  reads=[igB], writes=[igB, idxB])
            kb.op("dve", lambda: nc.vector.tensor_copy(out=idxi[:, 0:ne2], in_=idxf_all[:, 0:ne2]), reads=[igB], writes=[igB])
            kb.op("dve", lambda: nc.vector.tensor_tensor(out=gg[:, 0:ne2], in0=ig_all[:, 0:ne2, 2], in1=ig_all[:, 0:ne2, 3], op=ALU.add), reads=[igB], writes=[igB])
            kb.op("dve", lambda: nc.vector.tensor_tensor(out=gg[:, 0:ne2], in0=gg[:, 0:ne2], in1=ig_all[:, 0:ne2, 4], op=ALU.add), reads=[igB], writes=[igB])
            kb.barrier()
            sx.close()

            ex = ExitStack()
            xs = [sb(ex, f"mxs{i}", [128, 2, D], BF16) for i in range(2)]
            xsB = kb.bufs(2)
            xsT = [sb(ex, f"mxsT{i}", [128, NC_, 256], BF16) for i in range(2)]
            xsTB = kb.bufs(2)
            NWB = 3
            wg = [sb(ex, f"mwg{i}", [128, NC_, 512], BF16) for i in range(NWB)]
            wu = [sb(ex, f"mwu{i}", [128, NC_, 512], BF16) for i in range(NWB)]
            wd = [sb(ex, f"mwd{i}", [128, NC_, 512], BF16) for i in range(2)]
            wgB, wuB, wdB = kb.bufs(NWB), kb.bufs(NWB), kb.bufs(2)
            sgt = [sb(ex, f"msg{i}", [128, 256], F32) for i in range(2)]
            sgB = kb.bufs(2)
            hidT = sb(ex, "mhidT", [128, NC_, 256], BF16)
            hidB = kb.buf()
            yeb = [sb(ex, f"myeb{i}", [128, 2, D], BF16) for i in range(2)]
            yeB = kb.bufs(2)
            nw = 0
            nwd = 0
            nsg = 0

            def gather(e):
                for half in range(2):
                    kb.dma("pool", xs[e % 2][:, half, :], h2_d, reads=[igB, B_["h2"]], writes=[xsB[e % 2]],
                           indirect=dict(out_offset=None, in_offset=bass.IndirectOffsetOnAxis(ap=idxi[:, 2 * e + half:2 * e + half + 1], axis=0)))
            gather(0)
            for e in range(nexp):
                e2 = e % 2
                if e + 1 < nexp:
                    gather(e + 1)
                for half in range(2):
                    for cg in range(2):
                        pb = half * 2 + cg
                        psv = PS[pb][:].bitcast(BF16)

                        def trx(half=half, cg=cg, psv=psv):
                            for k in range(8):
                                cc = cg * 8 + k
                                ins = nc.tensor.transpose(psv[:, k * 128:(k + 1) * 128], xs[e2][:, half, cc * 128:(cc + 1) * 128], identb[:])
                            return ins
                        kb.op("pe", trx, reads=[xsB[e2], cB], writes=[PSB[pb]])
                        evac(xsT[e2][:, cg * 8:(cg + 1) * 8, half * 128:(half + 1) * 128], psv.rearrange("p (k n) -> p k n", k=8), [PSB[pb]], [xsTB[e2]])
                for fb in range(4):
                    w2 = nw % NWB
                    nw += 1
                    kb.dma("pool", wg[w2][:], I["w_gate"][l, e, :, fb * 512:(fb + 1) * 512].rearrange("(c p) n -> p c n", p=128), writes=[wgB[w2]])
                    kb.dma("pool", wu[w2][:], I["w_up"][l, e, :, fb * 512:(fb + 1) * 512].rearrange("(c p) n -> p c n", p=128), writes=[wuB[w2]])
                    for fc in range(4):
                        pg, pu = 4 + (fc % 2) * 2, 5 + (fc % 2) * 2

                        def mmg(wt, p_, fc=fc):
                            for cc in range(NC_):
                                ins = nc.tensor.matmul(PS[p_][:, 0:256], lhsT=wt[:, cc, fc * 128:(fc + 1) * 128], rhs=xsT[e2][:, cc, :],
                                                       start=(cc == 0), stop=(cc == NC_ - 1))
                            return ins
                        kb.op("pe", lambda: mmg(wg[w2], pg), reads=[wgB[w2], xsTB[e2]], writes=[PSB[pg]])
                        kb.op("pe", lambda: mmg(wu[w2], pu), reads=[wuB[w2], xsTB[e2]], writes=[PSB[pu]])
                        s2 = nsg % 2
                        nsg += 1
                        kb.op("act", lambda: nc.scalar.activation(out=sgt[s2][:], in_=PS[pg][:, 0:256], func=AF.Silu), reads=[PSB[pg]], writes=[sgB[s2]])
                        kb.op("dve", lambda: nc.vector.tensor_tensor(out=hidT[:, fb * 4 + fc, :], in0=PS[pu][:, 0:256], in1=sgt[s2][:], op=ALU.mult),
                              reads=[PSB[pu], sgB[s2]], writes=[hidB])
                for db in range(4):
                    w2 = nwd % 2
                    nwd += 1
                    kb.dma("pool", wd[w2][:], I["w_down"][l, e, :, db * 512:(db + 1) * 512].rearrange("(c p) n -> p c n", p=128), writes=[wdB[w2]])
                    for half in range(2):
                        pb = (db * 2 + half) % 4

                        def mmd(half=half, pb=pb):
                            for fc in range(NC_):
                                ins = nc.tensor.matmul(PS[pb][:], lhsT=hidT[:, fc, half * 128:(half + 1) * 128], rhs=wd[w2][:, fc, :],
                                                       start=(fc == 0), stop=(fc == NC_ - 1))
                            return ins
                        kb.op("pe", mmd, reads=[hidB, wdB[w2]], writes=[PSB[pb]])
                        kb.op("act", lambda half=half, pb=pb: nc.scalar.activation(out=yeb[e2][:, half, db * 512:(db + 1) * 512], in_=PS[pb][:], func=AF.Copy,
                                                                                  scale=gg[:, 2 * e + half:2 * e + half + 1]), reads=[PSB[pb], igB], writes=[yeB[e2]])
                for half in range(2):
                    r0 = (e * 2 + half) * 128
                    kb.dma("sp", ye_d[r0:r0 + 128, :], yeb[e2][:, half, :], reads=[yeB[e2]], writes=[B_["ye"]])
            kb.barrier()
            ex.close()

        def stage_combine(st, l, idxf_all, idxB, dst_d, dstB, nexp=NE):
            nk = 2 * nexp
            YE = sb(st, "cYE", [128, nk, D], BF16)
            YB = kb.buf()
            for db in range(4):
                kb.dma("sp", YE[:, :, db * 512:(db + 1) * 512], ye_d[0:nk * 128, db * 512:(db + 1) * 512].rearrange("(k p) n -> p k n", p=128),
                       reads=[B_["ye"]], writes=[YB])
            io128 = sb(st, "cio", [128, 128], F32)
            bt = {}
            bcB = kb.buf()
            kb.dma("sp", io128[:], C["iota256"][:, 0:128], writes=[bcB])
            for nm, row in (("g2", mod_d[l, 0:1, 5 * D:6 * D]), ("lg", I["ln2_g"][l:l + 1, :]), ("lb", I["ln2_b"][l:l + 1, :])):
                bt[nm] = sb(st, "qb_" + nm, [128, D], F32)
                load_bc(bt[nm], row, bcB)
            sl = [sb(st, f"csl{i}", [128, nk, 128], BF16) for i in range(2)]
            slB2 = kb.bufs(2)
            idt = sb(st, "cidt", [128, 2 * NE], F32)
            idB = kb.buf()
            xt = [sb(st, f"cxt{i}", [128, D], F32) for i in range(2)]
            xB = kb.bufs(2)
            rt_ = sb(st, "crt", [128, D], F32)
            rB = kb.buf()
            stt = sb(st, "qstt", [128, 4, 6], F32)
            mv = sb(st, "qmv", [128, 2], F32)
            rstd = sb(st, "qrstd", [128, 1], F32)
            nmr = sb(st, "qnmr", [128, 1], F32)
            smB = kb.buf()
            toks = []

            def prep(t):
                s2 = t % 2
                kb.dma("sp", xt[s2][:], x1_d[t * 128:(t + 1) * 128, :], reads=[B_["x1"]], writes=[xB[s2]])
                kb.op("dve", lambda: nc.vector.tensor_scalar(out=idt[:, 0:nk], in0=idxf_all[:, 0:nk], scalar1=float(-128 * t), scalar2=None, op0=ALU.add),
                      reads=[idxB], writes=[idB])
                kb.op("dve", lambda: nc.vector.tensor_tensor(out=sl[s2][:], in0=io128[:].unsqueeze(1).to_broadcast([128, nk, 128]),
                                                             in1=idt[:, 0:nk].unsqueeze(2).to_broadcast([128, nk, 128]), op=ALU.is_equal),
                      reads=[idB, bcB], writes=[slB2[s2]])

            def mm(t):
                s2 = t % 2
                for db in range(4):
                    pb = db + 4 * s2

                    def mmc(db=db, pb=pb):
                        for k in range(nk):
                            ins = nc.tensor.matmul(PS[pb][:], lhsT=sl[s2][:, k, :], rhs=YE[:, k, db * 512:(db + 1) * 512], start=(k == 0), stop=(k == nk - 1))
                        return ins
                    kb.op("pe", mmc, reads=[slB2[s2], YB], writes=[PSB[pb]])

            def post(t):
                s2 = t % 2
                x_, xb_ = xt[s2], xB[s2]
                for db in range(4):
                    pb = db + 4 * s2
                    kb.op("dve", lambda db=db, pb=pb: nc.vector.tensor_tensor(out=rt_[:, db * 512:(db + 1) * 512], in0=PS[pb][:], in1=bt["g2"][:, db * 512:(db + 1) * 512], op=ALU.mult),
                          reads=[PSB[pb], bcB], writes=[rB])
                kb.op("dve", lambda: nc.vector.scalar_tensor_tensor(out=x_[:], in0=x_[:], scalar=float(ALPHA), in1=rt_[:], op0=ALU.mult, op1=ALU.add),
                      reads=[xb_, rB], writes=[xb_])
                ln_stats(stt, mv, rstd, nmr, x_, [xb_], smB)
                kb.op("act", lambda: nc.scalar.activation(out=x_[:], in_=x_[:], func=AF.Identity, bias=nmr[:], scale=rstd[:]), reads=[smB, xb_], writes=[xb_])
                kb.op("dve", lambda: nc.vector.tensor_tensor(out=x_[:], in0=x_[:], in1=bt["lg"][:], op=ALU.mult), reads=[xb_, bcB], writes=[xb_])
                kb.op("pool", lambda: nc.gpsimd.tensor_tensor(out=x_[:], in0=x_[:], in1=bt["lb"][:], op=ALU.add), reads=[xb_, bcB], writes=[xb_])
                toks.append(kb.dma("sp", dst_d[t * 128:(t + 1) * 128, :], x_[:], reads=[xb_], writes=[dstB]))
            prep(0)
            for t in range(NT):
                mm(t)
                if t + 1 < NT:
                    prep(t + 1)
                post(t)
            return toks

        def stage_postmoe(st, l, dst_d, dstB):
            bt = {}
            bcB = kb.buf()
            for nm, row in (("g2", mod_d[l, 0:1, 5 * D:6 * D]), ("lg", I["ln2_g"][l:l + 1, :]), ("lb", I["ln2_b"][l:l + 1, :])):
                bt[nm] = sb(st, "qb_" + nm, [128, D], F32)
                load_bc(bt[nm], row, bcB)
            xt = [sb(st, f"qxt{i}", [128, D], F32) for i in range(2)]
            yt = [sb(st, f"qyt{i}", [128, D], F32) for i in range(2)]
            xB, yB = kb.bufs(2), kb.bufs(2)
            stt = sb(st, "qstt", [128, 4, 6], F32)
            mv = sb(st, "qmv", [128, 2], F32)
            rstd = sb(st, "qrstd", [128, 1], F32)
            nmr = sb(st, "qnmr", [128, 1], F32)
            smB = kb.buf()
            toks = []
            for t in range(NT):
                x_, y_, xb_, yb_ = xt[t % 2], yt[t % 2], xB[t % 2], yB[t % 2]
                kb.dma("sp", x_[:], x1_d[t * 128:(t + 1) * 128, :], reads=[B_["x1"]], writes=[xb_])
                kb.dma("sp", y_[:], moe_d[t * 128:(t + 1) * 128, :], reads=[B_["moe"]], writes=[yb_])
                kb.op("dve", lambda y_=y_: nc.vector.tensor_tensor(out=y_[:], in0=y_[:], in1=bt["g2"][:], op=ALU.mult), reads=[yb_, bcB], writes=[yb_])
                kb.op("dve", lambda x_=x_, y_=y_: nc.vector.scalar_tensor_tensor(out=x_[:], in0=x_[:], scalar=float(ALPHA), in1=y_[:], op0=ALU.mult, op1=ALU.add),
                      reads=[xb_, yb_], writes=[xb_])
                ln_stats(stt, mv, rstd, nmr, x_, [xb_], smB)
                kb.op("act", lambda x_=x_: nc.scalar.activation(out=x_[:], in_=x_[:], func=AF.Identity, bias=nmr[:], scale=rstd[:]), reads=[smB, xb_], writes=[xb_])
                kb.op("dve", lambda x_=x_: nc.vector.tensor_tensor(out=x_[:], in0=x_[:], in1=bt["lg"][:], op=ALU.mult), reads=[xb_, bcB], writes=[xb_])
                kb.op("dve", lambda x_=x_: nc.vector.tensor_tensor(out=x_[:], in0=x_[:], in1=bt["lb"][:], op=ALU.add), reads=[xb_, bcB], writes=[xb_])
                toks.append(kb.dma("sp", dst_d[t * 128:(t + 1) * 128, :], x_[:], reads=[xb_], writes=[dstB]))
            return toks

        xB_in = kb.buf("xin")
        stage_mod()
        hT_d = scratch("hT_d", [NC_, 128, SEQ + CTX], BF16)
        if upto >= 1:
            with ExitStack() as st:
                hT = sb(st, "hT0", [128, NC_, SEQ + CTX], BF16)
                hB = kb.buf()
                side = mod_gen(st, [(0, j) for j in range(8, 24)] + [(1, j) for j in range(24)], psb=(6, 7))
                next(side)
                with ExitStack() as s2:
                    stage_pre(s2, I["x"], xB_in, NT, mod_d[0, 0:1, 0:D], mod_d[0, 0:1, D:2 * D], hT, hB, 0, "a", side=side)
                    kb.barrier()
                if upto >= 2:
                    with ExitStack() as s2:
                        stage_pre(s2, I["ctx"], xB_in, CTX // 128, mod_d[0, 1:2, 0:D], mod_d[0, 1:2, D:2 * D], hT, hB, SEQ, "b", side=side)
                        kb.barrier()
                if "hT_d" in dbg:
                    kb.dma("sp", hT_d.rearrange("c p n -> p c n"), hT[:], reads=[hB], writes=[B_["mixT"]])
                if upto >= 3:
                    with ExitStack() as s2:
                        stage_inproj_ab(s2, hT, hB, side)
                        kb.barrier()
                for _ in side:
                    pass
                kb.barrier()
        if upto >= 4:
            with ExitStack() as st:
                mixT = sb(st, "mixT", [128, NC_, SEQ], BF16)
                mB = kb.buf()
                with ExitStack() as s2:
                    stage_attn(s2, mixT, mB)
                    kb.barrier()
                if upto >= 5:
                    with ExitStack() as s2:
                        stage_fourier(s2, mixT, mB)
                        kb.barrier()
                if "mixT_d" in dbg:
                    kb.dma("sp", mixT_d.rearrange("c p n -> p c n"), mixT[:], reads=[mB], writes=[B_["mixT"]])
                if upto >= 6:
                    with ExitStack() as s2:
                        stage_gemm_out(s2, mixT, mB, I["ab_w_out"][0], "go")
                        kb.barrier()
                kb.barrier()

        def moe_block(l, xsrc_d, xsrcB, dst_d, dstB, nexp=NE):
            with ExitStack() as so:
                idxf_all = sb(so, "idxf_all", [128, 2 * NE], F32)
                idxB = kb.buf()
                with ExitStack() as st:
                    aff = sb(st, "aff", [128, NT, NE], F32)
                    affB = kb.buf()
                    with ExitStack() as s2:
                        stage_post(s2, l, xsrc_d, xsrcB, aff, affB)
                        kb.barrier()
                    if "aff_d" in dbg:
                        kb.dma("sp", aff_d, aff[:].rearrange("p t e -> p (t e)"), reads=[affB], writes=[B_["aff"]])
                    with ExitStack() as s2:
                        stage_moe(s2, l, aff, affB, idxf_all, idxB, nexp)
                        kb.barrier()
                    kb.barrier()
                with ExitStack() as s2:
                    toks = stage_combine(s2, l, idxf_all, idxB, dst_d, dstB, nexp)
                    kb.barrier()
                kb.barrier()
            return toks

        out_toks = []
        if upto >= 7:
            moe_block(0, I["x"], xB_in, xl1_d, B_["xl1"], nexp=(NE if upto >= 8 else 1))
        if upto >= 9:
            with ExitStack() as st:
                hT = sb(st, "hT1", [128, NC_, SEQ], BF16)
                hB = kb.buf()
                with ExitStack() as s2:
                    stage_pre(s2, xl1_d, B_["xl1"], NT, mod_d[1, 0:1, 0:D], mod_d[1, 0:1, D:2 * D], hT, hB, 0, "c")
                    kb.barrier()
                with ExitStack() as s2:
                    stage_inproj_cd(s2, hT, hB)
                    kb.barrier()
                kb.barrier()
        if upto >= 10:
            with ExitStack() as st:
                mixT = sb(st, "mixT1", [128, NC_, SEQ], BF16)
                mB = kb.buf()
                with ExitStack() as s2:
                    stage_sg(s2, mixT, mB)
                    kb.barrier()
                with ExitStack() as s2:
                    stage_conv(s2, mixT, mB)
                    kb.barrier()
                if "mixT_d" in dbg:
                    kb.dma("sp", mixT_d.rearrange("c p n -> p c n"), mixT[:], reads=[mB], writes=[B_["mixT"]])
                with ExitStack() as s2:
                    stage_gemm_out(s2, mixT, mB, I["cd_w_out"][0], "go1")
                    kb.barrier()
                kb.barrier()
        if upto >= 11:
            moe_block(1, xl1_d, B_["xl1"], out_d, B_["out"])
        kb.barrier()
    return nc, hc


_CACHE = {}


def make_in_maps(inputs, hc, ncores=8):
    shared = {}
    for k in INPUT_SHAPES:
        if k in ("x", "c", "ctx"):
            continue
        a = np.ascontiguousarray(np.asarray(inputs[k], dtype=np.float32)).reshape(INPUT_SHAPES[k])
        shared[k] = a
    for k, v in hc.items():
        shared["k_" + k] = v
    maps = []
    for b in range(ncores):
        m = dict(shared)
        m["x"] = np.ascontiguousarray(inputs["x"][b], dtype=np.float32)
        m["c"] = np.ascontiguousarray(inputs["c"][b], dtype=np.float32).reshape(1, D)
        m["ctx"] = np.ascontiguousarray(inputs["ctx"][b], dtype=np.float32)
        maps.append(m)
    return maps


def kernel(**inputs):
    if "prog" not in _CACHE:
        _CACHE["prog"] = build_program()
    nc, hc = _CACHE["prog"]
    maps = make_in_maps(inputs, hc, 8)
    res = run_bass_kernel_spmd(nc, maps, core_ids=list(range(8)))
    return np.stack([np.asarray(r["out"], dtype=np.float32) for r in res.results], axis=0)
```

```python
import numpy as np
import ml_dtypes
from contextlib import ExitStack
import concourse.bass as bass
import concourse.mybir as mybir
from concourse.bass_utils import run_bass_kernel_spmd

F32 = mybir.dt.float32
BF16 = mybir.dt.bfloat16
I32 = mybir.dt.int32
ALU = mybir.AluOpType
AF = mybir.ActivationFunctionType
AX = mybir.AxisListType

D = 2048
SEQ = 2048
CTX = 256
NT = SEQ // 128
NC_ = D // 128
ALPHA = 4 ** 0.25
EPS = 1e-6
NE = 16
CAP = 256


class Buf:
    __slots__ = ("name", "last_w", "reads")

    def __init__(self, name=""):
        self.name = name
        self.last_w = None
        self.reads = []


class KB:
    ND = 10

    def __init__(self, nc, es):
        self.nc = nc
        self.engs = {"pe": nc.tensor, "act": nc.scalar, "dve": nc.vector, "pool": nc.gpsimd, "sp": nc.sync}
        self.sem, self.cnt, self.seen = {}, {}, {}
        for e in self.engs:
            self.sem[e] = es.enter_context(nc.semaphore("pg_" + e))
            self.cnt[e] = 0
            self.seen[e] = {}
        self.dsem, self.dcnt, self.drr = {}, {}, {}
        for q in ("sp", "pool"):
            self.dsem[q] = [es.enter_context(nc.semaphore(f"dq_{q}{i}")) for i in range(self.ND)]
            self.dcnt[q] = [0] * self.ND
            self.drr[q] = 0
        self.nbuf = 0

    def buf(self, name=""):
        self.nbuf += 1
        return Buf(name or f"b{self.nbuf}")

    def bufs(self, n, name=""):
        return [self.buf(f"{name}{i}") for i in range(n)]

    def _wait(self, eng, tok):
        sem, val = tok
        key = id(sem)
        if self.seen[eng].get(key, 0) >= val:
            return
        self.engs[eng].wait_ge(sem, val)
        self.seen[eng][key] = val

    def _dep1(self, eng, tok):
        if tok[0] is self.sem[eng] and eng == "pe":
            return
        self._wait(eng, tok)

    def _deps(self, eng, reads, writes):
        for b in reads:
            if b.last_w is not None:
                self._dep1(eng, b.last_w)
        for b in writes:
            if b.last_w is not None:
                self._dep1(eng, b.last_w)
            for t in b.reads:
                self._dep1(eng, t)

    def _upd(self, tok, reads, writes):
        for b in reads:
            b.reads.append(tok)
        for b in writes:
            b.last_w = tok
            b.reads = []

    def op(self, eng, fn, reads=(), writes=()):
        self._deps(eng, reads, writes)
        ins = fn()
        self.cnt[eng] += 1
        ins.then_inc(self.sem[eng], 1)
        tok = (self.sem[eng], self.cnt[eng])
        self._upd(tok, reads, writes)
        return tok

    def dma(self, q, out, in_, reads=(), writes=(), indirect=None, **kw):
        i = self.drr[q]
        self.drr[q] = (i + 1) % self.ND
        sem = self.dsem[q][i]
        if self.dcnt[q][i] > 0:
            self._wait(q, (sem, self.dcnt[q][i]))
        self._deps(q, reads, writes)
        if indirect is not None:
            ins = self.engs[q].indirect_dma_start(out=out, in_=in_, **indirect)
        else:
            ins = self.engs[q].dma_start(out=out, in_=in_, **kw)
        ins.then_inc(sem, 16)
        self.dcnt[q][i] += 16
        tok = (sem, self.dcnt[q][i])
        self._upd(tok, reads, writes)
        return tok

    def barrier(self):
        toks = [(self.sem[e], self.cnt[e]) for e in self.engs if self.cnt[e] > 0]
        for q in self.dsem:
            for i, s in enumerate(self.dsem[q]):
                if self.dcnt[q][i] > 0:
                    toks.append((s, self.dcnt[q][i]))
        for e in self.engs:
            for t in toks:
                if t[0] is self.sem[e]:
                    continue
                self._wait(e, t)


def host_consts():
    c = {}
    c["ident"] = np.eye(128, dtype=np.float32)
    c["identb"] = np.eye(128).astype(ml_dtypes.bfloat16)
    t = np.arange(SEQ)
    r = (t // 64).astype(np.float32)
    col = (t % 64).astype(np.float32)
    inv = (np.float32(10000.0) ** (-np.arange(32, dtype=np.float32) / np.float32(32))).astype(np.float32)
    ang_r = (r[:, None] * inv).astype(np.float32)
    ang_c = (col[:, None] * inv).astype(np.float32)
    cosT = np.zeros((128, SEQ), np.float32)
    sinT = np.zeros((128, SEQ), np.float32)
    for base, ang in ((0, ang_r), (64, ang_c)):
        cs = np.cos(ang).astype(np.float32).T
        sn = np.sin(ang).astype(np.float32).T
        cosT[base:base + 32] = cs
        cosT[base + 32:base + 64] = cs
        sinT[base:base + 32] = -sn
        sinT[base + 32:base + 64] = sn
    c["cosT"] = cosT
    c["sinT"] = sinT
    k = np.arange(SEQ, dtype=np.int64)
    ph = (np.outer(k, k) % SEQ).astype(np.float64) * (2 * np.pi / SEQ)
    c["dft_c"] = np.cos(ph).astype(ml_dtypes.bfloat16)
    c["dft_s"] = np.sin(ph).astype(ml_dtypes.bfloat16)
    kc = np.arange(128, dtype=np.int64)
    phc = (np.outer(kc, kc) % 128).astype(np.float64) * (2 * np.pi / 128)
    c["dftc_c"] = (np.cos(phc) / 512.0).astype(ml_dtypes.bfloat16)
    c["dftc_sn"] = (-np.sin(phc) / 512.0).astype(ml_dtypes.bfloat16)
    kk = np.arange(128)[:, None]
    qq = np.arange(128)[None, :]
    mlo = (qq <= kk).astype(np.float32)
    mhi = (kk <= qq).astype(np.float32)
    c["mask_lo"] = np.tile(mlo, (1, 3)).astype(ml_dtypes.bfloat16)
    c["mask_hi"] = np.tile(mhi, (1, 3)).astype(ml_dtypes.bfloat16)
    c["iota256"] = np.tile(np.arange(256, dtype=np.float32)[None, :], (128, 1))
    c["iota_tok"] = np.tile(np.arange(SEQ, dtype=np.float32)[None, :], (128, 1))
    c["pcol"] = np.arange(128, dtype=np.float32).reshape(128, 1)
    c["tcol"] = np.tile(np.repeat(np.arange(16, dtype=np.float32), 16)[None, :], (128, 1))
    c["tri"] = (np.arange(128)[:, None] <= np.arange(128)[None, :]).astype(ml_dtypes.bfloat16)
    c["onesb"] = np.ones((128, 128), ml_dtypes.bfloat16)
    c["onesf"] = np.ones((128, 128), np.float32)
    return c


CONST_DT = {"identb": BF16, "dft_c": BF16, "dft_s": BF16, "dftc_c": BF16, "dftc_sn": BF16, "mask_lo": BF16,
            "mask_hi": BF16, "tri": BF16, "onesb": BF16}

INPUT_SHAPES = {
    "x": [SEQ, D], "c": [1, D], "ctx": [CTX, D], "c_ctx": [1, D],
    "w_mod": [2, D, 6 * D], "b_mod": [2, 6 * D], "ln1_g": [2, D], "ln1_b": [2, D], "ln2_g": [2, D], "ln2_b": [2, D],
    "w_router": [2, D, NE], "w_gate": [2, NE, D, D], "w_up": [2, NE, D, D], "w_down": [2, NE, D, D],
    "ab_w_in": [1, D, 3072], "ab_w_out": [1, D, D], "sink": [1, 12],
    "cd_w_in": [1, D, 4096], "cd_w_out": [1, D, D], "sg_ln_g": [1, 1024], "sg_ln_b": [1, 1024],
    "sg_w": [8, 128, 128], "sg_b": [1, 1024], "conv_w": [31, 1024], "conv_b": [1, 1024],
    "conv_ln_g": [1, 1024], "conv_ln_b": [1, 1024],
}


def build_program(upto=99, dbg=()):
    nc = bass.Bass("TRN2", target_bir_lowering=False)
    es = ExitStack()
    with es:
        kb = KB(nc, es)
        I = {}
        for name, shp in INPUT_SHAPES.items():
            I[name] = nc.dram_tensor(name, list(shp), F32, kind="ExternalInput").ap()
        hc = host_consts()
        C = {}
        for name, arr in hc.items():
            C[name] = nc.dram_tensor("k_" + name, list(arr.shape), CONST_DT.get(name, F32), kind="ExternalInput").ap()
        out_d = nc.dram_tensor("out", [SEQ, D], F32, kind="ExternalOutput").ap()

        def scratch(name, shape, dt=F32):
            kind = "ExternalOutput" if name in dbg else "Internal"
            return nc.dram_tensor(name, list(shape), dt, kind=kind).ap()

        mod_d = scratch("mod_d", [2, 2, 6 * D])
        qT_d = scratch("qT_d", [12, 128, SEQ], BF16)
        kT_d = scratch("kT_d", [4, 128, SEQ + CTX], BF16)
        v_d = scratch("v_d", [SEQ + CTX, 512], BF16)
        z_d = scratch("z_d", [SEQ, 512], BF16)
        mixT_d = scratch("mixT_d", [16, 128, SEQ], BF16)
        y_d = scratch("y_d", [SEQ, D])
        x1_d = scratch("x1_d", [SEQ, D])
        h2_d = scratch("h2_d", [SEQ, D], BF16)
        ye_d = scratch("ye_d", [2 * NE * 128, D], BF16)
        selT_d = scratch("selT_d", [2 * NE, 128, SEQ], BF16)
        moe_d = scratch("moe_d", [SEQ, D])
        xl1_d = scratch("xl1_d", [SEQ, D])
        uT_d = scratch("uT_d", [8, 128, SEQ], BF16)
        vg_d = scratch("vg_d", [SEQ, 1024], BF16)
        xgT_d = scratch("xgT_d", [8, 128, SEQ], BF16)
        aff_d = scratch("aff_d", [128, 256])
        B_ = {n: kb.buf(n) for n in ("mod", "qT", "kT", "v", "z", "mixT", "y", "x1", "h2", "ye", "selT", "moe", "xl1",
                                     "uT", "vg", "xgT", "aff", "out")}

        sbn = [0]

        def sb(st, name, shape, dt):
            sbn[0] += 1
            return st.enter_context(nc.sbuf_tensor(f"{name}_{sbn[0]}", list(shape), dt))

        PS = [es.enter_context(nc.psum_tensor(f"ps{i}", [128, 512], F32)) for i in range(8)]
        PSB = kb.bufs(8, "ps")
        ident = sb(es, "ident", [128, 128], F32)
        identb = sb(es, "identb", [128, 128], BF16)
        onesb = sb(es, "onesb", [128, 128], BF16)
        cB = kb.buf("consts")
        kb.dma("sp", ident[:], C["ident"], writes=[cB])
        kb.dma("sp", identb[:], C["identb"], writes=[cB])
        kb.dma("sp", onesb[:], C["onesb"], writes=[cB])

        evac_rr = [0]

        def evac(out, in_, reads, writes):
            evac_rr[0] ^= 1
            if evac_rr[0]:
                return kb.op("act", lambda: nc.scalar.copy(out=out, in_=in_), reads=reads, writes=writes)
            return kb.op("dve", lambda: nc.vector.tensor_copy(out=out, in_=in_), reads=reads, writes=writes)

        def ln_stats(st_tile, mv, rstd, nmr, xin, rB, wB):
            for k in range(4):
                kb.op("dve", lambda k=k: nc.vector.bn_stats(out=st_tile[:, k, :], in_=xin[:, k * 512:(k + 1) * 512]),
                      reads=rB, writes=[wB])
            kb.op("dve", lambda: nc.vector.bn_aggr(out=mv[:], in_=st_tile[:].rearrange("p a b -> p (a b)")), reads=[wB], writes=[wB])
            kb.op("act", lambda: nc.scalar.activation(out=rstd[:], in_=mv[:, 1:2], func=AF.Sqrt, bias=EPS, scale=1.0),
                  reads=[wB], writes=[wB])
            kb.op("dve", lambda: nc.vector.reciprocal(out=rstd[:], in_=rstd[:]), reads=[wB], writes=[wB])
            kb.op("dve", lambda: nc.vector.scalar_tensor_tensor(out=nmr[:], in0=mv[:, 0:1], scalar=-1.0, in1=rstd[:],
                                                                 op0=ALU.mult, op1=ALU.mult), reads=[wB], writes=[wB])

        def ln_stats_g(st_tile, mv, rstd, nmr, xin, rB, wB):
            for k in range(4):
                kb.op("dve", lambda k=k: nc.vector.bn_stats(out=st_tile[:, k, :], in_=xin[:, k * 512:(k + 1) * 512]), reads=rB, writes=[wB])
            kb.op("dve", lambda: nc.vector.bn_aggr(out=mv[:], in_=st_tile[:].rearrange("p a b -> p (a b)")), reads=[wB], writes=[wB])
            kb.op("act", lambda: nc.scalar.activation(out=rstd[:], in_=mv[:, 1:2], func=AF.Sqrt, bias=EPS, scale=1.0), reads=[wB], writes=[wB])
            yield
            kb.op("dve", lambda: nc.vector.reciprocal(out=rstd[:], in_=rstd[:]), reads=[wB], writes=[wB])
            kb.op("dve", lambda: nc.vector.scalar_tensor_tensor(out=nmr[:], in0=mv[:, 0:1], scalar=-1.0, in1=rstd[:], op0=ALU.mult, op1=ALU.mult),
                  reads=[wB], writes=[wB])

        def interleave(gens, depth=2, side=None, side_every=3):
            pending = list(gens)
            active = []
            rnd = 0
            while pending or active:
                while len(active) < depth and pending:
                    active.append(pending.pop(0))
                for g in list(active):
                    try:
                        next(g)
                    except StopIteration:
                        active.remove(g)
                rnd += 1
                if side is not None and rnd % side_every == 0:
                    next(side, None)

        def load_bc(tile, row_ap, wB, plus_one=False):
            kb.dma("sp", tile[:], row_ap.partition_broadcast(128), reads=[B_["mod"]], writes=[wB])
            if plus_one:
                kb.op("dve", lambda: nc.vector.tensor_scalar(out=tile[:], in0=tile[:], scalar1=1.0, scalar2=None, op0=ALU.add),
                      reads=[wB], writes=[wB])

        def mod_gen(st, blocks, psb=(0, 1)):
            sT = sb(st, "sT", [128, NC_, 2], F32)
            sTb = sb(st, "sTb", [128, NC_, 2], BF16)
            bm = [sb(st, f"bm{i}", [2, 512], F32) for i in range(2)]
            mo = [sb(st, f"mo{i}", [2, 512], F32) for i in range(2)]
            wb = [sb(st, f"wmod{i}", [128, NC_, 512], BF16) for i in range(2)]
            wB = kb.bufs(2, "wmod")
            sB, bB, moB = kb.buf(), kb.bufs(2), kb.bufs(2)
            kb.dma("sp", sT[:, :, 0], I["c"][0].rearrange("(c p) -> p c", p=128), writes=[sB], allow_slow_non_contiguous=True)
            kb.dma("sp", sT[:, :, 1], I["c_ctx"][0].rearrange("(c p) -> p c", p=128), writes=[sB], allow_slow_non_contiguous=True)
            kb.op("act", lambda: nc.scalar.activation(out=sT[:], in_=sT[:], func=AF.Silu), reads=[sB], writes=[sB])
            kb.op("dve", lambda: nc.vector.tensor_copy(out=sTb[:], in_=sT[:]), reads=[sB], writes=[sB])
            for n, (l, j) in enumerate(blocks):
                k = n % 2
                w = wb[k]
                kb.dma("sp", bm[k][0:1, :], I["b_mod"][l:l + 1, j * 512:(j + 1) * 512], writes=[bB[k]])
                kb.dma("sp", bm[k][1:2, :], I["b_mod"][l:l + 1, j * 512:(j + 1) * 512], writes=[bB[k]])
                kb.dma("pool", w[:], I["w_mod"][l, :, j * 512:(j + 1) * 512].rearrange("(c p) n -> p c n", p=128), writes=[wB[k]])
                yield
                ps = PS[psb[k]]

                def mm():
                    for cc in range(NC_):
                        ins = nc.tensor.matmul(ps[0:2, :], lhsT=sTb[:, cc, :], rhs=w[:, cc, :], start=(cc == 0), stop=(cc == NC_ - 1))
                    return ins
                kb.op("pe", mm, reads=[sB, wB[k]], writes=[PSB[psb[k]]])
                kb.op("dve", lambda: nc.vector.tensor_tensor(out=mo[k][:], in0=ps[0:2, :], in1=bm[k][:], op=ALU.add),
                      reads=[PSB[psb[k]], bB[k]], writes=[moB[k]])
                kb.dma("sp", mod_d[l, :, j * 512:(j + 1) * 512], mo[k][:], reads=[moB[k]], writes=[B_["mod"]])

        def stage_mod():
            with ExitStack() as st:
                for _ in mod_gen(st, [(0, j) for j in range(8)]):
                    pass
                kb.barrier()

        def stage_pre(st, src_d, srcB, ntiles, sh_row, sc_row, hT, hB, col0, tag, side=None):
            bsc = sb(st, tag + "bsc", [128, D], F32)
            bsh = sb(st, tag + "bsh", [128, D], F32)
            bcB = kb.buf()
            load_bc(bsc, sc_row, bcB, plus_one=True)
            load_bc(bsh, sh_row, bcB)
            xt = [sb(st, f"{tag}xt{i}", [128, D], F32) for i in range(2)]
            xB = kb.bufs(2)
            stt = [sb(st, f"{tag}stt{i}", [128, 4, 6], F32) for i in range(2)]
            mv = [sb(st, f"{tag}mv{i}", [128, 2], F32) for i in range(2)]
            rstd = [sb(st, f"{tag}rstd{i}", [128, 1], F32) for i in range(2)]
            nmr = [sb(st, f"{tag}nmr{i}", [128, 1], F32) for i in range(2)]
            smB = kb.bufs(2)

            def tile(t):
                p = t % 2
                x_, xb_ = xt[p], xB[p]
                kb.dma("sp", x_[:], src_d[t * 128:(t + 1) * 128, :], reads=[srcB], writes=[xb_])
                yield
                yield from ln_stats_g(stt[p], mv[p], rstd[p], nmr[p], x_, [xb_], smB[p])
                kb.op("act", lambda: nc.scalar.activation(out=x_[:], in_=x_[:], func=AF.Identity, bias=nmr[p][:], scale=rstd[p][:]),
                      reads=[smB[p], xb_], writes=[xb_])
                yield
                kb.op("dve", lambda: nc.vector.tensor_tensor(out=x_[:], in0=x_[:], in1=bsc[:], op=ALU.mult), reads=[xb_, bcB], writes=[xb_])
                kb.op("pool", lambda: nc.gpsimd.tensor_tensor(out=x_[:], in0=x_[:], in1=bsh[:], op=ALU.add), reads=[xb_, bcB], writes=[xb_])
                yield
                for q4 in range(4):
                    pb = p * 4 + q4

                    def tr():
                        for k in range(4):
                            cc = q4 * 4 + k
                            ins = nc.tensor.transpose(PS[pb][:, k * 128:(k + 1) * 128], x_[:, cc * 128:(cc + 1) * 128], ident[:])
                        return ins
                    kb.op("pe", tr, reads=[xb_, cB], writes=[PSB[pb]])
                hb_t = kb.buf()
                for q4 in range(4):
                    pb = p * 4 + q4
                    evac(hT[:, q4 * 4:(q4 + 1) * 4, col0 + t * 128: col0 + (t + 1) * 128],
                         PS[pb][:].rearrange("p (k n) -> p k n", k=4), [PSB[pb]], [hb_t])
            interleave([tile(t) for t in range(ntiles)], side=side, side_every=3)

        def stage_gemm_out(st, mixT, mB, w_dram, tag):
            wb = [sb(st, f"{tag}w{i}", [128, NC_, 512], BF16) for i in range(2)]
            wB = kb.bufs(2)
            stg = [sb(st, f"{tag}stg{i}", [128, 512], F32) for i in range(3)]
            sgB = kb.bufs(3)
            n = 0
            for jb in range(4):
                kb.dma("pool", wb[jb % 2][:], w_dram[:, jb * 512:(jb + 1) * 512].rearrange("(c p) n -> p c n", p=128), writes=[wB[jb % 2]])
                for t in range(NT):
                    pb = n % 4

                    def mm(jb=jb, t=t, pb=pb):
                        for cc in range(NC_):
                            ins = nc.tensor.matmul(PS[pb][:], lhsT=mixT[:, cc, t * 128:(t + 1) * 128], rhs=wb[jb % 2][:, cc, :],
                                                   start=(cc == 0), stop=(cc == NC_ - 1))
                        return ins
                    kb.op("pe", mm, reads=[mB, wB[jb % 2]], writes=[PSB[pb]])
                    s_ = n % 3
                    evac(stg[s_][:], PS[pb][:], [PSB[pb]], [sgB[s_]])
                    kb.dma("sp", y_d[t * 128:(t + 1) * 128, jb * 512:(jb + 1) * 512], stg[s_][:], reads=[sgB[s_]], writes=[B_["y"]])
                    n += 1

        def stage_inproj_ab(st, hT, hB, side=None):
            stepn = [0]

            def step():
                stepn[0] += 1
                if side is not None and stepn[0] % 3 == 0:
                    next(side, None)
            TT = SEQ + CTX
            w_in = I["ab_w_in"][0]
            cosT = sb(st, "cosT", [128, SEQ], F32)
            sinT = sb(st, "sinT", [128, SEQ], F32)
            rB = kb.buf()
            kb.dma("sp", cosT[:], C["cosT"], writes=[rB])
            kb.dma("sp", sinT[:], C["sinT"], writes=[rB])
            wb = [sb(st, f"abw{i}", [128, NC_, 512], BF16) for i in range(2)]
            ws = sb(st, "abws", [128, NC_, 512], BF16)
            wB = kb.bufs(2)
            wsB = kb.buf()
            t1 = [sb(st, f"rt1_{i}", [128, 512], F32) for i in range(2)]
            t2 = [sb(st, f"rt2_{i}", [128, 512], F32) for i in range(2)]
            tB = kb.bufs(2)
            stg = [sb(st, f"abstg{i}", [128, 512], BF16) for i in range(3)]
            sgB = kb.bufs(3)
            n = 0
            ns = 0
            for jb in range(6):
                w = wb[jb % 2]
                kb.dma("pool", w[:], w_in[:, jb * 512:(jb + 1) * 512].rearrange("(c p) n -> p c n", p=128), writes=[wB[jb % 2]])
                if jb < 4:
                    wv = w[:].rearrange("p c (g t j) -> p (c g) t j", t=2, j=32)
                    sv = ws[:].rearrange("p c (g t j) -> p (c g) t j", t=2, j=32)
                    kb.op("act", lambda wv=wv, sv=sv: nc.scalar.copy(out=sv[:, :, 0, :], in_=wv[:, :, 1, :]), reads=[wB[jb % 2]], writes=[wsB])
                    kb.op("dve", lambda wv=wv, sv=sv: nc.vector.tensor_copy(out=sv[:, :, 1, :], in_=wv[:, :, 0, :]), reads=[wB[jb % 2]], writes=[wsB])
                    for hh in range(4):
                        head = jb * 4 + hh
                        isk = head >= 12
                        for tb in range(4):
                            pa, pb = (n * 2) % 8, (n * 2 + 1) % 8
                            n += 1

                            def mm(wt, p_, hh=hh, tb=tb):
                                for cc in range(NC_):
                                    ins = nc.tensor.matmul(PS[p_][:], lhsT=wt[:, cc, hh * 128:(hh + 1) * 128],
                                                           rhs=hT[:, cc, tb * 512:(tb + 1) * 512], start=(cc == 0), stop=(cc == NC_ - 1))
                                return ins
                            kb.op("pe", lambda: mm(w, pa), reads=[hB, wB[jb % 2]], writes=[PSB[pa]])
                            kb.op("pe", lambda: mm(ws, pb), reads=[hB, wsB], writes=[PSB[pb]])
                            k2 = ns % 2
                            s_ = ns % 3
                            ns += 1
                            kb.op("dve", lambda: nc.vector.tensor_tensor(out=t1[k2][:], in0=PS[pa][:], in1=cosT[:, tb * 512:(tb + 1) * 512], op=ALU.mult),
                                  reads=[PSB[pa], rB], writes=[tB[k2]])
                            kb.op("dve", lambda: nc.vector.tensor_tensor(out=t2[k2][:], in0=PS[pb][:], in1=sinT[:, tb * 512:(tb + 1) * 512], op=ALU.mult),
                                  reads=[PSB[pb], rB], writes=[tB[k2]])
                            kb.op("dve", lambda: nc.vector.tensor_tensor(out=stg[s_][:], in0=t1[k2][:], in1=t2[k2][:], op=ALU.add),
                                  reads=[tB[k2]], writes=[sgB[s_]])
                            if isk:
                                kb.dma("sp", kT_d[head - 12, :, tb * 512:(tb + 1) * 512], stg[s_][:], reads=[sgB[s_]], writes=[B_["kT"]])
                            else:
                                kb.dma("sp", qT_d[head, :, tb * 512:(tb + 1) * 512], stg[s_][:], reads=[sgB[s_]], writes=[B_["qT"]])
                            step()
                        if isk:
                            pa = (n * 2) % 8
                            n += 1

                            def mmc(hh=hh, pa=pa):
                                for cc in range(NC_):
                                    ins = nc.tensor.matmul(PS[pa][:, 0:CTX], lhsT=w[:, cc, hh * 128:(hh + 1) * 128],
                                                           rhs=hT[:, cc, SEQ:TT], start=(cc == 0), stop=(cc == NC_ - 1))
                                return ins
                            kb.op("pe", mmc, reads=[hB, wB[jb % 2]], writes=[PSB[pa]])
                            s_ = ns % 3
                            ns += 1
                            evac(stg[s_][:, 0:CTX], PS[pa][:, 0:CTX], [PSB[pa]], [sgB[s_]])
                            kb.dma("sp", kT_d[head - 12, :, SEQ:TT], stg[s_][:, 0:CTX], reads=[sgB[s_]], writes=[B_["kT"]])
                else:
                    ntile = TT // 128 if jb == 4 else NT
                    for t in range(ntile):
                        pa = (n * 2) % 8
                        n += 1

                        def mmv(t=t, pa=pa):
                            for cc in range(NC_):
                                ins = nc.tensor.matmul(PS[pa][:], lhsT=hT[:, cc, t * 128:(t + 1) * 128], rhs=w[:, cc, :],
                                                       start=(cc == 0), stop=(cc == NC_ - 1))
                            return ins
                        kb.op("pe", mmv, reads=[hB, wB[jb % 2]], writes=[PSB[pa]])
                        s_ = ns % 3
                        ns += 1
                        evac(stg[s_][:], PS[pa][:], [PSB[pa]], [sgB[s_]])
                        if jb == 4:
                            kb.dma("sp", v_d[t * 128:(t + 1) * 128, :], stg[s_][:], reads=[sgB[s_]], writes=[B_["v"]])
                        else:
                            kb.dma("sp", z_d[t * 128:(t + 1) * 128, :], stg[s_][:], reads=[sgB[s_]], writes=[B_["z"]])

            if side is not None:
                for _ in side:
                    pass

        def stage_attn(st, mixT, mB):
            TT = SEQ + CTX
            mlo = sb(st, "mlo", [128, 384], BF16)
            mhi = sb(st, "mhi", [128, 384], BF16)
            snk = sb(st, "snk", [128, 12], F32)
            esk = sb(st, "esk", [128, 12, 128], F32)
            kB_ = kb.buf()
            kb.dma("sp", mlo[:], C["mask_lo"], writes=[kB_])
            kb.dma("sp", mhi[:], C["mask_hi"], writes=[kB_])
            kb.dma("sp", snk[:], I["sink"][0:1, :].partition_broadcast(128), writes=[kB_])
            kb.op("act", lambda: nc.scalar.activation(out=snk[:], in_=snk[:], func=AF.Exp), reads=[kB_], writes=[kB_])
            kb.op("dve", lambda: nc.vector.tensor_copy(out=esk[:], in_=snk[:].unsqueeze(2).to_broadcast([128, 12, 128])), reads=[kB_], writes=[kB_])
            kT = [sb(st, f"kT{i}", [128, TT], BF16) for i in range(2)]
            vt = [sb(st, f"vt{i}", [128, TT // 128, 128], BF16) for i in range(2)]
            qT = [sb(st, f"qT{i}", [128, 3, SEQ], BF16) for i in range(2)]
            hdB = kb.bufs(2)
            pT = [sb(st, f"pT{i}", [128, 384], BF16) for i in range(4)]
            pB = kb.bufs(4)
            den = [sb(st, f"den{i}", [128, 384], F32) for i in range(2)]
            dB = kb.bufs(2)
            scale = 128 ** -0.5
            items = []
            it = 0
            for h in range(4):
                for i in range(NT):
                    blocks = []
                    if i > 0:
                        blocks.append((i - 1, mlo))
                    blocks.append((i, None))
                    if i < NT - 1:
                        blocks.append((i + 1, mhi))
                    blocks.append((16, None))
                    blocks.append((17, None))
                    for bi, (j, msk) in enumerate(blocks):
                        items.append(dict(h=h, i=i, j=j, msk=msk, first=(bi == 0), last=(bi == len(blocks) - 1), po=4 + (it % 2) * 2, d2=it % 2,
                                          n=len(items), newh=(i == 0 and bi == 0)))
                    it += 1

            def emitS(a):
                h, i, j, msk, n = a["h"], a["i"], a["j"], a["msk"], a["n"]
                k2 = h % 2
                if a["newh"]:
                    kb.dma("sp", kT[k2][:], kT_d[h], reads=[B_["kT"]], writes=[hdB[k2]])
                    kb.dma("sp", vt[k2][:], v_d[:, h * 128:(h + 1) * 128].rearrange("(t p) d -> p t d", p=128), reads=[B_["v"]], writes=[hdB[k2]])
                    kb.dma("sp", qT[k2][:], qT_d[3 * h:3 * h + 3].rearrange("g p n -> p g n"), reads=[B_["qT"]], writes=[hdB[k2]])
                sbk = n % 4
                kb.op("pe", lambda: nc.tensor.matmul(PS[sbk][:, 0:384], lhsT=kT[k2][:, j * 128:(j + 1) * 128],
                                                     rhs=qT[k2][:, :, i * 128:(i + 1) * 128], start=True, stop=True),
                      reads=[hdB[k2]], writes=[PSB[sbk]])
                kb.op("act", lambda: nc.scalar.activation(out=pT[sbk][:], in_=PS[sbk][:, 0:384], func=AF.Exp, scale=scale),
                      reads=[PSB[sbk]], writes=[pB[sbk]])
                if msk is not None:
                    kb.op("dve", lambda: nc.vector.tensor_tensor(out=pT[sbk][:], in0=pT[sbk][:], in1=msk[:], op=ALU.mult),
                          reads=[pB[sbk], kB_], writes=[pB[sbk]])

            def emitPV(a):
                h, i, j, n, po, d2 = a["h"], a["i"], a["j"], a["n"], a["po"], a["d2"]
                k2 = h % 2
                pp = n % 4

                def mm2():
                    nc.tensor.matmul(PS[po][:, 0:384], lhsT=vt[k2][:, j, :], rhs=pT[pp][:], start=a["first"], stop=a["last"])
                    return nc.tensor.matmul(PS[po + 1][:, 0:384], lhsT=onesb[:], rhs=pT[pp][:], start=a["first"], stop=a["last"])
                kb.op("pe", mm2, reads=[hdB[k2], pB[pp], cB], writes=[PSB[po], PSB[po + 1]])
                if a["last"]:
                    dn = den[d2]
                    kb.op("dve", lambda: nc.vector.tensor_tensor(out=dn[:], in0=PS[po + 1][:, 0:384],
                                                                 in1=esk[:, 3 * h:3 * h + 3, :].rearrange("p g n -> p (g n)"), op=ALU.add),
                          reads=[PSB[po + 1], kB_], writes=[dB[d2]])
                    kb.op("dve", lambda: nc.vector.reciprocal(out=dn[:], in_=dn[:]), reads=[dB[d2]], writes=[dB[d2]])
                    kb.op("dve", lambda: nc.vector.tensor_tensor(out=mixT[:, 3 * h:3 * h + 3, i * 128:(i + 1) * 128],
                                                                 in0=PS[po][:, 0:384].rearrange("p (g n) -> p g n", g=3),
                                                                 in1=dn[:].rearrange("p (g n) -> p g n", g=3), op=ALU.mult),
                          reads=[PSB[po], dB[d2]], writes=[mB])
            LA = 3
            for n in range(len(items) + LA):
                if n < len(items):
                    emitS(items[n])
                if n - LA >= 0:
                    emitPV(items[n - LA])

        def stage_fourier(st, mixT, mB):
            Z = sb(st, "fz", [128, NT, 512], BF16)
            zB = kb.buf()
            kb.dma("sp", Z[:], z_d.rearrange("(t p) c -> p t c", p=128), reads=[B_["z"]], writes=[zB])
            cc_ = sb(st, "fcc", [128, 128], BF16)
            csn = sb(st, "fcs", [128, 128], BF16)
            kb.dma("sp", cc_[:], C["dftc_c"], writes=[zB])
            kb.dma("sp", csn[:], C["dftc_sn"], writes=[zB])
            Cn = [sb(st, f"fCn{i}", [128, NT, 512], BF16) for i in range(2)]
            Sn = [sb(st, f"fSn{i}", [128, NT, 512], BF16) for i in range(2)]
            tbB = kb.bufs(2)
            ab = [sb(st, f"fab{i}", [128, 512], BF16) for i in range(2)]
            bb = [sb(st, f"fbb{i}", [128, 512], BF16) for i in range(2)]
            abB = kb.bufs(2)
            n = 0
            for nb in range(4):
                k2 = nb % 2
                kb.dma("sp", Cn[k2][:], C["dft_c"][:, nb * 512:(nb + 1) * 512].rearrange("(t p) n -> p t n", p=128), writes=[tbB[k2]])
                kb.dma("sp", Sn[k2][:], C["dft_s"][:, nb * 512:(nb + 1) * 512].rearrange("(t p) n -> p t n", p=128), writes=[tbB[k2]])
                for g in range(4):
                    pa, pb, pc = (n * 3) % 6, (n * 3 + 1) % 6, 6 + n % 2
                    a2 = n % 2
                    n += 1

                    def mmA(tab, p_, g=g):
                        for t in range(NT):
                            ins = nc.tensor.matmul(PS[p_][:], lhsT=Z[:, t, g * 128:(g + 1) * 128], rhs=tab[:, t, :], start=(t == 0), stop=(t == NT - 1))
                        return ins
                    kb.op("pe", lambda: mmA(Cn[k2], pa), reads=[zB, tbB[k2]], writes=[PSB[pa]])
                    kb.op("pe", lambda: mmA(Sn[k2], pb), reads=[zB, tbB[k2]], writes=[PSB[pb]])
                    kb.op("act", lambda: nc.scalar.copy(out=ab[a2][:], in_=PS[pa][:]), reads=[PSB[pa]], writes=[abB[a2]])
                    kb.op("dve", lambda: nc.vector.tensor_copy(out=bb[a2][:], in_=PS[pb][:]), reads=[PSB[pb]], writes=[abB[a2]])

                    def mmB():
                        nc.tensor.matmul(PS[pc][:], lhsT=cc_[:], rhs=ab[a2][:], start=True, stop=False)
                        return nc.tensor.matmul(PS[pc][:], lhsT=csn[:], rhs=bb[a2][:], start=False, stop=True)
                    kb.op("pe", mmB, reads=[zB, abB[a2]], writes=[PSB[pc]])
                    evac(mixT[:, 12 + g, nb * 512:(nb + 1) * 512], PS[pc][:], [PSB[pc]], [mB])

        def stage_inproj_cd(st, hT, hB):
            w_in = I["cd_w_in"][0]
            wb = [sb(st, f"cdw{i}", [128, NC_, 512], BF16) for i in range(4)]
            wB = kb.bufs(4)
            lng = sb(st, "sglng", [128, 1024], F32)
            lnb = sb(st, "sglnb", [128, 1024], F32)
            lB = kb.buf()
            kb.dma("sp", lng[:], I["sg_ln_g"][0:1, :].partition_broadcast(128), writes=[lB])
            kb.dma("sp", lnb[:], I["sg_ln_b"][0:1, :].partition_broadcast(128), writes=[lB])
            stg = [sb(st, f"cdstg{i}", [128, 512], BF16) for i in range(3)]
            sgB = kb.bufs(3)
            zt = [sb(st, f"cdz{i}", [128, 512], F32) for i in range(2)]
            zB = kb.bufs(2)
            stt = sb(st, "cdstt", [128, 4, 6], F32)
            mv4 = sb(st, "cdmv4", [128, 4, 2], F32)
            rs4 = sb(st, "cdrs4", [128, 4], F32)
            smB = kb.buf()
            nw = 0
            n = 0
            ns = 0

            def loadw(jb):
                nonlocal nw
                k = nw % 4
                nw += 1
                kb.dma("pool", wb[k][:], w_in[:, jb * 512:(jb + 1) * 512].rearrange("(c p) n -> p c n", p=128), writes=[wB[k]])
                return k
            for jb in range(2):
                k = loadw(jb)
                for fc in range(4):
                    for tb in range(4):
                        pa = n % 8
                        n += 1

                        def mm(k=k, fc=fc, tb=tb, pa=pa):
                            for cc in range(NC_):
                                ins = nc.tensor.matmul(PS[pa][:], lhsT=wb[k][:, cc, fc * 128:(fc + 1) * 128], rhs=hT[:, cc, tb * 512:(tb + 1) * 512],
                                                       start=(cc == 0), stop=(cc == NC_ - 1))
                            return ins
                        kb.op("pe", mm, reads=[hB, wB[k]], writes=[PSB[pa]])
                        s_ = ns % 3
                        ns += 1
                        kb.op("act", lambda: nc.scalar.activation(out=stg[s_][:], in_=PS[pa][:], func=AF.Gelu_apprx_tanh), reads=[PSB[pa]], writes=[sgB[s_]])
                        kb.dma("sp", uT_d[jb * 4 + fc, :, tb * 512:(tb + 1) * 512], stg[s_][:], reads=[sgB[s_]], writes=[B_["uT"]])
            for jb in range(2, 4):
                k = loadw(jb)
                for t in range(NT):
                    pa = n % 8
                    n += 1

                    def mmv(k=k, t=t, pa=pa):
                        for cc in range(NC_):
                            ins = nc.tensor.matmul(PS[pa][:], lhsT=hT[:, cc, t * 128:(t + 1) * 128], rhs=wb[k][:, cc, :], start=(cc == 0), stop=(cc == NC_ - 1))
                        return ins
                    kb.op("pe", mmv, reads=[hB, wB[k]], writes=[PSB[pa]])
                    z_ = zt[t % 2]
                    zb_ = zB[t % 2]
                    kb.op("act", lambda: nc.scalar.activation(out=z_[:], in_=PS[pa][:], func=AF.Gelu_apprx_tanh), reads=[PSB[pa]], writes=[zb_])
                    for g in range(4):
                        kb.op("dve", lambda g=g: nc.vector.bn_stats(out=stt[:, g, :], in_=z_[:, g * 128:(g + 1) * 128]), reads=[zb_], writes=[smB])
                    for g in range(4):
                        kb.op("dve", lambda g=g: nc.vector.bn_aggr(out=mv4[:, g, :], in_=stt[:, g, :]), reads=[smB], writes=[smB])
                    kb.op("act", lambda: nc.scalar.activation(out=rs4[:], in_=mv4[:, :, 1], func=AF.Sqrt, bias=EPS, scale=1.0), reads=[smB], writes=[smB])
                    kb.op("dve", lambda: nc.vector.reciprocal(out=rs4[:], in_=rs4[:]), reads=[smB], writes=[smB])
                    zv = z_[:].rearrange("p (g c) -> p g c", g=4)
                    kb.op("dve", lambda: nc.vector.tensor_tensor(out=zv, in0=zv, in1=mv4[:, :, 0].unsqueeze(2).to_broadcast([128, 4, 128]), op=ALU.subtract),
                          reads=[zb_, smB], writes=[zb_])
                    kb.op("dve", lambda: nc.vector.tensor_tensor(out=zv, in0=zv, in1=rs4[:].unsqueeze(2).to_broadcast([128, 4, 128]), op=ALU.mult),
                          reads=[zb_, smB], writes=[zb_])
                    c0 = (jb - 2) * 512
                    kb.op("dve", lambda: nc.vector.tensor_tensor(out=z_[:], in0=z_[:], in1=lng[:, c0:c0 + 512], op=ALU.mult), reads=[zb_, lB], writes=[zb_])
                    s_ = ns % 3
                    ns += 1
                    kb.op("dve", lambda: nc.vector.tensor_tensor(out=stg[s_][:], in0=z_[:], in1=lnb[:, c0:c0 + 512], op=ALU.add), reads=[zb_, lB], writes=[sgB[s_]])
                    kb.dma("sp", vg_d[t * 128:(t + 1) * 128, c0:c0 + 512], stg[s_][:], reads=[sgB[s_]], writes=[B_["vg"]])
            for jb in range(2):
                ka = loadw(4 + jb)
                kg = loadw(6 + jb)
                for fc in range(4):
                    for tb in range(4):
                        pa, pg = (n * 2) % 8, (n * 2 + 1) % 8
                        n += 1

                        def mm(k, p_, fc=fc, tb=tb):
                            for cc in range(NC_):
                                ins = nc.tensor.matmul(PS[p_][:], lhsT=wb[k][:, cc, fc * 128:(fc + 1) * 128], rhs=hT[:, cc, tb * 512:(tb + 1) * 512],
                                                       start=(cc == 0), stop=(cc == NC_ - 1))
                            return ins
                        kb.op("pe", lambda: mm(ka, pa), reads=[hB, wB[ka]], writes=[PSB[pa]])
                        kb.op("pe", lambda: mm(kg, pg), reads=[hB, wB[kg]], writes=[PSB[pg]])
                        z_ = zt[n % 2]
                        zb_ = zB[n % 2]
                        kb.op("act", lambda: nc.scalar.activation(out=z_[:], in_=PS[pg][:], func=AF.Sigmoid), reads=[PSB[pg]], writes=[zb_])
                        s_ = ns % 3
                        ns += 1
                        kb.op("dve", lambda: nc.vector.tensor_tensor(out=stg[s_][:], in0=PS[pa][:], in1=z_[:], op=ALU.mult), reads=[PSB[pa], zb_], writes=[sgB[s_]])
                        kb.dma("sp", xgT_d[jb * 4 + fc, :, tb * 512:(tb + 1) * 512], stg[s_][:], reads=[sgB[s_]], writes=[B_["xgT"]])

        def stage_sg(st, mixT, mB):
            swn = sb(st, "swn", [128, 8, 128], F32)
            swT = sb(st, "swT", [128, 8, 128], BF16)
            sgb = sb(st, "sgb", [128, 1024], F32)
            wB_ = kb.buf()
            kb.dma("sp", swn[:], I["sg_w"].rearrange("g p q -> p g q"), writes=[wB_])
            kb.dma("sp", sgb[:], I["sg_b"][0:1, :].partition_broadcast(128), writes=[wB_])
            for g4 in range(2):
                def tr(g4=g4):
                    for k in range(4):
                        g = g4 * 4 + k
                        ins = nc.tensor.transpose(PS[g4][:, k * 128:(k + 1) * 128], swn[:, g, :], ident[:])
                    return ins
                kb.op("pe", tr, reads=[wB_, cB], writes=[PSB[g4]])
                kb.op("dve", lambda g4=g4: nc.vector.tensor_copy(out=swT[:, g4 * 4:(g4 + 1) * 4, :], in_=PS[g4][:].rearrange("p (k n) -> p k n", k=4)),
                      reads=[PSB[g4]], writes=[wB_])
            vg = sb(st, "sgvg", [128, NT, 1024], BF16)
            uT = sb(st, "sguT", [128, 8, SEQ], BF16)
            dB = kb.buf()
            kb.dma("sp", vg[:], vg_d.rearrange("(t p) c -> p t c", p=128), reads=[B_["vg"]], writes=[dB])
            kb.dma("sp", uT[:], uT_d.rearrange("g p n -> p g n"), reads=[B_["uT"]], writes=[dB])
            tmp = [sb(st, f"sgtmp{i}", [128, 512], F32) for i in range(2)]
            tB = kb.bufs(2)
            n = 0
            for g in range(8):
                for n4 in range(4):
                    pb = n % 8
                    t2 = n % 2
                    n += 1

                    def mm(g=g, n4=n4, pb=pb):
                        for k in range(4):
                            ins = nc.tensor.matmul(PS[pb][:, k * 128:(k + 1) * 128], lhsT=vg[:, n4 * 4 + k, g * 128:(g + 1) * 128], rhs=swT[:, g, :],
                                                   start=True, stop=True)
                        return ins
                    kb.op("pe", mm, reads=[dB, wB_], writes=[PSB[pb]])
                    kb.op("dve", lambda: nc.vector.tensor_tensor(out=tmp[t2][:].rearrange("p (k n) -> p k n", k=4), in0=PS[pb][:].rearrange("p (k n) -> p k n", k=4),
                                                                 in1=sgb[:, g * 128:(g + 1) * 128].unsqueeze(1).to_broadcast([128, 4, 128]), op=ALU.add),
                          reads=[PSB[pb], wB_], writes=[tB[t2]])
                    kb.op("dve", lambda: nc.vector.tensor_tensor(out=mixT[:, g, n4 * 512:(n4 + 1) * 512], in0=tmp[t2][:], in1=uT[:, g, n4 * 512:(n4 + 1) * 512], op=ALU.mult),
                          reads=[tB[t2], dB], writes=[mB])

        def stage_conv(st, mixT, mB):
            PADW = SEQ + 30
            xg = sb(st, "cvxg", [128, 8, PADW], BF16)
            xB = kb.buf()
            kb.op("dve", lambda: nc.vector.memset(xg[:, :, 0:15], 0.0), writes=[xB])
            kb.op("dve", lambda: nc.vector.memset(xg[:, :, 15 + SEQ:PADW], 0.0), writes=[xB])
            kb.dma("sp", xg[:, :, 15:15 + SEQ], xgT_d.rearrange("g p n -> p g n"), reads=[B_["xgT"]], writes=[xB])
            cwn = sb(st, "cvwn", [31, 1024], F32)
            cw = sb(st, "cvw", [128, 8, 32], F32)
            cb = sb(st, "cvb", [128, 8], F32)
            lg = sb(st, "cvlg", [128, 8], F32)
            lb = sb(st, "cvlb", [128, 8], F32)
            onesf = sb(st, "cvones", [128, 128], F32)
            pB = kb.buf()
            kb.dma("sp", cwn[:], I["conv_w"], writes=[pB])
            kb.dma("sp", onesf[:], C["onesf"], writes=[pB])
            kb.dma("sp", cb[:], I["conv_b"][0].rearrange("(g p) -> p g", p=128), writes=[pB], allow_slow_non_contiguous=True)
            kb.dma("sp", lg[:], I["conv_ln_g"][0].rearrange("(g p) -> p g", p=128), writes=[pB], allow_slow_non_contiguous=True)
            kb.dma("sp", lb[:], I["conv_ln_b"][0].rearrange("(g p) -> p g", p=128), writes=[pB], allow_slow_non_contiguous=True)

            def trw():
                for g in range(8):
                    ins = nc.tensor.transpose(PS[0][:, g * 32:g * 32 + 31], cwn[0:31, g * 128:(g + 1) * 128], ident[0:31, 0:31])
                return ins
            kb.op("pe", trw, reads=[pB, cB], writes=[PSB[0]])
            kb.op("dve", lambda: nc.vector.tensor_copy(out=cw[:, :, 0:31], in_=PS[0][:, 0:256].rearrange("p (g k) -> p g k", g=8)[:, :, 0:31]), reads=[PSB[0]], writes=[pB])
            dg = sb(st, "cvdg", [128, 8, 31, 128], BF16)
            dgB = kb.buf()
            for g in range(8):
                for k in range(31):
                    kb.op("dve", lambda g=g, k=k: nc.vector.tensor_scalar(out=dg[:, g, k, :], in0=ident[:], scalar1=cw[:, g, k:k + 1], scalar2=None, op0=ALU.mult),
                          reads=[pB, cB], writes=[dgB])
            xc = sb(st, "cvxc", [128, 8, 512], F32)
            xcB = kb.buf()
            sq = [sb(st, f"cvsq{i}", [128, 512], F32) for i in range(2)]
            sqB = kb.bufs(2)
            mean = sb(st, "cvmean", [128, 512], F32)
            rstd = sb(st, "cvrstd", [128, 512], F32)
            stB = kb.buf()
            tmp = [sb(st, f"cvtmp{i}", [128, 512], F32) for i in range(2)]
            tB = kb.bufs(2)
            n = 0
            for tb in range(4):
                for g in range(8):
                    pb = n % 4
                    n += 1

                    def mmc(g=g, pb=pb):
                        for k in range(31):
                            ins = nc.tensor.matmul(PS[pb][:], lhsT=dg[:, g, k, :], rhs=xg[:, g, tb * 512 + k: tb * 512 + k + 512], start=(k == 0), stop=(k == 30))
                        return ins
                    kb.op("pe", mmc, reads=[dgB, xB], writes=[PSB[pb]])
                    kb.op("act", lambda g=g, pb=pb: nc.scalar.activation(out=xc[:, g, :], in_=PS[pb][:], func=AF.Identity, bias=cb[:, g:g + 1], scale=1.0),
                          reads=[PSB[pb], pB], writes=[xcB])
                for g in range(8):
                    s2 = g % 2
                    kb.op("act", lambda g=g, s2=s2: nc.scalar.activation(out=sq[s2][:], in_=xc[:, g, :], func=AF.Square), reads=[xcB], writes=[sqB[s2]])

                    def mms(g=g, s2=s2):
                        nc.tensor.matmul(PS[4][:], lhsT=onesf[:], rhs=xc[:, g, :], start=(g == 0), stop=(g == 7))
                        return nc.tensor.matmul(PS[5][:], lhsT=onesf[:], rhs=sq[s2][:], start=(g == 0), stop=(g == 7))
                    kb.op("pe", mms, reads=[xcB, sqB[s2], pB], writes=[PSB[4], PSB[5]])
                kb.op("act", lambda: nc.scalar.activation(out=mean[:], in_=PS[4][:], func=AF.Copy, scale=1.0 / 1024), reads=[PSB[4]], writes=[stB])
                kb.op("dve", lambda: nc.vector.tensor_tensor(out=rstd[:], in0=mean[:], in1=mean[:], op=ALU.mult), reads=[stB], writes=[stB])
                kb.op("dve", lambda: nc.vector.scalar_tensor_tensor(out=rstd[:], in0=PS[5][:], scalar=1.0 / 1024, in1=rstd[:], op0=ALU.mult, op1=ALU.subtract),
                      reads=[PSB[5], stB], writes=[stB])
                kb.op("act", lambda: nc.scalar.activation(out=rstd[:], in_=rstd[:], func=AF.Sqrt, bias=EPS, scale=1.0), reads=[stB], writes=[stB])
                kb.op("dve", lambda: nc.vector.reciprocal(out=rstd[:], in_=rstd[:]), reads=[stB], writes=[stB])
                for g in range(8):
                    t2 = g % 2
                    kb.op("dve", lambda g=g, t2=t2: nc.vector.tensor_tensor(out=tmp[t2][:], in0=xc[:, g, :], in1=mean[:], op=ALU.subtract), reads=[xcB, stB], writes=[tB[t2]])
                    kb.op("dve", lambda g=g, t2=t2: nc.vector.tensor_tensor(out=tmp[t2][:], in0=tmp[t2][:], in1=rstd[:], op=ALU.mult), reads=[tB[t2], stB], writes=[tB[t2]])
                    kb.op("act", lambda g=g, t2=t2: nc.scalar.activation(out=mixT[:, 8 + g, tb * 512:(tb + 1) * 512], in_=tmp[t2][:], func=AF.Silu,
                                                                       bias=lb[:, g:g + 1], scale=lg[:, g:g + 1]), reads=[tB[t2], pB], writes=[mB])

        def stage_post(st, l, xsrc_d, xsrcB, aff, affB):
            bt = {}
            bcB = kb.buf()
            for nm, row, p1 in (("g1", mod_d[l, 0:1, 2 * D:3 * D], False), ("sc2", mod_d[l, 0:1, 4 * D:5 * D], True),
                                ("sh2", mod_d[l, 0:1, 3 * D:4 * D], False), ("lg", I["ln1_g"][l:l + 1, :], False),
                                ("lb", I["ln1_b"][l:l + 1, :], False)):
                bt[nm] = sb(st, "pb_" + nm, [128, D], F32)
                load_bc(bt[nm], row, bcB, plus_one=p1)
            wr = sb(st, "wr", [128, NC_, NE], F32)
            kb.dma("sp", wr[:], I["w_router"][l].rearrange("(c p) e -> p c e", p=128), writes=[bcB])
            ND_ = 2
            xt = [sb(st, f"pxt{i}", [128, D], F32) for i in range(ND_)]
            yt = [sb(st, f"pyt{i}", [128, D], F32) for i in range(ND_)]
            xB = kb.bufs(ND_)
            yB = kb.bufs(ND_)
            hb = [sb(st, f"phb{i}", [128, D], BF16) for i in range(ND_)]
            hbB = kb.bufs(ND_)
            hT = [sb(st, f"phT{i}", [128, NC_, 128], F32) for i in range(ND_)]
            hTB = kb.bufs(ND_)
            stt = [sb(st, f"pstt{i}", [128, 4, 6], F32) for i in range(ND_)]
            mv = [sb(st, f"pmv{i}", [128, 2], F32) for i in range(ND_)]
            rstd = [sb(st, f"prstd{i}", [128, 1], F32) for i in range(ND_)]
            nmr = [sb(st, f"pnmr{i}", [128, 1], F32) for i in range(ND_)]
            smB = kb.bufs(ND_)
            lg_ = [sb(st, f"plg{i}", [128, NE], F32) for i in range(ND_)]
            mx = [sb(st, f"pmx{i}", [128, 1], F32) for i in range(ND_)]
            sm = [sb(st, f"psm{i}", [128, 1], F32) for i in range(ND_)]
            lB = kb.bufs(ND_)
            affBs = []

            def tile(t):
                p = t % ND_
                x_, y_, xb_, yb_ = xt[p], yt[p], xB[p], yB[p]
                kb.dma("sp", x_[:], xsrc_d[t * 128:(t + 1) * 128, :], reads=[xsrcB], writes=[xb_])
                kb.dma("sp", y_[:], y_d[t * 128:(t + 1) * 128, :], reads=[B_["y"]], writes=[yb_])
                yield
                kb.op("pool", lambda: nc.gpsimd.tensor_tensor(out=y_[:], in0=y_[:], in1=bt["g1"][:], op=ALU.mult), reads=[yb_, bcB], writes=[yb_])
                yield
                kb.op("dve", lambda: nc.vector.scalar_tensor_tensor(out=x_[:], in0=x_[:], scalar=float(ALPHA), in1=y_[:], op0=ALU.mult, op1=ALU.add),
                      reads=[xb_, yb_], writes=[xb_])
                yield from ln_stats_g(stt[p], mv[p], rstd[p], nmr[p], x_, [xb_], smB[p])
                kb.op("act", lambda: nc.scalar.activation(out=x_[:], in_=x_[:], func=AF.Identity, bias=nmr[p][:], scale=rstd[p][:]), reads=[smB[p], xb_], writes=[xb_])
                yield
                kb.op("dve", lambda: nc.vector.tensor_tensor(out=x_[:], in0=x_[:], in1=bt["lg"][:], op=ALU.mult), reads=[xb_, bcB], writes=[xb_])
                kb.op("pool", lambda: nc.gpsimd.tensor_tensor(out=x_[:], in0=x_[:], in1=bt["lb"][:], op=ALU.add), reads=[xb_, bcB], writes=[xb_])
                yield
                kb.dma("sp", x1_d[t * 128:(t + 1) * 128, :], x_[:], reads=[xb_], writes=[B_["x1"]])
                yield from ln_stats_g(stt[p], mv[p], rstd[p], nmr[p], x_, [xb_], smB[p])
                kb.op("act", lambda: nc.scalar.activation(out=y_[:], in_=x_[:], func=AF.Identity, bias=nmr[p][:], scale=rstd[p][:]), reads=[smB[p], xb_], writes=[yb_])
                yield
                kb.op("dve", lambda: nc.vector.tensor_tensor(out=y_[:], in0=y_[:], in1=bt["sc2"][:], op=ALU.mult), reads=[yb_, bcB], writes=[yb_])
                kb.op("pool", lambda: nc.gpsimd.tensor_tensor(out=y_[:], in0=y_[:], in1=bt["sh2"][:], op=ALU.add), reads=[yb_, bcB], writes=[yb_])
                yield
                h_ = hb[p]
                kb.op("act", lambda: nc.scalar.copy(out=h_[:], in_=y_[:]), reads=[yb_], writes=[hbB[p]])
                kb.dma("sp", h2_d[t * 128:(t + 1) * 128, :], h_[:], reads=[hbB[p]], writes=[B_["h2"]])
                pp = t % 2
                for q4 in range(4):
                    pb = pp * 4 + q4

                    def tr():
                        for k in range(4):
                            cc = q4 * 4 + k
                            ins = nc.tensor.transpose(PS[pb][:, k * 128:(k + 1) * 128], y_[:, cc * 128:(cc + 1) * 128], ident[:])
                        return ins
                    kb.op("pe", tr, reads=[yb_, cB], writes=[PSB[pb]])
                yield
                for q4 in range(4):
                    pb = pp * 4 + q4
                    evac(hT[p][:, q4 * 4:(q4 + 1) * 4, :], PS[pb][:].rearrange("p (k n) -> p k n", k=4), [PSB[pb]], [hTB[p]])
                pr = pp * 4

                def mmr():
                    for cc in range(NC_):
                        ins = nc.tensor.matmul(PS[pr][:, 0:NE], lhsT=hT[p][:, cc, :], rhs=wr[:, cc, :], start=(cc == 0), stop=(cc == NC_ - 1))
                    return ins
                kb.op("pe", mmr, reads=[hTB[p], bcB], writes=[PSB[pr]])
                yield
                kb.op("dve", lambda: nc.vector.tensor_copy(out=lg_[p][:], in_=PS[pr][:, 0:NE]), reads=[PSB[pr]], writes=[lB[p]])
                kb.op("dve", lambda: nc.vector.reduce_max(out=mx[p][:], in_=lg_[p][:], axis=AX.X), reads=[lB[p]], writes=[lB[p]])
                kb.op("dve", lambda: nc.vector.tensor_scalar(out=mx[p][:], in0=mx[p][:], scalar1=-1.0, scalar2=None, op0=ALU.mult), reads=[lB[p]], writes=[lB[p]])
                kb.op("act", lambda: nc.scalar.activation(out=lg_[p][:], in_=lg_[p][:], func=AF.Exp, bias=mx[p][:], scale=1.0), reads=[lB[p]], writes=[lB[p]])
                yield
                kb.op("dve", lambda: nc.vector.reduce_sum(out=sm[p][:], in_=lg_[p][:], axis=AX.X), reads=[lB[p]], writes=[lB[p]])
                kb.op("dve", lambda: nc.vector.reciprocal(out=sm[p][:], in_=sm[p][:]), reads=[lB[p]], writes=[lB[p]])
                ab_ = kb.buf()
                affBs.append(ab_)
                kb.op("dve", lambda: nc.vector.tensor_scalar(out=aff[:, t, :], in0=lg_[p][:], scalar1=sm[p][:], scalar2=None, op0=ALU.mult), reads=[lB[p]], writes=[ab_])
            interleave([tile(t) for t in range(NT)], depth=2)

        def stage_moe(st, l, aff, affB, idxf_all, idxB, nexp=NE):
            mask = sb(st, "mmask", [128, NT, NE], F32)
            maskb = sb(st, "mmaskb", [128, NT, NE], BF16)
            key = sb(st, "mkey", [128, NT, NE], F32)
            tri = sb(st, "mtri", [128, 128], BF16)
            R = sb(st, "mR", [128, NT, NE, 8], BF16)
            r1 = sb(st, "mr1", [128, NT, NE], F32)
            pcol = sb(st, "mpcol", [128, 1], F32)
            tcol = sb(st, "mtcol", [128, NT, NE], F32)
            rt = ExitStack()
            affT = sb(rt, "affT", [NE, SEQ], F32)
            work = sb(rt, "mwork", [NE, SEQ], F32)
            m8 = sb(rt, "m8", [NE, 8], F32)
            aTB = kb.buf()
            for t4 in range(4):
                def tr(t4=t4):
                    for k in range(4):
                        t = t4 * 4 + k
                        ins = nc.tensor.transpose(PS[t4][0:NE, k * 128:(k + 1) * 128], aff[:, t, :], ident[:])
                    return ins
                kb.op("pe", tr, reads=[affB, cB], writes=[PSB[t4]])
                kb.op("dve", lambda t4=t4: nc.vector.tensor_copy(out=affT[:, t4 * 512:(t4 + 1) * 512], in_=PS[t4][0:NE, :]), reads=[PSB[t4]], writes=[aTB])
            kb.op("dve", lambda: nc.vector.tensor_copy(out=work[:], in_=affT[:]), reads=[aTB], writes=[aTB])
            for r in range(CAP // 8):
                kb.op("dve", lambda: nc.vector.max(out=m8[:], in_=work[:]), reads=[aTB], writes=[aTB])
                if r < CAP // 8 - 1:
                    kb.op("dve", lambda: nc.vector.match_replace(out=work[:], in_to_replace=m8[:], in_values=work[:], imm_value=-1.0), reads=[aTB], writes=[aTB])
            kb.op("dve", lambda: nc.vector.tensor_scalar(out=work[:], in0=affT[:], scalar1=m8[:, 7:8], scalar2=None, op0=ALU.is_ge), reads=[aTB], writes=[aTB])
            mkB = kb.buf()
            for t4 in range(4):
                def tr2(t4=t4):
                    for k in range(4):
                        t = t4 * 4 + k
                        ins = nc.tensor.transpose(PS[4 + t4][:, k * NE:(k + 1) * NE], work[:, t * 128:(t + 1) * 128], ident[0:NE, 0:NE])
                    return ins
                kb.op("pe", tr2, reads=[aTB, cB], writes=[PSB[4 + t4]])
                kb.op("dve", lambda t4=t4: nc.vector.tensor_copy(out=mask[:, t4 * 4:(t4 + 1) * 4, :],
                                                               in_=PS[4 + t4][:, 0:4 * NE].rearrange("p (k e) -> p k e", k=4)), reads=[PSB[4 + t4]], writes=[mkB])
            kb.op("dve", lambda: nc.vector.tensor_copy(out=maskb[:], in_=mask[:]), reads=[mkB], writes=[mkB])
            kb.dma("sp", tri[:], C["tri"], writes=[mkB])

            def cums():
                for t in range(NT):
                    for i2 in range(t):
                        nc.tensor.matmul(PS[0][:, t * NE:(t + 1) * NE], lhsT=onesb[:], rhs=maskb[:, i2, :], start=(i2 == 0), stop=False)
                    ins = nc.tensor.matmul(PS[0][:, t * NE:(t + 1) * NE], lhsT=tri[:], rhs=maskb[:, t, :], start=(t == 0), stop=True)
                return ins
            kb.op("pe", cums, reads=[mkB, cB], writes=[PSB[0]])
            kb.op("dve", lambda: nc.vector.tensor_tensor(out=key[:].rearrange("p t e -> p (t e)"), in0=PS[0][:, 0:NT * NE],
                                                         in1=mask[:].rearrange("p t e -> p (t e)"), op=ALU.mult), reads=[PSB[0], mkB], writes=[mkB])
            kb.op("dve", lambda: nc.vector.tensor_scalar(out=key[:], in0=key[:], scalar1=-1.0, scalar2=None, op0=ALU.add), reads=[mkB], writes=[mkB])
            kb.dma("sp", pcol[:], C["pcol"], writes=[mkB])
            kb.dma("sp", tcol[:].rearrange("p t e -> p (t e)"), C["tcol"], writes=[mkB])
            kb.op("dve", lambda: nc.vector.memset(R[:], 0.0), writes=[mkB])
            kb.op("dve", lambda: nc.vector.tensor_copy(out=R[:, :, :, 0], in_=pcol[:].unsqueeze(2).to_broadcast([128, NT, NE])), reads=[mkB], writes=[mkB])
            kb.op("dve", lambda: nc.vector.tensor_copy(out=R[:, :, :, 1], in_=tcol[:]), reads=[mkB], writes=[mkB])
            kb.op("dve", lambda: nc.vector.tensor_copy(out=R[:, :, :, 2], in_=aff[:]), reads=[mkB, affB], writes=[mkB])
            kb.op("dve", lambda: nc.vector.tensor_tensor(out=r1[:], in0=aff[:], in1=R[:, :, :, 2], op=ALU.subtract), reads=[mkB, affB], writes=[mkB])
            kb.op("dve", lambda: nc.vector.tensor_copy(out=R[:, :, :, 3], in_=r1[:]), reads=[mkB], writes=[mkB])
            kb.op("dve", lambda: nc.vector.tensor_tensor(out=r1[:], in0=r1[:], in1=R[:, :, :, 3], op=ALU.subtract), reads=[mkB], writes=[mkB])
            kb.op("dve", lambda: nc.vector.tensor_copy(out=R[:, :, :, 4], in_=r1[:]), reads=[mkB], writes=[mkB])
            kb.barrier()
            rt.close()

            io256 = sb(st, "io256", [128, 256], F32)
            kb.dma("sp", io256[:], C["iota256"], writes=[mkB])
            ig_all = sb(st, "mig", [128, 2 * NE, 8], F32)
            idxi = sb(st, "midxi", [128, 2 * NE], I32)
            gg = sb(st, "mgg", [128, 2 * NE], F32)
            igB = kb.buf()
            sx = ExitStack()
            Sel = [sb(sx, f"mSel{i}", [128, NT, 256], BF16) for i in range(2)]
            selB = kb.bufs(2)
            for e in range(nexp):
                e2 = e % 2
                kb.op("dve", lambda: nc.vector.tensor_tensor(out=Sel[e2][:], in0=io256[:].unsqueeze(1).to_broadcast([128, NT, 256]),
                                                             in1=key[:, :, e].unsqueeze(2).to_broadcast([128, NT, 256]), op=ALU.is_equal),
                      reads=[mkB], writes=[selB[e2]])
                pi = 6 + e2

                def mmi():
                    for half in range(2):
                        for t in range(NT):
                            ins = nc.tensor.matmul(PS[pi][:, half * 8:half * 8 + 8], lhsT=Sel[e2][:, t, half * 128:(half + 1) * 128], rhs=R[:, t, e, :],
                                                   start=(t == 0), stop=(t == NT - 1))
                    return ins
                kb.op("pe", mmi, reads=[selB[e2], mkB], writes=[PSB[pi]])
                kb.op("act", lambda: nc.scalar.copy(out=ig_all[:, 2 * e:2 * e + 2, :], in_=PS[pi][:, 0:16].rearrange("p (h k) -> p h k", h=2)), reads=[PSB[pi]], writes=[igB])
            ne2 = 2 * nexp
            kb.op("dve", lambda: nc.vector.scalar_tensor_tensor(out=idxf_all[:, 0:ne2], in0=ig_all[:, 0:ne2, 1], scalar=128.0, in1=ig_all[:, 0:ne2, 0], op0=ALU.mult, op1=ALU.add),
                  reads=[igB], writes=[igB, idxB])
            kb.op("dve", lambda: nc.vector.tensor_copy(out=idxi[:, 0:ne2], in_=idxf_all[:, 0:ne2]), reads=[igB], writes=[igB])
            kb.op("dve", lambda: nc.vector.tensor_tensor(out=gg[:, 0:ne2], in0=ig_all[:, 0:ne2, 2], in1=ig_all[:, 0:ne2, 3], op=ALU.add), reads=[igB], writes=[igB])
            kb.op("dve", lambda: nc.vector.tensor_tensor(out=gg[:, 0:ne2], in0=gg[:, 0:ne2], in1=ig_all[:, 0:ne2, 4], op=ALU.add), reads=[igB], writes=[igB])
            kb.barrier()
            sx.close()

            ex = ExitStack()
            xs = [sb(ex, f"mxs{i}", [128, 2, D], BF16) for i in range(2)]
            xsB = kb.bufs(2)
            xsT = [sb(ex, f"mxsT{i}", [128, NC_, 256], BF16) for i in range(2)]
            xsTB = kb.bufs(2)
            NWB = 3
            wg = [sb(ex, f"mwg{i}", [128, NC_, 512], BF16) for i in range(NWB)]
            wu = [sb(ex, f"mwu{i}", [128, NC_, 512], BF16) for i in range(NWB)]
            wd = [sb(ex, f"mwd{i}", [128, NC_, 512], BF16) for i in range(2)]
            wgB, wuB, wdB = kb.bufs(NWB), kb.bufs(NWB), kb.bufs(2)
            sgt = [sb(ex, f"msg{i}", [128, 512], F32) for i in range(2)]
            sgB = kb.bufs(2)
            hid = [sb(ex, f"mhid{i}", [128, 512], BF16) for i in range(2)]
            hdB2 = kb.bufs(2)
            hidT = sb(ex, "mhidT", [128, NC_, 256], BF16)
            hidB = kb.buf()
            yeb = [sb(ex, f"myeb{i}", [128, 2, D], BF16) for i in range(2)]
            yeB = kb.bufs(2)
            nw = 0
            nwd = 0
            nsg = 0

            def gather(e):
                for half in range(2):
                    kb.dma("pool", xs[e % 2][:, half, :], h2_d, reads=[igB, B_["h2"]], writes=[xsB[e % 2]],
                           indirect=dict(out_offset=None, in_offset=bass.IndirectOffsetOnAxis(ap=idxi[:, 2 * e + half:2 * e + half + 1], axis=0)))
            def emit_tr(s2, fb, half):
                pt = 2 + s2
                ptv = PS[pt][:].bitcast(BF16)

                def trh():
                    for k in range(4):
                        ins = nc.tensor.transpose(ptv[:, k * 128:(k + 1) * 128], hid[s2][:, k * 128:(k + 1) * 128], identb[:])
                    return ins
                kb.op("pe", trh, reads=[hdB2[s2], cB], writes=[PSB[pt]])
                evac(hidT[:, fb * 4:(fb + 1) * 4, half * 128:(half + 1) * 128], ptv[:, 0:512].rearrange("p (k n) -> p k n", k=4), [PSB[pt]], [hidB])
            pend_tr = None
            gather(0)
            for e in range(nexp):
                e2 = e % 2
                if e + 1 < nexp:
                    gather(e + 1)
                for half in range(2):
                    for cg in range(2):
                        pb = half * 2 + cg
                        psv = PS[pb][:].bitcast(BF16)

                        def trx(half=half, cg=cg, psv=psv):
                            for k in range(8):
                                cc = cg * 8 + k
                                ins = nc.tensor.transpose(psv[:, k * 128:(k + 1) * 128], xs[e2][:, half, cc * 128:(cc + 1) * 128], identb[:])
                            return ins
                        kb.op("pe", trx, reads=[xsB[e2], cB], writes=[PSB[pb]])
                        evac(xsT[e2][:, cg * 8:(cg + 1) * 8, half * 128:(half + 1) * 128], psv.rearrange("p (k n) -> p k n", k=8), [PSB[pb]], [xsTB[e2]])
                for fb in range(4):
                    w2 = nw % NWB
                    nw += 1
                    kb.dma("pool", wg[w2][:], I["w_gate"][l, e, :, fb * 512:(fb + 1) * 512].rearrange("(c p) n -> p c n", p=128), writes=[wgB[w2]])
                    kb.dma("pool", wu[w2][:], I["w_up"][l, e, :, fb * 512:(fb + 1) * 512].rearrange("(c p) n -> p c n", p=128), writes=[wuB[w2]])
                    for half in range(2):
                        pg, pu = 4 + (half % 2) * 2, 5 + (half % 2) * 2

                        def mmg(wt, p_, half=half):
                            for cc in range(NC_):
                                ins = nc.tensor.matmul(PS[p_][:], lhsT=xsT[e2][:, cc, half * 128:(half + 1) * 128], rhs=wt[:, cc, :],
                                                       start=(cc == 0), stop=(cc == NC_ - 1))
                            return ins
                        kb.op("pe", lambda: mmg(wg[w2], pg), reads=[wgB[w2], xsTB[e2]], writes=[PSB[pg]])
                        kb.op("pe", lambda: mmg(wu[w2], pu), reads=[wuB[w2], xsTB[e2]], writes=[PSB[pu]])
                        s2 = nsg % 2
                        nsg += 1
                        kb.op("act", lambda: nc.scalar.activation(out=sgt[s2][:], in_=PS[pg][:], func=AF.Silu), reads=[PSB[pg]], writes=[sgB[s2]])
                        kb.op("dve", lambda: nc.vector.tensor_tensor(out=hid[s2][:], in0=PS[pu][:], in1=sgt[s2][:], op=ALU.mult),
                              reads=[PSB[pu], sgB[s2]], writes=[hdB2[s2]])
                        if pend_tr is not None:
                            emit_tr(*pend_tr)
                        pend_tr = (s2, fb, half)
                emit_tr(*pend_tr)
                pend_tr = None
                for db in range(4):
                    w2 = nwd % 2
                    nwd += 1
                    kb.dma("pool", wd[w2][:], I["w_down"][l, e, :, db * 512:(db + 1) * 512].rearrange("(c p) n -> p c n", p=128), writes=[wdB[w2]])
                    for half in range(2):
                        pb = (db * 2 + half) % 4

                        def mmd(half=half, pb=pb):
                            for fc in range(NC_):
                                ins = nc.tensor.matmul(PS[pb][:], lhsT=hidT[:, fc, half * 128:(half + 1) * 128], rhs=wd[w2][:, fc, :],
                                                       start=(fc == 0), stop=(fc == NC_ - 1))
                            return ins
                        kb.op("pe", mmd, reads=[hidB, wdB[w2]], writes=[PSB[pb]])
                        kb.op("act", lambda half=half, pb=pb: nc.scalar.activation(out=yeb[e2][:, half, db * 512:(db + 1) * 512], in_=PS[pb][:], func=AF.Copy,
                                                                                  scale=gg[:, 2 * e + half:2 * e + half + 1]), reads=[PSB[pb], igB], writes=[yeB[e2]])
                for half in range(2):
                    r0 = (e * 2 + half) * 128
                    kb.dma("sp", ye_d[r0:r0 + 128, :], yeb[e2][:, half, :], reads=[yeB[e2]], writes=[B_["ye"]])
            kb.barrier()
            ex.close()

        def stage_combine(st, l, idxf_all, idxB, dst_d, dstB, nexp=NE):
            nk = 2 * nexp
            YE = sb(st, "cYE", [128, nk, D], BF16)
            YB = kb.buf()
            for db in range(4):
                kb.dma("sp", YE[:, :, db * 512:(db + 1) * 512], ye_d[0:nk * 128, db * 512:(db + 1) * 512].rearrange("(k p) n -> p k n", p=128),
                       reads=[B_["ye"]], writes=[YB])
            io128 = sb(st, "cio", [128, 128], F32)
            bt = {}
            bcB = kb.buf()
            kb.dma("sp", io128[:], C["iota256"][:, 0:128], writes=[bcB])
            for nm, row in (("g2", mod_d[l, 0:1, 5 * D:6 * D]), ("lg", I["ln2_g"][l:l + 1, :]), ("lb", I["ln2_b"][l:l + 1, :])):
                bt[nm] = sb(st, "qb_" + nm, [128, D], F32)
                load_bc(bt[nm], row, bcB)
            sl = [sb(st, f"csl{i}", [128, nk, 128], BF16) for i in range(2)]
            slB2 = kb.bufs(2)
            idt = sb(st, "cidt", [128, 2 * NE], F32)
            idB = kb.buf()
            xt = [sb(st, f"cxt{i}", [128, D], F32) for i in range(2)]
            xB = kb.bufs(2)
            rt_ = sb(st, "crt", [128, D], F32)
            rB = kb.buf()
            stt = sb(st, "qstt", [128, 4, 6], F32)
            mv = sb(st, "qmv", [128, 2], F32)
            rstd = sb(st, "qrstd", [128, 1], F32)
            nmr = sb(st, "qnmr", [128, 1], F32)
            smB = kb.buf()
            toks = []

            def prep(t):
                s2 = t % 2
                kb.dma("sp", xt[s2][:], x1_d[t * 128:(t + 1) * 128, :], reads=[B_["x1"]], writes=[xB[s2]])
                kb.op("dve", lambda: nc.vector.tensor_scalar(out=idt[:, 0:nk], in0=idxf_all[:, 0:nk], scalar1=float(-128 * t), scalar2=None, op0=ALU.add),
                      reads=[idxB], writes=[idB])
                kb.op("dve", lambda: nc.vector.tensor_tensor(out=sl[s2][:], in0=io128[:].unsqueeze(1).to_broadcast([128, nk, 128]),
                                                             in1=idt[:, 0:nk].unsqueeze(2).to_broadcast([128, nk, 128]), op=ALU.is_equal),
                      reads=[idB, bcB], writes=[slB2[s2]])

            def mm(t):
                s2 = t % 2
                for db in range(4):
                    pb = db + 4 * s2

                    def mmc(db=db, pb=pb):
                        for k in range(nk):
                            ins = nc.tensor.matmul(PS[pb][:], lhsT=sl[s2][:, k, :], rhs=YE[:, k, db * 512:(db + 1) * 512], start=(k == 0), stop=(k == nk - 1))
                        return ins
                    kb.op("pe", mmc, reads=[slB2[s2], YB], writes=[PSB[pb]])

            def post(t):
                s2 = t % 2
                x_, xb_ = xt[s2], xB[s2]
                for db in range(4):
                    pb = db + 4 * s2
                    kb.op("dve", lambda db=db, pb=pb: nc.vector.tensor_tensor(out=rt_[:, db * 512:(db + 1) * 512], in0=PS[pb][:], in1=bt["g2"][:, db * 512:(db + 1) * 512], op=ALU.mult),
                          reads=[PSB[pb], bcB], writes=[rB])
                kb.op("dve", lambda: nc.vector.scalar_tensor_tensor(out=x_[:], in0=x_[:], scalar=float(ALPHA), in1=rt_[:], op0=ALU.mult, op1=ALU.add),
                      reads=[xb_, rB], writes=[xb_])
                ln_stats(stt, mv, rstd, nmr, x_, [xb_], smB)
                kb.op("act", lambda: nc.scalar.activation(out=x_[:], in_=x_[:], func=AF.Identity, bias=nmr[:], scale=rstd[:]), reads=[smB, xb_], writes=[xb_])
                kb.op("dve", lambda: nc.vector.tensor_tensor(out=x_[:], in0=x_[:], in1=bt["lg"][:], op=ALU.mult), reads=[xb_, bcB], writes=[xb_])
                kb.op("pool", lambda: nc.gpsimd.tensor_tensor(out=x_[:], in0=x_[:], in1=bt["lb"][:], op=ALU.add), reads=[xb_, bcB], writes=[xb_])
                toks.append(kb.dma("sp", dst_d[t * 128:(t + 1) * 128, :], x_[:], reads=[xb_], writes=[dstB]))
            prep(0)
            for t in range(NT):
                mm(t)
                if t + 1 < NT:
                    prep(t + 1)
                post(t)
            return toks

        def stage_postmoe(st, l, dst_d, dstB):
            bt = {}
            bcB = kb.buf()
            for nm, row in (("g2", mod_d[l, 0:1, 5 * D:6 * D]), ("lg", I["ln2_g"][l:l + 1, :]), ("lb", I["ln2_b"][l:l + 1, :])):
                bt[nm] = sb(st, "qb_" + nm, [128, D], F32)
                load_bc(bt[nm], row, bcB)
            xt = [sb(st, f"qxt{i}", [128, D], F32) for i in range(2)]
            yt = [sb(st, f"qyt{i}", [128, D], F32) for i in range(2)]
            xB, yB = kb.bufs(2), kb.bufs(2)
            stt = sb(st, "qstt", [128, 4, 6], F32)
            mv = sb(st, "qmv", [128, 2], F32)
            rstd = sb(st, "qrstd", [128, 1], F32)
            nmr = sb(st, "qnmr", [128, 1], F32)
            smB = kb.buf()
            toks = []
            for t in range(NT):
                x_, y_, xb_, yb_ = xt[t % 2], yt[t % 2], xB[t % 2], yB[t % 2]
                kb.dma("sp", x_[:], x1_d[t * 128:(t + 1) * 128, :], reads=[B_["x1"]], writes=[xb_])
                kb.dma("sp", y_[:], moe_d[t * 128:(t + 1) * 128, :], reads=[B_["moe"]], writes=[yb_])
                kb.op("dve", lambda y_=y_: nc.vector.tensor_tensor(out=y_[:], in0=y_[:], in1=bt["g2"][:], op=ALU.mult), reads=[yb_, bcB], writes=[yb_])
                kb.op("dve", lambda x_=x_, y_=y_: nc.vector.scalar_tensor_tensor(out=x_[:], in0=x_[:], scalar=float(ALPHA), in1=y_[:], op0=ALU.mult, op1=ALU.add),
                      reads=[xb_, yb_], writes=[xb_])
                ln_stats(stt, mv, rstd, nmr, x_, [xb_], smB)
                kb.op("act", lambda x_=x_: nc.scalar.activation(out=x_[:], in_=x_[:], func=AF.Identity, bias=nmr[:], scale=rstd[:]), reads=[smB, xb_], writes=[xb_])
                kb.op("dve", lambda x_=x_: nc.vector.tensor_tensor(out=x_[:], in0=x_[:], in1=bt["lg"][:], op=ALU.mult), reads=[xb_, bcB], writes=[xb_])
                kb.op("dve", lambda x_=x_: nc.vector.tensor_tensor(out=x_[:], in0=x_[:], in1=bt["lb"][:], op=ALU.add), reads=[xb_, bcB], writes=[xb_])
                toks.append(kb.dma("sp", dst_d[t * 128:(t + 1) * 128, :], x_[:], reads=[xb_], writes=[dstB]))
            return toks

        xB_in = kb.buf("xin")
        stage_mod()
        hT_d = scratch("hT_d", [NC_, 128, SEQ + CTX], BF16)
        if upto >= 1:
            with ExitStack() as st:
                hT = sb(st, "hT0", [128, NC_, SEQ + CTX], BF16)
                hB = kb.buf()
                side = mod_gen(st, [(0, j) for j in range(8, 24)] + [(1, j) for j in range(24)], psb=(6, 7))
                next(side)
                with ExitStack() as s2:
                    stage_pre(s2, I["x"], xB_in, NT, mod_d[0, 0:1, 0:D], mod_d[0, 0:1, D:2 * D], hT, hB, 0, "a", side=side)
                    kb.barrier()
                if upto >= 2:
                    with ExitStack() as s2:
                        stage_pre(s2, I["ctx"], xB_in, CTX // 128, mod_d[0, 1:2, 0:D], mod_d[0, 1:2, D:2 * D], hT, hB, SEQ, "b", side=side)
                        kb.barrier()
                if "hT_d" in dbg:
                    kb.dma("sp", hT_d.rearrange("c p n -> p c n"), hT[:], reads=[hB], writes=[B_["mixT"]])
                if upto >= 3:
                    with ExitStack() as s2:
                        stage_inproj_ab(s2, hT, hB, side)
                        kb.barrier()
                for _ in side:
                    pass
                kb.barrier()
        if upto >= 4:
            with ExitStack() as st:
                mixT = sb(st, "mixT", [128, NC_, SEQ], BF16)
                mB = kb.buf()
                with ExitStack() as s2:
                    stage_attn(s2, mixT, mB)
                    kb.barrier()
                if upto >= 5:
                    with ExitStack() as s2:
                        stage_fourier(s2, mixT, mB)
                        kb.barrier()
                if "mixT_d" in dbg:
                    kb.dma("sp", mixT_d.rearrange("c p n -> p c n"), mixT[:], reads=[mB], writes=[B_["mixT"]])
                if upto >= 6:
                    with ExitStack() as s2:
                        stage_gemm_out(s2, mixT, mB, I["ab_w_out"][0], "go")
                        kb.barrier()
                kb.barrier()

        def moe_block(l, xsrc_d, xsrcB, dst_d, dstB, nexp=NE):
            with ExitStack() as so:
                idxf_all = sb(so, "idxf_all", [128, 2 * NE], F32)
                idxB = kb.buf()
                with ExitStack() as st:
                    aff = sb(st, "aff", [128, NT, NE], F32)
                    affB = kb.buf()
                    with ExitStack() as s2:
                        stage_post(s2, l, xsrc_d, xsrcB, aff, affB)
                        kb.barrier()
                    if "aff_d" in dbg:
                        kb.dma("sp", aff_d, aff[:].rearrange("p t e -> p (t e)"), reads=[affB], writes=[B_["aff"]])
                    with ExitStack() as s2:
                        stage_moe(s2, l, aff, affB, idxf_all, idxB, nexp)
                        kb.barrier()
                    kb.barrier()
                with ExitStack() as s2:
                    toks = stage_combine(s2, l, idxf_all, idxB, dst_d, dstB, nexp)
                    kb.barrier()
                kb.barrier()
            return toks

        out_toks = []
        if upto >= 7:
            moe_block(0, I["x"], xB_in, xl1_d, B_["xl1"], nexp=(NE if upto >= 8 else 1))
        if upto >= 9:
            with ExitStack() as st:
                hT = sb(st, "hT1", [128, NC_, SEQ], BF16)
                hB = kb.buf()
                with ExitStack() as s2:
                    stage_pre(s2, xl1_d, B_["xl1"], NT, mod_d[1, 0:1, 0:D], mod_d[1, 0:1, D:2 * D], hT, hB, 0, "c")
                    kb.barrier()
                with ExitStack() as s2:
                    stage_inproj_cd(s2, hT, hB)
                    kb.barrier()
                kb.barrier()
        if upto >= 10:
            with ExitStack() as st:
                mixT = sb(st, "mixT1", [128, NC_, SEQ], BF16)
                mB = kb.buf()
                with ExitStack() as s2:
                    stage_sg(s2, mixT, mB)
                    kb.barrier()
                with ExitStack() as s2:
                    stage_conv(s2, mixT, mB)
                    kb.barrier()
                if "mixT_d" in dbg:
                    kb.dma("sp", mixT_d.rearrange("c p n -> p c n"), mixT[:], reads=[mB], writes=[B_["mixT"]])
                with ExitStack() as s2:
                    stage_gemm_out(s2, mixT, mB, I["cd_w_out"][0], "go1")
                    kb.barrier()
                kb.barrier()
        if upto >= 11:
            moe_block(1, xl1_d, B_["xl1"], out_d, B_["out"])
        kb.barrier()
    return nc, hc


_CACHE = {}


def make_in_maps(inputs, hc, ncores=8):
    shared = {}
    for k in INPUT_SHAPES:
        if k in ("x", "c", "ctx"):
            continue
        a = np.ascontiguousarray(np.asarray(inputs[k], dtype=np.float32)).reshape(INPUT_SHAPES[k])
        shared[k] = a
    for k, v in hc.items():
        shared["k_" + k] = v
    maps = []
    for b in range(ncores):
        m = dict(shared)
        m["x"] = np.ascontiguousarray(inputs["x"][b], dtype=np.float32)
        m["c"] = np.ascontiguousarray(inputs["c"][b], dtype=np.float32).reshape(1, D)
        m["ctx"] = np.ascontiguousarray(inputs["ctx"][b], dtype=np.float32)
        maps.append(m)
    return maps


def kernel(**inputs):
    if "prog" not in _CACHE:
        _CACHE["prog"] = build_program()
    nc, hc = _CACHE["prog"]
    maps = make_in_maps(inputs, hc, 8)
    res = run_bass_kernel_spmd(nc, maps, core_ids=list(range(8)))
    return np.stack([np.asarray(r["out"], dtype=np.float32) for r in res.results], axis=0)
```
